# Optimizing a Trainium2 kernel written in Bass

```python
import math, functools
import jax, jax.numpy as jnp
from jax import lax
import numpy as np

D_MODEL = 1024
BATCH = 8
SEQ = 2048
DEPTH = 1
DEC_BATCH = 32
DEC_SEQ = 8
PAST_LEN = 16384
PAGE_SIZE = 128

GLA_HEADS = 4
GLA_DK = 64
GLA_DV = 128
GLA_GATE_RANK = 16
GLA_GATE_NORM = 16.0
GLA_CHUNK = 32
MLA_HEADS = 8
MLA_Q_LORA = 256
MLA_KV_LORA = 128
MLA_NOPE = 64
MLA_ROPE = 32
MLA_V = 64
MLA_SCALE = (MLA_NOPE + MLA_ROPE) ** -0.5
ROPE_THETA = 10000.0
Q_BLOCK = 128
D_FF = 2816
CONV_W = 3
PLE_DIM = 256
EPS = 1e-6
NEG = -1e30

IN_SIZES = (GLA_HEADS * GLA_DK, GLA_HEADS * GLA_DK, GLA_HEADS * GLA_DV, GLA_HEADS * GLA_DV,
            GLA_GATE_RANK, MLA_Q_LORA, MLA_KV_LORA, MLA_ROPE)
IN_COLS = sum(IN_SIZES)
MIX_OUT = GLA_HEADS * GLA_DV + MLA_HEADS * MLA_V

kernel_name = "hymba_gla_mla_convffn_ple_step"


def rmsnorm(x, g):
    xf = x.astype(jnp.float32)
    y = xf * lax.rsqrt(jnp.mean(xf * xf, axis=-1, keepdims=True) + EPS)
    return (y * g.astype(jnp.float32)).astype(x.dtype)


def rope(x, pos):
    half = MLA_ROPE // 2
    inv = ROPE_THETA ** (-jnp.arange(half, dtype=jnp.float32) * 2.0 / MLA_ROPE)
    ang = pos.astype(jnp.float32)[:, None] * inv[None, :]
    shape = (1, pos.shape[0]) + (1,) * (x.ndim - 3) + (half,)
    cos = jnp.cos(ang).reshape(shape).astype(x.dtype)
    sin = jnp.sin(ang).reshape(shape).astype(x.dtype)
    x1, x2 = x[..., :half], x[..., half:]
    return jnp.concatenate([x1 * cos - x2 * sin, x2 * cos + x1 * sin], axis=-1)


def gla_chunked(q, k, v, log_a, s0):
    B, T, H, DK = q.shape
    DV = v.shape[-1]
    L = math.gcd(T, GLA_CHUNK)
    N = T // L
    f32 = jnp.float32
    q = q.astype(f32).reshape(B, N, L, H, DK)
    k = k.astype(f32).reshape(B, N, L, H, DK)
    v = v.astype(f32).reshape(B, N, L, H, DV)
    b = jnp.cumsum(log_a.astype(f32).reshape(B, N, L, H, DK), axis=2)
    b_last = b[:, :, -1:]
    q_dec = q * jnp.exp(b)
    k_inv = k * jnp.exp(-b)
    k_end = k * jnp.exp(b_last - b)
    causal = jnp.tril(jnp.ones((L, L), dtype=bool))
    A = jnp.einsum('bnthd,bnshd->bnhts', q_dec, k_inv)
    A = jnp.where(causal, A, 0.0)
    o_intra = jnp.einsum('bnhts,bnshv->bnthv', A, v)

    def step(S, xs):
        qd, ke, vv, bl = xs
        o = jnp.einsum('bthd,bhdv->bthv', qd, S)
        S = jnp.exp(bl)[..., None] * S + jnp.einsum('bshd,bshv->bhdv', ke, vv)
        return S, o

    xs = (q_dec.swapaxes(0, 1), k_end.swapaxes(0, 1), v.swapaxes(0, 1),
          b_last[:, :, 0].swapaxes(0, 1))
    S, o_inter = lax.scan(step, s0.astype(f32), xs)
    o = o_intra + o_inter.swapaxes(0, 1)
    return o.reshape(B, T, H, DV), S


def latent_to_kv(ckv, krope, w_kvup, g_kn):
    kv = (ckv @ w_kvup).reshape(ckv.shape[:-1] + (MLA_HEADS, MLA_NOPE + MLA_V))
    k_nope = rmsnorm(kv[..., :MLA_NOPE], g_kn)
    v = kv[..., MLA_NOPE:]
    kr = jnp.broadcast_to(krope[..., None, :], krope.shape[:-1] + (MLA_HEADS, MLA_ROPE))
    k = jnp.concatenate([k_nope, kr.astype(k_nope.dtype)], axis=-1)
    return k, v


def prompt_attention(q, k, v):
    B, S, H, DQK = q.shape
    nb = S // Q_BLOCK
    qb = q.reshape(B, nb, Q_BLOCK, H, DQK).transpose(1, 0, 2, 3, 4)
    starts = jnp.arange(nb) * Q_BLOCK
    kpos = jnp.arange(S)

    def one(args):
        qi, start = args
        s = jnp.einsum('bqhd,bkhd->bhqk', qi, k, preferred_element_type=jnp.float32) * MLA_SCALE
        qpos = start + jnp.arange(Q_BLOCK)
        s = jnp.where(kpos[None, :] <= qpos[:, None], s, NEG)
        pr = jax.nn.softmax(s, axis=-1)
        return jnp.einsum('bhqk,bkhd->bqhd', pr.astype(v.dtype), v)

    out = lax.map(one, (qb, starts))
    return out.transpose(1, 0, 2, 3, 4).reshape(B, S, H * MLA_V)


def block_stats(q, k, v, mask):
    s = jnp.einsum('bthd,bkhd->bhtk', q, k, preferred_element_type=jnp.float32) * MLA_SCALE
    if mask is not None:
        s = jnp.where(mask, s, NEG)
    m = jnp.max(s, axis=-1)
    p = jnp.exp(s - m[..., None])
    l = jnp.sum(p, axis=-1)
    acc = jnp.einsum('bhtk,bkhd->bhtd', p, v.astype(jnp.float32))
    return m, l, acc


def combine(a, b):
    m1, l1, acc1 = a
    m2, l2, acc2 = b
    m = jnp.maximum(m1, m2)
    c1 = jnp.exp(m1 - m)
    c2 = jnp.exp(m2 - m)
    return m, l1 * c1 + l2 * c2, acc1 * c1[..., None] + acc2 * c2[..., None]


def sample_attention(q, k_new, v_new, pool_ckv, pool_krope, page_table, w_kvup, g_kn):
    Bd, T = q.shape[0], q.shape[1]
    causal = jnp.arange(T)[None, :] <= jnp.arange(T)[:, None]
    stats = block_stats(q, k_new, v_new, causal)

    def body(carry, phys):
        c = pool_ckv[phys]
        r = pool_krope[phys]
        k, v = latent_to_kv(c, r, w_kvup, g_kn)
        return combine(carry, block_stats(q, k, v, None)), None

    (m, l, acc), _ = lax.scan(body, stats, page_table.T)
    out = acc / l[..., None]
    return out.transpose(0, 2, 1, 3).reshape(Bd, T, MLA_HEADS * MLA_V).astype(q.dtype)


def conv_ffn(n, prev, lw):
    a = n @ lw['ffn_w_gate']
    c = n @ lw['ffn_w_up']
    T = a.shape[1]
    full = jnp.concatenate([prev.astype(a.dtype), a], axis=1)
    w = lw['ffn_conv_w']
    conv = lw['ffn_conv_b'] + sum(w[j] * full[:, j:j + T] for j in range(CONV_W))
    y = (jax.nn.silu(conv) * c) @ lw['ffn_w_down']
    return y, full[:, full.shape[1] - (CONV_W - 1):]


def layer_forward(h, p, pos, gla_s0, conv_prev, attend, lw):
    B, T, _ = h.shape
    n = rmsnorm(h, lw['g_mix'])
    u = n @ lw['w_in']
    split_points = [int(s) for s in np.cumsum(IN_SIZES)[:-1]]
    gq, gk, gv, gg, ga, mq, mkv, mkr = jnp.split(u, split_points, axis=-1)
    q = gq.reshape(B, T, GLA_HEADS, GLA_DK) * (GLA_DK ** -0.5)
    k = gk.reshape(B, T, GLA_HEADS, GLA_DK)
    v = gv.reshape(B, T, GLA_HEADS, GLA_DV)
    z = (ga @ lw['gla_w_a2'] + lw['gla_b_a']).astype(jnp.float32)
    log_a = (jax.nn.log_sigmoid(z) / GLA_GATE_NORM).reshape(B, T, GLA_HEADS, GLA_DK)
    o_gla, s_new = gla_chunked(q, k, v, log_a, gla_s0)
    o_gla = rmsnorm(o_gla.astype(h.dtype), lw['gla_g_out']).reshape(B, T, -1) * jax.nn.silu(gg)
    cq = rmsnorm(mq, lw['mla_g_qa'])
    qf = (cq @ lw['mla_w_qup']).reshape(B, T, MLA_HEADS, MLA_NOPE + MLA_ROPE)
    q_nope = rmsnorm(qf[..., :MLA_NOPE], lw['mla_g_qn'])
    q_rope = rope(rmsnorm(qf[..., MLA_NOPE:], lw['mla_g_qr']), pos)
    q_m = jnp.concatenate([q_nope, q_rope], axis=-1)
    ckv = rmsnorm(mkv, lw['mla_g_kva'])
    krope = rope(rmsnorm(mkr, lw['mla_g_kr']), pos)
    k_m, v_m = latent_to_kv(ckv, krope, lw['mla_w_kvup'], lw['mla_g_kn'])
    o_mla = attend(q_m, k_m, v_m).astype(h.dtype)
    h = h + jnp.concatenate([o_gla, o_mla], axis=-1) @ lw['w_o']
    y, conv_new = conv_ffn(rmsnorm(h, lw['g_ffn']), conv_prev, lw)
    h = h + y
    h = h + (p.astype(h.dtype) @ lw['ple_w_proj']) * jax.nn.sigmoid(rmsnorm(h, lw['g_ple']) @ lw['ple_w_gate'])
    return h, ckv, krope, s_new, conv_new


def setup_inputs(seed: int = 0) -> dict:
    key = jax.random.key(seed)
    ks = jax.random.split(key, 40)
    f32 = jnp.float32
    n_pages = PAST_LEN // PAGE_SIZE
    n_phys = (DEC_BATCH * n_pages * 5) // 4

    def nrm(k, shape, scale):
        return jax.random.normal(k, shape, f32) * scale

    def gain(k, n):
        return 1.0 + 0.05 * jax.random.normal(k, (DEPTH, n), f32)

    page_table = jax.random.permutation(ks[0], n_phys)[:DEC_BATCH * n_pages]
    page_table = page_table.reshape(DEC_BATCH, n_pages).astype(jnp.int32)
    return {
        'x_prompt': nrm(ks[1], (BATCH, SEQ, D_MODEL), 1.0),
        'x_sample': nrm(ks[2], (DEC_BATCH, DEC_SEQ, D_MODEL), 1.0),
        'cache_ckv': nrm(ks[3], (DEPTH, n_phys, PAGE_SIZE, MLA_KV_LORA), 1.0),
        'cache_krope': nrm(ks[4], (DEPTH, n_phys, PAGE_SIZE, MLA_ROPE), 1.0),
        'state_gla': nrm(ks[5], (DEPTH, DEC_BATCH, GLA_HEADS, GLA_DK, GLA_DV), 0.3),
        'state_conv': nrm(ks[6], (DEPTH, DEC_BATCH, CONV_W - 1, D_FF), 1.0),
        'page_table': page_table,
        'p_prompt': nrm(ks[7], (DEPTH, BATCH, SEQ, PLE_DIM), 1.0),
        'p_sample': nrm(ks[8], (DEPTH, DEC_BATCH, DEC_SEQ, PLE_DIM), 1.0),
        'g_mix': gain(ks[9], D_MODEL),
        'w_in': nrm(ks[10], (DEPTH, D_MODEL, IN_COLS), D_MODEL ** -0.5),
        'gla_w_a2': nrm(ks[11], (DEPTH, GLA_GATE_RANK, GLA_HEADS * GLA_DK), GLA_GATE_RANK ** -0.5),
        'gla_b_a': nrm(ks[12], (DEPTH, GLA_HEADS * GLA_DK), 0.5),
        'gla_g_out': gain(ks[13], GLA_DV),
        'mla_g_qa': gain(ks[14], MLA_Q_LORA),
        'mla_w_qup': nrm(ks[15], (DEPTH, MLA_Q_LORA, MLA_HEADS * (MLA_NOPE + MLA_ROPE)), MLA_Q_LORA ** -0.5),
        'mla_g_qn': gain(ks[16], MLA_NOPE),
        'mla_g_qr': gain(ks[17], MLA_ROPE),
        'mla_g_kva': gain(ks[18], MLA_KV_LORA),
        'mla_g_kr': gain(ks[19], MLA_ROPE),
        'mla_w_kvup': nrm(ks[20], (DEPTH, MLA_KV_LORA, MLA_HEADS * (MLA_NOPE + MLA_V)), MLA_KV_LORA ** -0.5),
        'mla_g_kn': gain(ks[21], MLA_NOPE),
        'w_o': nrm(ks[22], (DEPTH, MIX_OUT, D_MODEL), MIX_OUT ** -0.5),
        'g_ffn': gain(ks[23], D_MODEL),
        'ffn_w_gate': nrm(ks[24], (DEPTH, D_MODEL, D_FF), D_MODEL ** -0.5),
        'ffn_w_up': nrm(ks[25], (DEPTH, D_MODEL, D_FF), D_MODEL ** -0.5),
        'ffn_conv_w': nrm(ks[26], (DEPTH, CONV_W, D_FF), CONV_W ** -0.5),
        'ffn_conv_b': nrm(ks[27], (DEPTH, D_FF), 0.02),
        'ffn_w_down': nrm(ks[28], (DEPTH, D_FF, D_MODEL), D_FF ** -0.5),
        'g_ple': gain(ks[29], D_MODEL),
        'ple_w_gate': nrm(ks[30], (DEPTH, D_MODEL, D_MODEL), D_MODEL ** -0.5),
        'ple_w_proj': nrm(ks[31], (DEPTH, PLE_DIM, D_MODEL), PLE_DIM ** -0.5),
    }


def reference(x_prompt, x_sample, cache_ckv, cache_krope, state_gla, state_conv, page_table,
              p_prompt, p_sample, g_mix, w_in, gla_w_a2, gla_b_a, gla_g_out, mla_g_qa, mla_w_qup,
              mla_g_qn, mla_g_qr, mla_g_kva, mla_g_kr, mla_w_kvup, mla_g_kn, w_o, g_ffn,
              ffn_w_gate, ffn_w_up, ffn_conv_w, ffn_conv_b, ffn_w_down, g_ple, ple_w_gate,
              ple_w_proj):
    B, S = x_prompt.shape[0], x_prompt.shape[1]
    pos_p = jnp.arange(S)
    pos_s = PAST_LEN + jnp.arange(x_sample.shape[1])
    y_p, y_s = x_prompt, x_sample
    ckv_p, kr_p, gla_p, conv_p = [], [], [], []
    ckv_s, kr_s, gla_s, conv_s = [], [], [], []
    for i in range(DEPTH):
        lw = dict(g_mix=g_mix[i], w_in=w_in[i], gla_w_a2=gla_w_a2[i], gla_b_a=gla_b_a[i],
                  gla_g_out=gla_g_out[i], mla_g_qa=mla_g_qa[i], mla_w_qup=mla_w_qup[i],
                  mla_g_qn=mla_g_qn[i], mla_g_qr=mla_g_qr[i], mla_g_kva=mla_g_kva[i],
                  mla_g_kr=mla_g_kr[i], mla_w_kvup=mla_w_kvup[i], mla_g_kn=mla_g_kn[i],
                  w_o=w_o[i], g_ffn=g_ffn[i], ffn_w_gate=ffn_w_gate[i], ffn_w_up=ffn_w_up[i],
                  ffn_conv_w=ffn_conv_w[i], ffn_conv_b=ffn_conv_b[i], ffn_w_down=ffn_w_down[i],
                  g_ple=g_ple[i], ple_w_gate=ple_w_gate[i], ple_w_proj=ple_w_proj[i])
        s0 = jnp.zeros((B, GLA_HEADS, GLA_DK, GLA_DV), jnp.float32)
        c0 = jnp.zeros((B, CONV_W - 1, D_FF), x_prompt.dtype)
        y_p, ckv, kr, s_new, c_new = layer_forward(y_p, p_prompt[i], pos_p, s0, c0,
                                                   prompt_attention, lw)
        ckv_p.append(ckv)
        kr_p.append(kr)
        gla_p.append(s_new.astype(x_prompt.dtype))
        conv_p.append(c_new)
        attend_s = functools.partial(sample_attention, pool_ckv=cache_ckv[i],
                                     pool_krope=cache_krope[i], page_table=page_table,
                                     w_kvup=lw['mla_w_kvup'], g_kn=lw['mla_g_kn'])
        y_s, ckv, kr, s_new, c_new = layer_forward(y_s, p_sample[i], pos_s, state_gla[i],
                                                   state_conv[i], attend_s, lw)
        ckv_s.append(ckv)
        kr_s.append(kr)
        gla_s.append(s_new.astype(state_gla.dtype))
        conv_s.append(c_new.astype(state_conv.dtype))
    return (y_p, y_s, jnp.stack(ckv_p), jnp.stack(kr_p), jnp.stack(gla_p), jnp.stack(conv_p),
            jnp.stack(ckv_s), jnp.stack(kr_s), jnp.stack(gla_s), jnp.stack(conv_s))
```

```python
import numpy as np
import ml_dtypes
import concourse.bass as bass
import concourse.mybir as mybir
from concourse.bass_utils import run_bass_kernel_spmd

F32 = mybir.dt.float32
BF16 = mybir.dt.bfloat16
I32 = mybir.dt.int32
AF = mybir.ActivationFunctionType
ALU = mybir.AluOpType
AX = mybir.AxisListType

D = 1024
NT = 17
ST = 16
DFF = 2816
NF = 22
INC = 1968
EPS = 1e-6
MLA_SCALE = 96.0 ** -0.5
NPG = 128
O_Q, O_K, O_V, O_G, O_A, O_MQ = 0, 256, 512, 1024, 1536, 1552


class Sched:
    def __init__(self, sems):
        self.sems = sems
        nxt = iter(range(len(sems)))
        self.q = {e: [] for e in ('pe', 'act', 'dve', 'pool', 'sp')}
        self.cnt = {e: 0 for e in ('pe', 'act', 'dve', 'pool')}
        self.csem = {e: next(nxt) for e in ('pe', 'act', 'dve', 'pool')}
        self.dsems = {'sp': [next(nxt) for _ in range(24)], 'pool': [next(nxt) for _ in range(4)],
                      'act': [next(nxt) for _ in range(8)]}
        self.dcnt = {'sp': 0, 'pool': 0, 'act': 0}
        self.lastw = {}
        self.readers = {}
        self.waited = {e: {} for e in self.q}
        self.all_tokens = {}
        self._bar = None

    def op(self, eng, fn, reads=(), writes=(), dma=False):
        deps = {}

        def add(tok):
            if tok is None:
                return
            for s, v in tok.items() if isinstance(tok, dict) else [tok]:
                if deps.get(s, 0) < v:
                    deps[s] = v
        if self._bar:
            add(self._bar)
        for k in reads:
            add(self.lastw.get(k))
        for k in writes:
            add(self.lastw.get(k))
            add(self.readers.get(k))
        if dma:
            pool = self.dsems[eng]
            i = self.dcnt[eng]
            self.dcnt[eng] += 1
            sem = pool[i % len(pool)]
            val = 16 * (i // len(pool) + 1)
            add((sem, val - 16))
            inc = 16
        else:
            self.cnt[eng] += 1
            sem = self.csem[eng]
            val = self.cnt[eng]
            inc = 1
        waits = []
        w = self.waited[eng]
        for s, v in deps.items():
            if v <= 0:
                continue
            if eng == 'pe' and s == self.csem['pe']:
                continue
            if w.get(s, 0) >= v:
                continue
            w[s] = v
            waits.append((s, v))
        self.q[eng].append((waits, fn, sem, inc))
        tok = (sem, val)
        self.all_tokens[sem] = max(self.all_tokens.get(sem, 0), val)
        for k in reads:
            r = self.readers.setdefault(k, {})
            if r.get(sem, 0) < val:
                r[sem] = val
        for k in writes:
            self.lastw[k] = tok
            self.readers[k] = {}
        return tok

    def barrier(self):
        snap = dict(self.all_tokens)
        for k in list(self.lastw.keys()):
            self.lastw[k] = snap
            self.readers[k] = {}
        self._bar = snap

    def emit(self, engname, engine):
        for waits, fn, sem, inc in self.q[engname]:
            for s, v in waits:
                engine.wait_ge(self.sems[s], v)
            ins = fn(engine)
            ins.then_inc(self.sems[sem], inc)

    def final_wait(self, engine):
        for s, v in self.all_tokens.items():
            engine.wait_ge(self.sems[s], v)


class _Stop(Exception):
    pass


import os
_STOP = float(os.environ.get("KSTOP", "0")) or None
_SKIP_PA = bool(os.environ.get("KSKIP"))
_SMALLPA = bool(os.environ.get("KSMALLPA"))
_NPHYS = 8 if (_SMALLPA or _SKIP_PA or (_STOP is not None and _STOP <= 7)) else 5120


_HOOK = [None]


def chk(stage, active=True):
    if not active:
        return
    if _STOP is not None and stage >= _STOP:
        if _HOOK[0] is not None and os.environ.get("KDBG") == "5":
            _HOOK[0]()
        raise _Stop()


def build_program():
    nc = bass.Bass("TRN2", target_bir_lowering=False)
    S = None
    try:
        S = _build_body(nc)
    except _Stop as e:
        S = e.args[0] if e.args else S
    _emit(nc, _S_HOLD[0])
    return nc


_S_HOLD = [None]


def _emit(nc, S):
    with nc.Block() as block:
        @block.sync
        def _(e):
            S.emit('sp', e)
            S.final_wait(e)

        @block.tensor
        def _(e):
            S.emit('pe', e)

        @block.scalar
        def _(e):
            S.emit('act', e)

        @block.vector
        def _(e):
            S.emit('dve', e)

        @block.gpsimd
        def _(e):
            S.emit('pool', e)


def _build_body(nc):

    def din(name, shape, dt=F32):
        return nc.dram_tensor(name, list(shape), dt, kind="ExternalInput").ap()

    def dout(name, shape, dt=F32):
        return nc.dram_tensor(name, list(shape), dt, kind="ExternalOutput").ap()

    x_d = din("x", [NT, 128, D])
    p_d = din("p", [NT, 128, 256])
    ckv_d = din("cache_ckv", [_NPHYS * 128, 128])
    ckr_d = din("cache_krope", [_NPHYS * 128, 32])
    sgla_d = din("state_gla", [4, 2, 128, 128])
    sconv_d = din("state_conv", [8, DFF])
    pt_d = din("page_table", [512], I32)
    w_in_d = din("w_in", [D, INC])
    wa2_d = din("gla_w_a2", [16, 256])
    ba_d = din("gla_b_a", [256])
    gout_d = din("gla_g_out", [128])
    gqa_d = din("mla_g_qa", [256])
    wqup_d = din("mla_w_qup", [256, 768])
    gqn_d = din("mla_g_qn", [64])
    gqr_d = din("mla_g_qr", [32])
    gkva_d = din("mla_g_kva", [128])
    gkr_d = din("mla_g_kr", [32])
    wkv_d = din("mla_w_kvup", [128, 1024])
    gkn_d = din("mla_g_kn", [64])
    wo_d = din("w_o", [D, D])
    gmix_d = din("g_mix", [D])
    gffn_d = din("g_ffn", [D])
    gple_d = din("g_ple", [D])
    wg_d = din("ffn_w_gate", [D, DFF])
    wu_d = din("ffn_w_up", [D, DFF])
    cw_d = din("ffn_conv_w", [3, DFF])
    cb_d = din("ffn_conv_b", [DFF])
    wd_d = din("ffn_w_down", [DFF, D])
    wpg_d = din("ple_w_gate", [D, D])
    wpp_d = din("ple_w_proj", [256, D])
    gqnb_d = din("b_gqn", [128, 64])
    gqrb_d = din("b_gqr", [128, 32])
    gkvab_d = din("b_gkva", [128, 128])
    gkrb_d = din("b_gkr", [128, 32])
    gknb_d = din("b_gkn", [128, 64])
    ptb_d = din("b_pt", [128, 512], I32)
    c_ident_d = din("c_ident", [128, 128], BF16)
    c_identf_d = din("c_identf", [128, 128])
    c_mask_d = din("c_mask", [128, 128], BF16)
    c_masks_d = din("c_masks", [128, 128], BF16)
    c_maska_d = din("c_maska", [128, 4, 64], BF16)
    c_cos_d = din("c_cos", [128, NT, 16])
    c_sin_d = din("c_sin", [128, NT, 16])
    c_rs_d = din("c_rs", [128, 2, 128])
    c_iota_d = din("c_iota", [128, 1])
    c_sel_d = din("c_sel", [128, 128], BF16)
    c_rowm_d = din("c_rowm", [128, 4])
    c_hm_d = din("c_hm", [128, 2])

    y_o = dout("y", [NT, 128, D])
    ckv_o = dout("ckv_o", [NT, 128, 128])
    kr_o = dout("kr_o", [NT, 128, 32])
    glap_o = dout("gla_p", [2, 128, 128])
    glas_o = dout("gla_s", [4, 2, 128, 128])
    convp_o = dout("conv_p", [2, DFF])
    convs_o = dout("conv_s", [8, DFF])

    sems = [nc.alloc_semaphore(f"s{i}") for i in range(60)]
    S = Sched(sems)
    _S_HOLD[0] = S

    class Arena:
        def __init__(self):
            self.base = 16512
            self.top = 229344
            self.off = self.base
            self.n = 0

        def alloc(self, name, shape, dt):
            sz = 1
            for s in shape[1:]:
                sz *= s
            sz *= mybir.dt.size(dt)
            sz = (sz + 31) // 32 * 32
            assert self.off + sz <= self.top, f"SBUF overflow at {name}: {self.off + sz}"
            self.n += 1
            t = nc.alloc_sbuf_tensor_at(f"{name}_{self.n}", list(shape), dt, offset=self.off)
            self.off += sz
            return t

    A = Arena()

    ptr = [nc.alloc_psum_tensor(f"ptr{i}", [128, 1024], BF16) for i in range(2)]
    pm = [nc.alloc_psum_tensor(f"pm{i}", [128, 512], F32) for i in range(2)]
    pw = [nc.alloc_psum_tensor(f"pw{i}", [128, 1024], F32) for i in range(2)]

    def dma(eng, out, in_, reads=(), writes=(), **kw):
        if out.dtype != in_.dtype:
            eng = 'pool'
        return S.op(eng, lambda e: e.dma_start(out=out, in_=in_, **kw), reads, writes, dma=True)

    def mm(out, lhsT, rhs, start, stop, reads, writes, **kw):
        return S.op('pe', lambda e: e.matmul(out, lhsT, rhs, start=start, stop=stop, **kw), reads, writes)

    def tr(out, in_, ident, reads, writes):
        return S.op('pe', lambda e: e.transpose(out, in_, ident), reads, writes)

    def act(out, in_, func, reads, writes, **kw):
        return S.op('act', lambda e: e.activation(out, in_, func, **kw), reads, writes)

    def tt(eng, out, in0, in1, op, reads, writes):
        return S.op(eng, lambda e: e.tensor_tensor(out, in0, in1, op), reads, writes)

    def ts(eng, out, in0, s1, s2, op0, op1, reads, writes):
        if s2 is None:
            return S.op(eng, lambda e: e.tensor_scalar(out, in0, s1, None, op0), reads, writes)
        return S.op(eng, lambda e: e.tensor_scalar(out, in0, s1, s2, op0, op1), reads, writes)

    def stt(out, in0, sc, in1, op0, op1, reads, writes):
        return S.op('dve', lambda e: e.scalar_tensor_tensor(out, in0, sc, in1, op0, op1), reads, writes)

    def cp(eng, out, in_, reads, writes):
        if eng == 'act':
            return S.op('act', lambda e: e.copy(out, in_), reads, writes)
        return S.op(eng, lambda e: e.tensor_copy(out, in_), reads, writes)

    def memset(eng, ap, val, writes):
        return S.op(eng, lambda e: e.memset(ap, val), (), writes)

    def bc_last(ap, n):
        sh = list(ap.shape)
        return ap.unsqueeze(len(sh)).broadcast_to(sh + [n])

    def bc_mid(ap, n):
        sh = list(ap.shape)
        return ap.unsqueeze(1).broadcast_to([sh[0], n] + sh[1:])

    ident = A.alloc("ident", [128, 128], BF16)
    identf = A.alloc("identf", [128, 128], F32)
    mask01 = A.alloc("mask01", [128, 128], BF16)
    masks = A.alloc("masks", [128, 128], BF16)
    maska = A.alloc("maska", [128, 4, 64], BF16)
    ones_bf = A.alloc("ones_bf", [128, 128], BF16)
    eps_c = A.alloc("eps_c", [128, 1], F32)
    gmixT = A.alloc("gmixT", [128, 8], F32)
    gffnT = A.alloc("gffnT", [128, 8], F32)
    gpleT = A.alloc("gpleT", [128, 8], F32)
    ogT = A.alloc("ogT", [128, 4, NT * 128], BF16)
    omT = A.alloc("omT", [128, 4, NT * 128], BF16)
    stat = A.alloc("stat", [128, 8], F32)
    xs = A.alloc("xs", [128, D], BF16)
    persist_off = A.off

    dma('sp', ident[:, :], c_ident_d, (), ['ident'])
    def _hook():
        S.barrier()
        dma('sp', y_o[14][:, 0:128], identf[:, :], ['identf'], ['y_o'])
    _HOOK[0] = _hook
    if os.environ.get("KDBG") == "5":
        dma('sp', identf[:, :], c_identf_d, (), ['identf'])
        dma('sp', y_o[15][:, 0:128], identf[:, :], ['identf'], ['y_o'])
    dma('sp', identf[:, :], c_identf_d, (), ['identf'])
    dma('sp', mask01[:, :], c_mask_d, (), ['mask01'])
    dma('sp', masks[:, :], c_masks_d, (), ['masks'])
    dma('sp', maska[:, :, :], c_maska_d, (), ['maska'])
    for gt, gd, k in ((gmixT, gmix_d, 'gmixT'), (gffnT, gffn_d, 'gffnT'), (gpleT, gple_d, 'gpleT')):
        dma('sp', gt[:, :], gd.rearrange("(c p) -> p c", p=128), (), [k], allow_slow_non_contiguous=True)
    memset('dve', ones_bf[:, :], 1.0, ['ones'])
    memset('dve', eps_c[:, :], EPS, ['eps'])

    def rstd_chain(ms_ap, out_ap, inv_n, rk, wk):
        act(out_ap, ms_ap, AF.Ln, list(rk) + ['eps'], wk, bias=eps_c[0:ms_ap.shape[0], 0:1], scale=inv_n)
        act(out_ap, out_ap, AF.Exp, wk, wk, scale=-0.5)

    def norm_tile(src, src_keys, gT, gkey, dst, dst_keys, pslot):
        act(xs[:, :], src, AF.Square, src_keys, ['xs', 'stat'], accum_out=stat[:, 0:1])
        rstd_chain(stat[:, 0:1], stat[:, 1:2], 1.0 / D, ['stat'], ['stat'])
        ts('dve', xs[:, :], src, stat[:, 1:2], None, ALU.mult, None, list(src_keys) + ['stat'], ['xs'])
        pk = f'ptr{pslot}'
        for k in range(8):
            tr(ptr[pslot][:, k * 128:(k + 1) * 128], xs[:, k * 128:(k + 1) * 128], ident[:, :],
               ['xs', 'ident'], [pk])
        tt('dve', dst, ptr[pslot][:, :].rearrange("p (c t) -> p c t", c=8), bc_last(gT[:, :], 128),
           ALU.mult, [pk, gkey], dst_keys)

    w_in_sb = A.alloc("w_in", [128, 8, INC], BF16)
    wqup_sb = A.alloc("wqup", [128, 2, 768], BF16)
    wkv_sb = A.alloc("wkv", [128, 1024], BF16)
    wkn_sb = A.alloc("wkn", [128, 512], BF16)
    wa2_sb = A.alloc("wa2", [16, 256], F32)
    nba = A.alloc("nba", [128, 2], F32)
    goutc = A.alloc("goutc", [128, 1], F32)
    gqa_c = A.alloc("gqa", [128, 2], F32)
    gknc = A.alloc("gknc", [128, 1], F32)
    Gqn = A.alloc("Gqn", [128, 64], F32)
    Gqr = A.alloc("Gqr", [128, 32], F32)
    Gkva = A.alloc("Gkva", [128, 128], F32)
    Gkr = A.alloc("Gkr", [128, 32], F32)
    Gkn = A.alloc("Gkn", [128, 64], F32)
    cos_sb = A.alloc("cos", [128, NT, 16], F32)
    sin_sb = A.alloc("sin", [128, NT, 16], F32)
    rs_sb = A.alloc("rs", [128, 2, 128], F32)
    iota_sb = A.alloc("iota", [128, 1], F32)
    x_sb = A.alloc("x_sb", [128, 2, D], F32)
    nT = A.alloc("nT", [128, 2, 8, 128], BF16)
    gaT = A.alloc("gaT", [16, 128], F32)
    sp_ = A.alloc("sp", [128, 2, 128], F32)
    bpos = A.alloc("bpos", [128, 2, 128], F32)
    etmp = A.alloc("etmp", [128, 2, 128], F32)
    nbl = A.alloc("nbl", [128, 8], F32)
    ebl = A.alloc("ebl", [128, 8], F32)
    qdT = A.alloc("qdT", [128, 4, 128], BF16)
    ke_z = A.alloc("ke_z", [128, 4, 256], BF16)
    rowm = A.alloc("rowm", [128, 4], F32)
    hm = A.alloc("hm", [128, 2], F32)
    wvp_sb = A.alloc("wvp", [128, 4, 128], BF16)
    kiT = A.alloc("kiT", [128, 2, 128], BF16)
    keT = A.alloc("keT", [128, 2, 128], BF16)
    ke_tok = A.alloc("ke_tok", [128, 256], BF16)
    v_tok = A.alloc("v_tok", [128, 512], BF16)
    sgT = A.alloc("sgT", [128, 4, 128], BF16)
    AT = A.alloc("AT", [128, 4, 128], BF16)
    S_f = A.alloc("S_f", [128, 2, 128], F32)
    S_b = A.alloc("S_b", [128, 2, 128], BF16)
    S0_f = A.alloc("S0_f", [128, 4, 2, 128], F32)
    S0_b = A.alloc("S0_b", [128, 4, 2, 128], BF16)
    osq = A.alloc("osq", [128, 512], BF16)
    orst = A.alloc("orst", [128, 512], F32)
    u_sb = A.alloc("u_sb", [128, 416], F32)
    ms = A.alloc("ms", [128, 32], F32)
    rst = A.alloc("rst", [128, 32], F32)
    cq = A.alloc("cq", [128, 256], BF16)
    cqT = A.alloc("cqT", [128, 2, 128], BF16)
    ckv_f = A.alloc("ckv_f", [128, 2, 128], F32)
    ckv_b = A.alloc("ckv_b", [128, 132], BF16)
    ckvT = A.alloc("ckvT", [128, 128], BF16)
    sqq = A.alloc("sqq", [128, 768], F32)
    sqk = A.alloc("sqk", [128, 8, 64], F32)
    Rn = A.alloc("Rn", [128, 9, 32], F32)
    Rr = A.alloc("Rr", [128, 2, 9, 32], F32)
    rt1 = A.alloc("rt1", [128, 9, 16], F32)
    rt2 = A.alloc("rt2", [128, 9, 16], F32)
    q_tok = A.alloc("q_tok", [128, 8, 64], BF16)
    qr_tok = A.alloc("qr_tok", [128, 8, 32], BF16)
    k_tok = A.alloc("k_tok", [128, 8, 64], BF16)
    kr_tok = A.alloc("kr_tok", [128, 32], BF16)
    QnT = A.alloc("QnT", [128, 8, 128], BF16)
    QrT = A.alloc("QrT", [128, 8, 128], BF16)
    KnT = A.alloc("KnT", [128, 4, NT * 128], BF16)
    KrT = A.alloc("KrT", [128, NT * 128], BF16)
    Vext = A.alloc("Vext", [128, 16, 8, 65], BF16)
    PT = A.alloc("PT", [128, 2, 4, 128], BF16)
    rl = A.alloc("rl", [128, 8], F32)
    o_tok = A.alloc("o_tok", [128, 8, 64], BF16)
    IDX = A.alloc("IDX", [128, 512], I32)
    PTi = A.alloc("PTi", [128, 512], I32)
    Cbuf = A.alloc("Cbuf", [128, 2, 4, 128], F32)
    KRbuf = A.alloc("KRbuf", [128, 2, 4, 32], F32)
    Cb = A.alloc("Cb", [128, 2, 4, 132], BF16)
    KRb = A.alloc("KRb", [128, 2, 4, 32], BF16)
    CT = A.alloc("CT", [128, 2, 4, 128], BF16)
    KRT = A.alloc("KRT", [128, 2, 4, 128], BF16)
    sqs = A.alloc("sqs", [128, 2, 1024], BF16)
    ssum = A.alloc("ssum", [128, 32], F32)
    rstk = A.alloc("rstk", [128, 32], F32)
    rstk2 = A.alloc("rstk2", [128, 2, 32], F32)
    sc1 = A.alloc("sc1", [128, 256], F32)
    sc2 = A.alloc("sc2", [128, 256], F32)
    PTs = A.alloc("PTs", [128, 2, 4, 64], BF16)
    WkT = A.alloc("WkT", [128, 4, 128], BF16)
    qgT = A.alloc("qgT", [128, 8, 128], BF16)
    qabsT = A.alloc("qabsT", [128, 8, 128], BF16)
    acc_sb = A.alloc("acc_sb", [64, 132], F32)
    accn = A.alloc("accn", [128, 128], BF16)
    accnT = A.alloc("accnT", [128, 128], BF16)

    for k in range(8):
        dma('sp', w_in_sb[:, k, :], w_in_d[k * 128:(k + 1) * 128, :], (), ['w_in'])
    dma('sp', wqup_sb[:, :, :], wqup_d.rearrange("(c p) n -> p c n", p=128), (), ['wqup'])
    dma('sp', wkv_sb[:, :], wkv_d, (), ['wkv'])
    dma('sp', wa2_sb[:, :], wa2_d, (), ['wa2'])
    dma('sp', nba[:, :], ba_d.rearrange("(c p) -> p c", p=128), (), ['nba'], allow_slow_non_contiguous=True)
    dma('sp', goutc[:, :], gout_d.rearrange("(p o) -> p o", o=1), (), ['goutc'], allow_slow_non_contiguous=True)
    dma('sp', gqa_c[:, :], gqa_d.rearrange("(c p) -> p c", p=128), (), ['gqa'], allow_slow_non_contiguous=True)
    dma('sp', gknc[0:64, :], gkn_d.rearrange("(p o) -> p o", o=1), (), ['gknc'], allow_slow_non_contiguous=True)
    dma('sp', gknc[64:128, :], gkn_d.rearrange("(p o) -> p o", o=1), (), ['gknc'], allow_slow_non_contiguous=True)
    for gt, gd, k in ((Gqn, gqnb_d, 'Gqn'), (Gqr, gqrb_d, 'Gqr'), (Gkva, gkvab_d, 'Gkva'), (Gkr, gkrb_d, 'Gkr'),
                      (Gkn, gknb_d, 'Gkn')):
        dma('sp', gt[:, :], gd, (), [k])
    dma('sp', cos_sb[:, :, :], c_cos_d, (), ['cos'])
    dma('sp', sin_sb[:, :, :], c_sin_d, (), ['sin'])
    dma('sp', rs_sb[:, :, :], c_rs_d, (), ['rs'])
    dma('sp', iota_sb[:, :], c_iota_d, (), ['iota'])
    dma('sp', PTi[:, :], ptb_d, (), ['PTi'])
    for b in range(4):
        for pr in range(2):
            dma('sp', S0_f[:, b, pr, :], sgla_d[b, pr], (), ['S0_f'])
    ts('dve', nba[:, :], nba[:, :], -1.0, None, ALU.mult, None, ['nba'], ['nba'])
    for c in range(2):
        ts('dve', wqup_sb[:, c, :], wqup_sb[:, c, :], gqa_c[:, c:c + 1], None, ALU.mult, None,
           ['wqup', 'gqa'], ['wqup'])
    cp('dve', wkn_sb[:, :].rearrange("p (h d) -> p h d", h=8),
       wkv_sb[:, :].rearrange("p (h e) -> p h e", h=8)[:, :, 0:64], ['wkv'], ['wkn'])
    memset('dve', S_f[:, :, :], 0.0, ['S_f'])
    memset('dve', nbl[:, :], 0.0, ['nbl'])
    memset('dve', qdT[:, :, :], 0.0, ['qdT'])
    memset('dve', QnT[:, :, :], 0.0, ['QnT'])
    memset('dve', QrT[:, :, :], 0.0, ['QrT'])
    memset('dve', KrT[:, :], 0.0, ['KrT'])
    memset('dve', KRT[:, :, :, :], 0.0, ['KRT0', 'KRT1'])
    memset('dve', accn[:, :], 0.0, ['accn'])
    dma('sp', rowm[:, :], c_rowm_d, (), ['rowm'])
    dma('sp', hm[:, :], c_hm_d, (), ['hm'])
    cp('dve', wvp_sb[:, :, :].rearrange("p c (a d) -> p c a d", a=2),
       wkv_sb[:, :].rearrange("p (c a e) -> p c a e", c=4, a=2)[:, :, :, 64:128], ['wkv'], ['wvp'])
    memset('dve', S_b[:, :, :], 0.0, ['S_b'])
    cp('dve', S0_b[:, :, :, :], S0_f[:, :, :, :], ['S0_f'], ['S0_b'])
    memset('dve', ckv_b[:, 128:132], 1.0, ['ckv_b1'])
    memset('dve', Cb[:, :, :, 128:132], 1.0, ['Cb1'])
    memset('dve', Vext[:, :, :, 64:65], 1.0, ['Vext1'])
    memset('dve', omT[:, :, ST * 128:], 0.0, ['omT'])
    ts('dve', IDX[:, :], PTi[:, :], 128.0, iota_sb[:, 0:1], ALU.mult, ALU.add, ['PTi', 'iota'], ['IDX'])

    def load_x(i):
        dma('sp', x_sb[:, i % 2, :], x_d[i], (), [f'x{i % 2}'])

    if os.environ.get("KDBG") == "2":
        load_x(0)
        dma('sp', y_o[0], x_sb[:, 0, :], ['x0'], ['y_o'])
        dma('pool', y_o[3][:, 0:1024], w_in_sb[:, 0, 0:1024], ['w_in'], ['y_o'])
        dma('sp', y_o[4][:, 0:128], identf[:, :], ['identf'], ['y_o'])
    chk(1)
    load_x(0)

    pmi = [0]

    def next_pm():
        pmi[0] ^= 1
        return pmi[0]

    for i in range(NT):
        samp = (i == ST)
        sl = i % 2
        xk = f'x{sl}'
        nk = f'nT{sl}'
        nTi = nT[:, sl, :, :]
        tcol = slice(i * 128, (i + 1) * 128)
        if i + 1 < NT:
            load_x(i + 1)
        norm_tile(x_sb[:, sl, :], [xk], gmixT, 'gmixT', nTi, [nk], 0)
        if i == 0 and os.environ.get("KDBG") == "3":
            dma('pool', y_o[2][:, 0:1024], nT[:, 0, :, :].rearrange("p c t -> p (c t)"), [nk], ['y_o'])
            dma('pool', y_o[4][:, 0:1024], xs[:, :], ['xs'], ['y_o'])
            dma('sp', y_o[5][:, 0:8], stat[:, :], ['stat'], ['y_o'])
            dma('sp', y_o[0], x_sb[:, 0, :], ['x0'], ['y_o'])
        if i == 0 and os.environ.get("KDBG") == "4":
            stg = x_sb[:, 1, :]
            cp('dve', stg, xs[:, :], ['xs'], ['x1'])
            dma('sp', y_o[4], stg, ['x1'], ['y_o'])
            cp('dve', stg, nT[:, 0, :, :].rearrange("p c t -> p (c t)"), [nk], ['x1'])
            dma('sp', y_o[2], stg, ['x1'], ['y_o'])
            cp('dve', x_sb[:, 1, 0:8], stat[:, :], ['stat'], ['x1'])
            cp('dve', x_sb[:, 1, 8:16], gmixT[:, :], ['gmixT'], ['x1'])
            dma('sp', y_o[5], stg, ['x1'], ['y_o'])
        chk(2, i == 0)

        j = next_pm()
        for k in range(8):
            mm(pm[j][0:16, 0:128], w_in_sb[:, k, O_A:O_A + 16], nTi[:, k, :], k == 0, k == 7,
               ['w_in', nk], [f'pm{j}'])
        cp('act', gaT[:, :], pm[j][0:16, 0:128], [f'pm{j}'], ['gaT'])
        chk(2.1, i == 0)
        j = next_pm()
        for pr in range(2):
            mm(pm[j][:, pr * 128:(pr + 1) * 128], wa2_sb[:, pr * 128:(pr + 1) * 128], gaT[:, :], True, True,
               ['wa2', 'gaT'], [f'pm{j}'])
        for pr in range(2):
            act(sp_[:, pr, :], pm[j][:, pr * 128:(pr + 1) * 128], AF.Exp, [f'pm{j}', 'nba'], ['sp'],
                bias=nba[:, pr:pr + 1], scale=-1.0)
        act(sp_[:, :, :], sp_[:, :, :], AF.Ln, ['sp'], ['sp'], bias=1.0, scale=1.0)
        chk(2.3, i == 0)
        rsi = 1 if samp else 0
        for pr in range(2):
            S.op('dve', lambda e, pr=pr, rsi=rsi: e.tensor_tensor_scan(bpos[:, pr, :], rs_sb[:, rsi, :], sp_[:, pr, :],
                                                              0.0, ALU.mult, ALU.add),
                 ['sp', 'rs'], ['bpos'])
        segs = [(b * 32, 32, b * 32 + 9, b) for b in range(4)] if samp else [(0, 128, 127, 0)]
        chk(2.4, i == 0)
        for (c0, ncol, lc, si) in segs:
            ts('dve', nbl[:, si * 2:si * 2 + 2], bpos[:, :, lc], -1.0 / 16, None, ALU.mult, None,
               ['bpos'], ['nbl'])
        act(ebl[:, :], nbl[:, :], AF.Exp, ['nbl'], ['ebl'])
        chk(2.5, i == 0)
        j = next_pm()
        for pr in range(2):
            for k in range(8):
                mm(pm[j][:, pr * 128:(pr + 1) * 128], w_in_sb[:, k, O_Q + pr * 128:O_Q + (pr + 1) * 128],
                   nTi[:, k, :], k == 0, k == 7, ['w_in', nk], [f'pm{j}'])
        act(etmp[:, :, :], bpos[:, :, :], AF.Exp, ['bpos'], ['etmp'], scale=-1.0 / 16)
        for hh in range(2):
            r = slice(hh * 64, hh * 64 + 64)
            stt(qdT[r, hh::2, :], pm[j][r, 0:256].rearrange("p (a t) -> p a t", a=2), 0.125, etmp[r, :, :],
                ALU.mult, ALU.mult, [f'pm{j}', 'etmp'], ['qdT'])
        chk(2.6, i == 0)
        j = next_pm()
        for pr in range(2):
            for k in range(8):
                mm(pm[j][:, pr * 128:(pr + 1) * 128], w_in_sb[:, k, O_K + pr * 128:O_K + (pr + 1) * 128],
                   nTi[:, k, :], k == 0, k == 7, ['w_in', nk], [f'pm{j}'])
        act(etmp[:, :, :], bpos[:, :, :], AF.Exp, ['bpos', 'qdT'], ['etmp'], scale=1.0 / 16)
        tt('dve', kiT[:, :, :], pm[j][:, 0:256].rearrange("p (a t) -> p a t", a=2), etmp[:, :, :], ALU.mult,
           [f'pm{j}', 'etmp'], ['kiT'])
        for (c0, ncol, lc, si) in segs:
            for pr in range(2):
                act(etmp[:, pr, c0:c0 + ncol], bpos[:, pr, c0:c0 + ncol], AF.Exp, ['bpos', 'nbl', 'kiT'],
                    ['etmp'], bias=nbl[:, si * 2 + pr:si * 2 + pr + 1], scale=1.0 / 16)
        tt('dve', keT[:, :, :], pm[j][:, 0:256].rearrange("p (a t) -> p a t", a=2), etmp[:, :, :], ALU.mult,
           [f'pm{j}', 'etmp'], ['keT'])
        for pr in range(2):
            tr(ptr[1][:, pr * 128:(pr + 1) * 128], keT[:, pr, :], ident[:, :], ['keT', 'ident'], ['ptr1'])
        cp('dve', ke_tok[:, :], ptr[1][:, 0:256], ['ptr1'], ['ke_tok'])
        chk(2.7, i == 0)
        j = next_pm()
        for c in range(4):
            for k in range(8):
                mm(pm[j][:, c * 128:(c + 1) * 128], w_in_sb[:, k, O_G + c * 128:O_G + (c + 1) * 128],
                   nTi[:, k, :], k == 0, k == 7, ['w_in', nk], [f'pm{j}'])
        act(sgT[:, :, :], pm[j][:, :].rearrange("p (c t) -> p c t", c=4), AF.Silu, [f'pm{j}'], ['sgT'])
        j = next_pm()
        for k in range(8):
            mm(pm[j][:, :], nTi[:, k, :], w_in_sb[:, k, O_V:O_V + 512], k == 0, k == 7, ['w_in', nk], [f'pm{j}'])
        cp('act', v_tok[:, :], pm[j][:, :], [f'pm{j}'], ['v_tok'])
        chk(2.8, i == 0)
        j = next_pm()
        for h in range(4):
            pr = h // 2
            mm(pm[j][:, h * 128:(h + 1) * 128], kiT[:, pr, :], qdT[:, h, :], True, True,
               ['kiT', 'qdT'], [f'pm{j}'])
        mk, mkey = (masks, 'masks') if samp else (mask01, 'mask01')
        tt('dve', AT[:, :, :], pm[j][:, :].rearrange("p (h t) -> p h t", h=4), bc_mid(mk[:, :], 4), ALU.mult,
           [f'pm{j}', mkey], ['AT'])
        chk(2.9, i == 0)
        j = next_pm()
        for h in range(4):
            pr, b0 = h // 2, (h % 2) * 64
            oh = pm[j][:, h * 128:(h + 1) * 128]
            mm(oh, v_tok[:, h * 128:(h + 1) * 128], AT[:, h, :], True, False, ['v_tok', 'AT'], [f'pm{j}'])
            if not samp:
                mm(oh, S_b[:, pr, :], qdT[:, h, :], False, True, ['S_b', 'qdT'], [f'pm{j}'])
            else:
                for b in range(4):
                    mm(pm[j][:, h * 128 + b * 32:h * 128 + (b + 1) * 32], S0_b[:, b, pr, :],
                       qdT[:, h, b * 32:(b + 1) * 32], False, b == 3, ['S0_b', 'qdT'], [f'pm{j}'])
        jo = j
        chk(2.95, i == 0)
        act(osq[:, :], pm[jo][:, :], AF.Square, [f'pm{jo}'], ['osq'])
        j = next_pm()
        mm(pm[j][:, :], ones_bf[:, :], osq[:, :], True, True, ['ones', 'osq'], [f'pm{j}'])
        rstd_chain(pm[j][:, :], orst[:, :], 1.0 / 128, [f'pm{j}'], ['orst'])
        tt('dve', orst[:, :], pm[jo][:, :], orst[:, :], ALU.mult, [f'pm{jo}', 'orst'], ['orst'])
        stt(ogT[:, :, tcol], orst[:, :].rearrange("p (h t) -> p h t", h=4), goutc[:, 0:1], sgT[:, :, :],
            ALU.mult, ALU.mult, ['orst', 'goutc', 'sgT'], ['ogT'])
        chk(2.97, i == 0)
        if not samp:
            for pr in range(2):
                j = next_pm()
                mm(pm[j][:, 0:256], ke_tok[:, pr * 128:(pr + 1) * 128], v_tok[:, pr * 256:(pr + 1) * 256],
                   True, True, ['ke_tok', 'v_tok'], [f'pm{j}'])
                ts('dve', orst[:, 0:128], pm[j][:, 0:128], hm[:, 0:1], None, ALU.mult, None, [f'pm{j}', 'hm'], ['orst'])
                stt(orst[:, 0:128], pm[j][:, 128:256], hm[:, 1:2], orst[:, 0:128], ALU.mult, ALU.add,
                    [f'pm{j}', 'hm', 'orst'], ['orst'])
                stt(S_f[:, pr, :], S_f[:, pr, :], ebl[:, pr:pr + 1], orst[:, 0:128], ALU.mult, ALU.add,
                    ['S_f', 'ebl', 'orst'], ['S_f'])
                chk(2.98, i == 0 and pr == 0)
                chk(2.99, i == 0 and pr == 1)
            cp('dve', S_b[:, :, :], S_f[:, :, :], ['S_f'], ['S_b'])
            if i == ST - 1:
                for pr in range(2):
                    dma('sp', glap_o[pr], S_f[:, pr, :], ['S_f'], ['glap_o'])
        else:
            for b in range(4):
                ts('dve', ke_z[:, b, :], ke_tok[:, :], rowm[:, b:b + 1], None, ALU.mult, None,
                   ['ke_tok', 'rowm'], ['ke_z'])
            for b in range(4):
                for pr in range(2):
                    j = next_pm()
                    mm(pm[j][:, 0:256], ke_z[:, b, pr * 128:(pr + 1) * 128],
                       v_tok[:, pr * 256:(pr + 1) * 256], True, True, ['ke_z', 'v_tok'], [f'pm{j}'])
                    ts('dve', orst[:, 0:128], pm[j][:, 0:128], hm[:, 0:1], None, ALU.mult, None, [f'pm{j}', 'hm'],
                       ['orst'])
                    stt(orst[:, 0:128], pm[j][:, 128:256], hm[:, 1:2], orst[:, 0:128], ALU.mult, ALU.add,
                        [f'pm{j}', 'hm', 'orst'], ['orst'])
                    stt(S0_f[:, b, pr, :], S0_f[:, b, pr, :], ebl[:, b * 2 + pr:b * 2 + pr + 1], orst[:, 0:128],
                        ALU.mult, ALU.add, ['S0_f', 'ebl', 'orst'], ['S0_f'])
                    dma('sp', glas_o[b, pr], S0_f[:, b, pr, :], ['S0_f'], ['glas_o'])

        chk(3, i == 0)
        j = next_pm()
        for k in range(8):
            mm(pm[j][:, 0:416], nTi[:, k, :], w_in_sb[:, k, O_MQ:O_MQ + 416], k == 0, k == 7,
               ['w_in', nk], [f'pm{j}'])
        cp('dve', u_sb[:, :], pm[j][:, 0:416], [f'pm{j}'], ['u_sb'])
        chk(3.1, i == 0)
        for (a, b_, col, n) in ((0, 256, 24, 256), (256, 384, 25, 128), (384, 416, 26, 32)):
            if True:
                act(sqq[:, a:b_], u_sb[:, a:b_], AF.Square, ['u_sb'], ['sqq', 'ms'], accum_out=ms[:, col:col + 1],
                    scale=float(n) ** -0.5)
            else:
                act(sqq[:, a:b_], u_sb[:, a:b_], AF.Square, ['u_sb'], ['sqq'], scale=float(n) ** -0.5)
                S.op('dve', lambda e, a=a, b_=b_, col=col: e.tensor_reduce(ms[:, col:col + 1], sqq[:, a:b_], AX.X, ALU.add),
                     ['sqq'], ['ms'])
        rstd_chain(ms[:, 24:27], rst[:, 24:27], 1.0, ['ms'], ['rst'])
        chk(3.2, i == 0)
        ts('dve', cq[:, :], u_sb[:, 0:256], rst[:, 24:25], None, ALU.mult, None, ['u_sb', 'rst'], ['cq'])
        cs = i % 2
        stt(ckv_f[:, cs, :], u_sb[:, 256:384], rst[:, 25:26], Gkva[:, :], ALU.mult, ALU.mult,
            ['u_sb', 'rst', 'Gkva'], [f'ckv_f{cs}'])
        dma('sp', ckv_o[i], ckv_f[:, cs, :], [f'ckv_f{cs}'], ['ckv_o'])
        cp('dve', ckv_b[:, 0:128], ckv_f[:, cs, :], [f'ckv_f{cs}'], ['ckv_b'])
        stt(Rn[:, 8, :], u_sb[:, 384:416], rst[:, 26:27], Gkr[:, :], ALU.mult, ALU.mult,
            ['u_sb', 'rst', 'Gkr'], ['Rn'])
        chk(3.3, i == 0)
        for c in range(2):
            tr(ptr[1][:, c * 128:(c + 1) * 128], cq[:, c * 128:(c + 1) * 128], ident[:, :], ['cq', 'ident'], ['ptr1'])
        tr(ptr[1][:, 256:384], ckv_b[:, 0:128], ident[:, :], ['ckv_b', 'ident'], ['ptr1'])
        cp('dve', cqT[:, :, :], ptr[1][:, 0:256].rearrange("p (c t) -> p c t", c=2), ['ptr1'], ['cqT'])
        cp('dve', ckvT[:, :], ptr[1][:, 256:384], ['ptr1'], ['ckvT'])
        chk(3.4, i == 0)
        for half, (n0, n1) in enumerate(((0, 512), (512, 768))):
            for c in range(2):
                mm(pw[0][:, n0:n1], cqT[:, c, :], wqup_sb[:, c, n0:n1], c == 0, c == 1, ['cqT', 'wqup'], ['pw0'])
        for half in range(2):
            mm(pw[1][:, half * 512:(half + 1) * 512], ckvT[:, :], wkv_sb[:, half * 512:(half + 1) * 512],
               True, True, ['ckvT', 'wkv'], ['pw1'])
        qfv = pw[0][:, 0:768].rearrange("p (h e) -> p h e", h=8)
        kvv = pw[1][:, :].rearrange("p (h e) -> p h e", h=8)
        chk(3.5, i == 0)
        act(sqq[:, :], pw[0][:, 0:768], AF.Square, ['pw0'], ['sqq'])
        sqv = sqq[:, :].rearrange("p (h e) -> p h e", h=8)
        S.op('dve', lambda e: e.tensor_reduce(ms[:, 0:8], sqv[:, :, 0:64], AX.X, ALU.add), ['sqq'], ['ms'])
        S.op('dve', lambda e: e.tensor_reduce(ms[:, 16:24], sqv[:, :, 64:96], AX.X, ALU.add), ['sqq'], ['ms'])
        act(sqk[:, :, :], kvv[:, :, 0:64], AF.Square, ['pw1'], ['sqk'])
        S.op('dve', lambda e: e.tensor_reduce(ms[:, 8:16], sqk[:, :, :], AX.X, ALU.add), ['sqk'], ['ms'])
        rstd_chain(ms[:, 0:16], rst[:, 0:16], 1.0 / 64, ['ms'], ['rst'])
        rstd_chain(ms[:, 16:24], rst[:, 16:24], 1.0 / 32, ['ms'], ['rst'])
        chk(3.6, i == 0)
        tt('dve', sqk[:, :, :], qfv[:, :, 0:64], bc_last(rst[:, 0:8], 64), ALU.mult, ['pw0', 'rst'], ['sqk'])
        tt('dve', q_tok[:, :, :], sqk[:, :, :], bc_mid(Gqn[:, :], 8), ALU.mult, ['sqk', 'Gqn'], ['q_tok'])
        tt('dve', sqk[:, :, :], kvv[:, :, 0:64], bc_last(rst[:, 8:16], 64), ALU.mult, ['pw1', 'rst', 'q_tok'],
           ['sqk'])
        tt('dve', k_tok[:, :, :], sqk[:, :, :], bc_mid(Gkn[:, :], 8), ALU.mult, ['sqk', 'Gkn'], ['k_tok'])
        chk(3.7, i == 0)
        if not samp:
            cp('act', Vext[:, i, :, 0:64], kvv[:, :, 64:128], ['pw1'], ['Vext'])
        tt('dve', Rn[:, 0:8, :], qfv[:, :, 64:96], bc_last(rst[:, 16:24], 32), ALU.mult, ['pw0', 'rst'], ['Rn'])
        tt('dve', Rn[:, 0:8, :], Rn[:, 0:8, :], bc_mid(Gqr[:, :], 8), ALU.mult, ['Rn', 'Gqr'], ['Rn'])
        cosb = bc_mid(cos_sb[:, i, :], 9)
        sinb = bc_mid(sin_sb[:, i, :], 9)
        x1, x2 = Rn[:, :, 0:16], Rn[:, :, 16:32]
        rk = f'Rr{cs}'
        tt('dve', rt1[:, :, :], x1, cosb, ALU.mult, ['Rn', 'cos'], ['rt1'])
        tt('dve', rt2[:, :, :], x2, sinb, ALU.mult, ['Rn', 'sin'], ['rt2'])
        tt('dve', Rr[:, cs, :, 0:16], rt1[:, :, :], rt2[:, :, :], ALU.subtract, ['rt1', 'rt2'], [rk])
        tt('dve', rt1[:, :, :], x2, cosb, ALU.mult, ['Rn', 'cos', rk], ['rt1'])
        tt('dve', rt2[:, :, :], x1, sinb, ALU.mult, ['Rn', 'sin', rk], ['rt2'])
        tt('dve', Rr[:, cs, :, 16:32], rt1[:, :, :], rt2[:, :, :], ALU.add, ['rt1', 'rt2'], [rk])
        dma('sp', kr_o[i], Rr[:, cs, 8, :], [rk], ['kr_o'])
        cp('dve', qr_tok[:, :, :], Rr[:, cs, 0:8, :], [rk], ['qr_tok'])
        cp('dve', kr_tok[:, :], Rr[:, cs, 8, :], [rk], ['kr_tok'])
        chk(3.8, i == 0)
        for c in range(4):
            tr(ptr[0][:, c * 128:(c + 1) * 128], q_tok[:, 2 * c:2 * c + 2, :], ident[:, :], ['q_tok', 'ident'], ['ptr0'])
            tr(ptr[0][:, 512 + c * 128:512 + (c + 1) * 128], k_tok[:, 2 * c:2 * c + 2, :], ident[:, :],
               ['k_tok', 'ident'], ['ptr0'])
        for hh in range(2):
            r = slice(hh * 64, hh * 64 + 64)
            cp('dve', QnT[r, hh::2, :], ptr[0][r, 0:512].rearrange("p (c t) -> p c t", c=4), ['ptr0'], ['QnT'])
        cp('dve', KnT[:, :, tcol], ptr[0][:, 512:1024].rearrange("p (c t) -> p c t", c=4), ['ptr0'], ['KnT'])
        for h in range(8):
            tr(ptr[1][0:32, h * 128:(h + 1) * 128], qr_tok[:, h, :], ident[:, :], ['qr_tok', 'ident'], ['ptr1'])
        cp('dve', QrT[0:32, :, :], ptr[1][0:32, :].rearrange("p (h t) -> p h t", h=8), ['ptr1'], ['QrT'])
        tr(ptr[1][0:32, 0:128], kr_tok[:, :], ident[:, :], ['kr_tok', 'ident', 'QrT'], ['ptr1'])
        cp('dve', KrT[0:32, tcol], ptr[1][0:32, 0:128], ['ptr1'], ['KrT'])

        if i == 0 and os.environ.get("KDBG"):
            dma('sp', y_o[0][:, 0:416], u_sb[:, :], ['u_sb'], ['y_o'])
            dma('sp', y_o[1][:, 0:32], rst[:, :], ['rst'], ['y_o'])
            dma('pool', y_o[2][:, 0:1024], nT[:, 0, :, :].rearrange("p c t -> p (c t)"), [nk], ['y_o'])
            dma('pool', y_o[3][:, 0:1024], w_in_sb[:, 0, 0:1024], ['w_in'], ['y_o'])
            dma('pool', y_o[4][:, 0:1024], xs[:, :], ['xs'], ['y_o'])
            dma('sp', y_o[5][:, 0:8], stat[:, :], ['stat'], ['y_o'])
            dma('sp', y_o[6][:, 0:256], ckv_f[:, :, :].rearrange("p a c -> p (a c)"), ['ckv_f0'], ['y_o'])
        if i == 0 and os.environ.get("KDBG") == "6":
            stg = x_sb[:, 1, :]
            cp('dve', x_sb[:, 1, 0:416], u_sb[:, :], ['u_sb'], ['x1'])
            cp('dve', x_sb[:, 1, 416:448], rst[:, :], ['rst'], ['x1'])
            cp('dve', x_sb[:, 1, 448:480], ms[:, :], ['ms'], ['x1'])
            cp('dve', x_sb[:, 1, 512:640], ckv_f[:, 0, :], ['ckv_f0'], ['x1'])
            cp('dve', x_sb[:, 1, 640:768], Gkva[:, :], ['Gkva'], ['x1'])
            cp('dve', x_sb[:, 1, 768:800], Rr[:, 0, 8, :], ['Rr0'], ['x1'])
            dma('sp', y_o[4], stg, ['x1'], ['y_o'])
        chk(4, i == 0)
        chk(7, samp)
        if not samp:
            groups = []
            for h in range(8):
                for kt0 in range(0, i + 1, 4):
                    groups.append((h, kt0, min(4, i + 1 - kt0)))

            def emit_scores(gi):
                h, kt0, nkt = groups[gi]
                pr, b0 = h // 2, (h % 2) * 64
                s = gi % 2
                for jj in range(nkt):
                    kt = kt0 + jj
                    kc = slice(kt * 128, (kt + 1) * 128)
                    o = pm[s][:, jj * 128:(jj + 1) * 128]
                    mm(o, KnT[:, pr, kc], QnT[:, h, :], True, False, ['KnT', 'QnT'], [f'pm{s}'])
                    mm(o, KrT[:, kc], QrT[:, h, :], False, True, ['KrT', 'QrT'], [f'pm{s}'])
                act(PT[:, s, 0:nkt, :], pm[s][:, 0:nkt * 128].rearrange("p (a t) -> p a t", a=nkt), AF.Exp,
                    [f'pm{s}'], [f'PT{s}'], scale=MLA_SCALE)
                if kt0 + nkt - 1 == i:
                    tt('dve', PT[:, s, nkt - 1, :], PT[:, s, nkt - 1, :], mask01[:, :], ALU.mult,
                       [f'PT{s}', 'mask01'], [f'PT{s}'])

            def emit_pv(gi):
                h, kt0, nkt = groups[gi]
                s = gi % 2
                bank = h // 4
                o = pw[bank][:, (h % 4) * 65:(h % 4) * 65 + 65]
                for jj in range(nkt):
                    kt = kt0 + jj
                    mm(o, PT[:, s, jj, :], Vext[:, kt, h, :], kt == 0, kt == i, [f'PT{s}', 'Vext', 'Vext1'],
                       [f'pw{bank}'])

            emit_scores(0)
            for gi in range(len(groups)):
                if gi + 1 < len(groups):
                    emit_scores(gi + 1)
                emit_pv(gi)
            for bank in range(2):
                ov = pw[bank][:, 0:260].rearrange("p (h e) -> p h e", h=4)
                S.op('dve', lambda e, ov=ov, bank=bank: e.reciprocal(rl[:, bank * 4:bank * 4 + 4], ov[:, :, 64]),
                     [f'pw{bank}'], ['rl'])
                tt('dve', o_tok[:, bank * 4:bank * 4 + 4, :], ov[:, :, 0:64], bc_last(rl[:, bank * 4:bank * 4 + 4], 64),
                   ALU.mult, [f'pw{bank}', 'rl'], ['o_tok'])
            for c in range(4):
                tr(ptr[0][:, c * 128:(c + 1) * 128], o_tok[:, 2 * c:2 * c + 2, :], ident[:, :], ['o_tok', 'ident'],
                   ['ptr0'])
            cp('dve', omT[:, :, tcol], ptr[0][:, 0:512].rearrange("p (c t) -> p c t", c=4), ['ptr0'], ['omT'])
            chk(5, i == 0)
            chk(6, i == ST - 1)
        elif not _SKIP_PA:
            for pr in range(4):
                tr(ptr[0][:, pr * 128:(pr + 1) * 128], wkn_sb[:, pr * 128:(pr + 1) * 128], ident[:, :],
                   ['wkn', 'ident'], ['ptr0'])
            cp('dve', WkT[:, :, :], ptr[0][:, 0:512].rearrange("p (c t) -> p c t", c=4), ['ptr0'], ['WkT'])
            ts('dve', qgT[:, :, :], QnT[:, :, :], gknc[:, 0:1], None, ALU.mult, None, ['QnT', 'gknc'], ['qgT'])
            for h in range(8):
                pr, b0 = h // 2, (h % 2) * 64
                bank = h // 4
                mm(pw[bank][:, (h % 4) * 128:(h % 4 + 1) * 128], WkT[:, pr, :], qgT[:, h, :],
                   True, True, ['WkT', 'qgT'], [f'pw{bank}'])
            for bank in range(2):
                cp('dve', qabsT[:, bank * 4:bank * 4 + 4, :],
                   pw[bank][:, 0:512].rearrange("p (h t) -> p h t", h=4), [f'pw{bank}'], ['qabsT'])
            cp('dve', rstk[:, 0:8], rst[:, 8:16], ['rst'], ['rstk_new'])
            chk(7.1)

            def gather(b, g):
                s = g % 2
                for jj in range(4):
                    col = b * NPG + g * 4 + jj
                    S.op('pool', lambda e, s=s, jj=jj, col=col: e.indirect_dma_start(
                        out=Cbuf[:, s, jj, :], out_offset=None, in_=ckv_d,
                        in_offset=bass.IndirectOffsetOnAxis(ap=IDX[:, col:col + 1], axis=0)),
                        ['IDX'], [f'Cbuf{s}'], dma=True)
                    S.op('pool', lambda e, s=s, jj=jj, col=col: e.indirect_dma_start(
                        out=KRbuf[:, s, jj, :], out_offset=None, in_=ckr_d,
                        in_offset=bass.IndirectOffsetOnAxis(ap=IDX[:, col:col + 1], axis=0)),
                        ['IDX'], [f'KRbuf{s}'], dma=True)

            def score_pv(b, lhs_c, lhs_r, rstd_ap, nj, rhs_pv, keys_r, first, last, maskap=None):
                qa = qabsT[:, :, b * 32 + 2:b * 32 + 10]
                qr = QrT[:, :, b * 32 + 2:b * 32 + 10]
                for jj in range(nj):
                    mm(pm[0][:, jj * 64:(jj + 1) * 64], lhs_c(jj), qa, True, True, keys_r + ['qabsT'], ['pm0'])
                    mm(pm[0][:, 256 + jj * 64:256 + (jj + 1) * 64], lhs_r(jj), qr, True, True, keys_r + ['QrT'],
                       ['pm0'])
                n = nj * 64
                tt('dve', sc1[:, 0:n].rearrange("p (a t) -> p a t", t=8),
                   pm[0][:, 0:n].rearrange("p (a t) -> p a t", t=8), bc_last(rstd_ap, 8), ALU.mult,
                   ['pm0', 'rstk', 'rstk_new'] + [k for k in keys_r if k.startswith('rk')], ['sc1'])
                tt('dve', sc2[:, 0:n], sc1[:, 0:n], pm[0][:, 256:256 + n], ALU.add, ['sc1', 'pm0'], ['sc2'])
                return n

            def stageA(b, g):
                s = g % 2
                cp('act', Cb[:, s, :, 0:128], Cbuf[:, s, :, :], [f'Cbuf{s}'], [f'Cb{s}'])
                cp('dve', KRb[:, s, :, :], KRbuf[:, s, :, :], [f'KRbuf{s}'], [f'KRb{s}'])
                for jj in range(4):
                    tr(ptr[0][:, jj * 128:(jj + 1) * 128], Cb[:, s, jj, 0:128], ident[:, :],
                       [f'Cb{s}', 'ident'], ['ptr0'])
                    tr(ptr[0][0:32, 512 + jj * 128:512 + (jj + 1) * 128], KRb[:, s, jj, :], ident[:, :],
                       [f'KRb{s}', 'ident'], ['ptr0'])
                cp('dve', CT[:, s, :, :], ptr[0][:, 0:512].rearrange("p (a t) -> p a t", a=4), ['ptr0'],
                   [f'CT{s}'])
                cp('dve', KRT[0:32, s, :, :], ptr[0][0:32, 512:1024].rearrange("p (a t) -> p a t", a=4), ['ptr0'],
                   [f'KRT{s}'])
                for hf in range(2):
                    for q2 in range(2):
                        jj = hf * 2 + q2
                        mm(pw[hf][:, q2 * 512:(q2 + 1) * 512], CT[:, s, jj, :], wkn_sb[:, :], True, True,
                           [f'CT{s}', 'wkn'], [f'pw{hf}'])
                    act(sqs[:, hf, :], pw[hf][:, :], AF.Square, [f'pw{hf}'], [f'sqs{hf}'])
                    S.op('dve', lambda e, hf=hf: e.tensor_reduce(
                        ssum[:, hf * 16:(hf + 1) * 16], sqs[:, hf, :].rearrange("p (a d) -> p a d", d=64),
                        AX.X, ALU.add), [f'sqs{hf}'], ['ssum'])
                rstd_chain(ssum[:, 0:32], rstk2[:, s, :], 1.0 / 64, ['ssum'], [f'rk{s}'])

            def stageB(b, g):
                s = g % 2
                score_pv(b, lambda jj: CT[:, s, jj, :], lambda jj: KRT[:, s, jj, :], rstk2[:, s, :], 4,
                         None, [f'CT{s}', f'KRT{s}', f'rk{s}'], None, None)
                act(PTs[:, s, :, :], sc2[:, 0:256].rearrange("p (a t) -> p a t", a=4), AF.Exp, ['sc2'],
                    [f'PTs{s}'], scale=MLA_SCALE)
                for jj in range(4):
                    mm(pm[1][0:64, 0:129], PTs[:, s, jj, :], Cb[:, s, jj, 0:129], g == 0 and jj == 0, False,
                       [f'PTs{s}', f'Cb{s}', 'Cb1'], ['pm1'])

            NG = NPG // 4
            for b in range(4):
                gather(b, 0)
                gather(b, 1)
                stageA(b, 0)
                for g in range(NG):
                    if g + 2 < NG:
                        gather(b, g + 2)
                    if g + 1 < NG:
                        stageA(b, g + 1)
                    stageB(b, g)
                cp('dve', rstk[:, 0:8], rst[:, 8:16], ['rst'], ['rstk'])
                score_pv(b, lambda jj: ckvT[:, :], lambda jj: KrT[:, tcol], rstk[:, 0:8], 1, None,
                         ['ckvT', 'KrT'], None, None)
                act(PTs[:, 0, 0, :], sc2[:, 0:64], AF.Exp, ['sc2'], ['PTs0'], scale=MLA_SCALE)
                tt('dve', PTs[:, 0, 0, :], PTs[:, 0, 0, :], maska[:, b, :], ALU.mult, ['PTs0', 'maska'], ['PTs0'])
                mm(pm[1][0:64, 0:129], PTs[:, 0, 0, :], ckv_b[:, 0:129], False, True, ['PTs0', 'ckv_b', 'ckv_b1'],
                   ['pm1'])
                cp('dve', acc_sb[:, 0:129], pm[1][0:64, 0:129], ['pm1'], ['acc_sb'])
                S.op('dve', lambda e: e.reciprocal(acc_sb[:, 130:131], acc_sb[:, 128:129]), ['acc_sb'], ['acc_sb'])
                ts('dve', accn[0:64, :], acc_sb[:, 0:128], acc_sb[:, 130:131], None, ALU.mult, None, ['acc_sb'], ['accn'])
                tr(ptr[1][:, 0:128], accn[:, :], ident[:, :], ['accn', 'ident'], ['ptr1'])
                cp('dve', accnT[:, :], ptr[1][:, 0:128], ['ptr1'], ['accnT'])
                for c in range(4):
                    mm(pm[0][:, c * 16:(c + 1) * 16], wvp_sb[:, c, :], accnT[:, c * 16:(c + 1) * 16], True, True,
                       ['wvp', 'accnT'], ['pm0'])
                ocol = slice(ST * 128 + b * 32 + 2, ST * 128 + b * 32 + 10)
                pv4 = pm[0][:, 0:64].rearrange("p (c t) -> p c t", c=4)
                cp('dve', omT[0:64, :, ocol], pv4[0:64, :, 0:8], ['pm0'], ['omT'])
                cp('dve', omT[64:128, :, ocol], pv4[64:128, :, 8:16], ['pm0'], ['omT'])
                chk(7.6, b == 0)

    if os.environ.get("KDBG") == "7":
        for tt_ in range(2):
            cp('dve', x_sb[:, 0, 0:512].rearrange("p (c t) -> p c t", c=4), ogT[:, :, tt_ * 128:(tt_ + 1) * 128],
               ['ogT'], ['x0'])
            cp('dve', x_sb[:, 0, 512:1024].rearrange("p (c t) -> p c t", c=4), omT[:, :, tt_ * 128:(tt_ + 1) * 128],
               ['omT'], ['x0'])
            dma('sp', y_o[4 + tt_], x_sb[:, 0, :], ['x0'], ['y_o'])
    chk(8)
    S.barrier()
    A.off = persist_off
    wd_sb = A.alloc("wd", [128, NF, D], BF16)
    wo_sb = A.alloc("wo", [128, 8, D], BF16)
    wpp_sb = A.alloc("wpp", [128, 2, D], BF16)
    wst = A.alloc("wst", [128, 2, 2, 8, 512], BF16)
    h_g = A.alloc("h_g", [128, 4, D], F32)
    nT2 = A.alloc("nT2", [128, 8, 512], BF16)
    hidT = A.alloc("hidT", [128, NF, 512], BF16)
    a_sb = A.alloc("a_sb", [128, 520], F32)
    tbuf = A.alloc("tbuf", [128, 512], F32)
    big = A.alloc("big", [128, D], F32)
    p_sb = A.alloc("p_sb", [128, 256], F32)
    p_bf = A.alloc("p_bf", [128, 256], BF16)
    pT = A.alloc("pT", [128, 2, 128], BF16)
    cw_sb = A.alloc("cw", [128, 3, NF], F32)
    cb_sb = A.alloc("cb", [128, NF], F32)
    halo = A.alloc("halo", [128, NF, 2], F32)
    cst = A.alloc("cst", [128, NF, 8], F32)
    sc_bf = A.alloc("sc_bf", [128, DFF], BF16)
    sel_sb = A.alloc("sel", [128, 128], BF16)
    ctok = A.alloc("ctok", [8, 512], F32)

    for f in range(NF):
        dma('sp', wd_sb[:, f, :], wd_d[f * 128:(f + 1) * 128, :], (), ['wd'])
    for k in range(8):
        dma('sp', wo_sb[:, k, :], wo_d[k * 128:(k + 1) * 128, :], (), ['wo'])
    dma('sp', wpp_sb[:, :, :], wpp_d.rearrange("(c p) n -> p c n", p=128), (), ['wpp'])
    dma('sp', cw_sb[:, :, :], cw_d.rearrange("j (f p) -> p j f", p=128), (), ['cw'], allow_slow_non_contiguous=True)
    dma('sp', cb_sb[:, :], cb_d.rearrange("(f p) -> p f", p=128), (), ['cb'], allow_slow_non_contiguous=True)
    memset('dve', sc_bf[:, :], 0.0, ['sc_bf'])
    dma('sp', sc_bf[0:8, :], sconv_d, ['sc_bf'], ['sc_bf'])
    dma('sp', sel_sb[:, :], c_sel_d, (), ['sel'])
    memset('dve', halo[:, :, :], 0.0, ['halo'])
    memset('dve', a_sb[:, :], 0.0, ['a_sb'])

    blocks = [(0, 512), (512, 512), (1024, 512), (1536, 512), (2048, 512), (2560, 256)]
    wst_cnt = [0]

    def load_ffn_block(bi):
        c0, w = blocks[bi]
        s = wst_cnt[0] % 2
        wst_cnt[0] += 1
        for gu, wdram in ((0, wg_d), (1, wu_d)):
            for k in range(8):
                dma('sp', wst[:, s, gu, k, 0:w], wdram[k * 128:(k + 1) * 128, c0:c0 + w], (), [f'wst{s}'])
        return s

    def load_ple_gate():
        s = wst_cnt[0] % 2
        wst_cnt[0] += 1
        for half in range(2):
            for k in range(8):
                dma('sp', wst[:, s, half, k, :], wpg_d[k * 128:(k + 1) * 128, half * 512:(half + 1) * 512], (),
                    [f'wst{s}'])
        return s

    tgroups = [[0, 1, 2, 3], [4, 5, 6, 7], [8, 9, 10, 11], [12, 13, 14, 15], [ST]]
    for gi, tiles in enumerate(tgroups):
        samp = (tiles[0] == ST)
        N = 128 * len(tiles)
        slot0 = load_ffn_block(0)
        for ti, t in enumerate(tiles):
            tcol = slice(t * 128, (t + 1) * 128)
            dma('sp', big[:, :], x_d[t], (), ['big'])
            for half in range(2):
                for c in range(8):
                    src = ogT if c < 4 else omT
                    mm(pw[0][:, half * 512:(half + 1) * 512], src[:, c % 4, tcol],
                       wo_sb[:, c, half * 512:(half + 1) * 512], c == 0, c == 7, ['ogT', 'omT', 'wo'], ['pw0'])
            tt('dve', h_g[:, ti, :], big[:, :], pw[0][:, :], ALU.add, ['big', 'pw0'], [f'h{ti}'])
            norm_tile(h_g[:, ti, :], [f'h{ti}'], gffnT, 'gffnT', nT2[:, :, ti * 128:(ti + 1) * 128], ['nT2'], 0)
        chk(9, gi == 0)
        for bi, (c0, w) in enumerate(blocks):
            s = slot0 if bi == 0 else s_next
            if bi + 1 < len(blocks):
                s_next = load_ffn_block(bi + 1)
            else:
                s_pg = load_ple_gate()
            for fi in range(w // 128):
                f = c0 // 128 + fi
                fc = slice(fi * 128, (fi + 1) * 128)
                for k in range(8):
                    mm(pm[0][:, 0:N], wst[:, s, 0, k, fc], nT2[:, k, 0:N], k == 0, (k == 7 and not samp),
                       [f'wst{s}', 'nT2'], ['pm0'])
                if samp:
                    mm(pm[0][:, 0:N], sc_bf[:, f * 128:(f + 1) * 128], sel_sb[:, :], False, True,
                       ['sc_bf', 'sel'], ['pm0'])
                for k in range(8):
                    mm(pm[1][:, 0:N], wst[:, s, 1, k, fc], nT2[:, k, 0:N], k == 0, k == 7, [f'wst{s}', 'nT2'], ['pm1'])
                if not samp:
                    cp('dve', a_sb[:, 0:2], halo[:, f, :], ['halo'], ['a_sb'])
                cp('act', a_sb[:, 2:2 + N], pm[0][:, 0:N], ['pm0'], ['a_sb'])
                if not samp:
                    cp('dve', halo[:, f, :], a_sb[:, N:N + 2], ['a_sb'], ['halo'])
                else:
                    cp('dve', cst[:, f, :].rearrange("p (b j) -> p b j", b=4),
                       a_sb[:, 2:2 + 128].rearrange("p (b c) -> p b c", b=4)[:, :, 8:10], ['a_sb'], ['cst'])
                ts('dve', tbuf[:, 0:N], a_sb[:, 0:N], cw_sb[:, 0, f:f + 1], cb_sb[:, f:f + 1], ALU.mult, ALU.add,
                   ['a_sb', 'cw', 'cb'], ['tbuf'])
                stt(tbuf[:, 0:N], a_sb[:, 1:1 + N], cw_sb[:, 1, f:f + 1], tbuf[:, 0:N], ALU.mult, ALU.add,
                    ['a_sb', 'cw', 'tbuf'], ['tbuf'])
                stt(tbuf[:, 0:N], a_sb[:, 2:2 + N], cw_sb[:, 2, f:f + 1], tbuf[:, 0:N], ALU.mult, ALU.add,
                    ['a_sb', 'cw', 'tbuf'], ['tbuf'])
                act(tbuf[:, 0:N], tbuf[:, 0:N], AF.Silu, ['tbuf'], ['tbuf'])
                tt('dve', hidT[:, f, 0:N], tbuf[:, 0:N], pm[1][:, 0:N], ALU.mult, ['tbuf', 'pm1'], ['hidT'])
        if gi == 3 or samp:
            W = 8 if samp else 2
            stg = cst if samp else halo
            skey = 'cst' if samp else 'halo'
            for q6 in range(6):
                nf = min(4, NF - q6 * 4)
                for ff in range(nf):
                    f = q6 * 4 + ff
                    S.op('pe', lambda e, f=f, ff=ff, W=W, stg=stg: e.transpose(pm[0][0:W, ff * 128:(ff + 1) * 128],
                                                                 stg[:, f, 0:W], identf[:, :]),
                         [skey, 'identf'], ['pm0'])
                cp('dve', ctok[0:W, 0:nf * 128], pm[0][0:W, 0:nf * 128], ['pm0'], ['ctok'])
                dma('sp', (convs_o if samp else convp_o)[:, q6 * 512:q6 * 512 + nf * 128], ctok[0:W, 0:nf * 128],
                    ['ctok'], ['conv_o'])
        chk(10, gi == 0)
        for ti, t in enumerate(tiles):
            tc2 = slice(ti * 128, (ti + 1) * 128)
            for half in range(2):
                for f in range(NF):
                    mm(pw[0][:, half * 512:(half + 1) * 512], hidT[:, f, tc2], wd_sb[:, f, half * 512:(half + 1) * 512],
                       f == 0, f == NF - 1, ['hidT', 'wd'], ['pw0'])
            tt('dve', h_g[:, ti, :], h_g[:, ti, :], pw[0][:, :], ALU.add, [f'h{ti}', 'pw0'], [f'h{ti}'])
        for ti, t in enumerate(tiles):
            tc2 = slice(ti * 128, (ti + 1) * 128)
            norm_tile(h_g[:, ti, :], [f'h{ti}'], gpleT, 'gpleT', nT2[:, :, tc2], ['nT2'], 0)
            dma('sp', p_sb[:, :], p_d[t], (), ['p_sb'])
            cp('dve', p_bf[:, :], p_sb[:, :], ['p_sb'], ['p_bf'])
            for c in range(2):
                tr(ptr[1][:, c * 128:(c + 1) * 128], p_bf[:, c * 128:(c + 1) * 128], ident[:, :], ['p_bf', 'ident'],
                   ['ptr1'])
            cp('dve', pT[:, :, :], ptr[1][:, 0:256].rearrange("p (c t) -> p c t", c=2), ['ptr1'], ['pT'])
            for half in range(2):
                for k in range(8):
                    mm(pw[0][:, half * 512:(half + 1) * 512], nT2[:, k, tc2], wst[:, s_pg, half, k, :],
                       k == 0, k == 7, ['nT2', f'wst{s_pg}'], ['pw0'])
                for c in range(2):
                    mm(pw[1][:, half * 512:(half + 1) * 512], pT[:, c, :], wpp_sb[:, c, half * 512:(half + 1) * 512],
                       c == 0, c == 1, ['pT', 'wpp'], ['pw1'])
            act(big[:, :], pw[0][:, :], AF.Sigmoid, ['pw0'], ['big'])
            tt('dve', big[:, :], big[:, :], pw[1][:, :], ALU.mult, ['big', 'pw1'], ['big'])
            tt('dve', big[:, :], big[:, :], h_g[:, ti, :], ALU.add, ['big', f'h{ti}'], ['big'])
            dma('sp', y_o[t], big[:, :], ['big'], ['y_o'])
        chk(11, gi == 0)
        chk(12, gi == 3)

    return S


_NC_CACHE = {}


def _consts():
    bf = ml_dtypes.bfloat16
    c = {}
    c['c_ident'] = np.eye(128, dtype=np.float32).astype(bf)
    c['c_identf'] = np.eye(128, dtype=np.float32)
    s = np.arange(128)
    c['c_mask'] = (s[:, None] <= s[None, :]).astype(np.float32).astype(bf)
    grp = s // 32
    m = (grp[:, None] == grp[None, :]) & (s[:, None] <= s[None, :])
    c['c_masks'] = m.astype(np.float32).astype(bf)
    ma = np.zeros((128, 4, 8, 8), np.float32)
    for b in range(4):
        for r in range(8):
            for t in range(8):
                if r <= t:
                    ma[b * 32 + 2 + r, b, :, t] = 1.0
    c['c_maska'] = ma.reshape(128, 4, 64).astype(bf)
    inv = 10000.0 ** (-np.arange(16, dtype=np.float32) * 2.0 / 32)
    pos = np.zeros((128, NT), np.float64)
    for i in range(16):
        pos[:, i] = i * 128 + s
    for p in range(128):
        t = (p % 32) - 2
        pos[p, ST] = 16384 + (t if 0 <= t < 8 else 0)
    ang = (pos.astype(np.float32)[:, :, None] * inv[None, None, :]).astype(np.float32)
    c['c_cos'] = np.cos(ang.astype(np.float64)).astype(np.float32)
    c['c_sin'] = np.sin(ang.astype(np.float64)).astype(np.float32)
    rs = np.ones((128, 2, 128), np.float32)
    rs[:, 0, 0] = 0.0
    for b in range(4):
        rs[:, 1, b * 32 + 2] = 0.0
    c['c_rs'] = rs
    c['c_iota'] = s.astype(np.float32).reshape(128, 1)
    rowm = np.zeros((128, 4), np.float32)
    for b in range(4):
        rowm[b * 32:(b + 1) * 32, b] = 1.0
    c['c_rowm'] = rowm
    hm = np.zeros((128, 2), np.float32)
    hm[:64, 0] = 1.0
    hm[64:, 1] = 1.0
    c['c_hm'] = hm
    sel = np.zeros((128, 128), np.float32)
    for b in range(4):
        for j in range(2):
            sel[b * 2 + j, b * 32 + j] = 1.0
    c['c_sel'] = sel.astype(bf)
    return c


def kernel(x_prompt, x_sample, cache_ckv, cache_krope, state_gla, state_conv, page_table,
           p_prompt, p_sample, g_mix, w_in, gla_w_a2, gla_b_a, gla_g_out, mla_g_qa, mla_w_qup,
           mla_g_qn, mla_g_qr, mla_g_kva, mla_g_kr, mla_w_kvup, mla_g_kn, w_o, g_ffn,
           ffn_w_gate, ffn_w_up, ffn_conv_w, ffn_conv_b, ffn_w_down, g_ple, ple_w_gate, ple_w_proj):
    f32 = np.float32
    A_ = lambda a: np.ascontiguousarray(np.asarray(a))
    if 'nc' not in _NC_CACHE:
        _NC_CACHE['nc'] = build_program()
    nc = _NC_CACHE['nc']
    consts = _consts()
    shared = {
        'cache_ckv': A_(cache_ckv).reshape(5120 * 128, 128)[:_NPHYS * 128],
        'cache_krope': A_(cache_krope).reshape(5120 * 128, 32)[:_NPHYS * 128],
        'w_in': A_(w_in[0]), 'gla_w_a2': A_(gla_w_a2[0]), 'gla_b_a': A_(gla_b_a[0]), 'gla_g_out': A_(gla_g_out[0]),
        'mla_g_qa': A_(mla_g_qa[0]), 'mla_w_qup': A_(mla_w_qup[0]), 'mla_g_qn': A_(mla_g_qn[0]),
        'mla_g_qr': A_(mla_g_qr[0]), 'mla_g_kva': A_(mla_g_kva[0]), 'mla_g_kr': A_(mla_g_kr[0]),
        'mla_w_kvup': A_(mla_w_kvup[0]), 'mla_g_kn': A_(mla_g_kn[0]), 'w_o': A_(w_o[0]), 'g_mix': A_(g_mix[0]),
        'g_ffn': A_(g_ffn[0]), 'g_ple': A_(g_ple[0]), 'ffn_w_gate': A_(ffn_w_gate[0]), 'ffn_w_up': A_(ffn_w_up[0]),
        'ffn_conv_w': A_(ffn_conv_w[0]), 'ffn_conv_b': A_(ffn_conv_b[0]), 'ffn_w_down': A_(ffn_w_down[0]),
        'ple_w_gate': A_(ple_w_gate[0]), 'ple_w_proj': A_(ple_w_proj[0]),
    }
    def _bc(a):
        a = A_(a)
        return np.ascontiguousarray(np.broadcast_to(a[None, :], (128, a.shape[0])))
    shared['b_gqn'] = _bc(mla_g_qn[0])
    shared['b_gqr'] = _bc(mla_g_qr[0])
    shared['b_gkva'] = _bc(mla_g_kva[0])
    shared['b_gkr'] = _bc(mla_g_kr[0])
    shared['b_gkn'] = _bc(mla_g_kn[0])
    shared.update(consts)
    xp = A_(x_prompt)
    xsm = A_(x_sample)
    pp = A_(p_prompt)[0]
    psm = A_(p_sample)[0]
    in_maps = []
    rows = np.array([b * 32 + 2 + t for b in range(4) for t in range(8)])
    for c in range(8):
        x = np.zeros((NT, 128, D), f32)
        x[:16] = xp[c].reshape(16, 128, D)
        x[ST][rows] = xsm[4 * c:4 * c + 4].reshape(32, D)
        p = np.zeros((NT, 128, 256), f32)
        p[:16] = pp[c].reshape(16, 128, 256)
        p[ST][rows] = psm[4 * c:4 * c + 4].reshape(32, 256)
        m = dict(shared)
        m['x'] = x
        m['p'] = p
        m['state_gla'] = A_(state_gla[0][4 * c:4 * c + 4]).reshape(4, 2, 128, 128)
        m['state_conv'] = A_(state_conv[0][4 * c:4 * c + 4]).reshape(8, DFF)
        m['page_table'] = A_(page_table[4 * c:4 * c + 4]).reshape(512).astype(np.int32)
        if _SMALLPA:
            m['page_table'] = m['page_table'] % 8
        m['b_pt'] = _bc(m['page_table'])
        in_maps.append(m)
    res = run_bass_kernel_spmd(nc, in_maps, core_ids=list(range(8))).results

    y_p = np.zeros((8, 2048, D), f32)
    y_s = np.zeros((32, 8, D), f32)
    ckv_p = np.zeros((1, 8, 2048, 128), f32)
    kr_p = np.zeros((1, 8, 2048, 32), f32)
    gla_p = np.zeros((1, 8, 4, 64, 128), f32)
    conv_p = np.zeros((1, 8, 2, DFF), f32)
    ckv_s = np.zeros((1, 32, 8, 128), f32)
    kr_s = np.zeros((1, 32, 8, 32), f32)
    gla_s = np.zeros((1, 32, 4, 64, 128), f32)
    conv_s = np.zeros((1, 32, 2, DFF), f32)
    for c in range(8):
        r = res[c]
        y = np.asarray(r['y'])
        y_p[c] = y[:16].reshape(2048, D)
        y_s[4 * c:4 * c + 4] = y[ST][rows].reshape(4, 8, D)
        ck = np.asarray(r['ckv_o'])
        ckv_p[0, c] = ck[:16].reshape(2048, 128)
        ckv_s[0, 4 * c:4 * c + 4] = ck[ST][rows].reshape(4, 8, 128)
        kr = np.asarray(r['kr_o'])
        kr_p[0, c] = kr[:16].reshape(2048, 32)
        kr_s[0, 4 * c:4 * c + 4] = kr[ST][rows].reshape(4, 8, 32)
        gla_p[0, c] = np.asarray(r['gla_p']).reshape(4, 64, 128)
        gla_s[0, 4 * c:4 * c + 4] = np.asarray(r['gla_s']).reshape(4, 4, 64, 128)
        conv_p[0, c] = np.asarray(r['conv_p'])
        conv_s[0, 4 * c:4 * c + 4] = np.asarray(r['conv_s']).reshape(4, 2, DFF)
    return (y_p, y_s, ckv_p, kr_p, gla_p, conv_p, ckv_s, kr_s, gla_s, conv_s)
```

```python
import numpy as np
import ml_dtypes
import concourse.bass as bass
import concourse.mybir as mybir
from concourse.bass_utils import run_bass_kernel_spmd

F32 = mybir.dt.float32
BF16 = mybir.dt.bfloat16
I32 = mybir.dt.int32
AF = mybir.ActivationFunctionType
ALU = mybir.AluOpType
AX = mybir.AxisListType

D = 1024
NT = 17
ST = 16
DFF = 2816
NF = 22
INC = 1968
EPS = 1e-6
MLA_SCALE = 96.0 ** -0.5
NPG = 128
O_Q, O_K, O_V, O_G, O_A, O_MQ = 0, 256, 512, 1024, 1536, 1552


class Sched:
    def __init__(self, sems):
        self.sems = sems
        nxt = iter(range(len(sems)))
        self.q = {e: [] for e in ('pe', 'act', 'dve', 'pool', 'sp')}
        self.cnt = {e: 0 for e in ('pe', 'act', 'dve', 'pool')}
        self.csem = {e: next(nxt) for e in ('pe', 'act', 'dve', 'pool')}
        self.dsems = {'sp': [next(nxt) for _ in range(24)], 'pool': [next(nxt) for _ in range(4)],
                      'act': [next(nxt) for _ in range(8)]}
        self.dcnt = {'sp': 0, 'pool': 0, 'act': 0}
        self.lastw = {}
        self.readers = {}
        self.waited = {e: {} for e in self.q}
        self.all_tokens = {}
        self._bar = None

    def op(self, eng, fn, reads=(), writes=(), dma=False):
        deps = {}

        def add(tok):
            if tok is None:
                return
            for s, v in tok.items() if isinstance(tok, dict) else [tok]:
                if deps.get(s, 0) < v:
                    deps[s] = v
        if self._bar:
            add(self._bar)
        for k in reads:
            add(self.lastw.get(k))
        for k in writes:
            add(self.lastw.get(k))
            add(self.readers.get(k))
        if dma:
            pool = self.dsems[eng]
            i = self.dcnt[eng]
            self.dcnt[eng] += 1
            sem = pool[i % len(pool)]
            val = 16 * (i // len(pool) + 1)
            add((sem, val - 16))
            inc = 16
        else:
            self.cnt[eng] += 1
            sem = self.csem[eng]
            val = self.cnt[eng]
            inc = 1
        waits = []
        w = self.waited[eng]
        for s, v in deps.items():
            if v <= 0:
                continue
            if eng == 'pe' and s == self.csem['pe']:
                continue
            if w.get(s, 0) >= v:
                continue
            w[s] = v
            waits.append((s, v))
        self.q[eng].append((waits, fn, sem, inc))
        tok = (sem, val)
        self.all_tokens[sem] = max(self.all_tokens.get(sem, 0), val)
        for k in reads:
            r = self.readers.setdefault(k, {})
            if r.get(sem, 0) < val:
                r[sem] = val
        for k in writes:
            self.lastw[k] = tok
            self.readers[k] = {}
        return tok

    def barrier(self):
        snap = dict(self.all_tokens)
        for k in list(self.lastw.keys()):
            self.lastw[k] = snap
            self.readers[k] = {}
        self._bar = snap

    def emit(self, engname, engine):
        for waits, fn, sem, inc in self.q[engname]:
            for s, v in waits:
                engine.wait_ge(self.sems[s], v)
            ins = fn(engine)
            ins.then_inc(self.sems[sem], inc)

    def final_wait(self, engine):
        for s, v in self.all_tokens.items():
            engine.wait_ge(self.sems[s], v)


class _Stop(Exception):
    pass


import os
_STOP = float(os.environ.get("KSTOP", "0")) or None
_SKIP_PA = bool(os.environ.get("KSKIP"))
_SMALLPA = bool(os.environ.get("KSMALLPA"))
_NPHYS = 8 if (_SMALLPA or _SKIP_PA or (_STOP is not None and _STOP <= 7)) else 5120


_HOOK = [None]


def chk(stage, active=True):
    if not active:
        return
    if _STOP is not None and stage >= _STOP:
        if _HOOK[0] is not None and os.environ.get("KDBG") == "5":
            _HOOK[0]()
        raise _Stop()


def build_program():
    nc = bass.Bass("TRN2", target_bir_lowering=False)
    S = None
    try:
        S = _build_body(nc)
    except _Stop as e:
        S = e.args[0] if e.args else S
    _emit(nc, _S_HOLD[0])
    return nc


_S_HOLD = [None]


def _emit(nc, S):
    with nc.Block() as block:
        @block.sync
        def _(e):
            S.emit('sp', e)
            S.final_wait(e)

        @block.tensor
        def _(e):
            S.emit('pe', e)

        @block.scalar
        def _(e):
            S.emit('act', e)

        @block.vector
        def _(e):
            S.emit('dve', e)

        @block.gpsimd
        def _(e):
            S.emit('pool', e)


def _build_body(nc):

    def din(name, shape, dt=F32):
        return nc.dram_tensor(name, list(shape), dt, kind="ExternalInput").ap()

    def dout(name, shape, dt=F32):
        return nc.dram_tensor(name, list(shape), dt, kind="ExternalOutput").ap()

    x_d = din("x", [NT, 128, D])
    p_d = din("p", [NT, 128, 256])
    cc_d = din("cache_cat", [_NPHYS * 128, 160])
    sgla_d = din("state_gla", [4, 2, 128, 128])
    sconv_d = din("state_conv", [8, DFF])
    pt_d = din("page_table", [512], I32)
    w_in_d = din("w_in", [D, INC])
    wa2_d = din("gla_w_a2", [16, 256])
    ba_d = din("gla_b_a", [256])
    gout_d = din("gla_g_out", [128])
    gqa_d = din("mla_g_qa", [256])
    wqup_d = din("mla_w_qup", [256, 768])
    gqn_d = din("mla_g_qn", [64])
    gqr_d = din("mla_g_qr", [32])
    gkva_d = din("mla_g_kva", [128])
    gkr_d = din("mla_g_kr", [32])
    wkv_d = din("mla_w_kvup", [128, 1024])
    gkn_d = din("mla_g_kn", [64])
    wo_d = din("w_o", [D, D])
    gmix_d = din("g_mix", [D])
    gffn_d = din("g_ffn", [D])
    gple_d = din("g_ple", [D])
    wg_d = din("ffn_w_gate", [D, DFF])
    wu_d = din("ffn_w_up", [D, DFF])
    cw_d = din("ffn_conv_w", [3, DFF])
    cb_d = din("ffn_conv_b", [DFF])
    wd_d = din("ffn_w_down", [DFF, D])
    wpg_d = din("ple_w_gate", [D, D])
    wpp_d = din("ple_w_proj", [256, D])
    gqnb_d = din("b_gqn", [128, 64])
    gqrb_d = din("b_gqr", [128, 32])
    gkvab_d = din("b_gkva", [128, 128])
    gkrb_d = din("b_gkr", [128, 32])
    gknb_d = din("b_gkn", [128, 64])
    ptb_d = din("b_pt", [128, 512], I32)
    c_ident_d = din("c_ident", [128, 128], BF16)
    c_identf_d = din("c_identf", [128, 128])
    c_mask_d = din("c_mask", [128, 128], BF16)
    c_masks_d = din("c_masks", [128, 128], BF16)
    c_maska_d = din("c_maska", [128, 4, 64], BF16)
    c_cos_d = din("c_cos", [128, NT, 16])
    c_sin_d = din("c_sin", [128, NT, 16])
    c_rs_d = din("c_rs", [128, 2, 128])
    c_iota_d = din("c_iota", [128, 1])
    c_sel_d = din("c_sel", [128, 128], BF16)
    c_rowm_d = din("c_rowm", [128, 4])
    c_hm_d = din("c_hm", [128, 2])

    y_o = dout("y", [NT, 128, D])
    ckv_o = dout("ckv_o", [NT, 128, 128])
    kr_o = dout("kr_o", [NT, 128, 32])
    glap_o = dout("gla_p", [2, 128, 128])
    glas_o = dout("gla_s", [4, 2, 128, 128])
    convp_o = dout("conv_p", [2, DFF])
    convs_o = dout("conv_s", [8, DFF])

    sems = [nc.alloc_semaphore(f"s{i}") for i in range(60)]
    S = Sched(sems)
    _S_HOLD[0] = S

    class Arena:
        def __init__(self):
            self.base = 16512
            self.top = 229344
            self.off = self.base
            self.n = 0

        def alloc(self, name, shape, dt):
            sz = 1
            for s in shape[1:]:
                sz *= s
            sz *= mybir.dt.size(dt)
            sz = (sz + 31) // 32 * 32
            assert self.off + sz <= self.top, f"SBUF overflow at {name}: {self.off + sz}"
            self.n += 1
            t = nc.alloc_sbuf_tensor_at(f"{name}_{self.n}", list(shape), dt, offset=self.off)
            self.off += sz
            return t

    A = Arena()

    ptr = [nc.alloc_psum_tensor(f"ptr{i}", [128, 1024], BF16) for i in range(2)]
    pm = [nc.alloc_psum_tensor(f"pm{i}", [128, 512], F32) for i in range(2)]
    pw = [nc.alloc_psum_tensor(f"pw{i}", [128, 1024], F32) for i in range(2)]

    def dma(eng, out, in_, reads=(), writes=(), **kw):
        if out.dtype != in_.dtype:
            eng = 'pool'
        return S.op(eng, lambda e: e.dma_start(out=out, in_=in_, **kw), reads, writes, dma=True)

    def mm(out, lhsT, rhs, start, stop, reads, writes, **kw):
        return S.op('pe', lambda e: e.matmul(out, lhsT, rhs, start=start, stop=stop, **kw), reads, writes)

    def tr(out, in_, ident, reads, writes):
        return S.op('pe', lambda e: e.transpose(out, in_, ident), reads, writes)

    def act(out, in_, func, reads, writes, **kw):
        return S.op('act', lambda e: e.activation(out, in_, func, **kw), reads, writes)

    def tt(eng, out, in0, in1, op, reads, writes):
        return S.op(eng, lambda e: e.tensor_tensor(out, in0, in1, op), reads, writes)

    def ts(eng, out, in0, s1, s2, op0, op1, reads, writes):
        if s2 is None:
            return S.op(eng, lambda e: e.tensor_scalar(out, in0, s1, None, op0), reads, writes)
        return S.op(eng, lambda e: e.tensor_scalar(out, in0, s1, s2, op0, op1), reads, writes)

    def stt(out, in0, sc, in1, op0, op1, reads, writes):
        return S.op('dve', lambda e: e.scalar_tensor_tensor(out, in0, sc, in1, op0, op1), reads, writes)

    def cp(eng, out, in_, reads, writes):
        if eng == 'act':
            return S.op('act', lambda e: e.copy(out, in_), reads, writes)
        return S.op(eng, lambda e: e.tensor_copy(out, in_), reads, writes)

    def memset(eng, ap, val, writes):
        return S.op(eng, lambda e: e.memset(ap, val), (), writes)

    def bc_last(ap, n):
        sh = list(ap.shape)
        return ap.unsqueeze(len(sh)).broadcast_to(sh + [n])

    def bc_mid(ap, n):
        sh = list(ap.shape)
        return ap.unsqueeze(1).broadcast_to([sh[0], n] + sh[1:])

    ident = A.alloc("ident", [128, 128], BF16)
    identf = A.alloc("identf", [128, 128], F32)
    mask01 = A.alloc("mask01", [128, 128], BF16)
    masks = A.alloc("masks", [128, 128], BF16)
    maska = A.alloc("maska", [128, 4, 64], BF16)
    ones_bf = A.alloc("ones_bf", [128, 128], BF16)
    eps_c = A.alloc("eps_c", [128, 1], F32)
    gmixT = A.alloc("gmixT", [128, 8], F32)
    gffnT = A.alloc("gffnT", [128, 8], F32)
    gpleT = A.alloc("gpleT", [128, 8], F32)
    ogT = A.alloc("ogT", [128, 4, NT * 128], BF16)
    omT = A.alloc("omT", [128, 4, NT * 128], BF16)
    stat = A.alloc("stat", [128, 8], F32)
    xs = A.alloc("xs", [128, D], BF16)
    persist_off = A.off

    dma('sp', ident[:, :], c_ident_d, (), ['ident'])
    def _hook():
        S.barrier()
        dma('sp', y_o[14][:, 0:128], identf[:, :], ['identf'], ['y_o'])
    _HOOK[0] = _hook
    if os.environ.get("KDBG") == "5":
        dma('sp', identf[:, :], c_identf_d, (), ['identf'])
        dma('sp', y_o[15][:, 0:128], identf[:, :], ['identf'], ['y_o'])
    dma('sp', identf[:, :], c_identf_d, (), ['identf'])
    dma('sp', mask01[:, :], c_mask_d, (), ['mask01'])
    dma('sp', masks[:, :], c_masks_d, (), ['masks'])
    dma('sp', maska[:, :, :], c_maska_d, (), ['maska'])
    for gt, gd, k in ((gmixT, gmix_d, 'gmixT'), (gffnT, gffn_d, 'gffnT'), (gpleT, gple_d, 'gpleT')):
        dma('sp', gt[:, :], gd.rearrange("(c p) -> p c", p=128), (), [k], allow_slow_non_contiguous=True)
    memset('dve', ones_bf[:, :], 1.0, ['ones'])
    memset('dve', eps_c[:, :], EPS, ['eps'])

    def rstd_chain(ms_ap, out_ap, inv_n, rk, wk):
        act(out_ap, ms_ap, AF.Ln, list(rk) + ['eps'], wk, bias=eps_c[0:ms_ap.shape[0], 0:1], scale=inv_n)
        act(out_ap, out_ap, AF.Exp, wk, wk, scale=-0.5)

    def norm_tile(src, src_keys, gT, gkey, dst, dst_keys, pslot):
        act(xs[:, :], src, AF.Square, src_keys, ['xs', 'stat'], accum_out=stat[:, 0:1])
        rstd_chain(stat[:, 0:1], stat[:, 1:2], 1.0 / D, ['stat'], ['stat'])
        ts('dve', xs[:, :], src, stat[:, 1:2], None, ALU.mult, None, list(src_keys) + ['stat'], ['xs'])
        pk = f'ptr{pslot}'
        for k in range(8):
            tr(ptr[pslot][:, k * 128:(k + 1) * 128], xs[:, k * 128:(k + 1) * 128], ident[:, :],
               ['xs', 'ident'], [pk])
        tt('dve', dst, ptr[pslot][:, :].rearrange("p (c t) -> p c t", c=8), bc_last(gT[:, :], 128),
           ALU.mult, [pk, gkey], dst_keys)

    w_in_sb = A.alloc("w_in", [128, 8, INC], BF16)
    wqup_sb = A.alloc("wqup", [128, 2, 768], BF16)
    wkv_sb = A.alloc("wkv", [128, 1024], BF16)
    wkn_sb = A.alloc("wkn", [128, 512], BF16)
    wa2_sb = A.alloc("wa2", [16, 256], F32)
    nba = A.alloc("nba", [128, 2], F32)
    goutc = A.alloc("goutc", [128, 1], F32)
    gqa_c = A.alloc("gqa", [128, 2], F32)
    gknc = A.alloc("gknc", [128, 1], F32)
    Gqn = A.alloc("Gqn", [128, 64], F32)
    Gqr = A.alloc("Gqr", [128, 32], F32)
    Gkva = A.alloc("Gkva", [128, 128], F32)
    Gkr = A.alloc("Gkr", [128, 32], F32)
    Gkn = A.alloc("Gkn", [128, 64], F32)
    cos_sb = A.alloc("cos", [128, NT, 16], F32)
    sin_sb = A.alloc("sin", [128, NT, 16], F32)
    rs_sb = A.alloc("rs", [128, 2, 128], F32)
    iota_sb = A.alloc("iota", [128, 1], F32)
    x_sb = A.alloc("x_sb", [128, 2, D], F32)
    nT = A.alloc("nT", [128, 2, 8, 128], BF16)
    gaT = A.alloc("gaT", [16, 128], F32)
    sp_ = A.alloc("sp", [128, 2, 128], F32)
    bpos = A.alloc("bpos", [128, 2, 128], F32)
    etmp = A.alloc("etmp", [128, 2, 128], F32)
    nbl = A.alloc("nbl", [128, 8], F32)
    ebl = A.alloc("ebl", [128, 8], F32)
    qdT = A.alloc("qdT", [128, 4, 128], BF16)
    ke_z = A.alloc("ke_z", [128, 4, 256], BF16)
    rowm = A.alloc("rowm", [128, 4], F32)
    hm = A.alloc("hm", [128, 2], F32)
    wvp_sb = A.alloc("wvp", [128, 4, 128], BF16)
    kiT = A.alloc("kiT", [128, 2, 128], BF16)
    keT = A.alloc("keT", [128, 2, 128], BF16)
    ke_tok = A.alloc("ke_tok", [128, 256], BF16)
    v_tok = A.alloc("v_tok", [128, 512], BF16)
    sgT = A.alloc("sgT", [128, 4, 128], BF16)
    AT = A.alloc("AT", [128, 4, 128], BF16)
    S_f = A.alloc("S_f", [128, 2, 128], F32)
    S_b = A.alloc("S_b", [128, 2, 128], BF16)
    S0_f = A.alloc("S0_f", [128, 4, 2, 128], F32)
    S0_b = A.alloc("S0_b", [128, 4, 2, 128], BF16)
    osq = A.alloc("osq", [128, 512], BF16)
    orst = A.alloc("orst", [128, 512], F32)
    u_sb = A.alloc("u_sb", [128, 416], F32)
    ms = A.alloc("ms", [128, 32], F32)
    rst = A.alloc("rst", [128, 32], F32)
    cq = A.alloc("cq", [128, 256], BF16)
    cqT = A.alloc("cqT", [128, 2, 128], BF16)
    ckv_f = A.alloc("ckv_f", [128, 2, 128], F32)
    ckv_b = A.alloc("ckv_b", [128, 132], BF16)
    ckvT = A.alloc("ckvT", [128, 128], BF16)
    sqq = A.alloc("sqq", [128, 768], F32)
    sqk = A.alloc("sqk", [128, 8, 64], F32)
    Rn = A.alloc("Rn", [128, 9, 32], F32)
    Rr = A.alloc("Rr", [128, 2, 9, 32], F32)
    rt1 = A.alloc("rt1", [128, 9, 16], F32)
    rt2 = A.alloc("rt2", [128, 9, 16], F32)
    q_tok = A.alloc("q_tok", [128, 8, 64], BF16)
    qr_tok = A.alloc("qr_tok", [128, 8, 32], BF16)
    k_tok = A.alloc("k_tok", [128, 8, 64], BF16)
    kr_tok = A.alloc("kr_tok", [128, 32], BF16)
    QnT = A.alloc("QnT", [128, 8, 128], BF16)
    QrT = A.alloc("QrT", [128, 8, 128], BF16)
    KnT = A.alloc("KnT", [128, 4, NT * 128], BF16)
    KrT = A.alloc("KrT", [128, NT * 128], BF16)
    Vext = A.alloc("Vext", [128, 16, 8, 65], BF16)
    PT = A.alloc("PT", [128, 2, 4, 128], BF16)
    rl = A.alloc("rl", [128, 8], F32)
    o_tok = A.alloc("o_tok", [128, 8, 64], BF16)
    IDX = A.alloc("IDX", [128, 512], I32)
    PTi = A.alloc("PTi", [128, 512], I32)
    Cbuf = A.alloc("Cbuf", [128, 2, 4, 160], F32)
    Cb = A.alloc("Cb", [128, 2, 4, 132], BF16)
    KRb = A.alloc("KRb", [128, 2, 4, 32], BF16)
    CT = A.alloc("CT", [128, 2, 4, 128], BF16)
    KRT = A.alloc("KRT", [128, 2, 4, 128], BF16)
    sqs = A.alloc("sqs", [128, 2, 1024], BF16)
    ssum = A.alloc("ssum", [128, 32], F32)
    rstk = A.alloc("rstk", [128, 32], F32)
    rstk2 = A.alloc("rstk2", [128, 2, 32], F32)
    sc1 = A.alloc("sc1", [128, 256], F32)
    sc2 = A.alloc("sc2", [128, 256], F32)
    PTs = A.alloc("PTs", [128, 2, 4, 64], BF16)
    WkT = A.alloc("WkT", [128, 4, 128], BF16)
    qgT = A.alloc("qgT", [128, 8, 128], BF16)
    qabsT = A.alloc("qabsT", [128, 8, 128], BF16)
    acc_sb = A.alloc("acc_sb", [64, 132], F32)
    accn = A.alloc("accn", [128, 128], BF16)
    accnT = A.alloc("accnT", [128, 128], BF16)

    for k in range(8):
        dma('sp', w_in_sb[:, k, :], w_in_d[k * 128:(k + 1) * 128, :], (), ['w_in'])
    dma('sp', wqup_sb[:, :, :], wqup_d.rearrange("(c p) n -> p c n", p=128), (), ['wqup'])
    dma('sp', wkv_sb[:, :], wkv_d, (), ['wkv'])
    dma('sp', wa2_sb[:, :], wa2_d, (), ['wa2'])
    dma('sp', nba[:, :], ba_d.rearrange("(c p) -> p c", p=128), (), ['nba'], allow_slow_non_contiguous=True)
    dma('sp', goutc[:, :], gout_d.rearrange("(p o) -> p o", o=1), (), ['goutc'], allow_slow_non_contiguous=True)
    dma('sp', gqa_c[:, :], gqa_d.rearrange("(c p) -> p c", p=128), (), ['gqa'], allow_slow_non_contiguous=True)
    dma('sp', gknc[0:64, :], gkn_d.rearrange("(p o) -> p o", o=1), (), ['gknc'], allow_slow_non_contiguous=True)
    dma('sp', gknc[64:128, :], gkn_d.rearrange("(p o) -> p o", o=1), (), ['gknc'], allow_slow_non_contiguous=True)
    for gt, gd, k in ((Gqn, gqnb_d, 'Gqn'), (Gqr, gqrb_d, 'Gqr'), (Gkva, gkvab_d, 'Gkva'), (Gkr, gkrb_d, 'Gkr'),
                      (Gkn, gknb_d, 'Gkn')):
        dma('sp', gt[:, :], gd, (), [k])
    dma('sp', cos_sb[:, :, :], c_cos_d, (), ['cos'])
    dma('sp', sin_sb[:, :, :], c_sin_d, (), ['sin'])
    dma('sp', rs_sb[:, :, :], c_rs_d, (), ['rs'])
    dma('sp', iota_sb[:, :], c_iota_d, (), ['iota'])
    dma('sp', PTi[:, :], ptb_d, (), ['PTi'])
    for b in range(4):
        for pr in range(2):
            dma('sp', S0_f[:, b, pr, :], sgla_d[b, pr], (), ['S0_f'])
    ts('dve', nba[:, :], nba[:, :], -1.0, None, ALU.mult, None, ['nba'], ['nba'])
    for c in range(2):
        ts('dve', wqup_sb[:, c, :], wqup_sb[:, c, :], gqa_c[:, c:c + 1], None, ALU.mult, None,
           ['wqup', 'gqa'], ['wqup'])
    cp('dve', wkn_sb[:, :].rearrange("p (h d) -> p h d", h=8),
       wkv_sb[:, :].rearrange("p (h e) -> p h e", h=8)[:, :, 0:64], ['wkv'], ['wkn'])
    memset('dve', S_f[:, :, :], 0.0, ['S_f'])
    memset('dve', nbl[:, :], 0.0, ['nbl'])
    memset('dve', qdT[:, :, :], 0.0, ['qdT'])
    memset('dve', QnT[:, :, :], 0.0, ['QnT'])
    memset('dve', QrT[:, :, :], 0.0, ['QrT'])
    memset('dve', KrT[:, :], 0.0, ['KrT'])
    memset('dve', KRT[:, :, :, :], 0.0, ['KRT0', 'KRT1'])
    memset('dve', accn[:, :], 0.0, ['accn'])
    dma('sp', rowm[:, :], c_rowm_d, (), ['rowm'])
    dma('sp', hm[:, :], c_hm_d, (), ['hm'])
    cp('dve', wvp_sb[:, :, :].rearrange("p c (a d) -> p c a d", a=2),
       wkv_sb[:, :].rearrange("p (c a e) -> p c a e", c=4, a=2)[:, :, :, 64:128], ['wkv'], ['wvp'])
    memset('dve', S_b[:, :, :], 0.0, ['S_b'])
    cp('dve', S0_b[:, :, :, :], S0_f[:, :, :, :], ['S0_f'], ['S0_b'])
    memset('dve', ckv_b[:, 128:132], 1.0, ['ckv_b1'])
    memset('dve', Cb[:, :, :, 128:132], 1.0, ['Cb1'])
    memset('dve', Vext[:, :, :, 64:65], 1.0, ['Vext1'])
    memset('dve', omT[:, :, ST * 128:], 0.0, ['omT'])
    ts('dve', IDX[:, :], PTi[:, :], 128.0, iota_sb[:, 0:1], ALU.mult, ALU.add, ['PTi', 'iota'], ['IDX'])

    def load_x(i):
        dma('sp', x_sb[:, i % 2, :], x_d[i], (), [f'x{i % 2}'])

    if os.environ.get("KDBG") == "2":
        load_x(0)
        dma('sp', y_o[0], x_sb[:, 0, :], ['x0'], ['y_o'])
        dma('pool', y_o[3][:, 0:1024], w_in_sb[:, 0, 0:1024], ['w_in'], ['y_o'])
        dma('sp', y_o[4][:, 0:128], identf[:, :], ['identf'], ['y_o'])
    chk(1)
    load_x(0)

    pmi = [0]

    def next_pm():
        pmi[0] ^= 1
        return pmi[0]

    for i in range(NT):
        samp = (i == ST)
        sl = i % 2
        xk = f'x{sl}'
        nk = f'nT{sl}'
        nTi = nT[:, sl, :, :]
        tcol = slice(i * 128, (i + 1) * 128)
        if i + 1 < NT:
            load_x(i + 1)
        norm_tile(x_sb[:, sl, :], [xk], gmixT, 'gmixT', nTi, [nk], 0)
        if i == 0 and os.environ.get("KDBG") == "3":
            dma('pool', y_o[2][:, 0:1024], nT[:, 0, :, :].rearrange("p c t -> p (c t)"), [nk], ['y_o'])
            dma('pool', y_o[4][:, 0:1024], xs[:, :], ['xs'], ['y_o'])
            dma('sp', y_o[5][:, 0:8], stat[:, :], ['stat'], ['y_o'])
            dma('sp', y_o[0], x_sb[:, 0, :], ['x0'], ['y_o'])
        if i == 0 and os.environ.get("KDBG") == "4":
            stg = x_sb[:, 1, :]
            cp('dve', stg, xs[:, :], ['xs'], ['x1'])
            dma('sp', y_o[4], stg, ['x1'], ['y_o'])
            cp('dve', stg, nT[:, 0, :, :].rearrange("p c t -> p (c t)"), [nk], ['x1'])
            dma('sp', y_o[2], stg, ['x1'], ['y_o'])
            cp('dve', x_sb[:, 1, 0:8], stat[:, :], ['stat'], ['x1'])
            cp('dve', x_sb[:, 1, 8:16], gmixT[:, :], ['gmixT'], ['x1'])
            dma('sp', y_o[5], stg, ['x1'], ['y_o'])
        chk(2, i == 0)

        j = next_pm()
        for k in range(8):
            mm(pm[j][0:16, 0:128], w_in_sb[:, k, O_A:O_A + 16], nTi[:, k, :], k == 0, k == 7,
               ['w_in', nk], [f'pm{j}'])
        cp('act', gaT[:, :], pm[j][0:16, 0:128], [f'pm{j}'], ['gaT'])
        chk(2.1, i == 0)
        j = next_pm()
        for pr in range(2):
            mm(pm[j][:, pr * 128:(pr + 1) * 128], wa2_sb[:, pr * 128:(pr + 1) * 128], gaT[:, :], True, True,
               ['wa2', 'gaT'], [f'pm{j}'])
        for pr in range(2):
            act(sp_[:, pr, :], pm[j][:, pr * 128:(pr + 1) * 128], AF.Exp, [f'pm{j}', 'nba'], ['sp'],
                bias=nba[:, pr:pr + 1], scale=-1.0)
        act(sp_[:, :, :], sp_[:, :, :], AF.Ln, ['sp'], ['sp'], bias=1.0, scale=1.0)
        chk(2.3, i == 0)
        rsi = 1 if samp else 0
        for pr in range(2):
            S.op('dve', lambda e, pr=pr, rsi=rsi: e.tensor_tensor_scan(bpos[:, pr, :], rs_sb[:, rsi, :], sp_[:, pr, :],
                                                              0.0, ALU.mult, ALU.add),
                 ['sp', 'rs'], ['bpos'])
        segs = [(b * 32, 32, b * 32 + 9, b) for b in range(4)] if samp else [(0, 128, 127, 0)]
        chk(2.4, i == 0)
        for (c0, ncol, lc, si) in segs:
            ts('dve', nbl[:, si * 2:si * 2 + 2], bpos[:, :, lc], -1.0 / 16, None, ALU.mult, None,
               ['bpos'], ['nbl'])
        act(ebl[:, :], nbl[:, :], AF.Exp, ['nbl'], ['ebl'])
        chk(2.5, i == 0)
        j = next_pm()
        for pr in range(2):
            for k in range(8):
                mm(pm[j][:, pr * 128:(pr + 1) * 128], w_in_sb[:, k, O_Q + pr * 128:O_Q + (pr + 1) * 128],
                   nTi[:, k, :], k == 0, k == 7, ['w_in', nk], [f'pm{j}'])
        act(etmp[:, :, :], bpos[:, :, :], AF.Exp, ['bpos'], ['etmp'], scale=-1.0 / 16)
        for hh in range(2):
            r = slice(hh * 64, hh * 64 + 64)
            stt(qdT[r, hh::2, :], pm[j][r, 0:256].rearrange("p (a t) -> p a t", a=2), 0.125, etmp[r, :, :],
                ALU.mult, ALU.mult, [f'pm{j}', 'etmp'], ['qdT'])
        chk(2.6, i == 0)
        j = next_pm()
        for pr in range(2):
            for k in range(8):
                mm(pm[j][:, pr * 128:(pr + 1) * 128], w_in_sb[:, k, O_K + pr * 128:O_K + (pr + 1) * 128],
                   nTi[:, k, :], k == 0, k == 7, ['w_in', nk], [f'pm{j}'])
        act(etmp[:, :, :], bpos[:, :, :], AF.Exp, ['bpos', 'qdT'], ['etmp'], scale=1.0 / 16)
        tt('dve', kiT[:, :, :], pm[j][:, 0:256].rearrange("p (a t) -> p a t", a=2), etmp[:, :, :], ALU.mult,
           [f'pm{j}', 'etmp'], ['kiT'])
        for (c0, ncol, lc, si) in segs:
            for pr in range(2):
                act(etmp[:, pr, c0:c0 + ncol], bpos[:, pr, c0:c0 + ncol], AF.Exp, ['bpos', 'nbl', 'kiT'],
                    ['etmp'], bias=nbl[:, si * 2 + pr:si * 2 + pr + 1], scale=1.0 / 16)
        tt('dve', keT[:, :, :], pm[j][:, 0:256].rearrange("p (a t) -> p a t", a=2), etmp[:, :, :], ALU.mult,
           [f'pm{j}', 'etmp'], ['keT'])
        for pr in range(2):
            tr(ptr[1][:, pr * 128:(pr + 1) * 128], keT[:, pr, :], ident[:, :], ['keT', 'ident'], ['ptr1'])
        cp('dve', ke_tok[:, :], ptr[1][:, 0:256], ['ptr1'], ['ke_tok'])
        chk(2.7, i == 0)
        j = next_pm()
        for c in range(4):
            for k in range(8):
                mm(pm[j][:, c * 128:(c + 1) * 128], w_in_sb[:, k, O_G + c * 128:O_G + (c + 1) * 128],
                   nTi[:, k, :], k == 0, k == 7, ['w_in', nk], [f'pm{j}'])
        act(sgT[:, :, :], pm[j][:, :].rearrange("p (c t) -> p c t", c=4), AF.Silu, [f'pm{j}'], ['sgT'])
        j = next_pm()
        for k in range(8):
            mm(pm[j][:, :], nTi[:, k, :], w_in_sb[:, k, O_V:O_V + 512], k == 0, k == 7, ['w_in', nk], [f'pm{j}'])
        cp('act', v_tok[:, :], pm[j][:, :], [f'pm{j}'], ['v_tok'])
        chk(2.8, i == 0)
        j = next_pm()
        for h in range(4):
            pr = h // 2
            mm(pm[j][:, h * 128:(h + 1) * 128], kiT[:, pr, :], qdT[:, h, :], True, True,
               ['kiT', 'qdT'], [f'pm{j}'])
        mk, mkey = (masks, 'masks') if samp else (mask01, 'mask01')
        tt('dve', AT[:, :, :], pm[j][:, :].rearrange("p (h t) -> p h t", h=4), bc_mid(mk[:, :], 4), ALU.mult,
           [f'pm{j}', mkey], ['AT'])
        chk(2.9, i == 0)
        j = next_pm()
        for h in range(4):
            pr, b0 = h // 2, (h % 2) * 64
            oh = pm[j][:, h * 128:(h + 1) * 128]
            mm(oh, v_tok[:, h * 128:(h + 1) * 128], AT[:, h, :], True, False, ['v_tok', 'AT'], [f'pm{j}'])
            if not samp:
                mm(oh, S_b[:, pr, :], qdT[:, h, :], False, True, ['S_b', 'qdT'], [f'pm{j}'])
            else:
                for b in range(4):
                    mm(pm[j][:, h * 128 + b * 32:h * 128 + (b + 1) * 32], S0_b[:, b, pr, :],
                       qdT[:, h, b * 32:(b + 1) * 32], False, b == 3, ['S0_b', 'qdT'], [f'pm{j}'])
        jo = j
        chk(2.95, i == 0)
        act(osq[:, :], pm[jo][:, :], AF.Square, [f'pm{jo}'], ['osq'])
        j = next_pm()
        mm(pm[j][:, :], ones_bf[:, :], osq[:, :], True, True, ['ones', 'osq'], [f'pm{j}'])
        rstd_chain(pm[j][:, :], orst[:, :], 1.0 / 128, [f'pm{j}'], ['orst'])
        tt('dve', orst[:, :], pm[jo][:, :], orst[:, :], ALU.mult, [f'pm{jo}', 'orst'], ['orst'])
        stt(ogT[:, :, tcol], orst[:, :].rearrange("p (h t) -> p h t", h=4), goutc[:, 0:1], sgT[:, :, :],
            ALU.mult, ALU.mult, ['orst', 'goutc', 'sgT'], ['ogT'])
        chk(2.97, i == 0)
        if not samp:
            for pr in range(2):
                j = next_pm()
                mm(pm[j][:, 0:256], ke_tok[:, pr * 128:(pr + 1) * 128], v_tok[:, pr * 256:(pr + 1) * 256],
                   True, True, ['ke_tok', 'v_tok'], [f'pm{j}'])
                ts('dve', orst[:, 0:128], pm[j][:, 0:128], hm[:, 0:1], None, ALU.mult, None, [f'pm{j}', 'hm'], ['orst'])
                stt(orst[:, 0:128], pm[j][:, 128:256], hm[:, 1:2], orst[:, 0:128], ALU.mult, ALU.add,
                    [f'pm{j}', 'hm', 'orst'], ['orst'])
                stt(S_f[:, pr, :], S_f[:, pr, :], ebl[:, pr:pr + 1], orst[:, 0:128], ALU.mult, ALU.add,
                    ['S_f', 'ebl', 'orst'], ['S_f'])
                chk(2.98, i == 0 and pr == 0)
                chk(2.99, i == 0 and pr == 1)
            cp('dve', S_b[:, :, :], S_f[:, :, :], ['S_f'], ['S_b'])
            if i == ST - 1:
                for pr in range(2):
                    dma('sp', glap_o[pr], S_f[:, pr, :], ['S_f'], ['glap_o'])
        else:
            for b in range(4):
                ts('dve', ke_z[:, b, :], ke_tok[:, :], rowm[:, b:b + 1], None, ALU.mult, None,
                   ['ke_tok', 'rowm'], ['ke_z'])
            for b in range(4):
                for pr in range(2):
                    j = next_pm()
                    mm(pm[j][:, 0:256], ke_z[:, b, pr * 128:(pr + 1) * 128],
                       v_tok[:, pr * 256:(pr + 1) * 256], True, True, ['ke_z', 'v_tok'], [f'pm{j}'])
                    ts('dve', orst[:, 0:128], pm[j][:, 0:128], hm[:, 0:1], None, ALU.mult, None, [f'pm{j}', 'hm'],
                       ['orst'])
                    stt(orst[:, 0:128], pm[j][:, 128:256], hm[:, 1:2], orst[:, 0:128], ALU.mult, ALU.add,
                        [f'pm{j}', 'hm', 'orst'], ['orst'])
                    stt(S0_f[:, b, pr, :], S0_f[:, b, pr, :], ebl[:, b * 2 + pr:b * 2 + pr + 1], orst[:, 0:128],
                        ALU.mult, ALU.add, ['S0_f', 'ebl', 'orst'], ['S0_f'])
                    dma('sp', glas_o[b, pr], S0_f[:, b, pr, :], ['S0_f'], ['glas_o'])

        chk(3, i == 0)
        j = next_pm()
        for k in range(8):
            mm(pm[j][:, 0:416], nTi[:, k, :], w_in_sb[:, k, O_MQ:O_MQ + 416], k == 0, k == 7,
               ['w_in', nk], [f'pm{j}'])
        cp('dve', u_sb[:, :], pm[j][:, 0:416], [f'pm{j}'], ['u_sb'])
        chk(3.1, i == 0)
        for (a, b_, col, n) in ((0, 256, 24, 256), (256, 384, 25, 128), (384, 416, 26, 32)):
            if True:
                act(sqq[:, a:b_], u_sb[:, a:b_], AF.Square, ['u_sb'], ['sqq', 'ms'], accum_out=ms[:, col:col + 1],
                    scale=float(n) ** -0.5)
            else:
                act(sqq[:, a:b_], u_sb[:, a:b_], AF.Square, ['u_sb'], ['sqq'], scale=float(n) ** -0.5)
                S.op('dve', lambda e, a=a, b_=b_, col=col: e.tensor_reduce(ms[:, col:col + 1], sqq[:, a:b_], AX.X, ALU.add),
                     ['sqq'], ['ms'])
        rstd_chain(ms[:, 24:27], rst[:, 24:27], 1.0, ['ms'], ['rst'])
        chk(3.2, i == 0)
        ts('dve', cq[:, :], u_sb[:, 0:256], rst[:, 24:25], None, ALU.mult, None, ['u_sb', 'rst'], ['cq'])
        cs = i % 2
        stt(ckv_f[:, cs, :], u_sb[:, 256:384], rst[:, 25:26], Gkva[:, :], ALU.mult, ALU.mult,
            ['u_sb', 'rst', 'Gkva'], [f'ckv_f{cs}'])
        dma('sp', ckv_o[i], ckv_f[:, cs, :], [f'ckv_f{cs}'], ['ckv_o'])
        cp('dve', ckv_b[:, 0:128], ckv_f[:, cs, :], [f'ckv_f{cs}'], ['ckv_b'])
        stt(Rn[:, 8, :], u_sb[:, 384:416], rst[:, 26:27], Gkr[:, :], ALU.mult, ALU.mult,
            ['u_sb', 'rst', 'Gkr'], ['Rn'])
        chk(3.3, i == 0)
        for c in range(2):
            tr(ptr[1][:, c * 128:(c + 1) * 128], cq[:, c * 128:(c + 1) * 128], ident[:, :], ['cq', 'ident'], ['ptr1'])
        tr(ptr[1][:, 256:384], ckv_b[:, 0:128], ident[:, :], ['ckv_b', 'ident'], ['ptr1'])
        cp('dve', cqT[:, :, :], ptr[1][:, 0:256].rearrange("p (c t) -> p c t", c=2), ['ptr1'], ['cqT'])
        cp('dve', ckvT[:, :], ptr[1][:, 256:384], ['ptr1'], ['ckvT'])
        chk(3.4, i == 0)
        for half, (n0, n1) in enumerate(((0, 512), (512, 768))):
            for c in range(2):
                mm(pw[0][:, n0:n1], cqT[:, c, :], wqup_sb[:, c, n0:n1], c == 0, c == 1, ['cqT', 'wqup'], ['pw0'])
        for half in range(2):
            mm(pw[1][:, half * 512:(half + 1) * 512], ckvT[:, :], wkv_sb[:, half * 512:(half + 1) * 512],
               True, True, ['ckvT', 'wkv'], ['pw1'])
        qfv = pw[0][:, 0:768].rearrange("p (h e) -> p h e", h=8)
        kvv = pw[1][:, :].rearrange("p (h e) -> p h e", h=8)
        chk(3.5, i == 0)
        act(sqq[:, :], pw[0][:, 0:768], AF.Square, ['pw0'], ['sqq'])
        sqv = sqq[:, :].rearrange("p (h e) -> p h e", h=8)
        S.op('dve', lambda e: e.tensor_reduce(ms[:, 0:8], sqv[:, :, 0:64], AX.X, ALU.add), ['sqq'], ['ms'])
        S.op('dve', lambda e: e.tensor_reduce(ms[:, 16:24], sqv[:, :, 64:96], AX.X, ALU.add), ['sqq'], ['ms'])
        act(sqk[:, :, :], kvv[:, :, 0:64], AF.Square, ['pw1'], ['sqk'])
        S.op('dve', lambda e: e.tensor_reduce(ms[:, 8:16], sqk[:, :, :], AX.X, ALU.add), ['sqk'], ['ms'])
        rstd_chain(ms[:, 0:16], rst[:, 0:16], 1.0 / 64, ['ms'], ['rst'])
        rstd_chain(ms[:, 16:24], rst[:, 16:24], 1.0 / 32, ['ms'], ['rst'])
        chk(3.6, i == 0)
        tt('dve', sqk[:, :, :], qfv[:, :, 0:64], bc_last(rst[:, 0:8], 64), ALU.mult, ['pw0', 'rst'], ['sqk'])
        tt('dve', q_tok[:, :, :], sqk[:, :, :], bc_mid(Gqn[:, :], 8), ALU.mult, ['sqk', 'Gqn'], ['q_tok'])
        tt('dve', sqk[:, :, :], kvv[:, :, 0:64], bc_last(rst[:, 8:16], 64), ALU.mult, ['pw1', 'rst', 'q_tok'],
           ['sqk'])
        tt('dve', k_tok[:, :, :], sqk[:, :, :], bc_mid(Gkn[:, :], 8), ALU.mult, ['sqk', 'Gkn'], ['k_tok'])
        chk(3.7, i == 0)
        if not samp:
            cp('act', Vext[:, i, :, 0:64], kvv[:, :, 64:128], ['pw1'], ['Vext'])
        tt('dve', Rn[:, 0:8, :], qfv[:, :, 64:96], bc_last(rst[:, 16:24], 32), ALU.mult, ['pw0', 'rst'], ['Rn'])
        tt('dve', Rn[:, 0:8, :], Rn[:, 0:8, :], bc_mid(Gqr[:, :], 8), ALU.mult, ['Rn', 'Gqr'], ['Rn'])
        cosb = bc_mid(cos_sb[:, i, :], 9)
        sinb = bc_mid(sin_sb[:, i, :], 9)
        x1, x2 = Rn[:, :, 0:16], Rn[:, :, 16:32]
        rk = f'Rr{cs}'
        tt('dve', rt1[:, :, :], x1, cosb, ALU.mult, ['Rn', 'cos'], ['rt1'])
        tt('dve', rt2[:, :, :], x2, sinb, ALU.mult, ['Rn', 'sin'], ['rt2'])
        tt('dve', Rr[:, cs, :, 0:16], rt1[:, :, :], rt2[:, :, :], ALU.subtract, ['rt1', 'rt2'], [rk])
        tt('dve', rt1[:, :, :], x2, cosb, ALU.mult, ['Rn', 'cos', rk], ['rt1'])
        tt('dve', rt2[:, :, :], x1, sinb, ALU.mult, ['Rn', 'sin', rk], ['rt2'])
        tt('dve', Rr[:, cs, :, 16:32], rt1[:, :, :], rt2[:, :, :], ALU.add, ['rt1', 'rt2'], [rk])
        dma('sp', kr_o[i], Rr[:, cs, 8, :], [rk], ['kr_o'])
        cp('dve', qr_tok[:, :, :], Rr[:, cs, 0:8, :], [rk], ['qr_tok'])
        cp('dve', kr_tok[:, :], Rr[:, cs, 8, :], [rk], ['kr_tok'])
        chk(3.8, i == 0)
        for c in range(4):
            tr(ptr[0][:, c * 128:(c + 1) * 128], q_tok[:, 2 * c:2 * c + 2, :], ident[:, :], ['q_tok', 'ident'], ['ptr0'])
            tr(ptr[0][:, 512 + c * 128:512 + (c + 1) * 128], k_tok[:, 2 * c:2 * c + 2, :], ident[:, :],
               ['k_tok', 'ident'], ['ptr0'])
        for hh in range(2):
            r = slice(hh * 64, hh * 64 + 64)
            cp('dve', QnT[r, hh::2, :], ptr[0][r, 0:512].rearrange("p (c t) -> p c t", c=4), ['ptr0'], ['QnT'])
        cp('dve', KnT[:, :, tcol], ptr[0][:, 512:1024].rearrange("p (c t) -> p c t", c=4), ['ptr0'], ['KnT'])
        for h in range(8):
            tr(ptr[1][0:32, h * 128:(h + 1) * 128], qr_tok[:, h, :], ident[:, :], ['qr_tok', 'ident'], ['ptr1'])
        cp('dve', QrT[0:32, :, :], ptr[1][0:32, :].rearrange("p (h t) -> p h t", h=8), ['ptr1'], ['QrT'])
        tr(ptr[1][0:32, 0:128], kr_tok[:, :], ident[:, :], ['kr_tok', 'ident', 'QrT'], ['ptr1'])
        cp('dve', KrT[0:32, tcol], ptr[1][0:32, 0:128], ['ptr1'], ['KrT'])

        if i == 0 and os.environ.get("KDBG"):
            dma('sp', y_o[0][:, 0:416], u_sb[:, :], ['u_sb'], ['y_o'])
            dma('sp', y_o[1][:, 0:32], rst[:, :], ['rst'], ['y_o'])
            dma('pool', y_o[2][:, 0:1024], nT[:, 0, :, :].rearrange("p c t -> p (c t)"), [nk], ['y_o'])
            dma('pool', y_o[3][:, 0:1024], w_in_sb[:, 0, 0:1024], ['w_in'], ['y_o'])
            dma('pool', y_o[4][:, 0:1024], xs[:, :], ['xs'], ['y_o'])
            dma('sp', y_o[5][:, 0:8], stat[:, :], ['stat'], ['y_o'])
            dma('sp', y_o[6][:, 0:256], ckv_f[:, :, :].rearrange("p a c -> p (a c)"), ['ckv_f0'], ['y_o'])
        if i == 0 and os.environ.get("KDBG") == "6":
            stg = x_sb[:, 1, :]
            cp('dve', x_sb[:, 1, 0:416], u_sb[:, :], ['u_sb'], ['x1'])
            cp('dve', x_sb[:, 1, 416:448], rst[:, :], ['rst'], ['x1'])
            cp('dve', x_sb[:, 1, 448:480], ms[:, :], ['ms'], ['x1'])
            cp('dve', x_sb[:, 1, 512:640], ckv_f[:, 0, :], ['ckv_f0'], ['x1'])
            cp('dve', x_sb[:, 1, 640:768], Gkva[:, :], ['Gkva'], ['x1'])
            cp('dve', x_sb[:, 1, 768:800], Rr[:, 0, 8, :], ['Rr0'], ['x1'])
            dma('sp', y_o[4], stg, ['x1'], ['y_o'])
        chk(4, i == 0)
        chk(7, samp)
        if not samp:
            groups = []
            for h in range(8):
                for kt0 in range(0, i + 1, 4):
                    groups.append((h, kt0, min(4, i + 1 - kt0)))

            def emit_scores(gi):
                h, kt0, nkt = groups[gi]
                pr, b0 = h // 2, (h % 2) * 64
                s = gi % 2
                for jj in range(nkt):
                    kt = kt0 + jj
                    kc = slice(kt * 128, (kt + 1) * 128)
                    o = pm[s][:, jj * 128:(jj + 1) * 128]
                    mm(o, KnT[:, pr, kc], QnT[:, h, :], True, False, ['KnT', 'QnT'], [f'pm{s}'])
                    mm(o, KrT[:, kc], QrT[:, h, :], False, True, ['KrT', 'QrT'], [f'pm{s}'])
                act(PT[:, s, 0:nkt, :], pm[s][:, 0:nkt * 128].rearrange("p (a t) -> p a t", a=nkt), AF.Exp,
                    [f'pm{s}'], [f'PT{s}'], scale=MLA_SCALE)
                if kt0 + nkt - 1 == i:
                    tt('dve', PT[:, s, nkt - 1, :], PT[:, s, nkt - 1, :], mask01[:, :], ALU.mult,
                       [f'PT{s}', 'mask01'], [f'PT{s}'])

            def emit_pv(gi):
                h, kt0, nkt = groups[gi]
                s = gi % 2
                bank = h // 4
                o = pw[bank][:, (h % 4) * 65:(h % 4) * 65 + 65]
                for jj in range(nkt):
                    kt = kt0 + jj
                    mm(o, PT[:, s, jj, :], Vext[:, kt, h, :], kt == 0, kt == i, [f'PT{s}', 'Vext', 'Vext1'],
                       [f'pw{bank}'])

            emit_scores(0)
            for gi in range(len(groups)):
                if gi + 1 < len(groups):
                    emit_scores(gi + 1)
                emit_pv(gi)
            for bank in range(2):
                ov = pw[bank][:, 0:260].rearrange("p (h e) -> p h e", h=4)
                S.op('dve', lambda e, ov=ov, bank=bank: e.reciprocal(rl[:, bank * 4:bank * 4 + 4], ov[:, :, 64]),
                     [f'pw{bank}'], ['rl'])
                tt('dve', o_tok[:, bank * 4:bank * 4 + 4, :], ov[:, :, 0:64], bc_last(rl[:, bank * 4:bank * 4 + 4], 64),
                   ALU.mult, [f'pw{bank}', 'rl'], ['o_tok'])
            for c in range(4):
                tr(ptr[0][:, c * 128:(c + 1) * 128], o_tok[:, 2 * c:2 * c + 2, :], ident[:, :], ['o_tok', 'ident'],
                   ['ptr0'])
            cp('dve', omT[:, :, tcol], ptr[0][:, 0:512].rearrange("p (c t) -> p c t", c=4), ['ptr0'], ['omT'])
            chk(5, i == 0)
            chk(6, i == ST - 1)
        elif not _SKIP_PA:
            for pr in range(4):
                tr(ptr[0][:, pr * 128:(pr + 1) * 128], wkn_sb[:, pr * 128:(pr + 1) * 128], ident[:, :],
                   ['wkn', 'ident'], ['ptr0'])
            cp('dve', WkT[:, :, :], ptr[0][:, 0:512].rearrange("p (c t) -> p c t", c=4), ['ptr0'], ['WkT'])
            ts('dve', qgT[:, :, :], QnT[:, :, :], gknc[:, 0:1], None, ALU.mult, None, ['QnT', 'gknc'], ['qgT'])
            for h in range(8):
                pr, b0 = h // 2, (h % 2) * 64
                bank = h // 4
                mm(pw[bank][:, (h % 4) * 128:(h % 4 + 1) * 128], WkT[:, pr, :], qgT[:, h, :],
                   True, True, ['WkT', 'qgT'], [f'pw{bank}'])
            for bank in range(2):
                cp('dve', qabsT[:, bank * 4:bank * 4 + 4, :],
                   pw[bank][:, 0:512].rearrange("p (h t) -> p h t", h=4), [f'pw{bank}'], ['qabsT'])
            cp('dve', rstk[:, 0:8], rst[:, 8:16], ['rst'], ['rstk_new'])
            chk(7.1)

            def gather(b, g):
                s = g % 2
                for jj in range(4):
                    col = b * NPG + g * 4 + jj
                    S.op('pool', lambda e, s=s, jj=jj, col=col: e.indirect_dma_start(
                        out=Cbuf[:, s, jj, :], out_offset=None, in_=cc_d,
                        in_offset=bass.IndirectOffsetOnAxis(ap=IDX[:, col:col + 1], axis=0)),
                        ['IDX'], [f'Cbuf{s}'], dma=True)

            def score_pv(b, lhs_c, lhs_r, rstd_ap, nj, rhs_pv, keys_r, first, last, maskap=None):
                qa = qabsT[:, :, b * 32 + 2:b * 32 + 10]
                qr = QrT[:, :, b * 32 + 2:b * 32 + 10]
                for jj in range(nj):
                    mm(pm[0][:, jj * 64:(jj + 1) * 64], lhs_c(jj), qa, True, True, keys_r + ['qabsT'], ['pm0'])
                    mm(pm[0][:, 256 + jj * 64:256 + (jj + 1) * 64], lhs_r(jj), qr, True, True, keys_r + ['QrT'],
                       ['pm0'])
                n = nj * 64
                tt('dve', sc1[:, 0:n].rearrange("p (a t) -> p a t", t=8),
                   pm[0][:, 0:n].rearrange("p (a t) -> p a t", t=8), bc_last(rstd_ap, 8), ALU.mult,
                   ['pm0', 'rstk', 'rstk_new'] + [k for k in keys_r if k.startswith('rk')], ['sc1'])
                tt('dve', sc2[:, 0:n], sc1[:, 0:n], pm[0][:, 256:256 + n], ALU.add, ['sc1', 'pm0'], ['sc2'])
                return n

            def stageA(b, g):
                s = g % 2
                cp('act', Cb[:, s, :, 0:128], Cbuf[:, s, :, 0:128], [f'Cbuf{s}'], [f'Cb{s}'])
                cp('dve', KRb[:, s, :, :], Cbuf[:, s, :, 128:160], [f'Cbuf{s}'], [f'KRb{s}'])
                for jj in range(4):
                    tr(ptr[0][:, jj * 128:(jj + 1) * 128], Cb[:, s, jj, 0:128], ident[:, :],
                       [f'Cb{s}', 'ident'], ['ptr0'])
                    tr(ptr[0][0:32, 512 + jj * 128:512 + (jj + 1) * 128], KRb[:, s, jj, :], ident[:, :],
                       [f'KRb{s}', 'ident'], ['ptr0'])
                cp('dve', CT[:, s, :, :], ptr[0][:, 0:512].rearrange("p (a t) -> p a t", a=4), ['ptr0'],
                   [f'CT{s}'])
                cp('dve', KRT[0:32, s, :, :], ptr[0][0:32, 512:1024].rearrange("p (a t) -> p a t", a=4), ['ptr0'],
                   [f'KRT{s}'])
                for hf in range(2):
                    for q2 in range(2):
                        jj = hf * 2 + q2
                        mm(pw[hf][:, q2 * 512:(q2 + 1) * 512], CT[:, s, jj, :], wkn_sb[:, :], True, True,
                           [f'CT{s}', 'wkn'], [f'pw{hf}'])
                    act(sqs[:, hf, :], pw[hf][:, :], AF.Square, [f'pw{hf}'], [f'sqs{hf}'])
                    S.op('dve', lambda e, hf=hf: e.tensor_reduce(
                        ssum[:, hf * 16:(hf + 1) * 16], sqs[:, hf, :].rearrange("p (a d) -> p a d", d=64),
                        AX.X, ALU.add), [f'sqs{hf}'], ['ssum'])
                rstd_chain(ssum[:, 0:32], rstk2[:, s, :], 1.0 / 64, ['ssum'], [f'rk{s}'])

            def stageB(b, g):
                s = g % 2
                score_pv(b, lambda jj: CT[:, s, jj, :], lambda jj: KRT[:, s, jj, :], rstk2[:, s, :], 4,
                         None, [f'CT{s}', f'KRT{s}', f'rk{s}'], None, None)
                act(PTs[:, s, :, :], sc2[:, 0:256].rearrange("p (a t) -> p a t", a=4), AF.Exp, ['sc2'],
                    [f'PTs{s}'], scale=MLA_SCALE)
                for jj in range(4):
                    mm(pm[1][0:64, 0:129], PTs[:, s, jj, :], Cb[:, s, jj, 0:129], g == 0 and jj == 0, False,
                       [f'PTs{s}', f'Cb{s}', 'Cb1'], ['pm1'])

            NG = NPG // 4
            for b in range(4):
                gather(b, 0)
                gather(b, 1)
                stageA(b, 0)
                for g in range(NG):
                    if g + 2 < NG:
                        gather(b, g + 2)
                    if g + 1 < NG:
                        stageA(b, g + 1)
                    stageB(b, g)
                cp('dve', rstk[:, 0:8], rst[:, 8:16], ['rst'], ['rstk'])
                score_pv(b, lambda jj: ckvT[:, :], lambda jj: KrT[:, tcol], rstk[:, 0:8], 1, None,
                         ['ckvT', 'KrT'], None, None)
                act(PTs[:, 0, 0, :], sc2[:, 0:64], AF.Exp, ['sc2'], ['PTs0'], scale=MLA_SCALE)
                tt('dve', PTs[:, 0, 0, :], PTs[:, 0, 0, :], maska[:, b, :], ALU.mult, ['PTs0', 'maska'], ['PTs0'])
                mm(pm[1][0:64, 0:129], PTs[:, 0, 0, :], ckv_b[:, 0:129], False, True, ['PTs0', 'ckv_b', 'ckv_b1'],
                   ['pm1'])
                cp('dve', acc_sb[:, 0:129], pm[1][0:64, 0:129], ['pm1'], ['acc_sb'])
                S.op('dve', lambda e: e.reciprocal(acc_sb[:, 130:131], acc_sb[:, 128:129]), ['acc_sb'], ['acc_sb'])
                ts('dve', accn[0:64, :], acc_sb[:, 0:128], acc_sb[:, 130:131], None, ALU.mult, None, ['acc_sb'], ['accn'])
                tr(ptr[1][:, 0:128], accn[:, :], ident[:, :], ['accn', 'ident'], ['ptr1'])
                cp('dve', accnT[:, :], ptr[1][:, 0:128], ['ptr1'], ['accnT'])
                for c in range(4):
                    mm(pm[0][:, c * 16:(c + 1) * 16], wvp_sb[:, c, :], accnT[:, c * 16:(c + 1) * 16], True, True,
                       ['wvp', 'accnT'], ['pm0'])
                ocol = slice(ST * 128 + b * 32 + 2, ST * 128 + b * 32 + 10)
                pv4 = pm[0][:, 0:64].rearrange("p (c t) -> p c t", c=4)
                cp('dve', omT[0:64, :, ocol], pv4[0:64, :, 0:8], ['pm0'], ['omT'])
                cp('dve', omT[64:128, :, ocol], pv4[64:128, :, 8:16], ['pm0'], ['omT'])
                chk(7.6, b == 0)

    if os.environ.get("KDBG") == "7":
        for tt_ in range(2):
            cp('dve', x_sb[:, 0, 0:512].rearrange("p (c t) -> p c t", c=4), ogT[:, :, tt_ * 128:(tt_ + 1) * 128],
               ['ogT'], ['x0'])
            cp('dve', x_sb[:, 0, 512:1024].rearrange("p (c t) -> p c t", c=4), omT[:, :, tt_ * 128:(tt_ + 1) * 128],
               ['omT'], ['x0'])
            dma('sp', y_o[4 + tt_], x_sb[:, 0, :], ['x0'], ['y_o'])
    chk(8)
    S.barrier()
    A.off = persist_off
    wd_sb = A.alloc("wd", [128, NF, D], BF16)
    wo_sb = A.alloc("wo", [128, 8, D], BF16)
    wpp_sb = A.alloc("wpp", [128, 2, D], BF16)
    wst = A.alloc("wst", [128, 2, 2, 8, 512], BF16)
    h_g = A.alloc("h_g", [128, 4, D], F32)
    nT2 = A.alloc("nT2", [128, 8, 512], BF16)
    hidT = A.alloc("hidT", [128, NF, 512], BF16)
    a_sb = A.alloc("a_sb", [128, 520], F32)
    tbuf = A.alloc("tbuf", [128, 512], F32)
    big = A.alloc("big", [128, D], F32)
    p_sb = A.alloc("p_sb", [128, 256], F32)
    p_bf = A.alloc("p_bf", [128, 256], BF16)
    pT = A.alloc("pT", [128, 2, 128], BF16)
    cw_sb = A.alloc("cw", [128, 3, NF], F32)
    cb_sb = A.alloc("cb", [128, NF], F32)
    halo = A.alloc("halo", [128, NF, 2], F32)
    cst = A.alloc("cst", [128, NF, 8], F32)
    sc_bf = A.alloc("sc_bf", [128, DFF], BF16)
    sel_sb = A.alloc("sel", [128, 128], BF16)
    ctok = A.alloc("ctok", [8, 512], F32)

    for f in range(NF):
        dma('sp', wd_sb[:, f, :], wd_d[f * 128:(f + 1) * 128, :], (), ['wd'])
    for k in range(8):
        dma('sp', wo_sb[:, k, :], wo_d[k * 128:(k + 1) * 128, :], (), ['wo'])
    dma('sp', wpp_sb[:, :, :], wpp_d.rearrange("(c p) n -> p c n", p=128), (), ['wpp'])
    dma('sp', cw_sb[:, :, :], cw_d.rearrange("j (f p) -> p j f", p=128), (), ['cw'], allow_slow_non_contiguous=True)
    dma('sp', cb_sb[:, :], cb_d.rearrange("(f p) -> p f", p=128), (), ['cb'], allow_slow_non_contiguous=True)
    memset('dve', sc_bf[:, :], 0.0, ['sc_bf'])
    dma('sp', sc_bf[0:8, :], sconv_d, ['sc_bf'], ['sc_bf'])
    dma('sp', sel_sb[:, :], c_sel_d, (), ['sel'])
    memset('dve', halo[:, :, :], 0.0, ['halo'])
    memset('dve', a_sb[:, :], 0.0, ['a_sb'])

    blocks = [(0, 512), (512, 512), (1024, 512), (1536, 512), (2048, 512), (2560, 256)]
    wst_cnt = [0]

    def load_ffn_block(bi):
        c0, w = blocks[bi]
        s = wst_cnt[0] % 2
        wst_cnt[0] += 1
        for gu, wdram in ((0, wg_d), (1, wu_d)):
            for k in range(8):
                dma('sp', wst[:, s, gu, k, 0:w], wdram[k * 128:(k + 1) * 128, c0:c0 + w], (), [f'wst{s}'])
        return s

    def load_ple_gate():
        s = wst_cnt[0] % 2
        wst_cnt[0] += 1
        for half in range(2):
            for k in range(8):
                dma('sp', wst[:, s, half, k, :], wpg_d[k * 128:(k + 1) * 128, half * 512:(half + 1) * 512], (),
                    [f'wst{s}'])
        return s

    tgroups = [[0, 1, 2, 3], [4, 5, 6, 7], [8, 9, 10, 11], [12, 13, 14, 15], [ST]]
    for gi, tiles in enumerate(tgroups):
        samp = (tiles[0] == ST)
        N = 128 * len(tiles)
        slot0 = load_ffn_block(0)
        for ti, t in enumerate(tiles):
            tcol = slice(t * 128, (t + 1) * 128)
            dma('sp', big[:, :], x_d[t], (), ['big'])
            for half in range(2):
                for c in range(8):
                    src = ogT if c < 4 else omT
                    mm(pw[0][:, half * 512:(half + 1) * 512], src[:, c % 4, tcol],
                       wo_sb[:, c, half * 512:(half + 1) * 512], c == 0, c == 7, ['ogT', 'omT', 'wo'], ['pw0'])
            tt('dve', h_g[:, ti, :], big[:, :], pw[0][:, :], ALU.add, ['big', 'pw0'], [f'h{ti}'])
            norm_tile(h_g[:, ti, :], [f'h{ti}'], gffnT, 'gffnT', nT2[:, :, ti * 128:(ti + 1) * 128], ['nT2'], 0)
        chk(9, gi == 0)
        for bi, (c0, w) in enumerate(blocks):
            s = slot0 if bi == 0 else s_next
            if bi + 1 < len(blocks):
                s_next = load_ffn_block(bi + 1)
            else:
                s_pg = load_ple_gate()
            for fi in range(w // 128):
                f = c0 // 128 + fi
                fc = slice(fi * 128, (fi + 1) * 128)
                for k in range(8):
                    mm(pm[0][:, 0:N], wst[:, s, 0, k, fc], nT2[:, k, 0:N], k == 0, (k == 7 and not samp),
                       [f'wst{s}', 'nT2'], ['pm0'])
                if samp:
                    mm(pm[0][:, 0:N], sc_bf[:, f * 128:(f + 1) * 128], sel_sb[:, :], False, True,
                       ['sc_bf', 'sel'], ['pm0'])
                for k in range(8):
                    mm(pm[1][:, 0:N], wst[:, s, 1, k, fc], nT2[:, k, 0:N], k == 0, k == 7, [f'wst{s}', 'nT2'], ['pm1'])
                if not samp:
                    cp('dve', a_sb[:, 0:2], halo[:, f, :], ['halo'], ['a_sb'])
                cp('act', a_sb[:, 2:2 + N], pm[0][:, 0:N], ['pm0'], ['a_sb'])
                if not samp:
                    cp('dve', halo[:, f, :], a_sb[:, N:N + 2], ['a_sb'], ['halo'])
                else:
                    cp('dve', cst[:, f, :].rearrange("p (b j) -> p b j", b=4),
                       a_sb[:, 2:2 + 128].rearrange("p (b c) -> p b c", b=4)[:, :, 8:10], ['a_sb'], ['cst'])
                ts('dve', tbuf[:, 0:N], a_sb[:, 0:N], cw_sb[:, 0, f:f + 1], cb_sb[:, f:f + 1], ALU.mult, ALU.add,
                   ['a_sb', 'cw', 'cb'], ['tbuf'])
                stt(tbuf[:, 0:N], a_sb[:, 1:1 + N], cw_sb[:, 1, f:f + 1], tbuf[:, 0:N], ALU.mult, ALU.add,
                    ['a_sb', 'cw', 'tbuf'], ['tbuf'])
                stt(tbuf[:, 0:N], a_sb[:, 2:2 + N], cw_sb[:, 2, f:f + 1], tbuf[:, 0:N], ALU.mult, ALU.add,
                    ['a_sb', 'cw', 'tbuf'], ['tbuf'])
                act(tbuf[:, 0:N], tbuf[:, 0:N], AF.Silu, ['tbuf'], ['tbuf'])
                tt('dve', hidT[:, f, 0:N], tbuf[:, 0:N], pm[1][:, 0:N], ALU.mult, ['tbuf', 'pm1'], ['hidT'])
        if gi == 3 or samp:
            W = 8 if samp else 2
            stg = cst if samp else halo
            skey = 'cst' if samp else 'halo'
            for q6 in range(6):
                nf = min(4, NF - q6 * 4)
                for ff in range(nf):
                    f = q6 * 4 + ff
                    S.op('pe', lambda e, f=f, ff=ff, W=W, stg=stg: e.transpose(pm[0][0:W, ff * 128:(ff + 1) * 128],
                                                                 stg[:, f, 0:W], identf[:, :]),
                         [skey, 'identf'], ['pm0'])
                cp('dve', ctok[0:W, 0:nf * 128], pm[0][0:W, 0:nf * 128], ['pm0'], ['ctok'])
                dma('sp', (convs_o if samp else convp_o)[:, q6 * 512:q6 * 512 + nf * 128], ctok[0:W, 0:nf * 128],
                    ['ctok'], ['conv_o'])
        chk(10, gi == 0)
        for ti, t in enumerate(tiles):
            tc2 = slice(ti * 128, (ti + 1) * 128)
            for half in range(2):
                for f in range(NF):
                    mm(pw[0][:, half * 512:(half + 1) * 512], hidT[:, f, tc2], wd_sb[:, f, half * 512:(half + 1) * 512],
                       f == 0, f == NF - 1, ['hidT', 'wd'], ['pw0'])
            tt('dve', h_g[:, ti, :], h_g[:, ti, :], pw[0][:, :], ALU.add, [f'h{ti}', 'pw0'], [f'h{ti}'])
        for ti, t in enumerate(tiles):
            tc2 = slice(ti * 128, (ti + 1) * 128)
            norm_tile(h_g[:, ti, :], [f'h{ti}'], gpleT, 'gpleT', nT2[:, :, tc2], ['nT2'], 0)
            dma('sp', p_sb[:, :], p_d[t], (), ['p_sb'])
            cp('dve', p_bf[:, :], p_sb[:, :], ['p_sb'], ['p_bf'])
            for c in range(2):
                tr(ptr[1][:, c * 128:(c + 1) * 128], p_bf[:, c * 128:(c + 1) * 128], ident[:, :], ['p_bf', 'ident'],
                   ['ptr1'])
            cp('dve', pT[:, :, :], ptr[1][:, 0:256].rearrange("p (c t) -> p c t", c=2), ['ptr1'], ['pT'])
            for half in range(2):
                for k in range(8):
                    mm(pw[0][:, half * 512:(half + 1) * 512], nT2[:, k, tc2], wst[:, s_pg, half, k, :],
                       k == 0, k == 7, ['nT2', f'wst{s_pg}'], ['pw0'])
                for c in range(2):
                    mm(pw[1][:, half * 512:(half + 1) * 512], pT[:, c, :], wpp_sb[:, c, half * 512:(half + 1) * 512],
                       c == 0, c == 1, ['pT', 'wpp'], ['pw1'])
            act(big[:, :], pw[0][:, :], AF.Sigmoid, ['pw0'], ['big'])
            tt('dve', big[:, :], big[:, :], pw[1][:, :], ALU.mult, ['big', 'pw1'], ['big'])
            tt('dve', big[:, :], big[:, :], h_g[:, ti, :], ALU.add, ['big', f'h{ti}'], ['big'])
            dma('sp', y_o[t], big[:, :], ['big'], ['y_o'])
        chk(11, gi == 0)
        chk(12, gi == 3)

    return S


_NC_CACHE = {}


def _consts():
    bf = ml_dtypes.bfloat16
    c = {}
    c['c_ident'] = np.eye(128, dtype=np.float32).astype(bf)
    c['c_identf'] = np.eye(128, dtype=np.float32)
    s = np.arange(128)
    c['c_mask'] = (s[:, None] <= s[None, :]).astype(np.float32).astype(bf)
    grp = s // 32
    m = (grp[:, None] == grp[None, :]) & (s[:, None] <= s[None, :])
    c['c_masks'] = m.astype(np.float32).astype(bf)
    ma = np.zeros((128, 4, 8, 8), np.float32)
    for b in range(4):
        for r in range(8):
            for t in range(8):
                if r <= t:
                    ma[b * 32 + 2 + r, b, :, t] = 1.0
    c['c_maska'] = ma.reshape(128, 4, 64).astype(bf)
    inv = 10000.0 ** (-np.arange(16, dtype=np.float32) * 2.0 / 32)
    pos = np.zeros((128, NT), np.float64)
    for i in range(16):
        pos[:, i] = i * 128 + s
    for p in range(128):
        t = (p % 32) - 2
        pos[p, ST] = 16384 + (t if 0 <= t < 8 else 0)
    ang = (pos.astype(np.float32)[:, :, None] * inv[None, None, :]).astype(np.float32)
    c['c_cos'] = np.cos(ang.astype(np.float64)).astype(np.float32)
    c['c_sin'] = np.sin(ang.astype(np.float64)).astype(np.float32)
    rs = np.ones((128, 2, 128), np.float32)
    rs[:, 0, 0] = 0.0
    for b in range(4):
        rs[:, 1, b * 32 + 2] = 0.0
    c['c_rs'] = rs
    c['c_iota'] = s.astype(np.float32).reshape(128, 1)
    rowm = np.zeros((128, 4), np.float32)
    for b in range(4):
        rowm[b * 32:(b + 1) * 32, b] = 1.0
    c['c_rowm'] = rowm
    hm = np.zeros((128, 2), np.float32)
    hm[:64, 0] = 1.0
    hm[64:, 1] = 1.0
    c['c_hm'] = hm
    sel = np.zeros((128, 128), np.float32)
    for b in range(4):
        for j in range(2):
            sel[b * 2 + j, b * 32 + j] = 1.0
    c['c_sel'] = sel.astype(bf)
    return c


def kernel(x_prompt, x_sample, cache_ckv, cache_krope, state_gla, state_conv, page_table,
           p_prompt, p_sample, g_mix, w_in, gla_w_a2, gla_b_a, gla_g_out, mla_g_qa, mla_w_qup,
           mla_g_qn, mla_g_qr, mla_g_kva, mla_g_kr, mla_w_kvup, mla_g_kn, w_o, g_ffn,
           ffn_w_gate, ffn_w_up, ffn_conv_w, ffn_conv_b, ffn_w_down, g_ple, ple_w_gate, ple_w_proj):
    f32 = np.float32
    A_ = lambda a: np.ascontiguousarray(np.asarray(a))
    if 'nc' not in _NC_CACHE:
        _NC_CACHE['nc'] = build_program()
    nc = _NC_CACHE['nc']
    consts = _consts()
    shared = {
        'cache_cat': np.ascontiguousarray(np.concatenate(
            [A_(cache_ckv).reshape(5120 * 128, 128)[:_NPHYS * 128],
             A_(cache_krope).reshape(5120 * 128, 32)[:_NPHYS * 128]], axis=1)),
        'w_in': A_(w_in[0]), 'gla_w_a2': A_(gla_w_a2[0]), 'gla_b_a': A_(gla_b_a[0]), 'gla_g_out': A_(gla_g_out[0]),
        'mla_g_qa': A_(mla_g_qa[0]), 'mla_w_qup': A_(mla_w_qup[0]), 'mla_g_qn': A_(mla_g_qn[0]),
        'mla_g_qr': A_(mla_g_qr[0]), 'mla_g_kva': A_(mla_g_kva[0]), 'mla_g_kr': A_(mla_g_kr[0]),
        'mla_w_kvup': A_(mla_w_kvup[0]), 'mla_g_kn': A_(mla_g_kn[0]), 'w_o': A_(w_o[0]), 'g_mix': A_(g_mix[0]),
        'g_ffn': A_(g_ffn[0]), 'g_ple': A_(g_ple[0]), 'ffn_w_gate': A_(ffn_w_gate[0]), 'ffn_w_up': A_(ffn_w_up[0]),
        'ffn_conv_w': A_(ffn_conv_w[0]), 'ffn_conv_b': A_(ffn_conv_b[0]), 'ffn_w_down': A_(ffn_w_down[0]),
        'ple_w_gate': A_(ple_w_gate[0]), 'ple_w_proj': A_(ple_w_proj[0]),
    }
    def _bc(a):
        a = A_(a)
        return np.ascontiguousarray(np.broadcast_to(a[None, :], (128, a.shape[0])))
    shared['b_gqn'] = _bc(mla_g_qn[0])
    shared['b_gqr'] = _bc(mla_g_qr[0])
    shared['b_gkva'] = _bc(mla_g_kva[0])
    shared['b_gkr'] = _bc(mla_g_kr[0])
    shared['b_gkn'] = _bc(mla_g_kn[0])
    shared.update(consts)
    xp = A_(x_prompt)
    xsm = A_(x_sample)
    pp = A_(p_prompt)[0]
    psm = A_(p_sample)[0]
    in_maps = []
    rows = np.array([b * 32 + 2 + t for b in range(4) for t in range(8)])
    for c in range(8):
        x = np.zeros((NT, 128, D), f32)
        x[:16] = xp[c].reshape(16, 128, D)
        x[ST][rows] = xsm[4 * c:4 * c + 4].reshape(32, D)
        p = np.zeros((NT, 128, 256), f32)
        p[:16] = pp[c].reshape(16, 128, 256)
        p[ST][rows] = psm[4 * c:4 * c + 4].reshape(32, 256)
        m = dict(shared)
        m['x'] = x
        m['p'] = p
        m['state_gla'] = A_(state_gla[0][4 * c:4 * c + 4]).reshape(4, 2, 128, 128)
        m['state_conv'] = A_(state_conv[0][4 * c:4 * c + 4]).reshape(8, DFF)
        m['page_table'] = A_(page_table[4 * c:4 * c + 4]).reshape(512).astype(np.int32)
        if _SMALLPA:
            m['page_table'] = m['page_table'] % 8
        m['b_pt'] = _bc(m['page_table'])
        in_maps.append(m)
    res = run_bass_kernel_spmd(nc, in_maps, core_ids=list(range(8))).results

    y_p = np.zeros((8, 2048, D), f32)
    y_s = np.zeros((32, 8, D), f32)
    ckv_p = np.zeros((1, 8, 2048, 128), f32)
    kr_p = np.zeros((1, 8, 2048, 32), f32)
    gla_p = np.zeros((1, 8, 4, 64, 128), f32)
    conv_p = np.zeros((1, 8, 2, DFF), f32)
    ckv_s = np.zeros((1, 32, 8, 128), f32)
    kr_s = np.zeros((1, 32, 8, 32), f32)
    gla_s = np.zeros((1, 32, 4, 64, 128), f32)
    conv_s = np.zeros((1, 32, 2, DFF), f32)
    for c in range(8):
        r = res[c]
        y = np.asarray(r['y'])
        y_p[c] = y[:16].reshape(2048, D)
        y_s[4 * c:4 * c + 4] = y[ST][rows].reshape(4, 8, D)
        ck = np.asarray(r['ckv_o'])
        ckv_p[0, c] = ck[:16].reshape(2048, 128)
        ckv_s[0, 4 * c:4 * c + 4] = ck[ST][rows].reshape(4, 8, 128)
        kr = np.asarray(r['kr_o'])
        kr_p[0, c] = kr[:16].reshape(2048, 32)
        kr_s[0, 4 * c:4 * c + 4] = kr[ST][rows].reshape(4, 8, 32)
        gla_p[0, c] = np.asarray(r['gla_p']).reshape(4, 64, 128)
        gla_s[0, 4 * c:4 * c + 4] = np.asarray(r['gla_s']).reshape(4, 4, 64, 128)
        conv_p[0, c] = np.asarray(r['conv_p'])
        conv_s[0, 4 * c:4 * c + 4] = np.asarray(r['conv_s']).reshape(4, 2, DFF)
    return (y_p, y_s, ckv_p, kr_p, gla_p, conv_p, ckv_s, kr_s, gla_s, conv_s)
```

```python
import numpy as np
import ml_dtypes
import concourse.bass as bass
import concourse.mybir as mybir
from concourse.bass_utils import run_bass_kernel_spmd

F32 = mybir.dt.float32
BF16 = mybir.dt.bfloat16
I32 = mybir.dt.int32
AF = mybir.ActivationFunctionType
ALU = mybir.AluOpType
AX = mybir.AxisListType

D = 1024
NT = 17
ST = 16
DFF = 2816
NF = 22
INC = 1968
EPS = 1e-6
MLA_SCALE = 96.0 ** -0.5
NPG = 128
O_Q, O_K, O_V, O_G, O_A, O_MQ = 0, 256, 512, 1024, 1536, 1552


class Sched:
    def __init__(self, sems):
        self.sems = sems
        nxt = iter(range(len(sems)))
        self.q = {e: [] for e in ('pe', 'act', 'dve', 'pool', 'sp')}
        self.cnt = {e: 0 for e in ('pe', 'act', 'dve', 'pool')}
        self.csem = {e: next(nxt) for e in ('pe', 'act', 'dve', 'pool')}
        self.dsems = {'sp': [next(nxt) for _ in range(24)], 'pool': [next(nxt) for _ in range(2)],
                      'act': [next(nxt) for _ in range(8)]}
        self.dcnt = {'sp': 0, 'pool': 0, 'act': 0}
        self.lastw = {}
        self.readers = {}
        self.waited = {e: {} for e in self.q}
        self.all_tokens = {}
        self._bar = None

    def op(self, eng, fn, reads=(), writes=(), dma=False):
        deps = {}

        def add(tok):
            if tok is None:
                return
            for s, v in tok.items() if isinstance(tok, dict) else [tok]:
                if deps.get(s, 0) < v:
                    deps[s] = v
        own = self.csem.get(eng) if (not dma and eng in ('act', 'dve')) else None

        def add_nr(tok):
            if tok is None:
                return
            for s_, v_ in tok.items() if isinstance(tok, dict) else [tok]:
                if own is not None and s_ == own:
                    continue
                if deps.get(s_, 0) < v_:
                    deps[s_] = v_
        if self._bar:
            add(self._bar)
        for k in reads:
            add(self.lastw.get(k))
        for k in writes:
            add_nr(self.lastw.get(k))
            add_nr(self.readers.get(k))
        if dma:
            pool = self.dsems[eng]
            i = self.dcnt[eng]
            self.dcnt[eng] += 1
            sem = pool[i % len(pool)]
            val = 16 * (i // len(pool) + 1)
            add((sem, val - 16))
            inc = 16
        else:
            self.cnt[eng] += 1
            sem = self.csem[eng]
            val = self.cnt[eng]
            inc = 1
        waits = []
        w = self.waited[eng]
        for s, v in deps.items():
            if v <= 0:
                continue
            if eng == 'pe' and s == self.csem['pe']:
                continue
            if w.get(s, 0) >= v:
                continue
            w[s] = v
            waits.append((s, v))
        self.q[eng].append((waits, fn, sem, inc))
        tok = (sem, val)
        self.all_tokens[sem] = max(self.all_tokens.get(sem, 0), val)
        for k in reads:
            r = self.readers.setdefault(k, {})
            if r.get(sem, 0) < val:
                r[sem] = val
        for k in writes:
            self.lastw[k] = tok
            self.readers[k] = {}
        return tok

    def barrier(self):
        snap = dict(self.all_tokens)
        for k in list(self.lastw.keys()):
            self.lastw[k] = snap
            self.readers[k] = {}
        self._bar = snap

    def emit(self, engname, engine):
        for waits, fn, sem, inc in self.q[engname]:
            for s, v in waits:
                engine.wait_ge(self.sems[s], v)
            ins = fn(engine)
            ins.then_inc(self.sems[sem], inc)

    def final_wait(self, engine):
        for s, v in self.all_tokens.items():
            engine.wait_ge(self.sems[s], v)


class _Stop(Exception):
    pass


import os
_STOP = float(os.environ.get("KSTOP", "0")) or None
_SKIP_PA = bool(os.environ.get("KSKIP"))
_SMALLPA = bool(os.environ.get("KSMALLPA"))
_NPHYS = 8 if (_SMALLPA or _SKIP_PA or (_STOP is not None and _STOP <= 7)) else 5120


_HOOK = [None]


def chk(stage, active=True):
    if not active:
        return
    if _STOP is not None and stage >= _STOP:
        if _HOOK[0] is not None and os.environ.get("KDBG") == "5":
            _HOOK[0]()
        raise _Stop()


def build_program():
    nc = bass.Bass("TRN2", target_bir_lowering=False)
    S = None
    try:
        S = _build_body(nc)
    except _Stop as e:
        S = e.args[0] if e.args else S
    _emit(nc, _S_HOLD[0])
    return nc


_S_HOLD = [None]


def _emit(nc, S):
    with nc.Block() as block:
        @block.sync
        def _(e):
            S.emit('sp', e)
            S.final_wait(e)

        @block.tensor
        def _(e):
            S.emit('pe', e)

        @block.scalar
        def _(e):
            S.emit('act', e)

        @block.vector
        def _(e):
            S.emit('dve', e)

        @block.gpsimd
        def _(e):
            S.emit('pool', e)


def _build_body(nc):

    def din(name, shape, dt=F32):
        return nc.dram_tensor(name, list(shape), dt, kind="ExternalInput").ap()

    def dout(name, shape, dt=F32):
        return nc.dram_tensor(name, list(shape), dt, kind="ExternalOutput").ap()

    x_d = din("x", [NT, 128, D])
    p_d = din("p", [NT, 128, 256])
    cc_d = din("cache_cat", [_NPHYS * 128, 160])
    sgla_d = din("state_gla", [4, 2, 128, 128])
    sconv_d = din("state_conv", [8, DFF])
    pt_d = din("page_table", [512], I32)
    w_in_d = din("w_in", [D, INC])
    wa2_d = din("gla_w_a2", [16, 256])
    ba_d = din("gla_b_a", [256])
    gout_d = din("gla_g_out", [128])
    gqa_d = din("mla_g_qa", [256])
    wqup_d = din("mla_w_qup", [256, 768])
    gqn_d = din("mla_g_qn", [64])
    gqr_d = din("mla_g_qr", [32])
    gkva_d = din("mla_g_kva", [128])
    gkr_d = din("mla_g_kr", [32])
    wkv_d = din("mla_w_kvup", [128, 1024])
    gkn_d = din("mla_g_kn", [64])
    wo_d = din("w_o", [D, D])
    gmix_d = din("g_mix", [D])
    gffn_d = din("g_ffn", [D])
    gple_d = din("g_ple", [D])
    wg_d = din("ffn_w_gate", [D, DFF])
    wu_d = din("ffn_w_up", [D, DFF])
    cw_d = din("ffn_conv_w", [3, DFF])
    cb_d = din("ffn_conv_b", [DFF])
    wd_d = din("ffn_w_down", [DFF, D])
    wpg_d = din("ple_w_gate", [D, D])
    wpp_d = din("ple_w_proj", [256, D])
    gqnb_d = din("b_gqn", [128, 64])
    gqrb_d = din("b_gqr", [128, 32])
    gkvab_d = din("b_gkva", [128, 128])
    gkrb_d = din("b_gkr", [128, 32])
    gknb_d = din("b_gkn", [128, 64])
    ptb_d = din("b_pt", [128, 512], I32)
    c_ident_d = din("c_ident", [128, 128], BF16)
    c_identf_d = din("c_identf", [128, 128])
    c_mask_d = din("c_mask", [128, 128], BF16)
    c_masks_d = din("c_masks", [128, 128], BF16)
    c_maska_d = din("c_maska", [128, 4, 64], BF16)
    c_cos_d = din("c_cos", [128, NT, 16])
    c_sin_d = din("c_sin", [128, NT, 16])
    c_rs_d = din("c_rs", [128, 2, 128])
    c_iota_d = din("c_iota", [128, 1])
    c_sel_d = din("c_sel", [128, 128], BF16)
    c_rowm_d = din("c_rowm", [128, 4])
    c_hm_d = din("c_hm", [128, 2])

    y_o = dout("y", [NT, 128, D])
    ckv_o = dout("ckv_o", [NT, 128, 128])
    kr_o = dout("kr_o", [NT, 128, 32])
    glap_o = dout("gla_p", [2, 128, 128])
    glas_o = dout("gla_s", [4, 2, 128, 128])
    convp_o = dout("conv_p", [2, DFF])
    convs_o = dout("conv_s", [8, DFF])

    sems = [nc.alloc_semaphore(f"s{i}") for i in range(60)]
    S = Sched(sems)
    _S_HOLD[0] = S

    class Arena:
        def __init__(self):
            self.base = 16512
            self.top = 229344
            self.off = self.base
            self.n = 0

        def alloc(self, name, shape, dt):
            sz = 1
            for s in shape[1:]:
                sz *= s
            sz *= mybir.dt.size(dt)
            sz = (sz + 31) // 32 * 32
            assert self.off + sz <= self.top, f"SBUF overflow at {name}: {self.off + sz}"
            self.n += 1
            t = nc.alloc_sbuf_tensor_at(f"{name}_{self.n}", list(shape), dt, offset=self.off)
            self.off += sz
            return t

    A = Arena()

    ptr = [nc.alloc_psum_tensor(f"ptr{i}", [128, 1024], BF16) for i in range(2)]
    pm = [nc.alloc_psum_tensor(f"pm{i}", [128, 512], F32) for i in range(2)]
    pw = [nc.alloc_psum_tensor(f"pw{i}", [128, 1024], F32) for i in range(2)]

    def dma(eng, out, in_, reads=(), writes=(), **kw):
        if out.dtype != in_.dtype:
            eng = 'pool'
        return S.op(eng, lambda e: e.dma_start(out=out, in_=in_, **kw), reads, writes, dma=True)

    def mm(out, lhsT, rhs, start, stop, reads, writes, **kw):
        return S.op('pe', lambda e: e.matmul(out, lhsT, rhs, start=start, stop=stop, **kw), reads, writes)

    def tr(out, in_, ident, reads, writes):
        return S.op('pe', lambda e: e.transpose(out, in_, ident), reads, writes)

    def act(out, in_, func, reads, writes, **kw):
        return S.op('act', lambda e: e.activation(out, in_, func, **kw), reads, writes)

    def tt(eng, out, in0, in1, op, reads, writes):
        return S.op(eng, lambda e: e.tensor_tensor(out, in0, in1, op), reads, writes)

    def ts(eng, out, in0, s1, s2, op0, op1, reads, writes):
        if s2 is None:
            return S.op(eng, lambda e: e.tensor_scalar(out, in0, s1, None, op0), reads, writes)
        return S.op(eng, lambda e: e.tensor_scalar(out, in0, s1, s2, op0, op1), reads, writes)

    def stt(out, in0, sc, in1, op0, op1, reads, writes):
        return S.op('dve', lambda e: e.scalar_tensor_tensor(out, in0, sc, in1, op0, op1), reads, writes)

    def cp(eng, out, in_, reads, writes):
        if eng == 'act':
            return S.op('act', lambda e: e.copy(out, in_), reads, writes)
        return S.op(eng, lambda e: e.tensor_copy(out, in_), reads, writes)

    def memset(eng, ap, val, writes):
        return S.op(eng, lambda e: e.memset(ap, val), (), writes)

    def bc_last(ap, n):
        sh = list(ap.shape)
        return ap.unsqueeze(len(sh)).broadcast_to(sh + [n])

    def bc_mid(ap, n):
        sh = list(ap.shape)
        return ap.unsqueeze(1).broadcast_to([sh[0], n] + sh[1:])

    ident = A.alloc("ident", [128, 128], BF16)
    identf = A.alloc("identf", [128, 128], F32)
    mask01 = A.alloc("mask01", [128, 128], BF16)
    masks = A.alloc("masks", [128, 128], BF16)
    maska = A.alloc("maska", [128, 4, 64], BF16)
    ones_bf = A.alloc("ones_bf", [128, 128], BF16)
    eps_c = A.alloc("eps_c", [128, 1], F32)
    gmixT = A.alloc("gmixT", [128, 8], F32)
    gffnT = A.alloc("gffnT", [128, 8], F32)
    gpleT = A.alloc("gpleT", [128, 8], F32)
    ogT = A.alloc("ogT", [128, 4, NT * 128], BF16)
    omT = A.alloc("omT", [128, 4, NT * 128], BF16)
    stat = A.alloc("stat", [128, 8], F32)
    xs = A.alloc("xs", [128, D], BF16)
    persist_off = A.off

    dma('sp', ident[:, :], c_ident_d, (), ['ident'])
    def _hook():
        S.barrier()
        dma('sp', y_o[14][:, 0:128], identf[:, :], ['identf'], ['y_o'])
    _HOOK[0] = _hook
    if os.environ.get("KDBG") == "5":
        dma('sp', identf[:, :], c_identf_d, (), ['identf'])
        dma('sp', y_o[15][:, 0:128], identf[:, :], ['identf'], ['y_o'])
    dma('sp', identf[:, :], c_identf_d, (), ['identf'])
    dma('sp', mask01[:, :], c_mask_d, (), ['mask01'])
    dma('sp', masks[:, :], c_masks_d, (), ['masks'])
    dma('sp', maska[:, :, :], c_maska_d, (), ['maska'])
    for gt, gd, k in ((gmixT, gmix_d, 'gmixT'), (gffnT, gffn_d, 'gffnT'), (gpleT, gple_d, 'gpleT')):
        dma('sp', gt[:, :], gd.rearrange("(c p) -> p c", p=128), (), [k], allow_slow_non_contiguous=True)
    memset('dve', ones_bf[:, :], 1.0, ['ones'])
    memset('dve', eps_c[:, :], EPS, ['eps'])

    def rstd_chain(ms_ap, out_ap, inv_n, rk, wk):
        act(out_ap, ms_ap, AF.Ln, list(rk) + ['eps'], wk, bias=eps_c[0:ms_ap.shape[0], 0:1], scale=inv_n)
        act(out_ap, out_ap, AF.Exp, wk, wk, scale=-0.5)

    def norm_tile(src, src_keys, gT, gkey, dst, dst_keys, pslot):
        act(xs[:, :], src, AF.Square, src_keys, ['xs', 'stat'], accum_out=stat[:, 0:1])
        rstd_chain(stat[:, 0:1], stat[:, 1:2], 1.0 / D, ['stat'], ['stat'])
        ts('dve', xs[:, :], src, stat[:, 1:2], None, ALU.mult, None, list(src_keys) + ['stat'], ['xs'])
        pk = f'ptr{pslot}'
        for k in range(8):
            tr(ptr[pslot][:, k * 128:(k + 1) * 128], xs[:, k * 128:(k + 1) * 128], ident[:, :],
               ['xs', 'ident'], [pk])
        tt('dve', dst, ptr[pslot][:, :].rearrange("p (c t) -> p c t", c=8), bc_last(gT[:, :], 128),
           ALU.mult, [pk, gkey], dst_keys)

    w_in_sb = A.alloc("w_in", [128, 8, INC], BF16)
    wqup_sb = A.alloc("wqup", [128, 2, 768], BF16)
    wkv_sb = A.alloc("wkv", [128, 1024], BF16)
    wkn_sb = A.alloc("wkn", [128, 512], BF16)
    wa2_sb = A.alloc("wa2", [16, 256], F32)
    nba = A.alloc("nba", [128, 2], F32)
    goutc = A.alloc("goutc", [128, 1], F32)
    gqa_c = A.alloc("gqa", [128, 2], F32)
    gknc = A.alloc("gknc", [128, 1], F32)
    Gqn = A.alloc("Gqn", [128, 64], F32)
    Gqr = A.alloc("Gqr", [128, 32], F32)
    Gkva = A.alloc("Gkva", [128, 128], F32)
    Gkr = A.alloc("Gkr", [128, 32], F32)
    Gkn = A.alloc("Gkn", [128, 64], F32)
    cos_sb = A.alloc("cos", [128, NT, 16], F32)
    sin_sb = A.alloc("sin", [128, NT, 16], F32)
    rs_sb = A.alloc("rs", [128, 2, 128], F32)
    iota_sb = A.alloc("iota", [128, 1], F32)
    x_sb = A.alloc("x_sb", [128, 2, D], F32)
    nT = A.alloc("nT", [128, 2, 8, 128], BF16)
    gaT = A.alloc("gaT", [16, 128], F32)
    sp_ = A.alloc("sp", [128, 2, 128], F32)
    bpos = A.alloc("bpos", [128, 2, 128], F32)
    etmp = A.alloc("etmp", [128, 2, 128], F32)
    nbl = A.alloc("nbl", [128, 8], F32)
    ebl = A.alloc("ebl", [128, 8], F32)
    qdT = A.alloc("qdT", [128, 4, 128], BF16)
    ke_z = A.alloc("ke_z", [128, 4, 256], BF16)
    rowm = A.alloc("rowm", [128, 4], F32)
    hm = A.alloc("hm", [128, 2], F32)
    wvp_sb = A.alloc("wvp", [128, 4, 128], BF16)
    kiT = A.alloc("kiT", [128, 2, 128], BF16)
    keT = A.alloc("keT", [128, 2, 128], BF16)
    ke_tok = A.alloc("ke_tok", [128, 256], BF16)
    v_tok = A.alloc("v_tok", [128, 512], BF16)
    sgT = A.alloc("sgT", [128, 4, 128], BF16)
    AT = A.alloc("AT", [128, 4, 128], BF16)
    S_f = A.alloc("S_f", [128, 2, 128], F32)
    S_b = A.alloc("S_b", [128, 2, 128], BF16)
    S0_f = A.alloc("S0_f", [128, 4, 2, 128], F32)
    S0_b = A.alloc("S0_b", [128, 4, 2, 128], BF16)
    osq = A.alloc("osq", [128, 512], BF16)
    orst = A.alloc("orst", [128, 512], F32)
    u_sb = A.alloc("u_sb", [128, 416], F32)
    ms = A.alloc("ms", [128, 32], F32)
    rst = A.alloc("rst", [128, 32], F32)
    cq = A.alloc("cq", [128, 256], BF16)
    cqT = A.alloc("cqT", [128, 2, 128], BF16)
    ckv_f = A.alloc("ckv_f", [128, 2, 128], F32)
    ckv_b = A.alloc("ckv_b", [128, 132], BF16)
    ckvT = A.alloc("ckvT", [128, 128], BF16)
    sqq = A.alloc("sqq", [128, 768], F32)
    sqk = A.alloc("sqk", [128, 8, 64], F32)
    Rn = A.alloc("Rn", [128, 9, 32], F32)
    Rr = A.alloc("Rr", [128, 2, 9, 32], F32)
    rt1 = A.alloc("rt1", [128, 9, 16], F32)
    rt2 = A.alloc("rt2", [128, 9, 16], F32)
    q_tok = A.alloc("q_tok", [128, 8, 64], BF16)
    qr_tok = A.alloc("qr_tok", [128, 8, 32], BF16)
    k_tok = A.alloc("k_tok", [128, 8, 64], BF16)
    kr_tok = A.alloc("kr_tok", [128, 32], BF16)
    QnT = A.alloc("QnT", [128, 8, 128], BF16)
    QrT = A.alloc("QrT", [128, 8, 128], BF16)
    KnT = A.alloc("KnT", [128, 4, NT * 128], BF16)
    KrT = A.alloc("KrT", [128, NT * 128], BF16)
    Vext = A.alloc("Vext", [128, 16, 8, 65], BF16)
    PT = A.alloc("PT", [128, 2, 4, 128], BF16)
    rl = A.alloc("rl", [128, 8], F32)
    o_tok = A.alloc("o_tok", [128, 8, 64], BF16)
    IDX = A.alloc("IDX", [128, 512], I32)
    PTi = A.alloc("PTi", [128, 512], I32)
    Cbuf = A.alloc("Cbuf", [128, 2, 4, 160], F32)
    Cb = A.alloc("Cb", [128, 2, 4, 132], BF16)
    KRb = A.alloc("KRb", [128, 2, 4, 32], BF16)
    CT = A.alloc("CT", [128, 2, 4, 128], BF16)
    KRT = A.alloc("KRT", [128, 2, 4, 128], BF16)
    sqs = A.alloc("sqs", [128, 2, 1024], BF16)
    ssum = A.alloc("ssum", [128, 32], F32)
    rstk = A.alloc("rstk", [128, 32], F32)
    rstk2 = A.alloc("rstk2", [128, 2, 32], F32)
    sc1 = A.alloc("sc1", [128, 256], F32)
    sc2 = A.alloc("sc2", [128, 256], F32)
    PTs = A.alloc("PTs", [128, 2, 4, 64], BF16)
    WkT = A.alloc("WkT", [128, 4, 128], BF16)
    qgT = A.alloc("qgT", [128, 8, 128], BF16)
    qabsT = A.alloc("qabsT", [128, 8, 128], BF16)
    acc_sb = A.alloc("acc_sb", [64, 132], F32)
    accn = A.alloc("accn", [128, 128], BF16)
    accnT = A.alloc("accnT", [128, 128], BF16)

    for k in range(8):
        dma('sp', w_in_sb[:, k, :], w_in_d[k * 128:(k + 1) * 128, :], (), ['w_in'])
    dma('sp', wqup_sb[:, :, :], wqup_d.rearrange("(c p) n -> p c n", p=128), (), ['wqup'])
    dma('sp', wkv_sb[:, :], wkv_d, (), ['wkv'])
    dma('sp', wa2_sb[:, :], wa2_d, (), ['wa2'])
    dma('sp', nba[:, :], ba_d.rearrange("(c p) -> p c", p=128), (), ['nba'], allow_slow_non_contiguous=True)
    dma('sp', goutc[:, :], gout_d.rearrange("(p o) -> p o", o=1), (), ['goutc'], allow_slow_non_contiguous=True)
    dma('sp', gqa_c[:, :], gqa_d.rearrange("(c p) -> p c", p=128), (), ['gqa'], allow_slow_non_contiguous=True)
    dma('sp', gknc[0:64, :], gkn_d.rearrange("(p o) -> p o", o=1), (), ['gknc'], allow_slow_non_contiguous=True)
    dma('sp', gknc[64:128, :], gkn_d.rearrange("(p o) -> p o", o=1), (), ['gknc'], allow_slow_non_contiguous=True)
    for gt, gd, k in ((Gqn, gqnb_d, 'Gqn'), (Gqr, gqrb_d, 'Gqr'), (Gkva, gkvab_d, 'Gkva'), (Gkr, gkrb_d, 'Gkr'),
                      (Gkn, gknb_d, 'Gkn')):
        dma('sp', gt[:, :], gd, (), [k])
    dma('sp', cos_sb[:, :, :], c_cos_d, (), ['cos'])
    dma('sp', sin_sb[:, :, :], c_sin_d, (), ['sin'])
    dma('sp', rs_sb[:, :, :], c_rs_d, (), ['rs'])
    dma('sp', iota_sb[:, :], c_iota_d, (), ['iota'])
    dma('sp', PTi[:, :], ptb_d, (), ['PTi'])
    for b in range(4):
        for pr in range(2):
            dma('sp', S0_f[:, b, pr, :], sgla_d[b, pr], (), ['S0_f'])
    ts('dve', nba[:, :], nba[:, :], -1.0, None, ALU.mult, None, ['nba'], ['nba'])
    for c in range(2):
        ts('dve', wqup_sb[:, c, :], wqup_sb[:, c, :], gqa_c[:, c:c + 1], None, ALU.mult, None,
           ['wqup', 'gqa'], ['wqup'])
    cp('dve', wkn_sb[:, :].rearrange("p (h d) -> p h d", h=8),
       wkv_sb[:, :].rearrange("p (h e) -> p h e", h=8)[:, :, 0:64], ['wkv'], ['wkn'])
    memset('dve', S_f[:, :, :], 0.0, ['S_f'])
    memset('dve', nbl[:, :], 0.0, ['nbl'])
    memset('dve', qdT[:, :, :], 0.0, ['qdT'])
    memset('dve', QnT[:, :, :], 0.0, ['QnT'])
    memset('dve', QrT[:, :, :], 0.0, ['QrT'])
    memset('dve', KrT[:, :], 0.0, ['KrT'])
    memset('dve', KRT[:, :, :, :], 0.0, ['KRT0', 'KRT1'])
    memset('dve', accn[:, :], 0.0, ['accn'])
    dma('sp', rowm[:, :], c_rowm_d, (), ['rowm'])
    dma('sp', hm[:, :], c_hm_d, (), ['hm'])
    cp('dve', wvp_sb[:, :, :].rearrange("p c (a d) -> p c a d", a=2),
       wkv_sb[:, :].rearrange("p (c a e) -> p c a e", c=4, a=2)[:, :, :, 64:128], ['wkv'], ['wvp'])
    memset('dve', S_b[:, :, :], 0.0, ['S_b'])
    cp('dve', S0_b[:, :, :, :], S0_f[:, :, :, :], ['S0_f'], ['S0_b'])
    memset('dve', ckv_b[:, 128:132], 1.0, ['ckv_b1'])
    memset('dve', Cb[:, :, :, 128:132], 1.0, ['Cb1'])
    memset('dve', Vext[:, :, :, 64:65], 1.0, ['Vext1'])
    memset('dve', omT[:, :, ST * 128:], 0.0, ['omT'])
    ts('dve', IDX[:, :], PTi[:, :], 128.0, iota_sb[:, 0:1], ALU.mult, ALU.add, ['PTi', 'iota'], ['IDX'])

    def load_x(i):
        dma('sp', x_sb[:, i % 2, :], x_d[i], (), [f'x{i % 2}'])

    if os.environ.get("KDBG") == "2":
        load_x(0)
        dma('sp', y_o[0], x_sb[:, 0, :], ['x0'], ['y_o'])
        dma('pool', y_o[3][:, 0:1024], w_in_sb[:, 0, 0:1024], ['w_in'], ['y_o'])
        dma('sp', y_o[4][:, 0:128], identf[:, :], ['identf'], ['y_o'])
    chk(1)
    load_x(0)

    pmi = [0]

    def next_pm():
        pmi[0] ^= 1
        return pmi[0]

    for i in range(NT):
        samp = (i == ST)
        sl = i % 2
        xk = f'x{sl}'
        nk = f'nT{sl}'
        nTi = nT[:, sl, :, :]
        tcol = slice(i * 128, (i + 1) * 128)
        if i + 1 < NT:
            load_x(i + 1)
        norm_tile(x_sb[:, sl, :], [xk], gmixT, 'gmixT', nTi, [nk], 0)
        if i == 0 and os.environ.get("KDBG") == "3":
            dma('pool', y_o[2][:, 0:1024], nT[:, 0, :, :].rearrange("p c t -> p (c t)"), [nk], ['y_o'])
            dma('pool', y_o[4][:, 0:1024], xs[:, :], ['xs'], ['y_o'])
            dma('sp', y_o[5][:, 0:8], stat[:, :], ['stat'], ['y_o'])
            dma('sp', y_o[0], x_sb[:, 0, :], ['x0'], ['y_o'])
        if i == 0 and os.environ.get("KDBG") == "4":
            stg = x_sb[:, 1, :]
            cp('dve', stg, xs[:, :], ['xs'], ['x1'])
            dma('sp', y_o[4], stg, ['x1'], ['y_o'])
            cp('dve', stg, nT[:, 0, :, :].rearrange("p c t -> p (c t)"), [nk], ['x1'])
            dma('sp', y_o[2], stg, ['x1'], ['y_o'])
            cp('dve', x_sb[:, 1, 0:8], stat[:, :], ['stat'], ['x1'])
            cp('dve', x_sb[:, 1, 8:16], gmixT[:, :], ['gmixT'], ['x1'])
            dma('sp', y_o[5], stg, ['x1'], ['y_o'])
        chk(2, i == 0)

        j = next_pm()
        for k in range(8):
            mm(pm[j][0:16, 0:128], w_in_sb[:, k, O_A:O_A + 16], nTi[:, k, :], k == 0, k == 7,
               ['w_in', nk], [f'pm{j}'])
        cp('act', gaT[:, :], pm[j][0:16, 0:128], [f'pm{j}'], ['gaT'])
        chk(2.1, i == 0)
        j = next_pm()
        for pr in range(2):
            mm(pm[j][:, pr * 128:(pr + 1) * 128], wa2_sb[:, pr * 128:(pr + 1) * 128], gaT[:, :], True, True,
               ['wa2', 'gaT'], [f'pm{j}'])
        for pr in range(2):
            act(sp_[:, pr, :], pm[j][:, pr * 128:(pr + 1) * 128], AF.Exp, [f'pm{j}', 'nba'], ['sp'],
                bias=nba[:, pr:pr + 1], scale=-1.0)
        act(sp_[:, :, :], sp_[:, :, :], AF.Ln, ['sp'], ['sp'], bias=1.0, scale=1.0)
        chk(2.3, i == 0)
        rsi = 1 if samp else 0
        for pr in range(2):
            S.op('dve', lambda e, pr=pr, rsi=rsi: e.tensor_tensor_scan(bpos[:, pr, :], rs_sb[:, rsi, :], sp_[:, pr, :],
                                                              0.0, ALU.mult, ALU.add),
                 ['sp', 'rs'], ['bpos'])
        segs = [(b * 32, 32, b * 32 + 9, b) for b in range(4)] if samp else [(0, 128, 127, 0)]
        chk(2.4, i == 0)
        for (c0, ncol, lc, si) in segs:
            ts('dve', nbl[:, si * 2:si * 2 + 2], bpos[:, :, lc], -1.0 / 16, None, ALU.mult, None,
               ['bpos'], ['nbl'])
        act(ebl[:, :], nbl[:, :], AF.Exp, ['nbl'], ['ebl'])
        chk(2.5, i == 0)
        j = next_pm()
        for pr in range(2):
            for k in range(8):
                mm(pm[j][:, pr * 128:(pr + 1) * 128], w_in_sb[:, k, O_Q + pr * 128:O_Q + (pr + 1) * 128],
                   nTi[:, k, :], k == 0, k == 7, ['w_in', nk], [f'pm{j}'])
        act(etmp[:, :, :], bpos[:, :, :], AF.Exp, ['bpos'], ['etmp'], scale=-1.0 / 16)
        for hh in range(2):
            r = slice(hh * 64, hh * 64 + 64)
            stt(qdT[r, hh::2, :], pm[j][r, 0:256].rearrange("p (a t) -> p a t", a=2), 0.125, etmp[r, :, :],
                ALU.mult, ALU.mult, [f'pm{j}', 'etmp'], ['qdT'])
        chk(2.6, i == 0)
        j = next_pm()
        for pr in range(2):
            for k in range(8):
                mm(pm[j][:, pr * 128:(pr + 1) * 128], w_in_sb[:, k, O_K + pr * 128:O_K + (pr + 1) * 128],
                   nTi[:, k, :], k == 0, k == 7, ['w_in', nk], [f'pm{j}'])
        act(etmp[:, :, :], bpos[:, :, :], AF.Exp, ['bpos', 'qdT'], ['etmp'], scale=1.0 / 16)
        tt('dve', kiT[:, :, :], pm[j][:, 0:256].rearrange("p (a t) -> p a t", a=2), etmp[:, :, :], ALU.mult,
           [f'pm{j}', 'etmp'], ['kiT'])
        for (c0, ncol, lc, si) in segs:
            for pr in range(2):
                act(etmp[:, pr, c0:c0 + ncol], bpos[:, pr, c0:c0 + ncol], AF.Exp, ['bpos', 'nbl', 'kiT'],
                    ['etmp'], bias=nbl[:, si * 2 + pr:si * 2 + pr + 1], scale=1.0 / 16)
        tt('dve', keT[:, :, :], pm[j][:, 0:256].rearrange("p (a t) -> p a t", a=2), etmp[:, :, :], ALU.mult,
           [f'pm{j}', 'etmp'], ['keT'])
        for pr in range(2):
            tr(ptr[1][:, pr * 128:(pr + 1) * 128], keT[:, pr, :], ident[:, :], ['keT', 'ident'], ['ptr1'])
        cp('dve', ke_tok[:, :], ptr[1][:, 0:256], ['ptr1'], ['ke_tok'])
        chk(2.7, i == 0)
        j = next_pm()
        for c in range(4):
            for k in range(8):
                mm(pm[j][:, c * 128:(c + 1) * 128], w_in_sb[:, k, O_G + c * 128:O_G + (c + 1) * 128],
                   nTi[:, k, :], k == 0, k == 7, ['w_in', nk], [f'pm{j}'])
        act(sgT[:, :, :], pm[j][:, :].rearrange("p (c t) -> p c t", c=4), AF.Silu, [f'pm{j}'], ['sgT'])
        j = next_pm()
        for k in range(8):
            mm(pm[j][:, :], nTi[:, k, :], w_in_sb[:, k, O_V:O_V + 512], k == 0, k == 7, ['w_in', nk], [f'pm{j}'])
        cp('act', v_tok[:, :], pm[j][:, :], [f'pm{j}'], ['v_tok'])
        chk(2.8, i == 0)
        j = next_pm()
        for h in range(4):
            pr = h // 2
            mm(pm[j][:, h * 128:(h + 1) * 128], kiT[:, pr, :], qdT[:, h, :], True, True,
               ['kiT', 'qdT'], [f'pm{j}'])
        mk, mkey = (masks, 'masks') if samp else (mask01, 'mask01')
        tt('dve', AT[:, :, :], pm[j][:, :].rearrange("p (h t) -> p h t", h=4), bc_mid(mk[:, :], 4), ALU.mult,
           [f'pm{j}', mkey], ['AT'])
        chk(2.9, i == 0)
        j = next_pm()
        for h in range(4):
            pr, b0 = h // 2, (h % 2) * 64
            oh = pm[j][:, h * 128:(h + 1) * 128]
            mm(oh, v_tok[:, h * 128:(h + 1) * 128], AT[:, h, :], True, False, ['v_tok', 'AT'], [f'pm{j}'])
            if not samp:
                mm(oh, S_b[:, pr, :], qdT[:, h, :], False, True, ['S_b', 'qdT'], [f'pm{j}'])
            else:
                for b in range(4):
                    mm(pm[j][:, h * 128 + b * 32:h * 128 + (b + 1) * 32], S0_b[:, b, pr, :],
                       qdT[:, h, b * 32:(b + 1) * 32], False, b == 3, ['S0_b', 'qdT'], [f'pm{j}'])
        jo = j
        chk(2.95, i == 0)
        act(osq[:, :], pm[jo][:, :], AF.Square, [f'pm{jo}'], ['osq'])
        j = next_pm()
        mm(pm[j][:, :], ones_bf[:, :], osq[:, :], True, True, ['ones', 'osq'], [f'pm{j}'])
        rstd_chain(pm[j][:, :], orst[:, :], 1.0 / 128, [f'pm{j}'], ['orst'])
        tt('dve', orst[:, :], pm[jo][:, :], orst[:, :], ALU.mult, [f'pm{jo}', 'orst'], ['orst'])
        stt(ogT[:, :, tcol], orst[:, :].rearrange("p (h t) -> p h t", h=4), goutc[:, 0:1], sgT[:, :, :],
            ALU.mult, ALU.mult, ['orst', 'goutc', 'sgT'], ['ogT'])
        chk(2.97, i == 0)
        if not samp:
            for pr in range(2):
                j = next_pm()
                mm(pm[j][:, 0:256], ke_tok[:, pr * 128:(pr + 1) * 128], v_tok[:, pr * 256:(pr + 1) * 256],
                   True, True, ['ke_tok', 'v_tok'], [f'pm{j}'])
                ts('dve', orst[:, 0:128], pm[j][:, 0:128], hm[:, 0:1], None, ALU.mult, None, [f'pm{j}', 'hm'], ['orst'])
                stt(orst[:, 0:128], pm[j][:, 128:256], hm[:, 1:2], orst[:, 0:128], ALU.mult, ALU.add,
                    [f'pm{j}', 'hm', 'orst'], ['orst'])
                stt(S_f[:, pr, :], S_f[:, pr, :], ebl[:, pr:pr + 1], orst[:, 0:128], ALU.mult, ALU.add,
                    ['S_f', 'ebl', 'orst'], ['S_f'])
                chk(2.98, i == 0 and pr == 0)
                chk(2.99, i == 0 and pr == 1)
            cp('dve', S_b[:, :, :], S_f[:, :, :], ['S_f'], ['S_b'])
            if i == ST - 1:
                for pr in range(2):
                    dma('sp', glap_o[pr], S_f[:, pr, :], ['S_f'], ['glap_o'])
        else:
            for b in range(4):
                ts('dve', ke_z[:, b, :], ke_tok[:, :], rowm[:, b:b + 1], None, ALU.mult, None,
                   ['ke_tok', 'rowm'], ['ke_z'])
            for b in range(4):
                for pr in range(2):
                    j = next_pm()
                    mm(pm[j][:, 0:256], ke_z[:, b, pr * 128:(pr + 1) * 128],
                       v_tok[:, pr * 256:(pr + 1) * 256], True, True, ['ke_z', 'v_tok'], [f'pm{j}'])
                    ts('dve', orst[:, 0:128], pm[j][:, 0:128], hm[:, 0:1], None, ALU.mult, None, [f'pm{j}', 'hm'],
                       ['orst'])
                    stt(orst[:, 0:128], pm[j][:, 128:256], hm[:, 1:2], orst[:, 0:128], ALU.mult, ALU.add,
                        [f'pm{j}', 'hm', 'orst'], ['orst'])
                    stt(S0_f[:, b, pr, :], S0_f[:, b, pr, :], ebl[:, b * 2 + pr:b * 2 + pr + 1], orst[:, 0:128],
                        ALU.mult, ALU.add, ['S0_f', 'ebl', 'orst'], ['S0_f'])
                    dma('sp', glas_o[b, pr], S0_f[:, b, pr, :], ['S0_f'], ['glas_o'])

        chk(3, i == 0)
        j = next_pm()
        for k in range(8):
            mm(pm[j][:, 0:416], nTi[:, k, :], w_in_sb[:, k, O_MQ:O_MQ + 416], k == 0, k == 7,
               ['w_in', nk], [f'pm{j}'])
        cp('dve', u_sb[:, :], pm[j][:, 0:416], [f'pm{j}'], ['u_sb'])
        chk(3.1, i == 0)
        for (a, b_, col, n) in ((0, 256, 24, 256), (256, 384, 25, 128), (384, 416, 26, 32)):
            if True:
                act(sqq[:, a:b_], u_sb[:, a:b_], AF.Square, ['u_sb'], ['sqq', 'ms'], accum_out=ms[:, col:col + 1],
                    scale=float(n) ** -0.5)
            else:
                act(sqq[:, a:b_], u_sb[:, a:b_], AF.Square, ['u_sb'], ['sqq'], scale=float(n) ** -0.5)
                S.op('dve', lambda e, a=a, b_=b_, col=col: e.tensor_reduce(ms[:, col:col + 1], sqq[:, a:b_], AX.X, ALU.add),
                     ['sqq'], ['ms'])
        rstd_chain(ms[:, 24:27], rst[:, 24:27], 1.0, ['ms'], ['rst'])
        chk(3.2, i == 0)
        ts('dve', cq[:, :], u_sb[:, 0:256], rst[:, 24:25], None, ALU.mult, None, ['u_sb', 'rst'], ['cq'])
        cs = i % 2
        stt(ckv_f[:, cs, :], u_sb[:, 256:384], rst[:, 25:26], Gkva[:, :], ALU.mult, ALU.mult,
            ['u_sb', 'rst', 'Gkva'], [f'ckv_f{cs}'])
        dma('sp', ckv_o[i], ckv_f[:, cs, :], [f'ckv_f{cs}'], ['ckv_o'])
        cp('dve', ckv_b[:, 0:128], ckv_f[:, cs, :], [f'ckv_f{cs}'], ['ckv_b'])
        stt(Rn[:, 8, :], u_sb[:, 384:416], rst[:, 26:27], Gkr[:, :], ALU.mult, ALU.mult,
            ['u_sb', 'rst', 'Gkr'], ['Rn'])
        chk(3.3, i == 0)
        for c in range(2):
            tr(ptr[1][:, c * 128:(c + 1) * 128], cq[:, c * 128:(c + 1) * 128], ident[:, :], ['cq', 'ident'], ['ptr1'])
        tr(ptr[1][:, 256:384], ckv_b[:, 0:128], ident[:, :], ['ckv_b', 'ident'], ['ptr1'])
        cp('dve', cqT[:, :, :], ptr[1][:, 0:256].rearrange("p (c t) -> p c t", c=2), ['ptr1'], ['cqT'])
        cp('dve', ckvT[:, :], ptr[1][:, 256:384], ['ptr1'], ['ckvT'])
        chk(3.4, i == 0)
        for half, (n0, n1) in enumerate(((0, 512), (512, 768))):
            for c in range(2):
                mm(pw[0][:, n0:n1], cqT[:, c, :], wqup_sb[:, c, n0:n1], c == 0, c == 1, ['cqT', 'wqup'], ['pw0'])
        for half in range(2):
            mm(pw[1][:, half * 512:(half + 1) * 512], ckvT[:, :], wkv_sb[:, half * 512:(half + 1) * 512],
               True, True, ['ckvT', 'wkv'], ['pw1'])
        qfv = pw[0][:, 0:768].rearrange("p (h e) -> p h e", h=8)
        kvv = pw[1][:, :].rearrange("p (h e) -> p h e", h=8)
        chk(3.5, i == 0)
        act(sqq[:, :], pw[0][:, 0:768], AF.Square, ['pw0'], ['sqq'])
        sqv = sqq[:, :].rearrange("p (h e) -> p h e", h=8)
        S.op('dve', lambda e: e.tensor_reduce(ms[:, 0:8], sqv[:, :, 0:64], AX.X, ALU.add), ['sqq'], ['ms'])
        S.op('dve', lambda e: e.tensor_reduce(ms[:, 16:24], sqv[:, :, 64:96], AX.X, ALU.add), ['sqq'], ['ms'])
        act(sqk[:, :, :], kvv[:, :, 0:64], AF.Square, ['pw1'], ['sqk'])
        S.op('dve', lambda e: e.tensor_reduce(ms[:, 8:16], sqk[:, :, :], AX.X, ALU.add), ['sqk'], ['ms'])
        rstd_chain(ms[:, 0:16], rst[:, 0:16], 1.0 / 64, ['ms'], ['rst'])
        rstd_chain(ms[:, 16:24], rst[:, 16:24], 1.0 / 32, ['ms'], ['rst'])
        chk(3.6, i == 0)
        tt('dve', sqk[:, :, :], qfv[:, :, 0:64], bc_last(rst[:, 0:8], 64), ALU.mult, ['pw0', 'rst'], ['sqk'])
        tt('dve', q_tok[:, :, :], sqk[:, :, :], bc_mid(Gqn[:, :], 8), ALU.mult, ['sqk', 'Gqn'], ['q_tok'])
        tt('dve', sqk[:, :, :], kvv[:, :, 0:64], bc_last(rst[:, 8:16], 64), ALU.mult, ['pw1', 'rst', 'q_tok'],
           ['sqk'])
        tt('dve', k_tok[:, :, :], sqk[:, :, :], bc_mid(Gkn[:, :], 8), ALU.mult, ['sqk', 'Gkn'], ['k_tok'])
        chk(3.7, i == 0)
        if not samp:
            cp('act', Vext[:, i, :, 0:64], kvv[:, :, 64:128], ['pw1'], ['Vext'])
        tt('dve', Rn[:, 0:8, :], qfv[:, :, 64:96], bc_last(rst[:, 16:24], 32), ALU.mult, ['pw0', 'rst'], ['Rn'])
        tt('dve', Rn[:, 0:8, :], Rn[:, 0:8, :], bc_mid(Gqr[:, :], 8), ALU.mult, ['Rn', 'Gqr'], ['Rn'])
        cosb = bc_mid(cos_sb[:, i, :], 9)
        sinb = bc_mid(sin_sb[:, i, :], 9)
        x1, x2 = Rn[:, :, 0:16], Rn[:, :, 16:32]
        rk = f'Rr{cs}'
        tt('dve', rt1[:, :, :], x1, cosb, ALU.mult, ['Rn', 'cos'], ['rt1'])
        tt('dve', rt2[:, :, :], x2, sinb, ALU.mult, ['Rn', 'sin'], ['rt2'])
        tt('dve', Rr[:, cs, :, 0:16], rt1[:, :, :], rt2[:, :, :], ALU.subtract, ['rt1', 'rt2'], [rk])
        tt('dve', rt1[:, :, :], x2, cosb, ALU.mult, ['Rn', 'cos', rk], ['rt1'])
        tt('dve', rt2[:, :, :], x1, sinb, ALU.mult, ['Rn', 'sin', rk], ['rt2'])
        tt('dve', Rr[:, cs, :, 16:32], rt1[:, :, :], rt2[:, :, :], ALU.add, ['rt1', 'rt2'], [rk])
        dma('sp', kr_o[i], Rr[:, cs, 8, :], [rk], ['kr_o'])
        cp('dve', qr_tok[:, :, :], Rr[:, cs, 0:8, :], [rk], ['qr_tok'])
        cp('dve', kr_tok[:, :], Rr[:, cs, 8, :], [rk], ['kr_tok'])
        chk(3.8, i == 0)
        for c in range(4):
            tr(ptr[0][:, c * 128:(c + 1) * 128], q_tok[:, 2 * c:2 * c + 2, :], ident[:, :], ['q_tok', 'ident'], ['ptr0'])
            tr(ptr[0][:, 512 + c * 128:512 + (c + 1) * 128], k_tok[:, 2 * c:2 * c + 2, :], ident[:, :],
               ['k_tok', 'ident'], ['ptr0'])
        for hh in range(2):
            r = slice(hh * 64, hh * 64 + 64)
            cp('dve', QnT[r, hh::2, :], ptr[0][r, 0:512].rearrange("p (c t) -> p c t", c=4), ['ptr0'], ['QnT'])
        cp('dve', KnT[:, :, tcol], ptr[0][:, 512:1024].rearrange("p (c t) -> p c t", c=4), ['ptr0'], ['KnT'])
        for h in range(8):
            tr(ptr[1][0:32, h * 128:(h + 1) * 128], qr_tok[:, h, :], ident[:, :], ['qr_tok', 'ident'], ['ptr1'])
        cp('dve', QrT[0:32, :, :], ptr[1][0:32, :].rearrange("p (h t) -> p h t", h=8), ['ptr1'], ['QrT'])
        tr(ptr[1][0:32, 0:128], kr_tok[:, :], ident[:, :], ['kr_tok', 'ident', 'QrT'], ['ptr1'])
        cp('dve', KrT[0:32, tcol], ptr[1][0:32, 0:128], ['ptr1'], ['KrT'])

        if i == 0 and os.environ.get("KDBG"):
            dma('sp', y_o[0][:, 0:416], u_sb[:, :], ['u_sb'], ['y_o'])
            dma('sp', y_o[1][:, 0:32], rst[:, :], ['rst'], ['y_o'])
            dma('pool', y_o[2][:, 0:1024], nT[:, 0, :, :].rearrange("p c t -> p (c t)"), [nk], ['y_o'])
            dma('pool', y_o[3][:, 0:1024], w_in_sb[:, 0, 0:1024], ['w_in'], ['y_o'])
            dma('pool', y_o[4][:, 0:1024], xs[:, :], ['xs'], ['y_o'])
            dma('sp', y_o[5][:, 0:8], stat[:, :], ['stat'], ['y_o'])
            dma('sp', y_o[6][:, 0:256], ckv_f[:, :, :].rearrange("p a c -> p (a c)"), ['ckv_f0'], ['y_o'])
        if i == 0 and os.environ.get("KDBG") == "6":
            stg = x_sb[:, 1, :]
            cp('dve', x_sb[:, 1, 0:416], u_sb[:, :], ['u_sb'], ['x1'])
            cp('dve', x_sb[:, 1, 416:448], rst[:, :], ['rst'], ['x1'])
            cp('dve', x_sb[:, 1, 448:480], ms[:, :], ['ms'], ['x1'])
            cp('dve', x_sb[:, 1, 512:640], ckv_f[:, 0, :], ['ckv_f0'], ['x1'])
            cp('dve', x_sb[:, 1, 640:768], Gkva[:, :], ['Gkva'], ['x1'])
            cp('dve', x_sb[:, 1, 768:800], Rr[:, 0, 8, :], ['Rr0'], ['x1'])
            dma('sp', y_o[4], stg, ['x1'], ['y_o'])
        chk(4, i == 0)
        chk(7, samp)
        if not samp:
            groups = []
            for h in range(8):
                for kt0 in range(0, i + 1, 4):
                    groups.append((h, kt0, min(4, i + 1 - kt0)))

            def emit_scores(gi):
                h, kt0, nkt = groups[gi]
                pr, b0 = h // 2, (h % 2) * 64
                s = gi % 2
                for jj in range(nkt):
                    kt = kt0 + jj
                    kc = slice(kt * 128, (kt + 1) * 128)
                    o = pm[s][:, jj * 128:(jj + 1) * 128]
                    mm(o, KnT[:, pr, kc], QnT[:, h, :], True, False, ['KnT', 'QnT'], [f'pm{s}'])
                    mm(o, KrT[:, kc], QrT[:, h, :], False, True, ['KrT', 'QrT'], [f'pm{s}'])
                act(PT[:, s, 0:nkt, :], pm[s][:, 0:nkt * 128].rearrange("p (a t) -> p a t", a=nkt), AF.Exp,
                    [f'pm{s}'], [f'PT{s}'], scale=MLA_SCALE)
                if kt0 + nkt - 1 == i:
                    tt('dve', PT[:, s, nkt - 1, :], PT[:, s, nkt - 1, :], mask01[:, :], ALU.mult,
                       [f'PT{s}', 'mask01'], [f'PT{s}'])

            def emit_pv(gi):
                h, kt0, nkt = groups[gi]
                s = gi % 2
                bank = h // 4
                o = pw[bank][:, (h % 4) * 65:(h % 4) * 65 + 65]
                for jj in range(nkt):
                    kt = kt0 + jj
                    mm(o, PT[:, s, jj, :], Vext[:, kt, h, :], kt == 0, kt == i, [f'PT{s}', 'Vext', 'Vext1'],
                       [f'pw{bank}'])

            emit_scores(0)
            for gi in range(len(groups)):
                if gi + 1 < len(groups):
                    emit_scores(gi + 1)
                emit_pv(gi)
            for bank in range(2):
                ov = pw[bank][:, 0:260].rearrange("p (h e) -> p h e", h=4)
                S.op('dve', lambda e, ov=ov, bank=bank: e.reciprocal(rl[:, bank * 4:bank * 4 + 4], ov[:, :, 64]),
                     [f'pw{bank}'], ['rl'])
                tt('dve', o_tok[:, bank * 4:bank * 4 + 4, :], ov[:, :, 0:64], bc_last(rl[:, bank * 4:bank * 4 + 4], 64),
                   ALU.mult, [f'pw{bank}', 'rl'], ['o_tok'])
            for c in range(4):
                tr(ptr[0][:, c * 128:(c + 1) * 128], o_tok[:, 2 * c:2 * c + 2, :], ident[:, :], ['o_tok', 'ident'],
                   ['ptr0'])
            cp('dve', omT[:, :, tcol], ptr[0][:, 0:512].rearrange("p (c t) -> p c t", c=4), ['ptr0'], ['omT'])
            chk(5, i == 0)
            chk(6, i == ST - 1)
        elif not _SKIP_PA:
            for pr in range(4):
                tr(ptr[0][:, pr * 128:(pr + 1) * 128], wkn_sb[:, pr * 128:(pr + 1) * 128], ident[:, :],
                   ['wkn', 'ident'], ['ptr0'])
            cp('dve', WkT[:, :, :], ptr[0][:, 0:512].rearrange("p (c t) -> p c t", c=4), ['ptr0'], ['WkT'])
            ts('dve', qgT[:, :, :], QnT[:, :, :], gknc[:, 0:1], None, ALU.mult, None, ['QnT', 'gknc'], ['qgT'])
            for h in range(8):
                pr, b0 = h // 2, (h % 2) * 64
                bank = h // 4
                mm(pw[bank][:, (h % 4) * 128:(h % 4 + 1) * 128], WkT[:, pr, :], qgT[:, h, :],
                   True, True, ['WkT', 'qgT'], [f'pw{bank}'])
            for bank in range(2):
                cp('dve', qabsT[:, bank * 4:bank * 4 + 4, :],
                   pw[bank][:, 0:512].rearrange("p (h t) -> p h t", h=4), [f'pw{bank}'], ['qabsT'])
            cp('dve', rstk[:, 0:8], rst[:, 8:16], ['rst'], ['rstk_new'])
            chk(7.1)

            def gather(b, g):
                s = g % 2
                for jj in range(4):
                    col = b * NPG + g * 4 + jj
                    S.op('pool', lambda e, s=s, jj=jj, col=col: e.indirect_dma_start(
                        out=Cbuf[:, s, jj, :], out_offset=None, in_=cc_d,
                        in_offset=bass.IndirectOffsetOnAxis(ap=IDX[:, col:col + 1], axis=0)),
                        ['IDX'], [f'Cbuf{s}'], dma=True)

            def score_pv(b, lhs_c, lhs_r, rstd_ap, nj, rhs_pv, keys_r, first, last, maskap=None):
                qa = qabsT[:, :, b * 32 + 2:b * 32 + 10]
                qr = QrT[:, :, b * 32 + 2:b * 32 + 10]
                for jj in range(nj):
                    mm(pm[0][:, jj * 64:(jj + 1) * 64], lhs_c(jj), qa, True, True, keys_r + ['qabsT'], ['pm0'])
                    mm(pm[0][:, 256 + jj * 64:256 + (jj + 1) * 64], lhs_r(jj), qr, True, True, keys_r + ['QrT'],
                       ['pm0'])
                n = nj * 64
                tt('dve', sc1[:, 0:n].rearrange("p (a t) -> p a t", t=8),
                   pm[0][:, 0:n].rearrange("p (a t) -> p a t", t=8), bc_last(rstd_ap, 8), ALU.mult,
                   ['pm0', 'rstk', 'rstk_new'] + [k for k in keys_r if k.startswith('rk')], ['sc1'])
                tt('dve', sc2[:, 0:n], sc1[:, 0:n], pm[0][:, 256:256 + n], ALU.add, ['sc1', 'pm0'], ['sc2'])
                return n

            def stageA(b, g):
                s = g % 2
                cp('act', Cb[:, s, :, 0:128], Cbuf[:, s, :, 0:128], [f'Cbuf{s}'], [f'Cb{s}'])
                cp('dve', KRb[:, s, :, :], Cbuf[:, s, :, 128:160], [f'Cbuf{s}'], [f'KRb{s}'])
                for jj in range(4):
                    tr(ptr[0][:, jj * 128:(jj + 1) * 128], Cb[:, s, jj, 0:128], ident[:, :],
                       [f'Cb{s}', 'ident'], ['ptr0'])
                    tr(ptr[0][0:32, 512 + jj * 128:512 + (jj + 1) * 128], KRb[:, s, jj, :], ident[:, :],
                       [f'KRb{s}', 'ident'], ['ptr0'])
                cp('dve', CT[:, s, :, :], ptr[0][:, 0:512].rearrange("p (a t) -> p a t", a=4), ['ptr0'],
                   [f'CT{s}'])
                cp('dve', KRT[0:32, s, :, :], ptr[0][0:32, 512:1024].rearrange("p (a t) -> p a t", a=4), ['ptr0'],
                   [f'KRT{s}'])
                for hf in range(2):
                    for q2 in range(2):
                        jj = hf * 2 + q2
                        mm(pw[hf][:, q2 * 512:(q2 + 1) * 512], CT[:, s, jj, :], wkn_sb[:, :], True, True,
                           [f'CT{s}', 'wkn'], [f'pw{hf}'])
                    act(sqs[:, hf, :], pw[hf][:, :], AF.Square, [f'pw{hf}'], [f'sqs{hf}'])
                    S.op('dve', lambda e, hf=hf: e.tensor_reduce(
                        ssum[:, hf * 16:(hf + 1) * 16], sqs[:, hf, :].rearrange("p (a d) -> p a d", d=64),
                        AX.X, ALU.add), [f'sqs{hf}'], ['ssum'])
                rstd_chain(ssum[:, 0:32], rstk2[:, s, :], 1.0 / 64, ['ssum'], [f'rk{s}'])

            def stageB(b, g):
                s = g % 2
                score_pv(b, lambda jj: CT[:, s, jj, :], lambda jj: KRT[:, s, jj, :], rstk2[:, s, :], 4,
                         None, [f'CT{s}', f'KRT{s}', f'rk{s}'], None, None)
                act(PTs[:, s, :, :], sc2[:, 0:256].rearrange("p (a t) -> p a t", a=4), AF.Exp, ['sc2'],
                    [f'PTs{s}'], scale=MLA_SCALE)
                for jj in range(4):
                    mm(pm[1][0:64, 0:129], PTs[:, s, jj, :], Cb[:, s, jj, 0:129], g == 0 and jj == 0, False,
                       [f'PTs{s}', f'Cb{s}', 'Cb1'], ['pm1'])

            NG = NPG // 4
            for b in range(4):
                gather(b, 0)
                gather(b, 1)
                stageA(b, 0)
                for g in range(NG):
                    if g + 2 < NG:
                        gather(b, g + 2)
                    if g + 1 < NG:
                        stageA(b, g + 1)
                    stageB(b, g)
                cp('dve', rstk[:, 0:8], rst[:, 8:16], ['rst'], ['rstk'])
                score_pv(b, lambda jj: ckvT[:, :], lambda jj: KrT[:, tcol], rstk[:, 0:8], 1, None,
                         ['ckvT', 'KrT'], None, None)
                act(PTs[:, 0, 0, :], sc2[:, 0:64], AF.Exp, ['sc2'], ['PTs0'], scale=MLA_SCALE)
                tt('dve', PTs[:, 0, 0, :], PTs[:, 0, 0, :], maska[:, b, :], ALU.mult, ['PTs0', 'maska'], ['PTs0'])
                mm(pm[1][0:64, 0:129], PTs[:, 0, 0, :], ckv_b[:, 0:129], False, True, ['PTs0', 'ckv_b', 'ckv_b1'],
                   ['pm1'])
                cp('dve', acc_sb[:, 0:129], pm[1][0:64, 0:129], ['pm1'], ['acc_sb'])
                S.op('dve', lambda e: e.reciprocal(acc_sb[:, 130:131], acc_sb[:, 128:129]), ['acc_sb'], ['acc_sb'])
                ts('dve', accn[0:64, :], acc_sb[:, 0:128], acc_sb[:, 130:131], None, ALU.mult, None, ['acc_sb'], ['accn'])
                tr(ptr[1][:, 0:128], accn[:, :], ident[:, :], ['accn', 'ident'], ['ptr1'])
                cp('dve', accnT[:, :], ptr[1][:, 0:128], ['ptr1'], ['accnT'])
                for c in range(4):
                    mm(pm[0][:, c * 16:(c + 1) * 16], wvp_sb[:, c, :], accnT[:, c * 16:(c + 1) * 16], True, True,
                       ['wvp', 'accnT'], ['pm0'])
                ocol = slice(ST * 128 + b * 32 + 2, ST * 128 + b * 32 + 10)
                pv4 = pm[0][:, 0:64].rearrange("p (c t) -> p c t", c=4)
                cp('dve', omT[0:64, :, ocol], pv4[0:64, :, 0:8], ['pm0'], ['omT'])
                cp('dve', omT[64:128, :, ocol], pv4[64:128, :, 8:16], ['pm0'], ['omT'])
                chk(7.6, b == 0)

    if os.environ.get("KDBG") == "7":
        for tt_ in range(2):
            cp('dve', x_sb[:, 0, 0:512].rearrange("p (c t) -> p c t", c=4), ogT[:, :, tt_ * 128:(tt_ + 1) * 128],
               ['ogT'], ['x0'])
            cp('dve', x_sb[:, 0, 512:1024].rearrange("p (c t) -> p c t", c=4), omT[:, :, tt_ * 128:(tt_ + 1) * 128],
               ['omT'], ['x0'])
            dma('sp', y_o[4 + tt_], x_sb[:, 0, :], ['x0'], ['y_o'])
    chk(8)
    S.barrier()
    A.off = persist_off
    wd_sb = A.alloc("wd", [128, NF, D], BF16)
    wo_sb = A.alloc("wo", [128, 8, D], BF16)
    wpp_sb = A.alloc("wpp", [128, 2, D], BF16)
    wst = A.alloc("wst", [128, 2, 2, 8, 512], BF16)
    h_g = A.alloc("h_g", [128, 4, D], F32)
    nT2 = A.alloc("nT2", [128, 8, 512], BF16)
    hidT = A.alloc("hidT", [128, NF, 512], BF16)
    a_sb = A.alloc("a_sb", [128, 520], F32)
    tbuf = A.alloc("tbuf", [128, 512], F32)
    big = A.alloc("big", [128, D], F32)
    p_sb = A.alloc("p_sb", [128, 256], F32)
    p_bf = A.alloc("p_bf", [128, 256], BF16)
    pT = A.alloc("pT", [128, 2, 128], BF16)
    cw_sb = A.alloc("cw", [128, 3, NF], F32)
    cb_sb = A.alloc("cb", [128, NF], F32)
    halo = A.alloc("halo", [128, NF, 2], F32)
    cst = A.alloc("cst", [128, NF, 8], F32)
    sc_bf = A.alloc("sc_bf", [128, DFF], BF16)
    sel_sb = A.alloc("sel", [128, 128], BF16)
    ctok = A.alloc("ctok", [8, 512], F32)

    for f in range(NF):
        dma('sp', wd_sb[:, f, :], wd_d[f * 128:(f + 1) * 128, :], (), ['wd'])
    for k in range(8):
        dma('sp', wo_sb[:, k, :], wo_d[k * 128:(k + 1) * 128, :], (), ['wo'])
    dma('sp', wpp_sb[:, :, :], wpp_d.rearrange("(c p) n -> p c n", p=128), (), ['wpp'])
    dma('sp', cw_sb[:, :, :], cw_d.rearrange("j (f p) -> p j f", p=128), (), ['cw'], allow_slow_non_contiguous=True)
    dma('sp', cb_sb[:, :], cb_d.rearrange("(f p) -> p f", p=128), (), ['cb'], allow_slow_non_contiguous=True)
    memset('dve', sc_bf[:, :], 0.0, ['sc_bf'])
    dma('sp', sc_bf[0:8, :], sconv_d, ['sc_bf'], ['sc_bf'])
    dma('sp', sel_sb[:, :], c_sel_d, (), ['sel'])
    memset('dve', halo[:, :, :], 0.0, ['halo'])
    memset('dve', a_sb[:, :], 0.0, ['a_sb'])

    blocks = [(0, 512), (512, 512), (1024, 512), (1536, 512), (2048, 512), (2560, 256)]
    wst_cnt = [0]

    def load_ffn_block(bi):
        c0, w = blocks[bi]
        s = wst_cnt[0] % 2
        wst_cnt[0] += 1
        for gu, wdram in ((0, wg_d), (1, wu_d)):
            for k in range(8):
                dma('sp', wst[:, s, gu, k, 0:w], wdram[k * 128:(k + 1) * 128, c0:c0 + w], (), [f'wst{s}'])
        return s

    def load_ple_gate():
        s = wst_cnt[0] % 2
        wst_cnt[0] += 1
        for half in range(2):
            for k in range(8):
                dma('sp', wst[:, s, half, k, :], wpg_d[k * 128:(k + 1) * 128, half * 512:(half + 1) * 512], (),
                    [f'wst{s}'])
        return s

    tgroups = [[0, 1, 2, 3], [4, 5, 6, 7], [8, 9, 10, 11], [12, 13, 14, 15], [ST]]
    for gi, tiles in enumerate(tgroups):
        samp = (tiles[0] == ST)
        N = 128 * len(tiles)
        slot0 = load_ffn_block(0)
        for ti, t in enumerate(tiles):
            tcol = slice(t * 128, (t + 1) * 128)
            dma('sp', big[:, :], x_d[t], (), ['big'])
            for half in range(2):
                for c in range(8):
                    src = ogT if c < 4 else omT
                    mm(pw[0][:, half * 512:(half + 1) * 512], src[:, c % 4, tcol],
                       wo_sb[:, c, half * 512:(half + 1) * 512], c == 0, c == 7, ['ogT', 'omT', 'wo'], ['pw0'])
            tt('dve', h_g[:, ti, :], big[:, :], pw[0][:, :], ALU.add, ['big', 'pw0'], [f'h{ti}'])
            norm_tile(h_g[:, ti, :], [f'h{ti}'], gffnT, 'gffnT', nT2[:, :, ti * 128:(ti + 1) * 128], ['nT2'], 0)
        chk(9, gi == 0)
        for bi, (c0, w) in enumerate(blocks):
            s = slot0 if bi == 0 else s_next
            if bi + 1 < len(blocks):
                s_next = load_ffn_block(bi + 1)
            else:
                s_pg = load_ple_gate()
            for fi in range(w // 128):
                f = c0 // 128 + fi
                fc = slice(fi * 128, (fi + 1) * 128)
                for k in range(8):
                    mm(pm[0][:, 0:N], wst[:, s, 0, k, fc], nT2[:, k, 0:N], k == 0, (k == 7 and not samp),
                       [f'wst{s}', 'nT2'], ['pm0'])
                if samp:
                    mm(pm[0][:, 0:N], sc_bf[:, f * 128:(f + 1) * 128], sel_sb[:, :], False, True,
                       ['sc_bf', 'sel'], ['pm0'])
                for k in range(8):
                    mm(pm[1][:, 0:N], wst[:, s, 1, k, fc], nT2[:, k, 0:N], k == 0, k == 7, [f'wst{s}', 'nT2'], ['pm1'])
                if not samp:
                    cp('dve', a_sb[:, 0:2], halo[:, f, :], ['halo'], ['a_sb'])
                cp('act', a_sb[:, 2:2 + N], pm[0][:, 0:N], ['pm0'], ['a_sb'])
                if not samp:
                    cp('dve', halo[:, f, :], a_sb[:, N:N + 2], ['a_sb'], ['halo'])
                else:
                    cp('dve', cst[:, f, :].rearrange("p (b j) -> p b j", b=4),
                       a_sb[:, 2:2 + 128].rearrange("p (b c) -> p b c", b=4)[:, :, 8:10], ['a_sb'], ['cst'])
                ts('dve', tbuf[:, 0:N], a_sb[:, 0:N], cw_sb[:, 0, f:f + 1], cb_sb[:, f:f + 1], ALU.mult, ALU.add,
                   ['a_sb', 'cw', 'cb'], ['tbuf'])
                stt(tbuf[:, 0:N], a_sb[:, 1:1 + N], cw_sb[:, 1, f:f + 1], tbuf[:, 0:N], ALU.mult, ALU.add,
                    ['a_sb', 'cw', 'tbuf'], ['tbuf'])
                stt(tbuf[:, 0:N], a_sb[:, 2:2 + N], cw_sb[:, 2, f:f + 1], tbuf[:, 0:N], ALU.mult, ALU.add,
                    ['a_sb', 'cw', 'tbuf'], ['tbuf'])
                act(tbuf[:, 0:N], tbuf[:, 0:N], AF.Silu, ['tbuf'], ['tbuf'])
                tt('dve', hidT[:, f, 0:N], tbuf[:, 0:N], pm[1][:, 0:N], ALU.mult, ['tbuf', 'pm1'], ['hidT'])
        if gi == 3 or samp:
            W = 8 if samp else 2
            stg = cst if samp else halo
            skey = 'cst' if samp else 'halo'
            for q6 in range(6):
                nf = min(4, NF - q6 * 4)
                for ff in range(nf):
                    f = q6 * 4 + ff
                    S.op('pe', lambda e, f=f, ff=ff, W=W, stg=stg: e.transpose(pm[0][0:W, ff * 128:(ff + 1) * 128],
                                                                 stg[:, f, 0:W], identf[:, :]),
                         [skey, 'identf'], ['pm0'])
                cp('dve', ctok[0:W, 0:nf * 128], pm[0][0:W, 0:nf * 128], ['pm0'], ['ctok'])
                dma('sp', (convs_o if samp else convp_o)[:, q6 * 512:q6 * 512 + nf * 128], ctok[0:W, 0:nf * 128],
                    ['ctok'], ['conv_o'])
        chk(10, gi == 0)
        for ti, t in enumerate(tiles):
            tc2 = slice(ti * 128, (ti + 1) * 128)
            for half in range(2):
                for f in range(NF):
                    mm(pw[0][:, half * 512:(half + 1) * 512], hidT[:, f, tc2], wd_sb[:, f, half * 512:(half + 1) * 512],
                       f == 0, f == NF - 1, ['hidT', 'wd'], ['pw0'])
            tt('dve', h_g[:, ti, :], h_g[:, ti, :], pw[0][:, :], ALU.add, [f'h{ti}', 'pw0'], [f'h{ti}'])
        for ti, t in enumerate(tiles):
            tc2 = slice(ti * 128, (ti + 1) * 128)
            norm_tile(h_g[:, ti, :], [f'h{ti}'], gpleT, 'gpleT', nT2[:, :, tc2], ['nT2'], 0)
            dma('sp', p_sb[:, :], p_d[t], (), ['p_sb'])
            cp('dve', p_bf[:, :], p_sb[:, :], ['p_sb'], ['p_bf'])
            for c in range(2):
                tr(ptr[1][:, c * 128:(c + 1) * 128], p_bf[:, c * 128:(c + 1) * 128], ident[:, :], ['p_bf', 'ident'],
                   ['ptr1'])
            cp('dve', pT[:, :, :], ptr[1][:, 0:256].rearrange("p (c t) -> p c t", c=2), ['ptr1'], ['pT'])
            for half in range(2):
                for k in range(8):
                    mm(pw[0][:, half * 512:(half + 1) * 512], nT2[:, k, tc2], wst[:, s_pg, half, k, :],
                       k == 0, k == 7, ['nT2', f'wst{s_pg}'], ['pw0'])
                for c in range(2):
                    mm(pw[1][:, half * 512:(half + 1) * 512], pT[:, c, :], wpp_sb[:, c, half * 512:(half + 1) * 512],
                       c == 0, c == 1, ['pT', 'wpp'], ['pw1'])
            act(big[:, :], pw[0][:, :], AF.Sigmoid, ['pw0'], ['big'])
            tt('dve', big[:, :], big[:, :], pw[1][:, :], ALU.mult, ['big', 'pw1'], ['big'])
            tt('dve', big[:, :], big[:, :], h_g[:, ti, :], ALU.add, ['big', f'h{ti}'], ['big'])
            dma('sp', y_o[t], big[:, :], ['big'], ['y_o'])
        chk(11, gi == 0)
        chk(12, gi == 3)

    return S


_NC_CACHE = {}


def _consts():
    bf = ml_dtypes.bfloat16
    c = {}
    c['c_ident'] = np.eye(128, dtype=np.float32).astype(bf)
    c['c_identf'] = np.eye(128, dtype=np.float32)
    s = np.arange(128)
    c['c_mask'] = (s[:, None] <= s[None, :]).astype(np.float32).astype(bf)
    grp = s // 32
    m = (grp[:, None] == grp[None, :]) & (s[:, None] <= s[None, :])
    c['c_masks'] = m.astype(np.float32).astype(bf)
    ma = np.zeros((128, 4, 8, 8), np.float32)
    for b in range(4):
        for r in range(8):
            for t in range(8):
                if r <= t:
                    ma[b * 32 + 2 + r, b, :, t] = 1.0
    c['c_maska'] = ma.reshape(128, 4, 64).astype(bf)
    inv = 10000.0 ** (-np.arange(16, dtype=np.float32) * 2.0 / 32)
    pos = np.zeros((128, NT), np.float64)
    for i in range(16):
        pos[:, i] = i * 128 + s
    for p in range(128):
        t = (p % 32) - 2
        pos[p, ST] = 16384 + (t if 0 <= t < 8 else 0)
    ang = (pos.astype(np.float32)[:, :, None] * inv[None, None, :]).astype(np.float32)
    c['c_cos'] = np.cos(ang.astype(np.float64)).astype(np.float32)
    c['c_sin'] = np.sin(ang.astype(np.float64)).astype(np.float32)
    rs = np.ones((128, 2, 128), np.float32)
    rs[:, 0, 0] = 0.0
    for b in range(4):
        rs[:, 1, b * 32 + 2] = 0.0
    c['c_rs'] = rs
    c['c_iota'] = s.astype(np.float32).reshape(128, 1)
    rowm = np.zeros((128, 4), np.float32)
    for b in range(4):
        rowm[b * 32:(b + 1) * 32, b] = 1.0
    c['c_rowm'] = rowm
    hm = np.zeros((128, 2), np.float32)
    hm[:64, 0] = 1.0
    hm[64:, 1] = 1.0
    c['c_hm'] = hm
    sel = np.zeros((128, 128), np.float32)
    for b in range(4):
        for j in range(2):
            sel[b * 2 + j, b * 32 + j] = 1.0
    c['c_sel'] = sel.astype(bf)
    return c


def kernel(x_prompt, x_sample, cache_ckv, cache_krope, state_gla, state_conv, page_table,
           p_prompt, p_sample, g_mix, w_in, gla_w_a2, gla_b_a, gla_g_out, mla_g_qa, mla_w_qup,
           mla_g_qn, mla_g_qr, mla_g_kva, mla_g_kr, mla_w_kvup, mla_g_kn, w_o, g_ffn,
           ffn_w_gate, ffn_w_up, ffn_conv_w, ffn_conv_b, ffn_w_down, g_ple, ple_w_gate, ple_w_proj):
    f32 = np.float32
    A_ = lambda a: np.ascontiguousarray(np.asarray(a))
    if 'nc' not in _NC_CACHE:
        _NC_CACHE['nc'] = build_program()
    nc = _NC_CACHE['nc']
    consts = _consts()
    shared = {
        'cache_cat': np.ascontiguousarray(np.concatenate(
            [A_(cache_ckv).reshape(5120 * 128, 128)[:_NPHYS * 128],
             A_(cache_krope).reshape(5120 * 128, 32)[:_NPHYS * 128]], axis=1)),
        'w_in': A_(w_in[0]), 'gla_w_a2': A_(gla_w_a2[0]), 'gla_b_a': A_(gla_b_a[0]), 'gla_g_out': A_(gla_g_out[0]),
        'mla_g_qa': A_(mla_g_qa[0]), 'mla_w_qup': A_(mla_w_qup[0]), 'mla_g_qn': A_(mla_g_qn[0]),
        'mla_g_qr': A_(mla_g_qr[0]), 'mla_g_kva': A_(mla_g_kva[0]), 'mla_g_kr': A_(mla_g_kr[0]),
        'mla_w_kvup': A_(mla_w_kvup[0]), 'mla_g_kn': A_(mla_g_kn[0]), 'w_o': A_(w_o[0]), 'g_mix': A_(g_mix[0]),
        'g_ffn': A_(g_ffn[0]), 'g_ple': A_(g_ple[0]), 'ffn_w_gate': A_(ffn_w_gate[0]), 'ffn_w_up': A_(ffn_w_up[0]),
        'ffn_conv_w': A_(ffn_conv_w[0]), 'ffn_conv_b': A_(ffn_conv_b[0]), 'ffn_w_down': A_(ffn_w_down[0]),
        'ple_w_gate': A_(ple_w_gate[0]), 'ple_w_proj': A_(ple_w_proj[0]),
    }
    def _bc(a):
        a = A_(a)
        return np.ascontiguousarray(np.broadcast_to(a[None, :], (128, a.shape[0])))
    shared['b_gqn'] = _bc(mla_g_qn[0])
    shared['b_gqr'] = _bc(mla_g_qr[0])
    shared['b_gkva'] = _bc(mla_g_kva[0])
    shared['b_gkr'] = _bc(mla_g_kr[0])
    shared['b_gkn'] = _bc(mla_g_kn[0])
    shared.update(consts)
    xp = A_(x_prompt)
    xsm = A_(x_sample)
    pp = A_(p_prompt)[0]
    psm = A_(p_sample)[0]
    in_maps = []
    rows = np.array([b * 32 + 2 + t for b in range(4) for t in range(8)])
    for c in range(8):
        x = np.zeros((NT, 128, D), f32)
        x[:16] = xp[c].reshape(16, 128, D)
        x[ST][rows] = xsm[4 * c:4 * c + 4].reshape(32, D)
        p = np.zeros((NT, 128, 256), f32)
        p[:16] = pp[c].reshape(16, 128, 256)
        p[ST][rows] = psm[4 * c:4 * c + 4].reshape(32, 256)
        m = dict(shared)
        m['x'] = x
        m['p'] = p
        m['state_gla'] = A_(state_gla[0][4 * c:4 * c + 4]).reshape(4, 2, 128, 128)
        m['state_conv'] = A_(state_conv[0][4 * c:4 * c + 4]).reshape(8, DFF)
        m['page_table'] = A_(page_table[4 * c:4 * c + 4]).reshape(512).astype(np.int32)
        if _SMALLPA:
            m['page_table'] = m['page_table'] % 8
        m['b_pt'] = _bc(m['page_table'])
        in_maps.append(m)
    res = run_bass_kernel_spmd(nc, in_maps, core_ids=list(range(8))).results

    y_p = np.zeros((8, 2048, D), f32)
    y_s = np.zeros((32, 8, D), f32)
    ckv_p = np.zeros((1, 8, 2048, 128), f32)
    kr_p = np.zeros((1, 8, 2048, 32), f32)
    gla_p = np.zeros((1, 8, 4, 64, 128), f32)
    conv_p = np.zeros((1, 8, 2, DFF), f32)
    ckv_s = np.zeros((1, 32, 8, 128), f32)
    kr_s = np.zeros((1, 32, 8, 32), f32)
    gla_s = np.zeros((1, 32, 4, 64, 128), f32)
    conv_s = np.zeros((1, 32, 2, DFF), f32)
    for c in range(8):
        r = res[c]
        y = np.asarray(r['y'])
        y_p[c] = y[:16].reshape(2048, D)
        y_s[4 * c:4 * c + 4] = y[ST][rows].reshape(4, 8, D)
        ck = np.asarray(r['ckv_o'])
        ckv_p[0, c] = ck[:16].reshape(2048, 128)
        ckv_s[0, 4 * c:4 * c + 4] = ck[ST][rows].reshape(4, 8, 128)
        kr = np.asarray(r['kr_o'])
        kr_p[0, c] = kr[:16].reshape(2048, 32)
        kr_s[0, 4 * c:4 * c + 4] = kr[ST][rows].reshape(4, 8, 32)
        gla_p[0, c] = np.asarray(r['gla_p']).reshape(4, 64, 128)
        gla_s[0, 4 * c:4 * c + 4] = np.asarray(r['gla_s']).reshape(4, 4, 64, 128)
        conv_p[0, c] = np.asarray(r['conv_p'])
        conv_s[0, 4 * c:4 * c + 4] = np.asarray(r['conv_s']).reshape(4, 2, DFF)
    return (y_p, y_s, ckv_p, kr_p, gla_p, conv_p, ckv_s, kr_s, gla_s, conv_s)
```

```python
import numpy as np
import ml_dtypes
import concourse.bass as bass
import concourse.mybir as mybir
from concourse.bass_utils import run_bass_kernel_spmd

F32 = mybir.dt.float32
BF16 = mybir.dt.bfloat16
I32 = mybir.dt.int32
AF = mybir.ActivationFunctionType
ALU = mybir.AluOpType
AX = mybir.AxisListType

D = 1024
NT = 17
ST = 16
DFF = 2816
NF = 22
INC = 1968
EPS = 1e-6
MLA_SCALE = 96.0 ** -0.5
NPG = 128
O_Q, O_K, O_V, O_G, O_A, O_MQ = 0, 256, 512, 1024, 1536, 1552


class Sched:
    def __init__(self, sems):
        self.sems = sems
        nxt = iter(range(len(sems)))
        self.q = {e: [] for e in ('pe', 'act', 'dve', 'pool', 'sp')}
        self.cnt = {e: 0 for e in ('pe', 'act', 'dve', 'pool')}
        self.csem = {e: next(nxt) for e in ('pe', 'act', 'dve', 'pool')}
        self.dsems = {'sp': [next(nxt) for _ in range(24)], 'pool': [next(nxt) for _ in range(2)],
                      'act': [next(nxt) for _ in range(8)]}
        self.dcnt = {'sp': 0, 'pool': 0, 'act': 0}
        self.lastw = {}
        self.readers = {}
        self.waited = {e: {} for e in self.q}
        self.all_tokens = {}
        self._bar = None

    def op(self, eng, fn, reads=(), writes=(), dma=False):
        deps = {}

        def add(tok):
            if tok is None:
                return
            for s, v in tok.items() if isinstance(tok, dict) else [tok]:
                if deps.get(s, 0) < v:
                    deps[s] = v
        own = self.csem.get(eng) if (not dma and eng in ('act', 'dve')) else None

        def add_nr(tok):
            if tok is None:
                return
            for s_, v_ in tok.items() if isinstance(tok, dict) else [tok]:
                if own is not None and s_ == own:
                    continue
                if deps.get(s_, 0) < v_:
                    deps[s_] = v_
        if self._bar:
            add(self._bar)
        for k in reads:
            add(self.lastw.get(k))
        for k in writes:
            add_nr(self.lastw.get(k))
            add_nr(self.readers.get(k))
        if dma:
            pool = self.dsems[eng]
            i = self.dcnt[eng]
            self.dcnt[eng] += 1
            sem = pool[i % len(pool)]
            val = 16 * (i // len(pool) + 1)
            add((sem, val - 16))
            inc = 16
        else:
            self.cnt[eng] += 1
            sem = self.csem[eng]
            val = self.cnt[eng]
            inc = 1
        waits = []
        w = self.waited[eng]
        for s, v in deps.items():
            if v <= 0:
                continue
            if eng == 'pe' and s == self.csem['pe']:
                continue
            if w.get(s, 0) >= v:
                continue
            w[s] = v
            waits.append((s, v))
        self.q[eng].append((waits, fn, sem, inc))
        tok = (sem, val)
        self.all_tokens[sem] = max(self.all_tokens.get(sem, 0), val)
        for k in reads:
            r = self.readers.setdefault(k, {})
            if r.get(sem, 0) < val:
                r[sem] = val
        for k in writes:
            self.lastw[k] = tok
            self.readers[k] = {}
        return tok

    def barrier(self):
        snap = dict(self.all_tokens)
        for k in list(self.lastw.keys()):
            self.lastw[k] = snap
            self.readers[k] = {}
        self._bar = snap

    def emit(self, engname, engine):
        for waits, fn, sem, inc in self.q[engname]:
            for s, v in waits:
                engine.wait_ge(self.sems[s], v)
            ins = fn(engine)
            ins.then_inc(self.sems[sem], inc)

    def final_wait(self, engine):
        for s, v in self.all_tokens.items():
            engine.wait_ge(self.sems[s], v)


class _Stop(Exception):
    pass


import os
_STOP = float(os.environ.get("KSTOP", "0")) or None
_SKIP_PA = bool(os.environ.get("KSKIP"))
_SMALLPA = bool(os.environ.get("KSMALLPA"))
_NPHYS = 8 if (_SMALLPA or _SKIP_PA or (_STOP is not None and _STOP <= 7)) else 5120


_HOOK = [None]


def chk(stage, active=True):
    if not active:
        return
    if _STOP is not None and stage >= _STOP:
        if _HOOK[0] is not None and os.environ.get("KDBG") == "5":
            _HOOK[0]()
        raise _Stop()


def build_program():
    nc = bass.Bass("TRN2", target_bir_lowering=False)
    S = None
    try:
        S = _build_body(nc)
    except _Stop as e:
        S = e.args[0] if e.args else S
    _emit(nc, _S_HOLD[0])
    return nc


_S_HOLD = [None]


def _emit(nc, S):
    with nc.Block() as block:
        @block.sync
        def _(e):
            S.emit('sp', e)
            S.final_wait(e)

        @block.tensor
        def _(e):
            S.emit('pe', e)

        @block.scalar
        def _(e):
            S.emit('act', e)

        @block.vector
        def _(e):
            S.emit('dve', e)

        @block.gpsimd
        def _(e):
            S.emit('pool', e)


def _build_body(nc):

    def din(name, shape, dt=F32):
        return nc.dram_tensor(name, list(shape), dt, kind="ExternalInput").ap()

    def dout(name, shape, dt=F32):
        return nc.dram_tensor(name, list(shape), dt, kind="ExternalOutput").ap()

    x_d = din("x", [NT, 128, D])
    p_d = din("p", [NT, 128, 256])
    cc_d = din("cache_cat", [_NPHYS * 128, 160])
    sgla_d = din("state_gla", [4, 2, 128, 128])
    sconv_d = din("state_conv", [8, DFF])
    pt_d = din("page_table", [512], I32)
    w_in_d = din("w_in", [D, INC])
    wa2_d = din("gla_w_a2", [16, 256])
    ba_d = din("gla_b_a", [256])
    gout_d = din("gla_g_out", [128])
    gqa_d = din("mla_g_qa", [256])
    wqup_d = din("mla_w_qup", [256, 768])
    gqn_d = din("mla_g_qn", [64])
    gqr_d = din("mla_g_qr", [32])
    gkva_d = din("mla_g_kva", [128])
    gkr_d = din("mla_g_kr", [32])
    wkv_d = din("mla_w_kvup", [128, 1024])
    gkn_d = din("mla_g_kn", [64])
    wo_d = din("w_o", [D, D])
    gmix_d = din("g_mix", [D])
    gffn_d = din("g_ffn", [D])
    gple_d = din("g_ple", [D])
    wg_d = din("ffn_w_gate", [D, DFF])
    wu_d = din("ffn_w_up", [D, DFF])
    cw_d = din("ffn_conv_w", [3, DFF])
    cb_d = din("ffn_conv_b", [DFF])
    wd_d = din("ffn_w_down", [DFF, D])
    wpg_d = din("ple_w_gate", [D, D])
    wpp_d = din("ple_w_proj", [256, D])
    gqnb_d = din("b_gqn", [128, 64])
    gqrb_d = din("b_gqr", [128, 32])
    gkvab_d = din("b_gkva", [128, 128])
    gkrb_d = din("b_gkr", [128, 32])
    gknb_d = din("b_gkn", [128, 64])
    ptb_d = din("b_pt", [128, 512], I32)
    c_ident_d = din("c_ident", [128, 128], BF16)
    c_identf_d = din("c_identf", [128, 128])
    c_mask_d = din("c_mask", [128, 128], BF16)
    c_masks_d = din("c_masks", [128, 128], BF16)
    c_maska_d = din("c_maska", [128, 4, 64], BF16)
    c_cos_d = din("c_cos", [128, NT, 16])
    c_sin_d = din("c_sin", [128, NT, 16])
    c_rs_d = din("c_rs", [128, 2, 128])
    c_iota_d = din("c_iota", [128, 1])
    c_sel_d = din("c_sel", [128, 128], BF16)
    c_rowm_d = din("c_rowm", [128, 4])
    c_hm_d = din("c_hm", [128, 2])

    y_o = dout("y", [NT, 128, D])
    ckv_o = dout("ckv_o", [NT, 128, 128])
    kr_o = dout("kr_o", [NT, 128, 32])
    glap_o = dout("gla_p", [2, 128, 128])
    glas_o = dout("gla_s", [4, 2, 128, 128])
    convp_o = dout("conv_p", [2, DFF])
    convs_o = dout("conv_s", [8, DFF])

    sems = [nc.alloc_semaphore(f"s{i}") for i in range(60)]
    S = Sched(sems)
    _S_HOLD[0] = S

    class Arena:
        def __init__(self):
            self.base = 16512
            self.top = 229344
            self.off = self.base
            self.n = 0

        def alloc(self, name, shape, dt):
            sz = 1
            for s in shape[1:]:
                sz *= s
            sz *= mybir.dt.size(dt)
            sz = (sz + 31) // 32 * 32
            assert self.off + sz <= self.top, f"SBUF overflow at {name}: {self.off + sz}"
            self.n += 1
            t = nc.alloc_sbuf_tensor_at(f"{name}_{self.n}", list(shape), dt, offset=self.off)
            self.off += sz
            return t

    A = Arena()

    ptr = [nc.alloc_psum_tensor(f"ptr{i}", [128, 1024], BF16) for i in range(2)]
    pm = [nc.alloc_psum_tensor(f"pm{i}", [128, 512], F32) for i in range(2)]
    pw = [nc.alloc_psum_tensor(f"pw{i}", [128, 1024], F32) for i in range(2)]

    def dma(eng, out, in_, reads=(), writes=(), **kw):
        if out.dtype != in_.dtype:
            eng = 'pool'
        return S.op(eng, lambda e: e.dma_start(out=out, in_=in_, **kw), reads, writes, dma=True)

    def mm(out, lhsT, rhs, start, stop, reads, writes, **kw):
        return S.op('pe', lambda e: e.matmul(out, lhsT, rhs, start=start, stop=stop, **kw), reads, writes)

    def tr(out, in_, ident, reads, writes):
        return S.op('pe', lambda e: e.transpose(out, in_, ident), reads, writes)

    def act(out, in_, func, reads, writes, **kw):
        return S.op('act', lambda e: e.activation(out, in_, func, **kw), reads, writes)

    def tt(eng, out, in0, in1, op, reads, writes):
        return S.op(eng, lambda e: e.tensor_tensor(out, in0, in1, op), reads, writes)

    def ts(eng, out, in0, s1, s2, op0, op1, reads, writes):
        if s2 is None:
            return S.op(eng, lambda e: e.tensor_scalar(out, in0, s1, None, op0), reads, writes)
        return S.op(eng, lambda e: e.tensor_scalar(out, in0, s1, s2, op0, op1), reads, writes)

    def stt(out, in0, sc, in1, op0, op1, reads, writes):
        return S.op('dve', lambda e: e.scalar_tensor_tensor(out, in0, sc, in1, op0, op1), reads, writes)

    def cp(eng, out, in_, reads, writes):
        if eng == 'act':
            return S.op('act', lambda e: e.copy(out, in_), reads, writes)
        return S.op(eng, lambda e: e.tensor_copy(out, in_), reads, writes)

    def memset(eng, ap, val, writes):
        return S.op(eng, lambda e: e.memset(ap, val), (), writes)

    def bc_last(ap, n):
        sh = list(ap.shape)
        return ap.unsqueeze(len(sh)).broadcast_to(sh + [n])

    def bc_mid(ap, n):
        sh = list(ap.shape)
        return ap.unsqueeze(1).broadcast_to([sh[0], n] + sh[1:])

    ident = A.alloc("ident", [128, 128], BF16)
    identf = A.alloc("identf", [128, 128], F32)
    mask01 = A.alloc("mask01", [128, 128], BF16)
    masks = A.alloc("masks", [128, 128], BF16)
    maska = A.alloc("maska", [128, 4, 64], BF16)
    ones_bf = A.alloc("ones_bf", [128, 128], BF16)
    eps_c = A.alloc("eps_c", [128, 1], F32)
    gmixT = A.alloc("gmixT", [128, 8], F32)
    gffnT = A.alloc("gffnT", [128, 8], F32)
    gpleT = A.alloc("gpleT", [128, 8], F32)
    ogT = A.alloc("ogT", [128, 4, NT * 128], BF16)
    omT = A.alloc("omT", [128, 4, NT * 128], BF16)
    stat = A.alloc("stat", [128, 8], F32)
    xs = A.alloc("xs", [128, D], BF16)
    persist_off = A.off

    dma('sp', ident[:, :], c_ident_d, (), ['ident'])
    def _hook():
        S.barrier()
        dma('sp', y_o[14][:, 0:128], identf[:, :], ['identf'], ['y_o'])
    _HOOK[0] = _hook
    if os.environ.get("KDBG") == "5":
        dma('sp', identf[:, :], c_identf_d, (), ['identf'])
        dma('sp', y_o[15][:, 0:128], identf[:, :], ['identf'], ['y_o'])
    dma('sp', identf[:, :], c_identf_d, (), ['identf'])
    dma('sp', mask01[:, :], c_mask_d, (), ['mask01'])
    dma('sp', masks[:, :], c_masks_d, (), ['masks'])
    dma('sp', maska[:, :, :], c_maska_d, (), ['maska'])
    for gt, gd, k in ((gmixT, gmix_d, 'gmixT'), (gffnT, gffn_d, 'gffnT'), (gpleT, gple_d, 'gpleT')):
        dma('sp', gt[:, :], gd.rearrange("(c p) -> p c", p=128), (), [k], allow_slow_non_contiguous=True)
    memset('dve', ones_bf[:, :], 1.0, ['ones'])
    memset('dve', eps_c[:, :], EPS, ['eps'])

    def rstd_chain(ms_ap, out_ap, inv_n, rk, wk):
        act(out_ap, ms_ap, AF.Ln, list(rk) + ['eps'], wk, bias=eps_c[0:ms_ap.shape[0], 0:1], scale=inv_n)
        act(out_ap, out_ap, AF.Exp, wk, wk, scale=-0.5)

    def norm_tile(src, src_keys, gT, gkey, dst, dst_keys, pslot):
        act(xs[:, :], src, AF.Square, src_keys, ['xs', 'stat'], accum_out=stat[:, 0:1])
        rstd_chain(stat[:, 0:1], stat[:, 1:2], 1.0 / D, ['stat'], ['stat'])
        ts('dve', xs[:, :], src, stat[:, 1:2], None, ALU.mult, None, list(src_keys) + ['stat'], ['xs'])
        pk = f'ptr{pslot}'
        for k in range(8):
            tr(ptr[pslot][:, k * 128:(k + 1) * 128], xs[:, k * 128:(k + 1) * 128], ident[:, :],
               ['xs', 'ident'], [pk])
        tt('dve', dst, ptr[pslot][:, :].rearrange("p (c t) -> p c t", c=8), bc_last(gT[:, :], 128),
           ALU.mult, [pk, gkey], dst_keys)

    w_in_sb = A.alloc("w_in", [128, 8, INC], BF16)
    wqup_sb = A.alloc("wqup", [128, 2, 768], BF16)
    wkv_sb = A.alloc("wkv", [128, 1024], BF16)
    wkn_sb = A.alloc("wkn", [128, 512], BF16)
    wa2_sb = A.alloc("wa2", [16, 256], F32)
    nba = A.alloc("nba", [128, 2], F32)
    goutc = A.alloc("goutc", [128, 1], F32)
    gqa_c = A.alloc("gqa", [128, 2], F32)
    gknc = A.alloc("gknc", [128, 1], F32)
    Gqn = A.alloc("Gqn", [128, 64], F32)
    Gqr = A.alloc("Gqr", [128, 32], F32)
    Gkva = A.alloc("Gkva", [128, 128], F32)
    Gkr = A.alloc("Gkr", [128, 32], F32)
    Gkn = A.alloc("Gkn", [128, 64], F32)
    cos_sb = A.alloc("cos", [128, NT, 16], F32)
    sin_sb = A.alloc("sin", [128, NT, 16], F32)
    rs_sb = A.alloc("rs", [128, 2, 128], F32)
    iota_sb = A.alloc("iota", [128, 1], F32)
    x_sb = A.alloc("x_sb", [128, 2, D], F32)
    nT = A.alloc("nT", [128, 2, 8, 128], BF16)
    gaT = A.alloc("gaT", [16, 128], F32)
    sp_ = A.alloc("sp", [128, 2, 128], F32)
    bpos = A.alloc("bpos", [128, 2, 128], F32)
    etmp = A.alloc("etmp", [128, 2, 128], F32)
    nbl = A.alloc("nbl", [128, 8], F32)
    ebl = A.alloc("ebl", [128, 8], F32)
    qdT = A.alloc("qdT", [128, 4, 128], BF16)
    ke_z = A.alloc("ke_z", [128, 4, 256], BF16)
    rowm = A.alloc("rowm", [128, 4], F32)
    hm = A.alloc("hm", [128, 2], F32)
    wvp_sb = A.alloc("wvp", [128, 4, 128], BF16)
    kiT = A.alloc("kiT", [128, 2, 128], BF16)
    keT = A.alloc("keT", [128, 2, 128], BF16)
    ke_tok = A.alloc("ke_tok", [128, 256], BF16)
    v_tok = A.alloc("v_tok", [128, 512], BF16)
    sgT = A.alloc("sgT", [128, 4, 128], BF16)
    AT = A.alloc("AT", [128, 4, 128], BF16)
    S_f = A.alloc("S_f", [128, 2, 128], F32)
    S_b = A.alloc("S_b", [128, 2, 128], BF16)
    S0_f = A.alloc("S0_f", [128, 4, 2, 128], F32)
    S0_b = A.alloc("S0_b", [128, 4, 2, 128], BF16)
    osq = A.alloc("osq", [128, 512], BF16)
    orst = A.alloc("orst", [128, 512], F32)
    u_sb = A.alloc("u_sb", [128, 416], F32)
    ms = A.alloc("ms", [128, 32], F32)
    rst = A.alloc("rst", [128, 32], F32)
    cq = A.alloc("cq", [128, 256], BF16)
    cqT = A.alloc("cqT", [128, 2, 128], BF16)
    ckv_f = A.alloc("ckv_f", [128, 2, 128], F32)
    ckv_b = A.alloc("ckv_b", [128, 132], BF16)
    ckvT = A.alloc("ckvT", [128, 128], BF16)
    sqq = A.alloc("sqq", [128, 768], F32)
    sqk = A.alloc("sqk", [128, 8, 64], F32)
    Rn = A.alloc("Rn", [128, 9, 32], F32)
    Rr = A.alloc("Rr", [128, 2, 9, 32], F32)
    rt1 = A.alloc("rt1", [128, 9, 16], F32)
    rt2 = A.alloc("rt2", [128, 9, 16], F32)
    q_tok = A.alloc("q_tok", [128, 8, 64], BF16)
    qr_tok = A.alloc("qr_tok", [128, 8, 32], BF16)
    k_tok = A.alloc("k_tok", [128, 8, 64], BF16)
    kr_tok = A.alloc("kr_tok", [128, 32], BF16)
    QnT = A.alloc("QnT", [128, 8, 128], BF16)
    QrT = A.alloc("QrT", [128, 8, 128], BF16)
    KnT = A.alloc("KnT", [128, 4, NT * 128], BF16)
    KrT = A.alloc("KrT", [128, NT * 128], BF16)
    Vext = A.alloc("Vext", [128, 16, 8, 65], BF16)
    PT = A.alloc("PT", [128, 2, 4, 128], BF16)
    rl = A.alloc("rl", [128, 8], F32)
    o_tok = A.alloc("o_tok", [128, 8, 64], BF16)
    IDX = A.alloc("IDX", [128, 512], I32)
    PTi = A.alloc("PTi", [128, 512], I32)
    Cbuf = A.alloc("Cbuf", [128, 2, 4, 160], F32)
    Cb = A.alloc("Cb", [128, 2, 4, 132], BF16)
    KRb = A.alloc("KRb", [128, 2, 4, 32], BF16)
    CT = A.alloc("CT", [128, 2, 4, 128], BF16)
    KRT = A.alloc("KRT", [128, 2, 4, 128], BF16)
    sqs = A.alloc("sqs", [128, 2, 1024], BF16)
    ssum = A.alloc("ssum", [128, 32], F32)
    rstk = A.alloc("rstk", [128, 32], F32)
    rstk2 = A.alloc("rstk2", [128, 2, 32], F32)
    sc1 = A.alloc("sc1", [128, 256], F32)
    sc2 = A.alloc("sc2", [128, 256], F32)
    PTs = A.alloc("PTs", [128, 2, 4, 64], BF16)
    WkT = A.alloc("WkT", [128, 4, 128], BF16)
    qgT = A.alloc("qgT", [128, 8, 128], BF16)
    qabsT = A.alloc("qabsT", [128, 8, 128], BF16)
    acc_sb = A.alloc("acc_sb", [64, 132], F32)
    accn = A.alloc("accn", [128, 128], BF16)
    accnT = A.alloc("accnT", [128, 128], BF16)

    for k in range(8):
        dma('sp', w_in_sb[:, k, :], w_in_d[k * 128:(k + 1) * 128, :], (), ['w_in'])
    dma('sp', wqup_sb[:, :, :], wqup_d.rearrange("(c p) n -> p c n", p=128), (), ['wqup'])
    dma('sp', wkv_sb[:, :], wkv_d, (), ['wkv'])
    dma('sp', wa2_sb[:, :], wa2_d, (), ['wa2'])
    dma('sp', nba[:, :], ba_d.rearrange("(c p) -> p c", p=128), (), ['nba'], allow_slow_non_contiguous=True)
    dma('sp', goutc[:, :], gout_d.rearrange("(p o) -> p o", o=1), (), ['goutc'], allow_slow_non_contiguous=True)
    dma('sp', gqa_c[:, :], gqa_d.rearrange("(c p) -> p c", p=128), (), ['gqa'], allow_slow_non_contiguous=True)
    dma('sp', gknc[0:64, :], gkn_d.rearrange("(p o) -> p o", o=1), (), ['gknc'], allow_slow_non_contiguous=True)
    dma('sp', gknc[64:128, :], gkn_d.rearrange("(p o) -> p o", o=1), (), ['gknc'], allow_slow_non_contiguous=True)
    for gt, gd, k in ((Gqn, gqnb_d, 'Gqn'), (Gqr, gqrb_d, 'Gqr'), (Gkva, gkvab_d, 'Gkva'), (Gkr, gkrb_d, 'Gkr'),
                      (Gkn, gknb_d, 'Gkn')):
        dma('sp', gt[:, :], gd, (), [k])
    dma('sp', cos_sb[:, :, :], c_cos_d, (), ['cos'])
    dma('sp', sin_sb[:, :, :], c_sin_d, (), ['sin'])
    dma('sp', rs_sb[:, :, :], c_rs_d, (), ['rs'])
    dma('sp', iota_sb[:, :], c_iota_d, (), ['iota'])
    dma('sp', PTi[:, :], ptb_d, (), ['PTi'])
    for b in range(4):
        for pr in range(2):
            dma('sp', S0_f[:, b, pr, :], sgla_d[b, pr], (), ['S0_f'])
    ts('dve', nba[:, :], nba[:, :], -1.0, None, ALU.mult, None, ['nba'], ['nba'])
    for c in range(2):
        ts('dve', wqup_sb[:, c, :], wqup_sb[:, c, :], gqa_c[:, c:c + 1], None, ALU.mult, None,
           ['wqup', 'gqa'], ['wqup'])
    cp('dve', wkn_sb[:, :].rearrange("p (h d) -> p h d", h=8),
       wkv_sb[:, :].rearrange("p (h e) -> p h e", h=8)[:, :, 0:64], ['wkv'], ['wkn'])
    memset('dve', S_f[:, :, :], 0.0, ['S_f'])
    memset('dve', nbl[:, :], 0.0, ['nbl'])
    memset('dve', qdT[:, :, :], 0.0, ['qdT'])
    memset('dve', QnT[:, :, :], 0.0, ['QnT'])
    memset('dve', QrT[:, :, :], 0.0, ['QrT'])
    memset('dve', KrT[:, :], 0.0, ['KrT'])
    memset('dve', KRT[:, :, :, :], 0.0, ['KRT0', 'KRT1'])
    memset('dve', accn[:, :], 0.0, ['accn'])
    dma('sp', rowm[:, :], c_rowm_d, (), ['rowm'])
    dma('sp', hm[:, :], c_hm_d, (), ['hm'])
    cp('dve', wvp_sb[:, :, :].rearrange("p c (a d) -> p c a d", a=2),
       wkv_sb[:, :].rearrange("p (c a e) -> p c a e", c=4, a=2)[:, :, :, 64:128], ['wkv'], ['wvp'])
    memset('dve', S_b[:, :, :], 0.0, ['S_b'])
    cp('dve', S0_b[:, :, :, :], S0_f[:, :, :, :], ['S0_f'], ['S0_b'])
    memset('dve', ckv_b[:, 128:132], 1.0, ['ckv_b1'])
    memset('dve', Cb[:, :, :, 128:132], 1.0, ['Cb1'])
    memset('dve', Vext[:, :, :, 64:65], 1.0, ['Vext1'])
    memset('dve', omT[:, :, ST * 128:], 0.0, ['omT'])
    ts('dve', IDX[:, :], PTi[:, :], 128.0, iota_sb[:, 0:1], ALU.mult, ALU.add, ['PTi', 'iota'], ['IDX'])

    def load_x(i):
        dma('sp', x_sb[:, i % 2, :], x_d[i], (), [f'x{i % 2}'])

    if os.environ.get("KDBG") == "2":
        load_x(0)
        dma('sp', y_o[0], x_sb[:, 0, :], ['x0'], ['y_o'])
        dma('pool', y_o[3][:, 0:1024], w_in_sb[:, 0, 0:1024], ['w_in'], ['y_o'])
        dma('sp', y_o[4][:, 0:128], identf[:, :], ['identf'], ['y_o'])
    chk(1)
    load_x(0)

    pmi = [0]

    def next_pm():
        pmi[0] ^= 1
        return pmi[0]

    for i in range(NT):
        samp = (i == ST)
        sl = i % 2
        xk = f'x{sl}'
        nk = f'nT{sl}'
        nTi = nT[:, sl, :, :]
        tcol = slice(i * 128, (i + 1) * 128)
        if i + 1 < NT:
            load_x(i + 1)
        norm_tile(x_sb[:, sl, :], [xk], gmixT, 'gmixT', nTi, [nk], 0)
        if i == 0 and os.environ.get("KDBG") == "3":
            dma('pool', y_o[2][:, 0:1024], nT[:, 0, :, :].rearrange("p c t -> p (c t)"), [nk], ['y_o'])
            dma('pool', y_o[4][:, 0:1024], xs[:, :], ['xs'], ['y_o'])
            dma('sp', y_o[5][:, 0:8], stat[:, :], ['stat'], ['y_o'])
            dma('sp', y_o[0], x_sb[:, 0, :], ['x0'], ['y_o'])
        if i == 0 and os.environ.get("KDBG") == "4":
            stg = x_sb[:, 1, :]
            cp('dve', stg, xs[:, :], ['xs'], ['x1'])
            dma('sp', y_o[4], stg, ['x1'], ['y_o'])
            cp('dve', stg, nT[:, 0, :, :].rearrange("p c t -> p (c t)"), [nk], ['x1'])
            dma('sp', y_o[2], stg, ['x1'], ['y_o'])
            cp('dve', x_sb[:, 1, 0:8], stat[:, :], ['stat'], ['x1'])
            cp('dve', x_sb[:, 1, 8:16], gmixT[:, :], ['gmixT'], ['x1'])
            dma('sp', y_o[5], stg, ['x1'], ['y_o'])
        chk(2, i == 0)

        j = next_pm()
        for k in range(8):
            mm(pm[j][0:16, 0:128], w_in_sb[:, k, O_A:O_A + 16], nTi[:, k, :], k == 0, k == 7,
               ['w_in', nk], [f'pm{j}'])
        cp('act', gaT[:, :], pm[j][0:16, 0:128], [f'pm{j}'], ['gaT'])
        chk(2.1, i == 0)
        j = next_pm()
        for pr in range(2):
            mm(pm[j][:, pr * 128:(pr + 1) * 128], wa2_sb[:, pr * 128:(pr + 1) * 128], gaT[:, :], True, True,
               ['wa2', 'gaT'], [f'pm{j}'])
        for pr in range(2):
            act(sp_[:, pr, :], pm[j][:, pr * 128:(pr + 1) * 128], AF.Exp, [f'pm{j}', 'nba'], ['sp'],
                bias=nba[:, pr:pr + 1], scale=-1.0)
        act(sp_[:, :, :], sp_[:, :, :], AF.Ln, ['sp'], ['sp'], bias=1.0, scale=1.0)
        chk(2.3, i == 0)
        rsi = 1 if samp else 0
        for pr in range(2):
            S.op('dve', lambda e, pr=pr, rsi=rsi: e.tensor_tensor_scan(bpos[:, pr, :], rs_sb[:, rsi, :], sp_[:, pr, :],
                                                              0.0, ALU.mult, ALU.add),
                 ['sp', 'rs'], ['bpos'])
        segs = [(b * 32, 32, b * 32 + 9, b) for b in range(4)] if samp else [(0, 128, 127, 0)]
        chk(2.4, i == 0)
        for (c0, ncol, lc, si) in segs:
            ts('dve', nbl[:, si * 2:si * 2 + 2], bpos[:, :, lc], -1.0 / 16, None, ALU.mult, None,
               ['bpos'], ['nbl'])
        act(ebl[:, :], nbl[:, :], AF.Exp, ['nbl'], ['ebl'])
        chk(2.5, i == 0)
        j = next_pm()
        for pr in range(2):
            for k in range(8):
                mm(pm[j][:, pr * 128:(pr + 1) * 128], w_in_sb[:, k, O_Q + pr * 128:O_Q + (pr + 1) * 128],
                   nTi[:, k, :], k == 0, k == 7, ['w_in', nk], [f'pm{j}'])
        act(etmp[:, :, :], bpos[:, :, :], AF.Exp, ['bpos'], ['etmp'], scale=-1.0 / 16)
        for hh in range(2):
            r = slice(hh * 64, hh * 64 + 64)
            stt(qdT[r, hh::2, :], pm[j][r, 0:256].rearrange("p (a t) -> p a t", a=2), 0.125, etmp[r, :, :],
                ALU.mult, ALU.mult, [f'pm{j}', 'etmp'], ['qdT'])
        chk(2.6, i == 0)
        j = next_pm()
        for pr in range(2):
            for k in range(8):
                mm(pm[j][:, pr * 128:(pr + 1) * 128], w_in_sb[:, k, O_K + pr * 128:O_K + (pr + 1) * 128],
                   nTi[:, k, :], k == 0, k == 7, ['w_in', nk], [f'pm{j}'])
        act(etmp[:, :, :], bpos[:, :, :], AF.Exp, ['bpos', 'qdT'], ['etmp'], scale=1.0 / 16)
        tt('dve', kiT[:, :, :], pm[j][:, 0:256].rearrange("p (a t) -> p a t", a=2), etmp[:, :, :], ALU.mult,
           [f'pm{j}', 'etmp'], ['kiT'])
        for (c0, ncol, lc, si) in segs:
            for pr in range(2):
                act(etmp[:, pr, c0:c0 + ncol], bpos[:, pr, c0:c0 + ncol], AF.Exp, ['bpos', 'nbl', 'kiT'],
                    ['etmp'], bias=nbl[:, si * 2 + pr:si * 2 + pr + 1], scale=1.0 / 16)
        tt('dve', keT[:, :, :], pm[j][:, 0:256].rearrange("p (a t) -> p a t", a=2), etmp[:, :, :], ALU.mult,
           [f'pm{j}', 'etmp'], ['keT'])
        for pr in range(2):
            tr(ptr[1][:, pr * 128:(pr + 1) * 128], keT[:, pr, :], ident[:, :], ['keT', 'ident'], ['ptr1'])
        cp('dve', ke_tok[:, :], ptr[1][:, 0:256], ['ptr1'], ['ke_tok'])
        chk(2.7, i == 0)
        j = next_pm()
        for c in range(4):
            for k in range(8):
                mm(pm[j][:, c * 128:(c + 1) * 128], w_in_sb[:, k, O_G + c * 128:O_G + (c + 1) * 128],
                   nTi[:, k, :], k == 0, k == 7, ['w_in', nk], [f'pm{j}'])
        act(sgT[:, :, :], pm[j][:, :].rearrange("p (c t) -> p c t", c=4), AF.Silu, [f'pm{j}'], ['sgT'])
        j = next_pm()
        for k in range(8):
            mm(pm[j][:, :], nTi[:, k, :], w_in_sb[:, k, O_V:O_V + 512], k == 0, k == 7, ['w_in', nk], [f'pm{j}'])
        cp('act', v_tok[:, :], pm[j][:, :], [f'pm{j}'], ['v_tok'])
        chk(2.8, i == 0)
        j = next_pm()
        for h in range(4):
            pr = h // 2
            mm(pm[j][:, h * 128:(h + 1) * 128], kiT[:, pr, :], qdT[:, h, :], True, True,
               ['kiT', 'qdT'], [f'pm{j}'])
        mk, mkey = (masks, 'masks') if samp else (mask01, 'mask01')
        tt('dve', AT[:, :, :], pm[j][:, :].rearrange("p (h t) -> p h t", h=4), bc_mid(mk[:, :], 4), ALU.mult,
           [f'pm{j}', mkey], ['AT'])
        chk(2.9, i == 0)
        j = next_pm()
        for h in range(4):
            pr, b0 = h // 2, (h % 2) * 64
            oh = pm[j][:, h * 128:(h + 1) * 128]
            mm(oh, v_tok[:, h * 128:(h + 1) * 128], AT[:, h, :], True, False, ['v_tok', 'AT'], [f'pm{j}'])
            if not samp:
                mm(oh, S_b[:, pr, :], qdT[:, h, :], False, True, ['S_b', 'qdT'], [f'pm{j}'])
            else:
                for b in range(4):
                    mm(pm[j][:, h * 128 + b * 32:h * 128 + (b + 1) * 32], S0_b[:, b, pr, :],
                       qdT[:, h, b * 32:(b + 1) * 32], False, b == 3, ['S0_b', 'qdT'], [f'pm{j}'])
        jo = j
        chk(2.95, i == 0)
        act(osq[:, :], pm[jo][:, :], AF.Square, [f'pm{jo}'], ['osq'])
        j = next_pm()
        mm(pm[j][:, :], ones_bf[:, :], osq[:, :], True, True, ['ones', 'osq'], [f'pm{j}'])
        rstd_chain(pm[j][:, :], orst[:, :], 1.0 / 128, [f'pm{j}'], ['orst'])
        tt('dve', orst[:, :], pm[jo][:, :], orst[:, :], ALU.mult, [f'pm{jo}', 'orst'], ['orst'])
        stt(ogT[:, :, tcol], orst[:, :].rearrange("p (h t) -> p h t", h=4), goutc[:, 0:1], sgT[:, :, :],
            ALU.mult, ALU.mult, ['orst', 'goutc', 'sgT'], ['ogT'])
        chk(2.97, i == 0)
        if not samp:
            for pr in range(2):
                j = next_pm()
                mm(pm[j][:, 0:256], ke_tok[:, pr * 128:(pr + 1) * 128], v_tok[:, pr * 256:(pr + 1) * 256],
                   True, True, ['ke_tok', 'v_tok'], [f'pm{j}'])
                ts('dve', orst[:, 0:128], pm[j][:, 0:128], hm[:, 0:1], None, ALU.mult, None, [f'pm{j}', 'hm'], ['orst'])
                stt(orst[:, 0:128], pm[j][:, 128:256], hm[:, 1:2], orst[:, 0:128], ALU.mult, ALU.add,
                    [f'pm{j}', 'hm', 'orst'], ['orst'])
                stt(S_f[:, pr, :], S_f[:, pr, :], ebl[:, pr:pr + 1], orst[:, 0:128], ALU.mult, ALU.add,
                    ['S_f', 'ebl', 'orst'], ['S_f'])
                chk(2.98, i == 0 and pr == 0)
                chk(2.99, i == 0 and pr == 1)
            cp('dve', S_b[:, :, :], S_f[:, :, :], ['S_f'], ['S_b'])
            if i == ST - 1:
                for pr in range(2):
                    dma('sp', glap_o[pr], S_f[:, pr, :], ['S_f'], ['glap_o'])
        else:
            for b in range(4):
                ts('dve', ke_z[:, b, :], ke_tok[:, :], rowm[:, b:b + 1], None, ALU.mult, None,
                   ['ke_tok', 'rowm'], ['ke_z'])
            for b in range(4):
                for pr in range(2):
                    j = next_pm()
                    mm(pm[j][:, 0:256], ke_z[:, b, pr * 128:(pr + 1) * 128],
                       v_tok[:, pr * 256:(pr + 1) * 256], True, True, ['ke_z', 'v_tok'], [f'pm{j}'])
                    ts('dve', orst[:, 0:128], pm[j][:, 0:128], hm[:, 0:1], None, ALU.mult, None, [f'pm{j}', 'hm'],
                       ['orst'])
                    stt(orst[:, 0:128], pm[j][:, 128:256], hm[:, 1:2], orst[:, 0:128], ALU.mult, ALU.add,
                        [f'pm{j}', 'hm', 'orst'], ['orst'])
                    stt(S0_f[:, b, pr, :], S0_f[:, b, pr, :], ebl[:, b * 2 + pr:b * 2 + pr + 1], orst[:, 0:128],
                        ALU.mult, ALU.add, ['S0_f', 'ebl', 'orst'], ['S0_f'])
                    dma('sp', glas_o[b, pr], S0_f[:, b, pr, :], ['S0_f'], ['glas_o'])

        chk(3, i == 0)
        j = next_pm()
        for k in range(8):
            mm(pm[j][:, 0:416], nTi[:, k, :], w_in_sb[:, k, O_MQ:O_MQ + 416], k == 0, k == 7,
               ['w_in', nk], [f'pm{j}'])
        cp('dve', u_sb[:, :], pm[j][:, 0:416], [f'pm{j}'], ['u_sb'])
        chk(3.1, i == 0)
        for (a, b_, col, n) in ((0, 256, 24, 256), (256, 384, 25, 128), (384, 416, 26, 32)):
            if True:
                act(sqq[:, a:b_], u_sb[:, a:b_], AF.Square, ['u_sb'], ['sqq', 'ms'], accum_out=ms[:, col:col + 1],
                    scale=float(n) ** -0.5)
            else:
                act(sqq[:, a:b_], u_sb[:, a:b_], AF.Square, ['u_sb'], ['sqq'], scale=float(n) ** -0.5)
                S.op('dve', lambda e, a=a, b_=b_, col=col: e.tensor_reduce(ms[:, col:col + 1], sqq[:, a:b_], AX.X, ALU.add),
                     ['sqq'], ['ms'])
        rstd_chain(ms[:, 24:27], rst[:, 24:27], 1.0, ['ms'], ['rst'])
        chk(3.2, i == 0)
        ts('dve', cq[:, :], u_sb[:, 0:256], rst[:, 24:25], None, ALU.mult, None, ['u_sb', 'rst'], ['cq'])
        cs = i % 2
        stt(ckv_f[:, cs, :], u_sb[:, 256:384], rst[:, 25:26], Gkva[:, :], ALU.mult, ALU.mult,
            ['u_sb', 'rst', 'Gkva'], [f'ckv_f{cs}'])
        dma('sp', ckv_o[i], ckv_f[:, cs, :], [f'ckv_f{cs}'], ['ckv_o'])
        cp('dve', ckv_b[:, 0:128], ckv_f[:, cs, :], [f'ckv_f{cs}'], ['ckv_b'])
        stt(Rn[:, 8, :], u_sb[:, 384:416], rst[:, 26:27], Gkr[:, :], ALU.mult, ALU.mult,
            ['u_sb', 'rst', 'Gkr'], ['Rn'])
        chk(3.3, i == 0)
        for c in range(2):
            tr(ptr[1][:, c * 128:(c + 1) * 128], cq[:, c * 128:(c + 1) * 128], ident[:, :], ['cq', 'ident'], ['ptr1'])
        tr(ptr[1][:, 256:384], ckv_b[:, 0:128], ident[:, :], ['ckv_b', 'ident'], ['ptr1'])
        cp('dve', cqT[:, :, :], ptr[1][:, 0:256].rearrange("p (c t) -> p c t", c=2), ['ptr1'], ['cqT'])
        cp('dve', ckvT[:, :], ptr[1][:, 256:384], ['ptr1'], ['ckvT'])
        chk(3.4, i == 0)
        for half, (n0, n1) in enumerate(((0, 512), (512, 768))):
            for c in range(2):
                mm(pw[0][:, n0:n1], cqT[:, c, :], wqup_sb[:, c, n0:n1], c == 0, c == 1, ['cqT', 'wqup'], ['pw0'])
        for half in range(2):
            mm(pw[1][:, half * 512:(half + 1) * 512], ckvT[:, :], wkv_sb[:, half * 512:(half + 1) * 512],
               True, True, ['ckvT', 'wkv'], ['pw1'])
        qfv = pw[0][:, 0:768].rearrange("p (h e) -> p h e", h=8)
        kvv = pw[1][:, :].rearrange("p (h e) -> p h e", h=8)
        chk(3.5, i == 0)
        act(sqq[:, :], pw[0][:, 0:768], AF.Square, ['pw0'], ['sqq'])
        sqv = sqq[:, :].rearrange("p (h e) -> p h e", h=8)
        S.op('dve', lambda e: e.tensor_reduce(ms[:, 0:8], sqv[:, :, 0:64], AX.X, ALU.add), ['sqq'], ['ms'])
        S.op('dve', lambda e: e.tensor_reduce(ms[:, 16:24], sqv[:, :, 64:96], AX.X, ALU.add), ['sqq'], ['ms'])
        act(sqk[:, :, :], kvv[:, :, 0:64], AF.Square, ['pw1'], ['sqk'])
        S.op('dve', lambda e: e.tensor_reduce(ms[:, 8:16], sqk[:, :, :], AX.X, ALU.add), ['sqk'], ['ms'])
        rstd_chain(ms[:, 0:16], rst[:, 0:16], 1.0 / 64, ['ms'], ['rst'])
        rstd_chain(ms[:, 16:24], rst[:, 16:24], 1.0 / 32, ['ms'], ['rst'])
        chk(3.6, i == 0)
        tt('dve', sqk[:, :, :], qfv[:, :, 0:64], bc_last(rst[:, 0:8], 64), ALU.mult, ['pw0', 'rst'], ['sqk'])
        tt('dve', q_tok[:, :, :], sqk[:, :, :], bc_mid(Gqn[:, :], 8), ALU.mult, ['sqk', 'Gqn'], ['q_tok'])
        tt('dve', sqk[:, :, :], kvv[:, :, 0:64], bc_last(rst[:, 8:16], 64), ALU.mult, ['pw1', 'rst', 'q_tok'],
           ['sqk'])
        tt('dve', k_tok[:, :, :], sqk[:, :, :], bc_mid(Gkn[:, :], 8), ALU.mult, ['sqk', 'Gkn'], ['k_tok'])
        chk(3.7, i == 0)
        if not samp:
            cp('act', Vext[:, i, :, 0:64], kvv[:, :, 64:128], ['pw1'], ['Vext'])
        tt('dve', Rn[:, 0:8, :], qfv[:, :, 64:96], bc_last(rst[:, 16:24], 32), ALU.mult, ['pw0', 'rst'], ['Rn'])
        tt('dve', Rn[:, 0:8, :], Rn[:, 0:8, :], bc_mid(Gqr[:, :], 8), ALU.mult, ['Rn', 'Gqr'], ['Rn'])
        cosb = bc_mid(cos_sb[:, i, :], 9)
        sinb = bc_mid(sin_sb[:, i, :], 9)
        x1, x2 = Rn[:, :, 0:16], Rn[:, :, 16:32]
        rk = f'Rr{cs}'
        tt('dve', rt1[:, :, :], x1, cosb, ALU.mult, ['Rn', 'cos'], ['rt1'])
        tt('dve', rt2[:, :, :], x2, sinb, ALU.mult, ['Rn', 'sin'], ['rt2'])
        tt('dve', Rr[:, cs, :, 0:16], rt1[:, :, :], rt2[:, :, :], ALU.subtract, ['rt1', 'rt2'], [rk])
        tt('dve', rt1[:, :, :], x2, cosb, ALU.mult, ['Rn', 'cos', rk], ['rt1'])
        tt('dve', rt2[:, :, :], x1, sinb, ALU.mult, ['Rn', 'sin', rk], ['rt2'])
        tt('dve', Rr[:, cs, :, 16:32], rt1[:, :, :], rt2[:, :, :], ALU.add, ['rt1', 'rt2'], [rk])
        dma('sp', kr_o[i], Rr[:, cs, 8, :], [rk], ['kr_o'])
        cp('dve', qr_tok[:, :, :], Rr[:, cs, 0:8, :], [rk], ['qr_tok'])
        cp('dve', kr_tok[:, :], Rr[:, cs, 8, :], [rk], ['kr_tok'])
        chk(3.8, i == 0)
        for c in range(4):
            tr(ptr[0][:, c * 128:(c + 1) * 128], q_tok[:, 2 * c:2 * c + 2, :], ident[:, :], ['q_tok', 'ident'], ['ptr0'])
            tr(ptr[0][:, 512 + c * 128:512 + (c + 1) * 128], k_tok[:, 2 * c:2 * c + 2, :], ident[:, :],
               ['k_tok', 'ident'], ['ptr0'])
        for hh in range(2):
            r = slice(hh * 64, hh * 64 + 64)
            cp('dve', QnT[r, hh::2, :], ptr[0][r, 0:512].rearrange("p (c t) -> p c t", c=4), ['ptr0'], ['QnT'])
        cp('dve', KnT[:, :, tcol], ptr[0][:, 512:1024].rearrange("p (c t) -> p c t", c=4), ['ptr0'], ['KnT'])
        for h in range(8):
            tr(ptr[1][0:32, h * 128:(h + 1) * 128], qr_tok[:, h, :], ident[:, :], ['qr_tok', 'ident'], ['ptr1'])
        cp('dve', QrT[0:32, :, :], ptr[1][0:32, :].rearrange("p (h t) -> p h t", h=8), ['ptr1'], ['QrT'])
        tr(ptr[1][0:32, 0:128], kr_tok[:, :], ident[:, :], ['kr_tok', 'ident', 'QrT'], ['ptr1'])
        cp('dve', KrT[0:32, tcol], ptr[1][0:32, 0:128], ['ptr1'], ['KrT'])

        if i == 0 and os.environ.get("KDBG"):
            dma('sp', y_o[0][:, 0:416], u_sb[:, :], ['u_sb'], ['y_o'])
            dma('sp', y_o[1][:, 0:32], rst[:, :], ['rst'], ['y_o'])
            dma('pool', y_o[2][:, 0:1024], nT[:, 0, :, :].rearrange("p c t -> p (c t)"), [nk], ['y_o'])
            dma('pool', y_o[3][:, 0:1024], w_in_sb[:, 0, 0:1024], ['w_in'], ['y_o'])
            dma('pool', y_o[4][:, 0:1024], xs[:, :], ['xs'], ['y_o'])
            dma('sp', y_o[5][:, 0:8], stat[:, :], ['stat'], ['y_o'])
            dma('sp', y_o[6][:, 0:256], ckv_f[:, :, :].rearrange("p a c -> p (a c)"), ['ckv_f0'], ['y_o'])
        if i == 0 and os.environ.get("KDBG") == "6":
            stg = x_sb[:, 1, :]
            cp('dve', x_sb[:, 1, 0:416], u_sb[:, :], ['u_sb'], ['x1'])
            cp('dve', x_sb[:, 1, 416:448], rst[:, :], ['rst'], ['x1'])
            cp('dve', x_sb[:, 1, 448:480], ms[:, :], ['ms'], ['x1'])
            cp('dve', x_sb[:, 1, 512:640], ckv_f[:, 0, :], ['ckv_f0'], ['x1'])
            cp('dve', x_sb[:, 1, 640:768], Gkva[:, :], ['Gkva'], ['x1'])
            cp('dve', x_sb[:, 1, 768:800], Rr[:, 0, 8, :], ['Rr0'], ['x1'])
            dma('sp', y_o[4], stg, ['x1'], ['y_o'])
        chk(4, i == 0)
        chk(7, samp)
        if not samp:
            groups = []
            for h in range(8):
                for kt0 in range(0, i + 1, 4):
                    groups.append((h, kt0, min(4, i + 1 - kt0)))

            def emit_scores(gi):
                h, kt0, nkt = groups[gi]
                pr, b0 = h // 2, (h % 2) * 64
                s = gi % 2
                for jj in range(nkt):
                    kt = kt0 + jj
                    kc = slice(kt * 128, (kt + 1) * 128)
                    o = pm[s][:, jj * 128:(jj + 1) * 128]
                    mm(o, KnT[:, pr, kc], QnT[:, h, :], True, False, ['KnT', 'QnT'], [f'pm{s}'])
                    mm(o, KrT[:, kc], QrT[:, h, :], False, True, ['KrT', 'QrT'], [f'pm{s}'])
                act(PT[:, s, 0:nkt, :], pm[s][:, 0:nkt * 128].rearrange("p (a t) -> p a t", a=nkt), AF.Exp,
                    [f'pm{s}'], [f'PT{s}'], scale=MLA_SCALE)
                if kt0 + nkt - 1 == i:
                    tt('dve', PT[:, s, nkt - 1, :], PT[:, s, nkt - 1, :], mask01[:, :], ALU.mult,
                       [f'PT{s}', 'mask01'], [f'PT{s}'])

            def emit_pv(gi):
                h, kt0, nkt = groups[gi]
                s = gi % 2
                bank = h // 4
                o = pw[bank][:, (h % 4) * 65:(h % 4) * 65 + 65]
                for jj in range(nkt):
                    kt = kt0 + jj
                    mm(o, PT[:, s, jj, :], Vext[:, kt, h, :], kt == 0, kt == i, [f'PT{s}', 'Vext', 'Vext1'],
                       [f'pw{bank}'])

            emit_scores(0)
            for gi in range(len(groups)):
                if gi + 1 < len(groups):
                    emit_scores(gi + 1)
                emit_pv(gi)
            for bank in range(2):
                ov = pw[bank][:, 0:260].rearrange("p (h e) -> p h e", h=4)
                S.op('dve', lambda e, ov=ov, bank=bank: e.reciprocal(rl[:, bank * 4:bank * 4 + 4], ov[:, :, 64]),
                     [f'pw{bank}'], ['rl'])
                tt('dve', o_tok[:, bank * 4:bank * 4 + 4, :], ov[:, :, 0:64], bc_last(rl[:, bank * 4:bank * 4 + 4], 64),
                   ALU.mult, [f'pw{bank}', 'rl'], ['o_tok'])
            for c in range(4):
                tr(ptr[0][:, c * 128:(c + 1) * 128], o_tok[:, 2 * c:2 * c + 2, :], ident[:, :], ['o_tok', 'ident'],
                   ['ptr0'])
            cp('dve', omT[:, :, tcol], ptr[0][:, 0:512].rearrange("p (c t) -> p c t", c=4), ['ptr0'], ['omT'])
            chk(5, i == 0)
            chk(6, i == ST - 1)
        elif not _SKIP_PA:
            for pr in range(4):
                tr(ptr[0][:, pr * 128:(pr + 1) * 128], wkn_sb[:, pr * 128:(pr + 1) * 128], ident[:, :],
                   ['wkn', 'ident'], ['ptr0'])
            cp('dve', WkT[:, :, :], ptr[0][:, 0:512].rearrange("p (c t) -> p c t", c=4), ['ptr0'], ['WkT'])
            ts('dve', qgT[:, :, :], QnT[:, :, :], gknc[:, 0:1], None, ALU.mult, None, ['QnT', 'gknc'], ['qgT'])
            for h in range(8):
                pr, b0 = h // 2, (h % 2) * 64
                bank = h // 4
                mm(pw[bank][:, (h % 4) * 128:(h % 4 + 1) * 128], WkT[:, pr, :], qgT[:, h, :],
                   True, True, ['WkT', 'qgT'], [f'pw{bank}'])
            for bank in range(2):
                cp('dve', qabsT[:, bank * 4:bank * 4 + 4, :],
                   pw[bank][:, 0:512].rearrange("p (h t) -> p h t", h=4), [f'pw{bank}'], ['qabsT'])
            cp('dve', rstk[:, 0:8], rst[:, 8:16], ['rst'], ['rstk_new'])
            chk(7.1)

            def gather(b, g):
                s = g % 2
                for jj in range(4):
                    col = b * NPG + g * 4 + jj
                    S.op('pool', lambda e, s=s, jj=jj, col=col: e.indirect_dma_start(
                        out=Cbuf[:, s, jj, :], out_offset=None, in_=cc_d,
                        in_offset=bass.IndirectOffsetOnAxis(ap=IDX[:, col:col + 1], axis=0)),
                        ['IDX'], [f'Cbuf{s}'], dma=True)

            def score_pv(b, lhs_c, lhs_r, rstd_ap, nj, rhs_pv, keys_r, first, last, maskap=None):
                qa = qabsT[:, :, b * 32 + 2:b * 32 + 10]
                qr = QrT[:, :, b * 32 + 2:b * 32 + 10]
                for jj in range(nj):
                    mm(pm[0][:, jj * 64:(jj + 1) * 64], lhs_c(jj), qa, True, True, keys_r + ['qabsT'], ['pm0'])
                    mm(pm[0][:, 256 + jj * 64:256 + (jj + 1) * 64], lhs_r(jj), qr, True, True, keys_r + ['QrT'],
                       ['pm0'])
                n = nj * 64
                tt('dve', sc1[:, 0:n].rearrange("p (a t) -> p a t", t=8),
                   pm[0][:, 0:n].rearrange("p (a t) -> p a t", t=8), bc_last(rstd_ap, 8), ALU.mult,
                   ['pm0', 'rstk', 'rstk_new'] + [k for k in keys_r if k.startswith('rk')], ['sc1'])
                tt('dve', sc2[:, 0:n], sc1[:, 0:n], pm[0][:, 256:256 + n], ALU.add, ['sc1', 'pm0'], ['sc2'])
                return n

            def stageA(b, g):
                s = g % 2
                cp('act', Cb[:, s, :, 0:128], Cbuf[:, s, :, 0:128], [f'Cbuf{s}'], [f'Cb{s}'])
                cp('dve', KRb[:, s, :, :], Cbuf[:, s, :, 128:160], [f'Cbuf{s}'], [f'KRb{s}'])
                for jj in range(4):
                    tr(ptr[0][:, jj * 128:(jj + 1) * 128], Cb[:, s, jj, 0:128], ident[:, :],
                       [f'Cb{s}', 'ident'], ['ptr0'])
                    tr(ptr[0][0:32, 512 + jj * 128:512 + (jj + 1) * 128], KRb[:, s, jj, :], ident[:, :],
                       [f'KRb{s}', 'ident'], ['ptr0'])
                cp('dve', CT[:, s, :, :], ptr[0][:, 0:512].rearrange("p (a t) -> p a t", a=4), ['ptr0'],
                   [f'CT{s}'])
                cp('dve', KRT[0:32, s, :, :], ptr[0][0:32, 512:1024].rearrange("p (a t) -> p a t", a=4), ['ptr0'],
                   [f'KRT{s}'])
                for hf in range(2):
                    for q2 in range(2):
                        jj = hf * 2 + q2
                        mm(pw[hf][:, q2 * 512:(q2 + 1) * 512], CT[:, s, jj, :], wkn_sb[:, :], True, True,
                           [f'CT{s}', 'wkn'], [f'pw{hf}'])
                    act(sqs[:, hf, :], pw[hf][:, :], AF.Square, [f'pw{hf}'], [f'sqs{hf}'])
                    S.op('dve', lambda e, hf=hf: e.tensor_reduce(
                        ssum[:, hf * 16:(hf + 1) * 16], sqs[:, hf, :].rearrange("p (a d) -> p a d", d=64),
                        AX.X, ALU.add), [f'sqs{hf}'], ['ssum'])
                rstd_chain(ssum[:, 0:32], rstk2[:, s, :], 1.0 / 64, ['ssum'], [f'rk{s}'])

            def stageB(b, g):
                s = g % 2
                score_pv(b, lambda jj: CT[:, s, jj, :], lambda jj: KRT[:, s, jj, :], rstk2[:, s, :], 4,
                         None, [f'CT{s}', f'KRT{s}', f'rk{s}'], None, None)
                act(PTs[:, s, :, :], sc2[:, 0:256].rearrange("p (a t) -> p a t", a=4), AF.Exp, ['sc2'],
                    [f'PTs{s}'], scale=MLA_SCALE)
                for jj in range(4):
                    mm(pm[1][0:64, 0:129], PTs[:, s, jj, :], Cb[:, s, jj, 0:129], g == 0 and jj == 0, False,
                       [f'PTs{s}', f'Cb{s}', 'Cb1'], ['pm1'])

            NG = NPG // 4
            for b in range(4):
                gather(b, 0)
                gather(b, 1)
                stageA(b, 0)
                for g in range(NG):
                    if g + 2 < NG:
                        gather(b, g + 2)
                    if g + 1 < NG:
                        stageA(b, g + 1)
                    stageB(b, g)
                cp('dve', rstk[:, 0:8], rst[:, 8:16], ['rst'], ['rstk'])
                score_pv(b, lambda jj: ckvT[:, :], lambda jj: KrT[:, tcol], rstk[:, 0:8], 1, None,
                         ['ckvT', 'KrT'], None, None)
                act(PTs[:, 0, 0, :], sc2[:, 0:64], AF.Exp, ['sc2'], ['PTs0'], scale=MLA_SCALE)
                tt('dve', PTs[:, 0, 0, :], PTs[:, 0, 0, :], maska[:, b, :], ALU.mult, ['PTs0', 'maska'], ['PTs0'])
                mm(pm[1][0:64, 0:129], PTs[:, 0, 0, :], ckv_b[:, 0:129], False, True, ['PTs0', 'ckv_b', 'ckv_b1'],
                   ['pm1'])
                cp('dve', acc_sb[:, 0:129], pm[1][0:64, 0:129], ['pm1'], ['acc_sb'])
                S.op('dve', lambda e: e.reciprocal(acc_sb[:, 130:131], acc_sb[:, 128:129]), ['acc_sb'], ['acc_sb'])
                ts('dve', accn[0:64, :], acc_sb[:, 0:128], acc_sb[:, 130:131], None, ALU.mult, None, ['acc_sb'], ['accn'])
                tr(ptr[1][:, 0:128], accn[:, :], ident[:, :], ['accn', 'ident'], ['ptr1'])
                cp('dve', accnT[:, :], ptr[1][:, 0:128], ['ptr1'], ['accnT'])
                for c in range(4):
                    mm(pm[0][:, c * 16:(c + 1) * 16], wvp_sb[:, c, :], accnT[:, c * 16:(c + 1) * 16], True, True,
                       ['wvp', 'accnT'], ['pm0'])
                ocol = slice(ST * 128 + b * 32 + 2, ST * 128 + b * 32 + 10)
                pv4 = pm[0][:, 0:64].rearrange("p (c t) -> p c t", c=4)
                cp('dve', omT[0:64, :, ocol], pv4[0:64, :, 0:8], ['pm0'], ['omT'])
                cp('dve', omT[64:128, :, ocol], pv4[64:128, :, 8:16], ['pm0'], ['omT'])
                chk(7.6, b == 0)

    if os.environ.get("KDBG") == "7":
        for tt_ in range(2):
            cp('dve', x_sb[:, 0, 0:512].rearrange("p (c t) -> p c t", c=4), ogT[:, :, tt_ * 128:(tt_ + 1) * 128],
               ['ogT'], ['x0'])
            cp('dve', x_sb[:, 0, 512:1024].rearrange("p (c t) -> p c t", c=4), omT[:, :, tt_ * 128:(tt_ + 1) * 128],
               ['omT'], ['x0'])
            dma('sp', y_o[4 + tt_], x_sb[:, 0, :], ['x0'], ['y_o'])
    chk(8)
    S.barrier()
    A.off = persist_off
    wd_sb = A.alloc("wd", [128, NF, D], BF16)
    wo_sb = A.alloc("wo", [128, 8, D], BF16)
    wpp_sb = A.alloc("wpp", [128, 2, D], BF16)
    wst = A.alloc("wst", [128, 2, 2, 8, 512], BF16)
    h_g = A.alloc("h_g", [128, 4, D], F32)
    nT2 = A.alloc("nT2", [128, 8, 512], BF16)
    hidT = A.alloc("hidT", [128, NF, 512], BF16)
    a_sb2 = A.alloc("a_sb", [128, 2, 520], F32)
    tbuf2 = A.alloc("tbuf", [128, 2, 512], F32)
    big = A.alloc("big", [128, D], F32)
    p_sb = A.alloc("p_sb", [128, 256], F32)
    p_bf = A.alloc("p_bf", [128, 256], BF16)
    pT = A.alloc("pT", [128, 2, 128], BF16)
    cw_sb = A.alloc("cw", [128, 3, NF], F32)
    cb_sb = A.alloc("cb", [128, NF], F32)
    halo = A.alloc("halo", [128, NF, 2], F32)
    cst = A.alloc("cst", [128, NF, 8], F32)
    sc_bf = A.alloc("sc_bf", [128, DFF], BF16)
    sel_sb = A.alloc("sel", [128, 128], BF16)
    ctok = A.alloc("ctok", [8, 512], F32)

    for f in range(NF):
        dma('sp', wd_sb[:, f, :], wd_d[f * 128:(f + 1) * 128, :], (), ['wd'])
    for k in range(8):
        dma('sp', wo_sb[:, k, :], wo_d[k * 128:(k + 1) * 128, :], (), ['wo'])
    dma('sp', wpp_sb[:, :, :], wpp_d.rearrange("(c p) n -> p c n", p=128), (), ['wpp'])
    dma('sp', cw_sb[:, :, :], cw_d.rearrange("j (f p) -> p j f", p=128), (), ['cw'], allow_slow_non_contiguous=True)
    dma('sp', cb_sb[:, :], cb_d.rearrange("(f p) -> p f", p=128), (), ['cb'], allow_slow_non_contiguous=True)
    memset('dve', sc_bf[:, :], 0.0, ['sc_bf'])
    dma('sp', sc_bf[0:8, :], sconv_d, ['sc_bf'], ['sc_bf'])
    dma('sp', sel_sb[:, :], c_sel_d, (), ['sel'])
    memset('dve', halo[:, :, :], 0.0, ['halo'])
    memset('dve', a_sb2[:, :, :], 0.0, ['a_sb0', 'a_sb1'])

    blocks = [(0, 512), (512, 512), (1024, 512), (1536, 512), (2048, 512), (2560, 256)]
    wst_cnt = [0]

    def load_ffn_block(bi):
        c0, w = blocks[bi]
        s = wst_cnt[0] % 2
        wst_cnt[0] += 1
        for gu, wdram in ((0, wg_d), (1, wu_d)):
            for k in range(8):
                dma('sp', wst[:, s, gu, k, 0:w], wdram[k * 128:(k + 1) * 128, c0:c0 + w], (), [f'wst{s}'])
        return s

    def load_ple_gate():
        s = wst_cnt[0] % 2
        wst_cnt[0] += 1
        for half in range(2):
            for k in range(8):
                dma('sp', wst[:, s, half, k, :], wpg_d[k * 128:(k + 1) * 128, half * 512:(half + 1) * 512], (),
                    [f'wst{s}'])
        return s

    tgroups = [[0, 1, 2, 3], [4, 5, 6, 7], [8, 9, 10, 11], [12, 13, 14, 15], [ST]]
    for gi, tiles in enumerate(tgroups):
        samp = (tiles[0] == ST)
        N = 128 * len(tiles)
        slot0 = load_ffn_block(0)
        for ti, t in enumerate(tiles):
            tcol = slice(t * 128, (t + 1) * 128)
            dma('sp', big[:, :], x_d[t], (), ['big'])
            for half in range(2):
                for c in range(8):
                    src = ogT if c < 4 else omT
                    mm(pw[0][:, half * 512:(half + 1) * 512], src[:, c % 4, tcol],
                       wo_sb[:, c, half * 512:(half + 1) * 512], c == 0, c == 7, ['ogT', 'omT', 'wo'], ['pw0'])
            tt('dve', h_g[:, ti, :], big[:, :], pw[0][:, :], ALU.add, ['big', 'pw0'], [f'h{ti}'])
            norm_tile(h_g[:, ti, :], [f'h{ti}'], gffnT, 'gffnT', nT2[:, :, ti * 128:(ti + 1) * 128], ['nT2'], 0)
        chk(9, gi == 0)
        for bi, (c0, w) in enumerate(blocks):
            s = slot0 if bi == 0 else s_next
            if bi + 1 < len(blocks):
                s_next = load_ffn_block(bi + 1)
            else:
                s_pg = load_ple_gate()
            for fi in range(w // 128):
                f = c0 // 128 + fi
                fc = slice(fi * 128, (fi + 1) * 128)
                par = f % 2
                pg, pgk = (pm[0], 'pm0') if par == 0 else (pw[0], 'pw0')
                pu, puk = (pm[1], 'pm1') if par == 0 else (pw[1], 'pw1')
                a_sb = a_sb2[:, par, :]
                tbuf = tbuf2[:, par, :]
                ak, tk = f'a_sb{par}', f'tbuf{par}'
                for k in range(8):
                    mm(pg[:, 0:N], wst[:, s, 0, k, fc], nT2[:, k, 0:N], k == 0, (k == 7 and not samp),
                       [f'wst{s}', 'nT2'], [pgk])
                if samp:
                    mm(pg[:, 0:N], sc_bf[:, f * 128:(f + 1) * 128], sel_sb[:, :], False, True,
                       ['sc_bf', 'sel'], [pgk])
                for k in range(8):
                    mm(pu[:, 0:N], wst[:, s, 1, k, fc], nT2[:, k, 0:N], k == 0, k == 7, [f'wst{s}', 'nT2'], [puk])
                if not samp:
                    cp('dve', a_sb[:, 0:2], halo[:, f, :], ['halo'], [ak])
                cp('act', a_sb[:, 2:2 + N], pg[:, 0:N], [pgk], [ak])
                if not samp:
                    cp('dve', halo[:, f, :], a_sb[:, N:N + 2], [ak], ['halo'])
                else:
                    cp('dve', cst[:, f, :].rearrange("p (b j) -> p b j", b=4),
                       a_sb[:, 2:2 + 128].rearrange("p (b c) -> p b c", b=4)[:, :, 8:10], [ak], ['cst'])
                ts('dve', tbuf[:, 0:N], a_sb[:, 0:N], cw_sb[:, 0, f:f + 1], cb_sb[:, f:f + 1], ALU.mult, ALU.add,
                   [ak, 'cw', 'cb'], [tk])
                stt(tbuf[:, 0:N], a_sb[:, 1:1 + N], cw_sb[:, 1, f:f + 1], tbuf[:, 0:N], ALU.mult, ALU.add,
                    [ak, 'cw', tk], [tk])
                stt(tbuf[:, 0:N], a_sb[:, 2:2 + N], cw_sb[:, 2, f:f + 1], tbuf[:, 0:N], ALU.mult, ALU.add,
                    [ak, 'cw', tk], [tk])
                act(tbuf[:, 0:N], tbuf[:, 0:N], AF.Silu, [tk], [tk])
                tt('dve', hidT[:, f, 0:N], tbuf[:, 0:N], pu[:, 0:N], ALU.mult, [tk, puk], ['hidT'])
        if gi == 3 or samp:
            W = 8 if samp else 2
            stg = cst if samp else halo
            skey = 'cst' if samp else 'halo'
            for q6 in range(6):
                nf = min(4, NF - q6 * 4)
                for ff in range(nf):
                    f = q6 * 4 + ff
                    S.op('pe', lambda e, f=f, ff=ff, W=W, stg=stg: e.transpose(pm[0][0:W, ff * 128:(ff + 1) * 128],
                                                                 stg[:, f, 0:W], identf[:, :]),
                         [skey, 'identf'], ['pm0'])
                cp('dve', ctok[0:W, 0:nf * 128], pm[0][0:W, 0:nf * 128], ['pm0'], ['ctok'])
                dma('sp', (convs_o if samp else convp_o)[:, q6 * 512:q6 * 512 + nf * 128], ctok[0:W, 0:nf * 128],
                    ['ctok'], ['conv_o'])
        chk(10, gi == 0)
        for ti, t in enumerate(tiles):
            tc2 = slice(ti * 128, (ti + 1) * 128)
            for half in range(2):
                for f in range(NF):
                    mm(pw[0][:, half * 512:(half + 1) * 512], hidT[:, f, tc2], wd_sb[:, f, half * 512:(half + 1) * 512],
                       f == 0, f == NF - 1, ['hidT', 'wd'], ['pw0'])
            tt('dve', h_g[:, ti, :], h_g[:, ti, :], pw[0][:, :], ALU.add, [f'h{ti}', 'pw0'], [f'h{ti}'])
        for ti, t in enumerate(tiles):
            tc2 = slice(ti * 128, (ti + 1) * 128)
            norm_tile(h_g[:, ti, :], [f'h{ti}'], gpleT, 'gpleT', nT2[:, :, tc2], ['nT2'], 0)
            dma('sp', p_sb[:, :], p_d[t], (), ['p_sb'])
            cp('dve', p_bf[:, :], p_sb[:, :], ['p_sb'], ['p_bf'])
            for c in range(2):
                tr(ptr[1][:, c * 128:(c + 1) * 128], p_bf[:, c * 128:(c + 1) * 128], ident[:, :], ['p_bf', 'ident'],
                   ['ptr1'])
            cp('dve', pT[:, :, :], ptr[1][:, 0:256].rearrange("p (c t) -> p c t", c=2), ['ptr1'], ['pT'])
            for half in range(2):
                for k in range(8):
                    mm(pw[0][:, half * 512:(half + 1) * 512], nT2[:, k, tc2], wst[:, s_pg, half, k, :],
                       k == 0, k == 7, ['nT2', f'wst{s_pg}'], ['pw0'])
                for c in range(2):
                    mm(pw[1][:, half * 512:(half + 1) * 512], pT[:, c, :], wpp_sb[:, c, half * 512:(half + 1) * 512],
                       c == 0, c == 1, ['pT', 'wpp'], ['pw1'])
            act(big[:, :], pw[0][:, :], AF.Sigmoid, ['pw0'], ['big'])
            tt('dve', big[:, :], big[:, :], pw[1][:, :], ALU.mult, ['big', 'pw1'], ['big'])
            tt('dve', big[:, :], big[:, :], h_g[:, ti, :], ALU.add, ['big', f'h{ti}'], ['big'])
            dma('sp', y_o[t], big[:, :], ['big'], ['y_o'])
        chk(11, gi == 0)
        chk(12, gi == 3)

    return S


_NC_CACHE = {}


def _consts():
    bf = ml_dtypes.bfloat16
    c = {}
    c['c_ident'] = np.eye(128, dtype=np.float32).astype(bf)
    c['c_identf'] = np.eye(128, dtype=np.float32)
    s = np.arange(128)
    c['c_mask'] = (s[:, None] <= s[None, :]).astype(np.float32).astype(bf)
    grp = s // 32
    m = (grp[:, None] == grp[None, :]) & (s[:, None] <= s[None, :])
    c['c_masks'] = m.astype(np.float32).astype(bf)
    ma = np.zeros((128, 4, 8, 8), np.float32)
    for b in range(4):
        for r in range(8):
            for t in range(8):
                if r <= t:
                    ma[b * 32 + 2 + r, b, :, t] = 1.0
    c['c_maska'] = ma.reshape(128, 4, 64).astype(bf)
    inv = 10000.0 ** (-np.arange(16, dtype=np.float32) * 2.0 / 32)
    pos = np.zeros((128, NT), np.float64)
    for i in range(16):
        pos[:, i] = i * 128 + s
    for p in range(128):
        t = (p % 32) - 2
        pos[p, ST] = 16384 + (t if 0 <= t < 8 else 0)
    ang = (pos.astype(np.float32)[:, :, None] * inv[None, None, :]).astype(np.float32)
    c['c_cos'] = np.cos(ang.astype(np.float64)).astype(np.float32)
    c['c_sin'] = np.sin(ang.astype(np.float64)).astype(np.float32)
    rs = np.ones((128, 2, 128), np.float32)
    rs[:, 0, 0] = 0.0
    for b in range(4):
        rs[:, 1, b * 32 + 2] = 0.0
    c['c_rs'] = rs
    c['c_iota'] = s.astype(np.float32).reshape(128, 1)
    rowm = np.zeros((128, 4), np.float32)
    for b in range(4):
        rowm[b * 32:(b + 1) * 32, b] = 1.0
    c['c_rowm'] = rowm
    hm = np.zeros((128, 2), np.float32)
    hm[:64, 0] = 1.0
    hm[64:, 1] = 1.0
    c['c_hm'] = hm
    sel = np.zeros((128, 128), np.float32)
    for b in range(4):
        for j in range(2):
            sel[b * 2 + j, b * 32 + j] = 1.0
    c['c_sel'] = sel.astype(bf)
    return c


def kernel(x_prompt, x_sample, cache_ckv, cache_krope, state_gla, state_conv, page_table,
           p_prompt, p_sample, g_mix, w_in, gla_w_a2, gla_b_a, gla_g_out, mla_g_qa, mla_w_qup,
           mla_g_qn, mla_g_qr, mla_g_kva, mla_g_kr, mla_w_kvup, mla_g_kn, w_o, g_ffn,
           ffn_w_gate, ffn_w_up, ffn_conv_w, ffn_conv_b, ffn_w_down, g_ple, ple_w_gate, ple_w_proj):
    f32 = np.float32
    A_ = lambda a: np.ascontiguousarray(np.asarray(a))
    if 'nc' not in _NC_CACHE:
        _NC_CACHE['nc'] = build_program()
    nc = _NC_CACHE['nc']
    consts = _consts()
    shared = {
        'cache_cat': np.ascontiguousarray(np.concatenate(
            [A_(cache_ckv).reshape(5120 * 128, 128)[:_NPHYS * 128],
             A_(cache_krope).reshape(5120 * 128, 32)[:_NPHYS * 128]], axis=1)),
        'w_in': A_(w_in[0]), 'gla_w_a2': A_(gla_w_a2[0]), 'gla_b_a': A_(gla_b_a[0]), 'gla_g_out': A_(gla_g_out[0]),
        'mla_g_qa': A_(mla_g_qa[0]), 'mla_w_qup': A_(mla_w_qup[0]), 'mla_g_qn': A_(mla_g_qn[0]),
        'mla_g_qr': A_(mla_g_qr[0]), 'mla_g_kva': A_(mla_g_kva[0]), 'mla_g_kr': A_(mla_g_kr[0]),
        'mla_w_kvup': A_(mla_w_kvup[0]), 'mla_g_kn': A_(mla_g_kn[0]), 'w_o': A_(w_o[0]), 'g_mix': A_(g_mix[0]),
        'g_ffn': A_(g_ffn[0]), 'g_ple': A_(g_ple[0]), 'ffn_w_gate': A_(ffn_w_gate[0]), 'ffn_w_up': A_(ffn_w_up[0]),
        'ffn_conv_w': A_(ffn_conv_w[0]), 'ffn_conv_b': A_(ffn_conv_b[0]), 'ffn_w_down': A_(ffn_w_down[0]),
        'ple_w_gate': A_(ple_w_gate[0]), 'ple_w_proj': A_(ple_w_proj[0]),
    }
    def _bc(a):
        a = A_(a)
        return np.ascontiguousarray(np.broadcast_to(a[None, :], (128, a.shape[0])))
    shared['b_gqn'] = _bc(mla_g_qn[0])
    shared['b_gqr'] = _bc(mla_g_qr[0])
    shared['b_gkva'] = _bc(mla_g_kva[0])
    shared['b_gkr'] = _bc(mla_g_kr[0])
    shared['b_gkn'] = _bc(mla_g_kn[0])
    shared.update(consts)
    xp = A_(x_prompt)
    xsm = A_(x_sample)
    pp = A_(p_prompt)[0]
    psm = A_(p_sample)[0]
    in_maps = []
    rows = np.array([b * 32 + 2 + t for b in range(4) for t in range(8)])
    for c in range(8):
        x = np.zeros((NT, 128, D), f32)
        x[:16] = xp[c].reshape(16, 128, D)
        x[ST][rows] = xsm[4 * c:4 * c + 4].reshape(32, D)
        p = np.zeros((NT, 128, 256), f32)
        p[:16] = pp[c].reshape(16, 128, 256)
        p[ST][rows] = psm[4 * c:4 * c + 4].reshape(32, 256)
        m = dict(shared)
        m['x'] = x
        m['p'] = p
        m['state_gla'] = A_(state_gla[0][4 * c:4 * c + 4]).reshape(4, 2, 128, 128)
        m['state_conv'] = A_(state_conv[0][4 * c:4 * c + 4]).reshape(8, DFF)
        m['page_table'] = A_(page_table[4 * c:4 * c + 4]).reshape(512).astype(np.int32)
        if _SMALLPA:
            m['page_table'] = m['page_table'] % 8
        m['b_pt'] = _bc(m['page_table'])
        in_maps.append(m)
    res = run_bass_kernel_spmd(nc, in_maps, core_ids=list(range(8))).results

    y_p = np.zeros((8, 2048, D), f32)
    y_s = np.zeros((32, 8, D), f32)
    ckv_p = np.zeros((1, 8, 2048, 128), f32)
    kr_p = np.zeros((1, 8, 2048, 32), f32)
    gla_p = np.zeros((1, 8, 4, 64, 128), f32)
    conv_p = np.zeros((1, 8, 2, DFF), f32)
    ckv_s = np.zeros((1, 32, 8, 128), f32)
    kr_s = np.zeros((1, 32, 8, 32), f32)
    gla_s = np.zeros((1, 32, 4, 64, 128), f32)
    conv_s = np.zeros((1, 32, 2, DFF), f32)
    for c in range(8):
        r = res[c]
        y = np.asarray(r['y'])
        y_p[c] = y[:16].reshape(2048, D)
        y_s[4 * c:4 * c + 4] = y[ST][rows].reshape(4, 8, D)
        ck = np.asarray(r['ckv_o'])
        ckv_p[0, c] = ck[:16].reshape(2048, 128)
        ckv_s[0, 4 * c:4 * c + 4] = ck[ST][rows].reshape(4, 8, 128)
        kr = np.asarray(r['kr_o'])
        kr_p[0, c] = kr[:16].reshape(2048, 32)
        kr_s[0, 4 * c:4 * c + 4] = kr[ST][rows].reshape(4, 8, 32)
        gla_p[0, c] = np.asarray(r['gla_p']).reshape(4, 64, 128)
        gla_s[0, 4 * c:4 * c + 4] = np.asarray(r['gla_s']).reshape(4, 4, 64, 128)
        conv_p[0, c] = np.asarray(r['conv_p'])
        conv_s[0, 4 * c:4 * c + 4] = np.asarray(r['conv_s']).reshape(4, 2, DFF)
    return (y_p, y_s, ckv_p, kr_p, gla_p, conv_p, ckv_s, kr_s, gla_s, conv_s)
```

```python
import numpy as np
import ml_dtypes
import concourse.bass as bass
import concourse.mybir as mybir
from concourse.bass_utils import run_bass_kernel_spmd

F32 = mybir.dt.float32
BF16 = mybir.dt.bfloat16
I32 = mybir.dt.int32
AF = mybir.ActivationFunctionType
ALU = mybir.AluOpType
AX = mybir.AxisListType

D = 1024
NT = 17
ST = 16
DFF = 2816
NF = 22
INC = 1968
EPS = 1e-6
MLA_SCALE = 96.0 ** -0.5
NPG = 128
O_Q, O_K, O_V, O_G, O_A, O_MQ = 0, 256, 512, 1024, 1536, 1552


class Sched:
    def __init__(self, sems):
        self.sems = sems
        nxt = iter(range(len(sems)))
        self.q = {e: [] for e in ('pe', 'act', 'dve', 'pool', 'sp')}
        self.cnt = {e: 0 for e in ('pe', 'act', 'dve', 'pool')}
        self.csem = {e: next(nxt) for e in ('pe', 'act', 'dve', 'pool')}
        self.dsems = {'sp': [next(nxt) for _ in range(24)], 'pool': [next(nxt) for _ in range(2)],
                      'act': [next(nxt) for _ in range(8)]}
        self.dcnt = {'sp': 0, 'pool': 0, 'act': 0}
        self.lastw = {}
        self.readers = {}
        self.waited = {e: {} for e in self.q}
        self.all_tokens = {}
        self._bar = None

    def op(self, eng, fn, reads=(), writes=(), dma=False):
        deps = {}

        def add(tok):
            if tok is None:
                return
            for s, v in tok.items() if isinstance(tok, dict) else [tok]:
                if deps.get(s, 0) < v:
                    deps[s] = v
        if self._bar:
            add(self._bar)
        for k in reads:
            add(self.lastw.get(k))
        for k in writes:
            add(self.lastw.get(k))
            add(self.readers.get(k))
        if dma:
            pool = self.dsems[eng]
            i = self.dcnt[eng]
            self.dcnt[eng] += 1
            sem = pool[i % len(pool)]
            val = 16 * (i // len(pool) + 1)
            add((sem, val - 16))
            inc = 16
        else:
            self.cnt[eng] += 1
            sem = self.csem[eng]
            val = self.cnt[eng]
            inc = 1
        waits = []
        w = self.waited[eng]
        for s, v in deps.items():
            if v <= 0:
                continue
            if eng == 'pe' and s == self.csem['pe']:
                continue
            if w.get(s, 0) >= v:
                continue
            w[s] = v
            waits.append((s, v))
        self.q[eng].append((waits, fn, sem, inc))
        tok = (sem, val)
        self.all_tokens[sem] = max(self.all_tokens.get(sem, 0), val)
        for k in reads:
            r = self.readers.setdefault(k, {})
            if r.get(sem, 0) < val:
                r[sem] = val
        for k in writes:
            self.lastw[k] = tok
            self.readers[k] = {}
        return tok

    def barrier(self):
        snap = dict(self.all_tokens)
        for k in list(self.lastw.keys()):
            self.lastw[k] = snap
            self.readers[k] = {}
        self._bar = snap

    def emit(self, engname, engine):
        for waits, fn, sem, inc in self.q[engname]:
            for s, v in waits:
                engine.wait_ge(self.sems[s], v)
            ins = fn(engine)
            ins.then_inc(self.sems[sem], inc)

    def final_wait(self, engine):
        for s, v in self.all_tokens.items():
            engine.wait_ge(self.sems[s], v)


class _Stop(Exception):
    pass


import os
_STOP = float(os.environ.get("KSTOP", "0")) or None
_SKIP_PA = bool(os.environ.get("KSKIP"))
_SMALLPA = bool(os.environ.get("KSMALLPA"))
_NPHYS = 8 if (_SMALLPA or _SKIP_PA or (_STOP is not None and _STOP <= 7)) else 5120


_HOOK = [None]


def chk(stage, active=True):
    if not active:
        return
    if _STOP is not None and stage >= _STOP:
        if _HOOK[0] is not None and os.environ.get("KDBG") == "5":
            _HOOK[0]()
        raise _Stop()


def build_program():
    nc = bass.Bass("TRN2", target_bir_lowering=False)
    S = None
    try:
        S = _build_body(nc)
    except _Stop as e:
        S = e.args[0] if e.args else S
    _emit(nc, _S_HOLD[0])
    return nc


_S_HOLD = [None]


def _emit(nc, S):
    with nc.Block() as block:
        @block.sync
        def _(e):
            S.emit('sp', e)
            S.final_wait(e)

        @block.tensor
        def _(e):
            S.emit('pe', e)

        @block.scalar
        def _(e):
            S.emit('act', e)

        @block.vector
        def _(e):
            S.emit('dve', e)

        @block.gpsimd
        def _(e):
            S.emit('pool', e)


def _build_body(nc):

    def din(name, shape, dt=F32):
        return nc.dram_tensor(name, list(shape), dt, kind="ExternalInput").ap()

    def dout(name, shape, dt=F32):
        return nc.dram_tensor(name, list(shape), dt, kind="ExternalOutput").ap()

    x_d = din("x", [NT, 128, D])
    p_d = din("p", [NT, 128, 256])
    cc_d = din("cache_cat", [_NPHYS * 128, 160])
    sgla_d = din("state_gla", [4, 2, 128, 128])
    sconv_d = din("state_conv", [8, DFF])
    pt_d = din("page_table", [512], I32)
    w_in_d = din("w_in", [D, INC])
    wa2_d = din("gla_w_a2", [16, 256])
    ba_d = din("gla_b_a", [256])
    gout_d = din("gla_g_out", [128])
    gqa_d = din("mla_g_qa", [256])
    wqup_d = din("mla_w_qup", [256, 768])
    gqn_d = din("mla_g_qn", [64])
    gqr_d = din("mla_g_qr", [32])
    gkva_d = din("mla_g_kva", [128])
    gkr_d = din("mla_g_kr", [32])
    wkv_d = din("mla_w_kvup", [128, 1024])
    gkn_d = din("mla_g_kn", [64])
    wo_d = din("w_o", [D, D])
    gmix_d = din("g_mix", [D])
    gffn_d = din("g_ffn", [D])
    gple_d = din("g_ple", [D])
    wg_d = din("ffn_w_gate", [D, DFF])
    wu_d = din("ffn_w_up", [D, DFF])
    cw_d = din("ffn_conv_w", [3, DFF])
    cb_d = din("ffn_conv_b", [DFF])
    wd_d = din("ffn_w_down", [DFF, D])
    wpg_d = din("ple_w_gate", [D, D])
    wpp_d = din("ple_w_proj", [256, D])
    gqnb_d = din("b_gqn", [128, 64])
    gqrb_d = din("b_gqr", [128, 32])
    gkvab_d = din("b_gkva", [128, 128])
    gkrb_d = din("b_gkr", [128, 32])
    gknb_d = din("b_gkn", [128, 64])
    ptb_d = din("b_pt", [128, 512], I32)
    c_ident_d = din("c_ident", [128, 128], BF16)
    c_identf_d = din("c_identf", [128, 128])
    c_mask_d = din("c_mask", [128, 128], BF16)
    c_masks_d = din("c_masks", [128, 128], BF16)
    c_maska_d = din("c_maska", [128, 4, 64], BF16)
    c_cos_d = din("c_cos", [128, NT, 16])
    c_sin_d = din("c_sin", [128, NT, 16])
    c_rs_d = din("c_rs", [128, 2, 128])
    c_iota_d = din("c_iota", [128, 1])
    c_sel_d = din("c_sel", [128, 128], BF16)
    c_rowm_d = din("c_rowm", [128, 4])
    c_hm_d = din("c_hm", [128, 2])

    y_o = dout("y", [NT, 128, D])
    ckv_o = dout("ckv_o", [NT, 128, 128])
    kr_o = dout("kr_o", [NT, 128, 32])
    glap_o = dout("gla_p", [2, 128, 128])
    glas_o = dout("gla_s", [4, 2, 128, 128])
    convp_o = dout("conv_p", [2, DFF])
    convs_o = dout("conv_s", [8, DFF])

    sems = [nc.alloc_semaphore(f"s{i}") for i in range(60)]
    S = Sched(sems)
    _S_HOLD[0] = S

    class Arena:
        def __init__(self):
            self.base = 16512
            self.top = 229344
            self.off = self.base
            self.n = 0

        def alloc(self, name, shape, dt):
            sz = 1
            for s in shape[1:]:
                sz *= s
            sz *= mybir.dt.size(dt)
            sz = (sz + 31) // 32 * 32
            assert self.off + sz <= self.top, f"SBUF overflow at {name}: {self.off + sz}"
            self.n += 1
            t = nc.alloc_sbuf_tensor_at(f"{name}_{self.n}", list(shape), dt, offset=self.off)
            self.off += sz
            return t

    A = Arena()

    ptr = [nc.alloc_psum_tensor(f"ptr{i}", [128, 1024], BF16) for i in range(2)]
    pm = [nc.alloc_psum_tensor(f"pm{i}", [128, 512], F32) for i in range(2)]
    pw = [nc.alloc_psum_tensor(f"pw{i}", [128, 1024], F32) for i in range(2)]

    def dma(eng, out, in_, reads=(), writes=(), **kw):
        if out.dtype != in_.dtype:
            eng = 'pool'
        return S.op(eng, lambda e: e.dma_start(out=out, in_=in_, **kw), reads, writes, dma=True)

    def mm(out, lhsT, rhs, start, stop, reads, writes, **kw):
        return S.op('pe', lambda e: e.matmul(out, lhsT, rhs, start=start, stop=stop, **kw), reads, writes)

    def tr(out, in_, ident, reads, writes):
        return S.op('pe', lambda e: e.transpose(out, in_, ident), reads, writes)

    def act(out, in_, func, reads, writes, **kw):
        return S.op('act', lambda e: e.activation(out, in_, func, **kw), reads, writes)

    def tt(eng, out, in0, in1, op, reads, writes):
        return S.op(eng, lambda e: e.tensor_tensor(out, in0, in1, op), reads, writes)

    def ts(eng, out, in0, s1, s2, op0, op1, reads, writes):
        if s2 is None:
            return S.op(eng, lambda e: e.tensor_scalar(out, in0, s1, None, op0), reads, writes)
        return S.op(eng, lambda e: e.tensor_scalar(out, in0, s1, s2, op0, op1), reads, writes)

    def stt(out, in0, sc, in1, op0, op1, reads, writes):
        return S.op('dve', lambda e: e.scalar_tensor_tensor(out, in0, sc, in1, op0, op1), reads, writes)

    def cp(eng, out, in_, reads, writes):
        if eng == 'act':
            return S.op('act', lambda e: e.copy(out, in_), reads, writes)
        return S.op(eng, lambda e: e.tensor_copy(out, in_), reads, writes)

    def memset(eng, ap, val, writes):
        return S.op(eng, lambda e: e.memset(ap, val), (), writes)

    def bc_last(ap, n):
        sh = list(ap.shape)
        return ap.unsqueeze(len(sh)).broadcast_to(sh + [n])

    def bc_mid(ap, n):
        sh = list(ap.shape)
        return ap.unsqueeze(1).broadcast_to([sh[0], n] + sh[1:])

    ident = A.alloc("ident", [128, 128], BF16)
    identf = A.alloc("identf", [128, 128], F32)
    mask01 = A.alloc("mask01", [128, 128], BF16)
    masks = A.alloc("masks", [128, 128], BF16)
    maska = A.alloc("maska", [128, 4, 64], BF16)
    ones_bf = A.alloc("ones_bf", [128, 128], BF16)
    eps_c = A.alloc("eps_c", [128, 1], F32)
    gmixT = A.alloc("gmixT", [128, 8], F32)
    gffnT = A.alloc("gffnT", [128, 8], F32)
    gpleT = A.alloc("gpleT", [128, 8], F32)
    ogT = A.alloc("ogT", [128, 4, NT * 128], BF16)
    omT = A.alloc("omT", [128, 4, NT * 128], BF16)
    stat = A.alloc("stat", [128, 8], F32)
    xs = A.alloc("xs", [128, D], BF16)
    persist_off = A.off

    dma('sp', ident[:, :], c_ident_d, (), ['ident'])
    def _hook():
        S.barrier()
        dma('sp', y_o[14][:, 0:128], identf[:, :], ['identf'], ['y_o'])
    _HOOK[0] = _hook
    if os.environ.get("KDBG") == "5":
        dma('sp', identf[:, :], c_identf_d, (), ['identf'])
        dma('sp', y_o[15][:, 0:128], identf[:, :], ['identf'], ['y_o'])
    dma('sp', identf[:, :], c_identf_d, (), ['identf'])
    dma('sp', mask01[:, :], c_mask_d, (), ['mask01'])
    dma('sp', masks[:, :], c_masks_d, (), ['masks'])
    dma('sp', maska[:, :, :], c_maska_d, (), ['maska'])
    for gt, gd, k in ((gmixT, gmix_d, 'gmixT'), (gffnT, gffn_d, 'gffnT'), (gpleT, gple_d, 'gpleT')):
        dma('sp', gt[:, :], gd.rearrange("(c p) -> p c", p=128), (), [k], allow_slow_non_contiguous=True)
    memset('dve', ones_bf[:, :], 1.0, ['ones'])
    memset('dve', eps_c[:, :], EPS, ['eps'])

    def rstd_chain(ms_ap, out_ap, inv_n, rk, wk):
        act(out_ap, ms_ap, AF.Ln, list(rk) + ['eps'], wk, bias=eps_c[0:ms_ap.shape[0], 0:1], scale=inv_n)
        act(out_ap, out_ap, AF.Exp, wk, wk, scale=-0.5)

    def norm_tile(src, src_keys, gT, gkey, dst, dst_keys, pslot):
        act(xs[:, :], src, AF.Square, src_keys, ['xs', 'stat'], accum_out=stat[:, 0:1])
        rstd_chain(stat[:, 0:1], stat[:, 1:2], 1.0 / D, ['stat'], ['stat'])
        ts('dve', xs[:, :], src, stat[:, 1:2], None, ALU.mult, None, list(src_keys) + ['stat'], ['xs'])
        pk = f'ptr{pslot}'
        for k in range(8):
            tr(ptr[pslot][:, k * 128:(k + 1) * 128], xs[:, k * 128:(k + 1) * 128], ident[:, :],
               ['xs', 'ident'], [pk])
        tt('dve', dst, ptr[pslot][:, :].rearrange("p (c t) -> p c t", c=8), bc_last(gT[:, :], 128),
           ALU.mult, [pk, gkey], dst_keys)

    w_in_sb = A.alloc("w_in", [128, 8, INC], BF16)
    wqup_sb = A.alloc("wqup", [128, 2, 768], BF16)
    wkv_sb = A.alloc("wkv", [128, 1024], BF16)
    wkn_sb = A.alloc("wkn", [128, 512], BF16)
    wa2_sb = A.alloc("wa2", [16, 256], F32)
    nba = A.alloc("nba", [128, 2], F32)
    goutc = A.alloc("goutc", [128, 1], F32)
    gqa_c = A.alloc("gqa", [128, 2], F32)
    gknc = A.alloc("gknc", [128, 1], F32)
    Gqn = A.alloc("Gqn", [128, 64], F32)
    Gqr = A.alloc("Gqr", [128, 32], F32)
    Gkva = A.alloc("Gkva", [128, 128], F32)
    Gkr = A.alloc("Gkr", [128, 32], F32)
    Gkn = A.alloc("Gkn", [128, 64], F32)
    cos_sb = A.alloc("cos", [128, NT, 16], F32)
    sin_sb = A.alloc("sin", [128, NT, 16], F32)
    rs_sb = A.alloc("rs", [128, 2, 128], F32)
    iota_sb = A.alloc("iota", [128, 1], F32)
    x_sb = A.alloc("x_sb", [128, 2, D], F32)
    nT = A.alloc("nT", [128, 2, 8, 128], BF16)
    gaT = A.alloc("gaT", [16, 128], F32)
    sp_ = A.alloc("sp", [128, 2, 128], F32)
    bpos = A.alloc("bpos", [128, 2, 128], F32)
    etmp = A.alloc("etmp", [128, 2, 128], F32)
    nbl = A.alloc("nbl", [128, 8], F32)
    ebl = A.alloc("ebl", [128, 8], F32)
    qdT = A.alloc("qdT", [128, 4, 128], BF16)
    ke_z = A.alloc("ke_z", [128, 4, 256], BF16)
    rowm = A.alloc("rowm", [128, 4], F32)
    hm = A.alloc("hm", [128, 2], F32)
    wvp_sb = A.alloc("wvp", [128, 4, 128], BF16)
    kiT = A.alloc("kiT", [128, 2, 128], BF16)
    keT = A.alloc("keT", [128, 2, 128], BF16)
    ke_tok = A.alloc("ke_tok", [128, 256], BF16)
    v_tok = A.alloc("v_tok", [128, 512], BF16)
    sgT = A.alloc("sgT", [128, 4, 128], BF16)
    AT = A.alloc("AT", [128, 4, 128], BF16)
    S_f = A.alloc("S_f", [128, 2, 128], F32)
    S_b = A.alloc("S_b", [128, 2, 128], BF16)
    S0_f = A.alloc("S0_f", [128, 4, 2, 128], F32)
    S0_b = A.alloc("S0_b", [128, 4, 2, 128], BF16)
    osq = A.alloc("osq", [128, 512], BF16)
    orst = A.alloc("orst", [128, 512], F32)
    u_sb = A.alloc("u_sb", [128, 416], F32)
    ms = A.alloc("ms", [128, 32], F32)
    rst = A.alloc("rst", [128, 32], F32)
    cq = A.alloc("cq", [128, 256], BF16)
    cqT = A.alloc("cqT", [128, 2, 128], BF16)
    ckv_f = A.alloc("ckv_f", [128, 2, 128], F32)
    ckv_b = A.alloc("ckv_b", [128, 132], BF16)
    ckvT = A.alloc("ckvT", [128, 128], BF16)
    sqq = A.alloc("sqq", [128, 768], F32)
    sqk = A.alloc("sqk", [128, 8, 64], F32)
    Rn = A.alloc("Rn", [128, 9, 32], F32)
    Rr = A.alloc("Rr", [128, 2, 9, 32], F32)
    rt1 = A.alloc("rt1", [128, 9, 16], F32)
    rt2 = A.alloc("rt2", [128, 9, 16], F32)
    q_tok = A.alloc("q_tok", [128, 8, 64], BF16)
    qr_tok = A.alloc("qr_tok", [128, 8, 32], BF16)
    k_tok = A.alloc("k_tok", [128, 8, 64], BF16)
    kr_tok = A.alloc("kr_tok", [128, 32], BF16)
    QnT = A.alloc("QnT", [128, 8, 128], BF16)
    QrT = A.alloc("QrT", [128, 8, 128], BF16)
    KnT = A.alloc("KnT", [128, 4, NT * 128], BF16)
    KrT = A.alloc("KrT", [128, NT * 128], BF16)
    Vext = A.alloc("Vext", [128, 16, 8, 65], BF16)
    PT = A.alloc("PT", [128, 2, 4, 128], BF16)
    rl = A.alloc("rl", [128, 8], F32)
    o_tok = A.alloc("o_tok", [128, 8, 64], BF16)
    IDX = A.alloc("IDX", [128, 512], I32)
    PTi = A.alloc("PTi", [128, 512], I32)
    Cbuf = A.alloc("Cbuf", [128, 2, 4, 160], F32)
    Cb = A.alloc("Cb", [128, 2, 4, 132], BF16)
    KRb = A.alloc("KRb", [128, 2, 4, 32], BF16)
    CT = A.alloc("CT", [128, 2, 4, 128], BF16)
    KRT = A.alloc("KRT", [128, 2, 4, 128], BF16)
    sqs = A.alloc("sqs", [128, 2, 1024], BF16)
    ssum = A.alloc("ssum", [128, 32], F32)
    rstk = A.alloc("rstk", [128, 32], F32)
    rstk2 = A.alloc("rstk2", [128, 2, 32], F32)
    sc1 = A.alloc("sc1", [128, 256], F32)
    sc2 = A.alloc("sc2", [128, 256], F32)
    PTs = A.alloc("PTs", [128, 2, 4, 64], BF16)
    WkT = A.alloc("WkT", [128, 4, 128], BF16)
    qgT = A.alloc("qgT", [128, 8, 128], BF16)
    qabsT = A.alloc("qabsT", [128, 8, 128], BF16)
    acc_sb = A.alloc("acc_sb", [64, 132], F32)
    accn = A.alloc("accn", [128, 128], BF16)
    accnT = A.alloc("accnT", [128, 128], BF16)

    for k in range(8):
        dma('sp', w_in_sb[:, k, :], w_in_d[k * 128:(k + 1) * 128, :], (), ['w_in'])
    dma('sp', wqup_sb[:, :, :], wqup_d.rearrange("(c p) n -> p c n", p=128), (), ['wqup'])
    dma('sp', wkv_sb[:, :], wkv_d, (), ['wkv'])
    dma('sp', wa2_sb[:, :], wa2_d, (), ['wa2'])
    dma('sp', nba[:, :], ba_d.rearrange("(c p) -> p c", p=128), (), ['nba'], allow_slow_non_contiguous=True)
    dma('sp', goutc[:, :], gout_d.rearrange("(p o) -> p o", o=1), (), ['goutc'], allow_slow_non_contiguous=True)
    dma('sp', gqa_c[:, :], gqa_d.rearrange("(c p) -> p c", p=128), (), ['gqa'], allow_slow_non_contiguous=True)
    dma('sp', gknc[0:64, :], gkn_d.rearrange("(p o) -> p o", o=1), (), ['gknc'], allow_slow_non_contiguous=True)
    dma('sp', gknc[64:128, :], gkn_d.rearrange("(p o) -> p o", o=1), (), ['gknc'], allow_slow_non_contiguous=True)
    for gt, gd, k in ((Gqn, gqnb_d, 'Gqn'), (Gqr, gqrb_d, 'Gqr'), (Gkva, gkvab_d, 'Gkva'), (Gkr, gkrb_d, 'Gkr'),
                      (Gkn, gknb_d, 'Gkn')):
        dma('sp', gt[:, :], gd, (), [k])
    dma('sp', cos_sb[:, :, :], c_cos_d, (), ['cos'])
    dma('sp', sin_sb[:, :, :], c_sin_d, (), ['sin'])
    dma('sp', rs_sb[:, :, :], c_rs_d, (), ['rs'])
    dma('sp', iota_sb[:, :], c_iota_d, (), ['iota'])
    dma('sp', PTi[:, :], ptb_d, (), ['PTi'])
    for b in range(4):
        for pr in range(2):
            dma('sp', S0_f[:, b, pr, :], sgla_d[b, pr], (), ['S0_f'])
    ts('dve', nba[:, :], nba[:, :], -1.0, None, ALU.mult, None, ['nba'], ['nba'])
    for c in range(2):
        ts('dve', wqup_sb[:, c, :], wqup_sb[:, c, :], gqa_c[:, c:c + 1], None, ALU.mult, None,
           ['wqup', 'gqa'], ['wqup'])
    cp('dve', wkn_sb[:, :].rearrange("p (h d) -> p h d", h=8),
       wkv_sb[:, :].rearrange("p (h e) -> p h e", h=8)[:, :, 0:64], ['wkv'], ['wkn'])
    memset('dve', S_f[:, :, :], 0.0, ['S_f'])
    memset('dve', nbl[:, :], 0.0, ['nbl'])
    memset('dve', qdT[:, :, :], 0.0, ['qdT'])
    memset('dve', QnT[:, :, :], 0.0, ['QnT'])
    memset('dve', QrT[:, :, :], 0.0, ['QrT'])
    memset('dve', KrT[:, :], 0.0, ['KrT'])
    memset('dve', KRT[:, :, :, :], 0.0, ['KRT0', 'KRT1'])
    memset('dve', accn[:, :], 0.0, ['accn'])
    dma('sp', rowm[:, :], c_rowm_d, (), ['rowm'])
    dma('sp', hm[:, :], c_hm_d, (), ['hm'])
    cp('dve', wvp_sb[:, :, :].rearrange("p c (a d) -> p c a d", a=2),
       wkv_sb[:, :].rearrange("p (c a e) -> p c a e", c=4, a=2)[:, :, :, 64:128], ['wkv'], ['wvp'])
    memset('dve', S_b[:, :, :], 0.0, ['S_b'])
    cp('dve', S0_b[:, :, :, :], S0_f[:, :, :, :], ['S0_f'], ['S0_b'])
    memset('dve', ckv_b[:, 128:132], 1.0, ['ckv_b1'])
    memset('dve', Cb[:, :, :, 128:132], 1.0, ['Cb1'])
    memset('dve', Vext[:, :, :, 64:65], 1.0, ['Vext1'])
    memset('dve', omT[:, :, ST * 128:], 0.0, ['omT'])
    ts('dve', IDX[:, :], PTi[:, :], 128.0, iota_sb[:, 0:1], ALU.mult, ALU.add, ['PTi', 'iota'], ['IDX'])

    def load_x(i):
        dma('sp', x_sb[:, i % 2, :], x_d[i], (), [f'x{i % 2}'])

    if os.environ.get("KDBG") == "2":
        load_x(0)
        dma('sp', y_o[0], x_sb[:, 0, :], ['x0'], ['y_o'])
        dma('pool', y_o[3][:, 0:1024], w_in_sb[:, 0, 0:1024], ['w_in'], ['y_o'])
        dma('sp', y_o[4][:, 0:128], identf[:, :], ['identf'], ['y_o'])
    chk(1)
    load_x(0)

    pmi = [0]

    def next_pm():
        pmi[0] ^= 1
        return pmi[0]

    for i in range(NT):
        samp = (i == ST)
        sl = i % 2
        xk = f'x{sl}'
        nk = f'nT{sl}'
        nTi = nT[:, sl, :, :]
        tcol = slice(i * 128, (i + 1) * 128)
        if i + 1 < NT:
            load_x(i + 1)
        norm_tile(x_sb[:, sl, :], [xk], gmixT, 'gmixT', nTi, [nk], 0)
        if i == 0 and os.environ.get("KDBG") == "3":
            dma('pool', y_o[2][:, 0:1024], nT[:, 0, :, :].rearrange("p c t -> p (c t)"), [nk], ['y_o'])
            dma('pool', y_o[4][:, 0:1024], xs[:, :], ['xs'], ['y_o'])
            dma('sp', y_o[5][:, 0:8], stat[:, :], ['stat'], ['y_o'])
            dma('sp', y_o[0], x_sb[:, 0, :], ['x0'], ['y_o'])
        if i == 0 and os.environ.get("KDBG") == "4":
            stg = x_sb[:, 1, :]
            cp('dve', stg, xs[:, :], ['xs'], ['x1'])
            dma('sp', y_o[4], stg, ['x1'], ['y_o'])
            cp('dve', stg, nT[:, 0, :, :].rearrange("p c t -> p (c t)"), [nk], ['x1'])
            dma('sp', y_o[2], stg, ['x1'], ['y_o'])
            cp('dve', x_sb[:, 1, 0:8], stat[:, :], ['stat'], ['x1'])
            cp('dve', x_sb[:, 1, 8:16], gmixT[:, :], ['gmixT'], ['x1'])
            dma('sp', y_o[5], stg, ['x1'], ['y_o'])
        chk(2, i == 0)

        j = next_pm()
        for k in range(8):
            mm(pm[j][0:16, 0:128], w_in_sb[:, k, O_A:O_A + 16], nTi[:, k, :], k == 0, k == 7,
               ['w_in', nk], [f'pm{j}'])
        cp('act', gaT[:, :], pm[j][0:16, 0:128], [f'pm{j}'], ['gaT'])
        chk(2.1, i == 0)
        j = next_pm()
        for pr in range(2):
            mm(pm[j][:, pr * 128:(pr + 1) * 128], wa2_sb[:, pr * 128:(pr + 1) * 128], gaT[:, :], True, True,
               ['wa2', 'gaT'], [f'pm{j}'])
        for pr in range(2):
            act(sp_[:, pr, :], pm[j][:, pr * 128:(pr + 1) * 128], AF.Exp, [f'pm{j}', 'nba'], ['sp'],
                bias=nba[:, pr:pr + 1], scale=-1.0)
        act(sp_[:, :, :], sp_[:, :, :], AF.Ln, ['sp'], ['sp'], bias=1.0, scale=1.0)
        chk(2.3, i == 0)
        rsi = 1 if samp else 0
        for pr in range(2):
            S.op('dve', lambda e, pr=pr, rsi=rsi: e.tensor_tensor_scan(bpos[:, pr, :], rs_sb[:, rsi, :], sp_[:, pr, :],
                                                              0.0, ALU.mult, ALU.add),
                 ['sp', 'rs'], ['bpos'])
        segs = [(b * 32, 32, b * 32 + 9, b) for b in range(4)] if samp else [(0, 128, 127, 0)]
        chk(2.4, i == 0)
        for (c0, ncol, lc, si) in segs:
            ts('dve', nbl[:, si * 2:si * 2 + 2], bpos[:, :, lc], -1.0 / 16, None, ALU.mult, None,
               ['bpos'], ['nbl'])
        act(ebl[:, :], nbl[:, :], AF.Exp, ['nbl'], ['ebl'])
        chk(2.5, i == 0)
        j = next_pm()
        for pr in range(2):
            for k in range(8):
                mm(pm[j][:, pr * 128:(pr + 1) * 128], w_in_sb[:, k, O_Q + pr * 128:O_Q + (pr + 1) * 128],
                   nTi[:, k, :], k == 0, k == 7, ['w_in', nk], [f'pm{j}'])
        act(etmp[:, :, :], bpos[:, :, :], AF.Exp, ['bpos'], ['etmp'], scale=-1.0 / 16)
        for hh in range(2):
            r = slice(hh * 64, hh * 64 + 64)
            stt(qdT[r, hh::2, :], pm[j][r, 0:256].rearrange("p (a t) -> p a t", a=2), 0.125, etmp[r, :, :],
                ALU.mult, ALU.mult, [f'pm{j}', 'etmp'], ['qdT'])
        chk(2.6, i == 0)
        j = next_pm()
        for pr in range(2):
            for k in range(8):
                mm(pm[j][:, pr * 128:(pr + 1) * 128], w_in_sb[:, k, O_K + pr * 128:O_K + (pr + 1) * 128],
                   nTi[:, k, :], k == 0, k == 7, ['w_in', nk], [f'pm{j}'])
        act(etmp[:, :, :], bpos[:, :, :], AF.Exp, ['bpos', 'qdT'], ['etmp'], scale=1.0 / 16)
        tt('dve', kiT[:, :, :], pm[j][:, 0:256].rearrange("p (a t) -> p a t", a=2), etmp[:, :, :], ALU.mult,
           [f'pm{j}', 'etmp'], ['kiT'])
        for (c0, ncol, lc, si) in segs:
            for pr in range(2):
                act(etmp[:, pr, c0:c0 + ncol], bpos[:, pr, c0:c0 + ncol], AF.Exp, ['bpos', 'nbl', 'kiT'],
                    ['etmp'], bias=nbl[:, si * 2 + pr:si * 2 + pr + 1], scale=1.0 / 16)
        tt('dve', keT[:, :, :], pm[j][:, 0:256].rearrange("p (a t) -> p a t", a=2), etmp[:, :, :], ALU.mult,
           [f'pm{j}', 'etmp'], ['keT'])
        for pr in range(2):
            tr(ptr[1][:, pr * 128:(pr + 1) * 128], keT[:, pr, :], ident[:, :], ['keT', 'ident'], ['ptr1'])
        cp('dve', ke_tok[:, :], ptr[1][:, 0:256], ['ptr1'], ['ke_tok'])
        chk(2.7, i == 0)
        j = next_pm()
        for c in range(4):
            for k in range(8):
                mm(pm[j][:, c * 128:(c + 1) * 128], w_in_sb[:, k, O_G + c * 128:O_G + (c + 1) * 128],
                   nTi[:, k, :], k == 0, k == 7, ['w_in', nk], [f'pm{j}'])
        act(sgT[:, :, :], pm[j][:, :].rearrange("p (c t) -> p c t", c=4), AF.Silu, [f'pm{j}'], ['sgT'])
        j = next_pm()
        for k in range(8):
            mm(pm[j][:, :], nTi[:, k, :], w_in_sb[:, k, O_V:O_V + 512], k == 0, k == 7, ['w_in', nk], [f'pm{j}'])
        cp('act', v_tok[:, :], pm[j][:, :], [f'pm{j}'], ['v_tok'])
        chk(2.8, i == 0)
        j = next_pm()
        for h in range(4):
            pr = h // 2
            mm(pm[j][:, h * 128:(h + 1) * 128], kiT[:, pr, :], qdT[:, h, :], True, True,
               ['kiT', 'qdT'], [f'pm{j}'])
        mk, mkey = (masks, 'masks') if samp else (mask01, 'mask01')
        tt('dve', AT[:, :, :], pm[j][:, :].rearrange("p (h t) -> p h t", h=4), bc_mid(mk[:, :], 4), ALU.mult,
           [f'pm{j}', mkey], ['AT'])
        chk(2.9, i == 0)
        j = next_pm()
        for h in range(4):
            pr, b0 = h // 2, (h % 2) * 64
            oh = pm[j][:, h * 128:(h + 1) * 128]
            mm(oh, v_tok[:, h * 128:(h + 1) * 128], AT[:, h, :], True, False, ['v_tok', 'AT'], [f'pm{j}'])
            if not samp:
                mm(oh, S_b[:, pr, :], qdT[:, h, :], False, True, ['S_b', 'qdT'], [f'pm{j}'])
            else:
                for b in range(4):
                    mm(pm[j][:, h * 128 + b * 32:h * 128 + (b + 1) * 32], S0_b[:, b, pr, :],
                       qdT[:, h, b * 32:(b + 1) * 32], False, b == 3, ['S0_b', 'qdT'], [f'pm{j}'])
        jo = j
        chk(2.95, i == 0)
        act(osq[:, :], pm[jo][:, :], AF.Square, [f'pm{jo}'], ['osq'])
        j = next_pm()
        mm(pm[j][:, :], ones_bf[:, :], osq[:, :], True, True, ['ones', 'osq'], [f'pm{j}'])
        rstd_chain(pm[j][:, :], orst[:, :], 1.0 / 128, [f'pm{j}'], ['orst'])
        tt('dve', orst[:, :], pm[jo][:, :], orst[:, :], ALU.mult, [f'pm{jo}', 'orst'], ['orst'])
        stt(ogT[:, :, tcol], orst[:, :].rearrange("p (h t) -> p h t", h=4), goutc[:, 0:1], sgT[:, :, :],
            ALU.mult, ALU.mult, ['orst', 'goutc', 'sgT'], ['ogT'])
        chk(2.97, i == 0)
        if not samp:
            for pr in range(2):
                j = next_pm()
                mm(pm[j][:, 0:256], ke_tok[:, pr * 128:(pr + 1) * 128], v_tok[:, pr * 256:(pr + 1) * 256],
                   True, True, ['ke_tok', 'v_tok'], [f'pm{j}'])
                ts('dve', orst[:, 0:128], pm[j][:, 0:128], hm[:, 0:1], None, ALU.mult, None, [f'pm{j}', 'hm'], ['orst'])
                stt(orst[:, 0:128], pm[j][:, 128:256], hm[:, 1:2], orst[:, 0:128], ALU.mult, ALU.add,
                    [f'pm{j}', 'hm', 'orst'], ['orst'])
                stt(S_f[:, pr, :], S_f[:, pr, :], ebl[:, pr:pr + 1], orst[:, 0:128], ALU.mult, ALU.add,
                    ['S_f', 'ebl', 'orst'], ['S_f'])
                chk(2.98, i == 0 and pr == 0)
                chk(2.99, i == 0 and pr == 1)
            cp('dve', S_b[:, :, :], S_f[:, :, :], ['S_f'], ['S_b'])
            if i == ST - 1:
                for pr in range(2):
                    dma('sp', glap_o[pr], S_f[:, pr, :], ['S_f'], ['glap_o'])
        else:
            for b in range(4):
                ts('dve', ke_z[:, b, :], ke_tok[:, :], rowm[:, b:b + 1], None, ALU.mult, None,
                   ['ke_tok', 'rowm'], ['ke_z'])
            for b in range(4):
                for pr in range(2):
                    j = next_pm()
                    mm(pm[j][:, 0:256], ke_z[:, b, pr * 128:(pr + 1) * 128],
                       v_tok[:, pr * 256:(pr + 1) * 256], True, True, ['ke_z', 'v_tok'], [f'pm{j}'])
                    ts('dve', orst[:, 0:128], pm[j][:, 0:128], hm[:, 0:1], None, ALU.mult, None, [f'pm{j}', 'hm'],
                       ['orst'])
                    stt(orst[:, 0:128], pm[j][:, 128:256], hm[:, 1:2], orst[:, 0:128], ALU.mult, ALU.add,
                        [f'pm{j}', 'hm', 'orst'], ['orst'])
                    stt(S0_f[:, b, pr, :], S0_f[:, b, pr, :], ebl[:, b * 2 + pr:b * 2 + pr + 1], orst[:, 0:128],
                        ALU.mult, ALU.add, ['S0_f', 'ebl', 'orst'], ['S0_f'])
                    dma('sp', glas_o[b, pr], S0_f[:, b, pr, :], ['S0_f'], ['glas_o'])

        chk(3, i == 0)
        j = next_pm()
        for k in range(8):
            mm(pm[j][:, 0:416], nTi[:, k, :], w_in_sb[:, k, O_MQ:O_MQ + 416], k == 0, k == 7,
               ['w_in', nk], [f'pm{j}'])
        cp('dve', u_sb[:, :], pm[j][:, 0:416], [f'pm{j}'], ['u_sb'])
        chk(3.1, i == 0)
        for (a, b_, col, n) in ((0, 256, 24, 256), (256, 384, 25, 128), (384, 416, 26, 32)):
            if True:
                act(sqq[:, a:b_], u_sb[:, a:b_], AF.Square, ['u_sb'], ['sqq', 'ms'], accum_out=ms[:, col:col + 1],
                    scale=float(n) ** -0.5)
            else:
                act(sqq[:, a:b_], u_sb[:, a:b_], AF.Square, ['u_sb'], ['sqq'], scale=float(n) ** -0.5)
                S.op('dve', lambda e, a=a, b_=b_, col=col: e.tensor_reduce(ms[:, col:col + 1], sqq[:, a:b_], AX.X, ALU.add),
                     ['sqq'], ['ms'])
        rstd_chain(ms[:, 24:27], rst[:, 24:27], 1.0, ['ms'], ['rst'])
        chk(3.2, i == 0)
        ts('dve', cq[:, :], u_sb[:, 0:256], rst[:, 24:25], None, ALU.mult, None, ['u_sb', 'rst'], ['cq'])
        cs = i % 2
        stt(ckv_f[:, cs, :], u_sb[:, 256:384], rst[:, 25:26], Gkva[:, :], ALU.mult, ALU.mult,
            ['u_sb', 'rst', 'Gkva'], [f'ckv_f{cs}'])
        dma('sp', ckv_o[i], ckv_f[:, cs, :], [f'ckv_f{cs}'], ['ckv_o'])
        cp('dve', ckv_b[:, 0:128], ckv_f[:, cs, :], [f'ckv_f{cs}'], ['ckv_b'])
        stt(Rn[:, 8, :], u_sb[:, 384:416], rst[:, 26:27], Gkr[:, :], ALU.mult, ALU.mult,
            ['u_sb', 'rst', 'Gkr'], ['Rn'])
        chk(3.3, i == 0)
        for c in range(2):
            tr(ptr[1][:, c * 128:(c + 1) * 128], cq[:, c * 128:(c + 1) * 128], ident[:, :], ['cq', 'ident'], ['ptr1'])
        tr(ptr[1][:, 256:384], ckv_b[:, 0:128], ident[:, :], ['ckv_b', 'ident'], ['ptr1'])
        cp('dve', cqT[:, :, :], ptr[1][:, 0:256].rearrange("p (c t) -> p c t", c=2), ['ptr1'], ['cqT'])
        cp('dve', ckvT[:, :], ptr[1][:, 256:384], ['ptr1'], ['ckvT'])
        chk(3.4, i == 0)
        for half, (n0, n1) in enumerate(((0, 512), (512, 768))):
            for c in range(2):
                mm(pw[0][:, n0:n1], cqT[:, c, :], wqup_sb[:, c, n0:n1], c == 0, c == 1, ['cqT', 'wqup'], ['pw0'])
        for half in range(2):
            mm(pw[1][:, half * 512:(half + 1) * 512], ckvT[:, :], wkv_sb[:, half * 512:(half + 1) * 512],
               True, True, ['ckvT', 'wkv'], ['pw1'])
        qfv = pw[0][:, 0:768].rearrange("p (h e) -> p h e", h=8)
        kvv = pw[1][:, :].rearrange("p (h e) -> p h e", h=8)
        chk(3.5, i == 0)
        act(sqq[:, :], pw[0][:, 0:768], AF.Square, ['pw0'], ['sqq'])
        sqv = sqq[:, :].rearrange("p (h e) -> p h e", h=8)
        S.op('dve', lambda e: e.tensor_reduce(ms[:, 0:8], sqv[:, :, 0:64], AX.X, ALU.add), ['sqq'], ['ms'])
        S.op('dve', lambda e: e.tensor_reduce(ms[:, 16:24], sqv[:, :, 64:96], AX.X, ALU.add), ['sqq'], ['ms'])
        act(sqk[:, :, :], kvv[:, :, 0:64], AF.Square, ['pw1'], ['sqk'])
        S.op('dve', lambda e: e.tensor_reduce(ms[:, 8:16], sqk[:, :, :], AX.X, ALU.add), ['sqk'], ['ms'])
        rstd_chain(ms[:, 0:16], rst[:, 0:16], 1.0 / 64, ['ms'], ['rst'])
        rstd_chain(ms[:, 16:24], rst[:, 16:24], 1.0 / 32, ['ms'], ['rst'])
        chk(3.6, i == 0)
        tt('dve', sqk[:, :, :], qfv[:, :, 0:64], bc_last(rst[:, 0:8], 64), ALU.mult, ['pw0', 'rst'], ['sqk'])
        tt('dve', q_tok[:, :, :], sqk[:, :, :], bc_mid(Gqn[:, :], 8), ALU.mult, ['sqk', 'Gqn'], ['q_tok'])
        tt('dve', sqk[:, :, :], kvv[:, :, 0:64], bc_last(rst[:, 8:16], 64), ALU.mult, ['pw1', 'rst', 'q_tok'],
           ['sqk'])
        tt('dve', k_tok[:, :, :], sqk[:, :, :], bc_mid(Gkn[:, :], 8), ALU.mult, ['sqk', 'Gkn'], ['k_tok'])
        chk(3.7, i == 0)
        if not samp:
            cp('act', Vext[:, i, :, 0:64], kvv[:, :, 64:128], ['pw1'], ['Vext'])
        tt('dve', Rn[:, 0:8, :], qfv[:, :, 64:96], bc_last(rst[:, 16:24], 32), ALU.mult, ['pw0', 'rst'], ['Rn'])
        tt('dve', Rn[:, 0:8, :], Rn[:, 0:8, :], bc_mid(Gqr[:, :], 8), ALU.mult, ['Rn', 'Gqr'], ['Rn'])
        cosb = bc_mid(cos_sb[:, i, :], 9)
        sinb = bc_mid(sin_sb[:, i, :], 9)
        x1, x2 = Rn[:, :, 0:16], Rn[:, :, 16:32]
        rk = f'Rr{cs}'
        tt('dve', rt1[:, :, :], x1, cosb, ALU.mult, ['Rn', 'cos'], ['rt1'])
        tt('dve', rt2[:, :, :], x2, sinb, ALU.mult, ['Rn', 'sin'], ['rt2'])
        tt('dve', Rr[:, cs, :, 0:16], rt1[:, :, :], rt2[:, :, :], ALU.subtract, ['rt1', 'rt2'], [rk])
        tt('dve', rt1[:, :, :], x2, cosb, ALU.mult, ['Rn', 'cos', rk], ['rt1'])
        tt('dve', rt2[:, :, :], x1, sinb, ALU.mult, ['Rn', 'sin', rk], ['rt2'])
        tt('dve', Rr[:, cs, :, 16:32], rt1[:, :, :], rt2[:, :, :], ALU.add, ['rt1', 'rt2'], [rk])
        dma('sp', kr_o[i], Rr[:, cs, 8, :], [rk], ['kr_o'])
        cp('dve', qr_tok[:, :, :], Rr[:, cs, 0:8, :], [rk], ['qr_tok'])
        cp('dve', kr_tok[:, :], Rr[:, cs, 8, :], [rk], ['kr_tok'])
        chk(3.8, i == 0)
        for c in range(4):
            tr(ptr[0][:, c * 128:(c + 1) * 128], q_tok[:, 2 * c:2 * c + 2, :], ident[:, :], ['q_tok', 'ident'], ['ptr0'])
            tr(ptr[0][:, 512 + c * 128:512 + (c + 1) * 128], k_tok[:, 2 * c:2 * c + 2, :], ident[:, :],
               ['k_tok', 'ident'], ['ptr0'])
        for hh in range(2):
            r = slice(hh * 64, hh * 64 + 64)
            cp('dve', QnT[r, hh::2, :], ptr[0][r, 0:512].rearrange("p (c t) -> p c t", c=4), ['ptr0'], ['QnT'])
        cp('dve', KnT[:, :, tcol], ptr[0][:, 512:1024].rearrange("p (c t) -> p c t", c=4), ['ptr0'], ['KnT'])
        for h in range(8):
            tr(ptr[1][0:32, h * 128:(h + 1) * 128], qr_tok[:, h, :], ident[:, :], ['qr_tok', 'ident'], ['ptr1'])
        cp('dve', QrT[0:32, :, :], ptr[1][0:32, :].rearrange("p (h t) -> p h t", h=8), ['ptr1'], ['QrT'])
        tr(ptr[1][0:32, 0:128], kr_tok[:, :], ident[:, :], ['kr_tok', 'ident', 'QrT'], ['ptr1'])
        cp('dve', KrT[0:32, tcol], ptr[1][0:32, 0:128], ['ptr1'], ['KrT'])

        if i == 0 and os.environ.get("KDBG"):
            dma('sp', y_o[0][:, 0:416], u_sb[:, :], ['u_sb'], ['y_o'])
            dma('sp', y_o[1][:, 0:32], rst[:, :], ['rst'], ['y_o'])
            dma('pool', y_o[2][:, 0:1024], nT[:, 0, :, :].rearrange("p c t -> p (c t)"), [nk], ['y_o'])
            dma('pool', y_o[3][:, 0:1024], w_in_sb[:, 0, 0:1024], ['w_in'], ['y_o'])
            dma('pool', y_o[4][:, 0:1024], xs[:, :], ['xs'], ['y_o'])
            dma('sp', y_o[5][:, 0:8], stat[:, :], ['stat'], ['y_o'])
            dma('sp', y_o[6][:, 0:256], ckv_f[:, :, :].rearrange("p a c -> p (a c)"), ['ckv_f0'], ['y_o'])
        if i == 0 and os.environ.get("KDBG") == "6":
            stg = x_sb[:, 1, :]
            cp('dve', x_sb[:, 1, 0:416], u_sb[:, :], ['u_sb'], ['x1'])
            cp('dve', x_sb[:, 1, 416:448], rst[:, :], ['rst'], ['x1'])
            cp('dve', x_sb[:, 1, 448:480], ms[:, :], ['ms'], ['x1'])
            cp('dve', x_sb[:, 1, 512:640], ckv_f[:, 0, :], ['ckv_f0'], ['x1'])
            cp('dve', x_sb[:, 1, 640:768], Gkva[:, :], ['Gkva'], ['x1'])
            cp('dve', x_sb[:, 1, 768:800], Rr[:, 0, 8, :], ['Rr0'], ['x1'])
            dma('sp', y_o[4], stg, ['x1'], ['y_o'])
        chk(4, i == 0)
        chk(7, samp)
        if not samp:
            groups = []
            for h in range(8):
                for kt0 in range(0, i + 1, 4):
                    groups.append((h, kt0, min(4, i + 1 - kt0)))

            def emit_scores(gi):
                h, kt0, nkt = groups[gi]
                pr, b0 = h // 2, (h % 2) * 64
                s = gi % 2
                for jj in range(nkt):
                    kt = kt0 + jj
                    kc = slice(kt * 128, (kt + 1) * 128)
                    o = pm[s][:, jj * 128:(jj + 1) * 128]
                    mm(o, KnT[:, pr, kc], QnT[:, h, :], True, False, ['KnT', 'QnT'], [f'pm{s}'])
                    mm(o, KrT[:, kc], QrT[:, h, :], False, True, ['KrT', 'QrT'], [f'pm{s}'])
                act(PT[:, s, 0:nkt, :], pm[s][:, 0:nkt * 128].rearrange("p (a t) -> p a t", a=nkt), AF.Exp,
                    [f'pm{s}'], [f'PT{s}'], scale=MLA_SCALE)
                if kt0 + nkt - 1 == i:
                    tt('dve', PT[:, s, nkt - 1, :], PT[:, s, nkt - 1, :], mask01[:, :], ALU.mult,
                       [f'PT{s}', 'mask01'], [f'PT{s}'])

            def emit_pv(gi):
                h, kt0, nkt = groups[gi]
                s = gi % 2
                bank = h // 4
                o = pw[bank][:, (h % 4) * 65:(h % 4) * 65 + 65]
                for jj in range(nkt):
                    kt = kt0 + jj
                    mm(o, PT[:, s, jj, :], Vext[:, kt, h, :], kt == 0, kt == i, [f'PT{s}', 'Vext', 'Vext1'],
                       [f'pw{bank}'])

            emit_scores(0)
            for gi in range(len(groups)):
                if gi + 1 < len(groups):
                    emit_scores(gi + 1)
                emit_pv(gi)
            for bank in range(2):
                ov = pw[bank][:, 0:260].rearrange("p (h e) -> p h e", h=4)
                S.op('dve', lambda e, ov=ov, bank=bank: e.reciprocal(rl[:, bank * 4:bank * 4 + 4], ov[:, :, 64]),
                     [f'pw{bank}'], ['rl'])
                tt('dve', o_tok[:, bank * 4:bank * 4 + 4, :], ov[:, :, 0:64], bc_last(rl[:, bank * 4:bank * 4 + 4], 64),
                   ALU.mult, [f'pw{bank}', 'rl'], ['o_tok'])
            for c in range(4):
                tr(ptr[0][:, c * 128:(c + 1) * 128], o_tok[:, 2 * c:2 * c + 2, :], ident[:, :], ['o_tok', 'ident'],
                   ['ptr0'])
            cp('dve', omT[:, :, tcol], ptr[0][:, 0:512].rearrange("p (c t) -> p c t", c=4), ['ptr0'], ['omT'])
            chk(5, i == 0)
            chk(6, i == ST - 1)
        elif not _SKIP_PA:
            for pr in range(4):
                tr(ptr[0][:, pr * 128:(pr + 1) * 128], wkn_sb[:, pr * 128:(pr + 1) * 128], ident[:, :],
                   ['wkn', 'ident'], ['ptr0'])
            cp('dve', WkT[:, :, :], ptr[0][:, 0:512].rearrange("p (c t) -> p c t", c=4), ['ptr0'], ['WkT'])
            ts('dve', qgT[:, :, :], QnT[:, :, :], gknc[:, 0:1], None, ALU.mult, None, ['QnT', 'gknc'], ['qgT'])
            for h in range(8):
                pr, b0 = h // 2, (h % 2) * 64
                bank = h // 4
                mm(pw[bank][:, (h % 4) * 128:(h % 4 + 1) * 128], WkT[:, pr, :], qgT[:, h, :],
                   True, True, ['WkT', 'qgT'], [f'pw{bank}'])
            for bank in range(2):
                cp('dve', qabsT[:, bank * 4:bank * 4 + 4, :],
                   pw[bank][:, 0:512].rearrange("p (h t) -> p h t", h=4), [f'pw{bank}'], ['qabsT'])
            cp('dve', rstk[:, 0:8], rst[:, 8:16], ['rst'], ['rstk_new'])
            chk(7.1)

            def gather(b, g):
                s = g % 2
                for jj in range(4):
                    col = b * NPG + g * 4 + jj
                    S.op('pool', lambda e, s=s, jj=jj, col=col: e.indirect_dma_start(
                        out=Cbuf[:, s, jj, :], out_offset=None, in_=cc_d,
                        in_offset=bass.IndirectOffsetOnAxis(ap=IDX[:, col:col + 1], axis=0)),
                        ['IDX'], [f'Cbuf{s}'], dma=True)

            def score_pv(b, lhs_c, lhs_r, rstd_ap, nj, rhs_pv, keys_r, first, last, maskap=None):
                qa = qabsT[:, :, b * 32 + 2:b * 32 + 10]
                qr = QrT[:, :, b * 32 + 2:b * 32 + 10]
                for jj in range(nj):
                    mm(pm[0][:, jj * 64:(jj + 1) * 64], lhs_c(jj), qa, True, True, keys_r + ['qabsT'], ['pm0'])
                    mm(pm[0][:, 256 + jj * 64:256 + (jj + 1) * 64], lhs_r(jj), qr, True, True, keys_r + ['QrT'],
                       ['pm0'])
                n = nj * 64
                tt('dve', sc1[:, 0:n].rearrange("p (a t) -> p a t", t=8),
                   pm[0][:, 0:n].rearrange("p (a t) -> p a t", t=8), bc_last(rstd_ap, 8), ALU.mult,
                   ['pm0', 'rstk', 'rstk_new'] + [k for k in keys_r if k.startswith('rk')], ['sc1'])
                tt('dve', sc2[:, 0:n], sc1[:, 0:n], pm[0][:, 256:256 + n], ALU.add, ['sc1', 'pm0'], ['sc2'])
                return n

            def stageA(b, g):
                s = g % 2
                cp('act', Cb[:, s, :, 0:128], Cbuf[:, s, :, 0:128], [f'Cbuf{s}'], [f'Cb{s}'])
                cp('dve', KRb[:, s, :, :], Cbuf[:, s, :, 128:160], [f'Cbuf{s}'], [f'KRb{s}'])
                for jj in range(4):
                    tr(ptr[0][:, jj * 128:(jj + 1) * 128], Cb[:, s, jj, 0:128], ident[:, :],
                       [f'Cb{s}', 'ident'], ['ptr0'])
                    tr(ptr[0][0:32, 512 + jj * 128:512 + (jj + 1) * 128], KRb[:, s, jj, :], ident[:, :],
                       [f'KRb{s}', 'ident'], ['ptr0'])
                cp('dve', CT[:, s, :, :], ptr[0][:, 0:512].rearrange("p (a t) -> p a t", a=4), ['ptr0'],
                   [f'CT{s}'])
                cp('dve', KRT[0:32, s, :, :], ptr[0][0:32, 512:1024].rearrange("p (a t) -> p a t", a=4), ['ptr0'],
                   [f'KRT{s}'])
                for hf in range(2):
                    for q2 in range(2):
                        jj = hf * 2 + q2
                        mm(pw[hf][:, q2 * 512:(q2 + 1) * 512], CT[:, s, jj, :], wkn_sb[:, :], True, True,
                           [f'CT{s}', 'wkn'], [f'pw{hf}'])
                    act(sqs[:, hf, :], pw[hf][:, :], AF.Square, [f'pw{hf}'], [f'sqs{hf}'])
                    S.op('dve', lambda e, hf=hf: e.tensor_reduce(
                        ssum[:, hf * 16:(hf + 1) * 16], sqs[:, hf, :].rearrange("p (a d) -> p a d", d=64),
                        AX.X, ALU.add), [f'sqs{hf}'], ['ssum'])
                rstd_chain(ssum[:, 0:32], rstk2[:, s, :], 1.0 / 64, ['ssum'], [f'rk{s}'])

            def stageB(b, g):
                s = g % 2
                score_pv(b, lambda jj: CT[:, s, jj, :], lambda jj: KRT[:, s, jj, :], rstk2[:, s, :], 4,
                         None, [f'CT{s}', f'KRT{s}', f'rk{s}'], None, None)
                act(PTs[:, s, :, :], sc2[:, 0:256].rearrange("p (a t) -> p a t", a=4), AF.Exp, ['sc2'],
                    [f'PTs{s}'], scale=MLA_SCALE)
                for jj in range(4):
                    mm(pm[1][0:64, 0:129], PTs[:, s, jj, :], Cb[:, s, jj, 0:129], g == 0 and jj == 0, False,
                       [f'PTs{s}', f'Cb{s}', 'Cb1'], ['pm1'])

            NG = NPG // 4
            for b in range(4):
                gather(b, 0)
                gather(b, 1)
                stageA(b, 0)
                for g in range(NG):
                    if g + 2 < NG:
                        gather(b, g + 2)
                    if g + 1 < NG:
                        stageA(b, g + 1)
                    stageB(b, g)
                cp('dve', rstk[:, 0:8], rst[:, 8:16], ['rst'], ['rstk'])
                score_pv(b, lambda jj: ckvT[:, :], lambda jj: KrT[:, tcol], rstk[:, 0:8], 1, None,
                         ['ckvT', 'KrT'], None, None)
                act(PTs[:, 0, 0, :], sc2[:, 0:64], AF.Exp, ['sc2'], ['PTs0'], scale=MLA_SCALE)
                tt('dve', PTs[:, 0, 0, :], PTs[:, 0, 0, :], maska[:, b, :], ALU.mult, ['PTs0', 'maska'], ['PTs0'])
                mm(pm[1][0:64, 0:129], PTs[:, 0, 0, :], ckv_b[:, 0:129], False, True, ['PTs0', 'ckv_b', 'ckv_b1'],
                   ['pm1'])
                cp('dve', acc_sb[:, 0:129], pm[1][0:64, 0:129], ['pm1'], ['acc_sb'])
                S.op('dve', lambda e: e.reciprocal(acc_sb[:, 130:131], acc_sb[:, 128:129]), ['acc_sb'], ['acc_sb'])
                ts('dve', accn[0:64, :], acc_sb[:, 0:128], acc_sb[:, 130:131], None, ALU.mult, None, ['acc_sb'], ['accn'])
                tr(ptr[1][:, 0:128], accn[:, :], ident[:, :], ['accn', 'ident'], ['ptr1'])
                cp('dve', accnT[:, :], ptr[1][:, 0:128], ['ptr1'], ['accnT'])
                for c in range(4):
                    mm(pm[0][:, c * 16:(c + 1) * 16], wvp_sb[:, c, :], accnT[:, c * 16:(c + 1) * 16], True, True,
                       ['wvp', 'accnT'], ['pm0'])
                ocol = slice(ST * 128 + b * 32 + 2, ST * 128 + b * 32 + 10)
                pv4 = pm[0][:, 0:64].rearrange("p (c t) -> p c t", c=4)
                cp('dve', omT[0:64, :, ocol], pv4[0:64, :, 0:8], ['pm0'], ['omT'])
                cp('dve', omT[64:128, :, ocol], pv4[64:128, :, 8:16], ['pm0'], ['omT'])
                chk(7.6, b == 0)

    if os.environ.get("KDBG") == "7":
        for tt_ in range(2):
            cp('dve', x_sb[:, 0, 0:512].rearrange("p (c t) -> p c t", c=4), ogT[:, :, tt_ * 128:(tt_ + 1) * 128],
               ['ogT'], ['x0'])
            cp('dve', x_sb[:, 0, 512:1024].rearrange("p (c t) -> p c t", c=4), omT[:, :, tt_ * 128:(tt_ + 1) * 128],
               ['omT'], ['x0'])
            dma('sp', y_o[4 + tt_], x_sb[:, 0, :], ['x0'], ['y_o'])
    chk(8)
    S.barrier()
    A.off = persist_off
    wd_sb = A.alloc("wd", [128, NF, D], BF16)
    wo_sb = A.alloc("wo", [128, 8, D], BF16)
    wpp_sb = A.alloc("wpp", [128, 2, D], BF16)
    wst = A.alloc("wst", [128, 2, 2, 8, 512], BF16)
    h_g = A.alloc("h_g", [128, 4, D], F32)
    nT2 = A.alloc("nT2", [128, 8, 512], BF16)
    hidT = A.alloc("hidT", [128, NF, 512], BF16)
    a_sb = A.alloc("a_sb", [128, 520], F32)
    tbuf = A.alloc("tbuf", [128, 512], F32)
    big = A.alloc("big", [128, D], F32)
    p_sb = A.alloc("p_sb", [128, 256], F32)
    p_bf = A.alloc("p_bf", [128, 256], BF16)
    pT = A.alloc("pT", [128, 2, 128], BF16)
    cw_sb = A.alloc("cw", [128, 3, NF], F32)
    cb_sb = A.alloc("cb", [128, NF], F32)
    halo = A.alloc("halo", [128, NF, 2], F32)
    cst = A.alloc("cst", [128, NF, 8], F32)
    sc_bf = A.alloc("sc_bf", [128, DFF], BF16)
    sel_sb = A.alloc("sel", [128, 128], BF16)
    ctok = A.alloc("ctok", [8, 512], F32)

    for f in range(NF):
        dma('sp', wd_sb[:, f, :], wd_d[f * 128:(f + 1) * 128, :], (), ['wd'])
    for k in range(8):
        dma('sp', wo_sb[:, k, :], wo_d[k * 128:(k + 1) * 128, :], (), ['wo'])
    dma('sp', wpp_sb[:, :, :], wpp_d.rearrange("(c p) n -> p c n", p=128), (), ['wpp'])
    dma('sp', cw_sb[:, :, :], cw_d.rearrange("j (f p) -> p j f", p=128), (), ['cw'], allow_slow_non_contiguous=True)
    dma('sp', cb_sb[:, :], cb_d.rearrange("(f p) -> p f", p=128), (), ['cb'], allow_slow_non_contiguous=True)
    memset('dve', sc_bf[:, :], 0.0, ['sc_bf'])
    dma('sp', sc_bf[0:8, :], sconv_d, ['sc_bf'], ['sc_bf'])
    dma('sp', sel_sb[:, :], c_sel_d, (), ['sel'])
    memset('dve', halo[:, :, :], 0.0, ['halo'])
    memset('dve', a_sb[:, :], 0.0, ['a_sb'])

    blocks = [(0, 512), (512, 512), (1024, 512), (1536, 512), (2048, 512), (2560, 256)]
    wst_cnt = [0]

    def load_ffn_block(bi):
        c0, w = blocks[bi]
        s = wst_cnt[0] % 2
        wst_cnt[0] += 1
        for gu, wdram in ((0, wg_d), (1, wu_d)):
            dma('pool', wst[:, s, gu, :, 0:w], wdram.rearrange("(k p) n -> p k n", p=128)[:, :, c0:c0 + w], (),
                [f'wst{s}'])
        return s

    def load_ple_gate():
        s = wst_cnt[0] % 2
        wst_cnt[0] += 1
        for half in range(2):
            dma('pool', wst[:, s, half, :, :],
                wpg_d.rearrange("(k p) n -> p k n", p=128)[:, :, half * 512:(half + 1) * 512], (), [f'wst{s}'])
        return s

    tgroups = [[0, 1, 2, 3], [4, 5, 6, 7], [8, 9, 10, 11], [12, 13, 14, 15], [ST]]
    for gi, tiles in enumerate(tgroups):
        samp = (tiles[0] == ST)
        N = 128 * len(tiles)
        slot0 = load_ffn_block(0)
        for ti, t in enumerate(tiles):
            tcol = slice(t * 128, (t + 1) * 128)
            dma('sp', big[:, :], x_d[t], (), ['big'])
            for half in range(2):
                for c in range(8):
                    src = ogT if c < 4 else omT
                    mm(pw[0][:, half * 512:(half + 1) * 512], src[:, c % 4, tcol],
                       wo_sb[:, c, half * 512:(half + 1) * 512], c == 0, c == 7, ['ogT', 'omT', 'wo'], ['pw0'])
            tt('dve', h_g[:, ti, :], big[:, :], pw[0][:, :], ALU.add, ['big', 'pw0'], [f'h{ti}'])
            norm_tile(h_g[:, ti, :], [f'h{ti}'], gffnT, 'gffnT', nT2[:, :, ti * 128:(ti + 1) * 128], ['nT2'], 0)
        chk(9, gi == 0)
        for bi, (c0, w) in enumerate(blocks):
            s = slot0 if bi == 0 else s_next
            if bi + 1 < len(blocks):
                s_next = load_ffn_block(bi + 1)
            else:
                s_pg = load_ple_gate()
            for fi in range(w // 128):
                f = c0 // 128 + fi
                fc = slice(fi * 128, (fi + 1) * 128)
                for k in range(8):
                    mm(pm[0][:, 0:N], wst[:, s, 0, k, fc], nT2[:, k, 0:N], k == 0, (k == 7 and not samp),
                       [f'wst{s}', 'nT2'], ['pm0'])
                if samp:
                    mm(pm[0][:, 0:N], sc_bf[:, f * 128:(f + 1) * 128], sel_sb[:, :], False, True,
                       ['sc_bf', 'sel'], ['pm0'])
                for k in range(8):
                    mm(pm[1][:, 0:N], wst[:, s, 1, k, fc], nT2[:, k, 0:N], k == 0, k == 7, [f'wst{s}', 'nT2'], ['pm1'])
                if not samp:
                    cp('dve', a_sb[:, 0:2], halo[:, f, :], ['halo'], ['a_sb'])
                cp('act', a_sb[:, 2:2 + N], pm[0][:, 0:N], ['pm0'], ['a_sb'])
                if not samp:
                    cp('dve', halo[:, f, :], a_sb[:, N:N + 2], ['a_sb'], ['halo'])
                else:
                    cp('dve', cst[:, f, :].rearrange("p (b j) -> p b j", b=4),
                       a_sb[:, 2:2 + 128].rearrange("p (b c) -> p b c", b=4)[:, :, 8:10], ['a_sb'], ['cst'])
                ts('dve', tbuf[:, 0:N], a_sb[:, 0:N], cw_sb[:, 0, f:f + 1], cb_sb[:, f:f + 1], ALU.mult, ALU.add,
                   ['a_sb', 'cw', 'cb'], ['tbuf'])
                stt(tbuf[:, 0:N], a_sb[:, 1:1 + N], cw_sb[:, 1, f:f + 1], tbuf[:, 0:N], ALU.mult, ALU.add,
                    ['a_sb', 'cw', 'tbuf'], ['tbuf'])
                stt(tbuf[:, 0:N], a_sb[:, 2:2 + N], cw_sb[:, 2, f:f + 1], tbuf[:, 0:N], ALU.mult, ALU.add,
                    ['a_sb', 'cw', 'tbuf'], ['tbuf'])
                act(tbuf[:, 0:N], tbuf[:, 0:N], AF.Silu, ['tbuf'], ['tbuf'])
                tt('dve', hidT[:, f, 0:N], tbuf[:, 0:N], pm[1][:, 0:N], ALU.mult, ['tbuf', 'pm1'], ['hidT'])
        if gi == 3 or samp:
            W = 8 if samp else 2
            stg = cst if samp else halo
            skey = 'cst' if samp else 'halo'
            for q6 in range(6):
                nf = min(4, NF - q6 * 4)
                for ff in range(nf):
                    f = q6 * 4 + ff
                    S.op('pe', lambda e, f=f, ff=ff, W=W, stg=stg: e.transpose(pm[0][0:W, ff * 128:(ff + 1) * 128],
                                                                 stg[:, f, 0:W], identf[:, :]),
                         [skey, 'identf'], ['pm0'])
                cp('dve', ctok[0:W, 0:nf * 128], pm[0][0:W, 0:nf * 128], ['pm0'], ['ctok'])
                dma('sp', (convs_o if samp else convp_o)[:, q6 * 512:q6 * 512 + nf * 128], ctok[0:W, 0:nf * 128],
                    ['ctok'], ['conv_o'])
        chk(10, gi == 0)
        for ti, t in enumerate(tiles):
            tc2 = slice(ti * 128, (ti + 1) * 128)
            for half in range(2):
                for f in range(NF):
                    mm(pw[0][:, half * 512:(half + 1) * 512], hidT[:, f, tc2], wd_sb[:, f, half * 512:(half + 1) * 512],
                       f == 0, f == NF - 1, ['hidT', 'wd'], ['pw0'])
            tt('dve', h_g[:, ti, :], h_g[:, ti, :], pw[0][:, :], ALU.add, [f'h{ti}', 'pw0'], [f'h{ti}'])
        for ti, t in enumerate(tiles):
            tc2 = slice(ti * 128, (ti + 1) * 128)
            norm_tile(h_g[:, ti, :], [f'h{ti}'], gpleT, 'gpleT', nT2[:, :, tc2], ['nT2'], 0)
            dma('sp', p_sb[:, :], p_d[t], (), ['p_sb'])
            cp('dve', p_bf[:, :], p_sb[:, :], ['p_sb'], ['p_bf'])
            for c in range(2):
                tr(ptr[1][:, c * 128:(c + 1) * 128], p_bf[:, c * 128:(c + 1) * 128], ident[:, :], ['p_bf', 'ident'],
                   ['ptr1'])
            cp('dve', pT[:, :, :], ptr[1][:, 0:256].rearrange("p (c t) -> p c t", c=2), ['ptr1'], ['pT'])
            for half in range(2):
                for k in range(8):
                    mm(pw[0][:, half * 512:(half + 1) * 512], nT2[:, k, tc2], wst[:, s_pg, half, k, :],
                       k == 0, k == 7, ['nT2', f'wst{s_pg}'], ['pw0'])
                for c in range(2):
                    mm(pw[1][:, half * 512:(half + 1) * 512], pT[:, c, :], wpp_sb[:, c, half * 512:(half + 1) * 512],
                       c == 0, c == 1, ['pT', 'wpp'], ['pw1'])
            act(big[:, :], pw[0][:, :], AF.Sigmoid, ['pw0'], ['big'])
            tt('dve', big[:, :], big[:, :], pw[1][:, :], ALU.mult, ['big', 'pw1'], ['big'])
            tt('dve', big[:, :], big[:, :], h_g[:, ti, :], ALU.add, ['big', f'h{ti}'], ['big'])
            dma('sp', y_o[t], big[:, :], ['big'], ['y_o'])
        chk(11, gi == 0)
        chk(12, gi == 3)

    return S


_NC_CACHE = {}


def _consts():
    bf = ml_dtypes.bfloat16
    c = {}
    c['c_ident'] = np.eye(128, dtype=np.float32).astype(bf)
    c['c_identf'] = np.eye(128, dtype=np.float32)
    s = np.arange(128)
    c['c_mask'] = (s[:, None] <= s[None, :]).astype(np.float32).astype(bf)
    grp = s // 32
    m = (grp[:, None] == grp[None, :]) & (s[:, None] <= s[None, :])
    c['c_masks'] = m.astype(np.float32).astype(bf)
    ma = np.zeros((128, 4, 8, 8), np.float32)
    for b in range(4):
        for r in range(8):
            for t in range(8):
                if r <= t:
                    ma[b * 32 + 2 + r, b, :, t] = 1.0
    c['c_maska'] = ma.reshape(128, 4, 64).astype(bf)
    inv = 10000.0 ** (-np.arange(16, dtype=np.float32) * 2.0 / 32)
    pos = np.zeros((128, NT), np.float64)
    for i in range(16):
        pos[:, i] = i * 128 + s
    for p in range(128):
        t = (p % 32) - 2
        pos[p, ST] = 16384 + (t if 0 <= t < 8 else 0)
    ang = (pos.astype(np.float32)[:, :, None] * inv[None, None, :]).astype(np.float32)
    c['c_cos'] = np.cos(ang.astype(np.float64)).astype(np.float32)
    c['c_sin'] = np.sin(ang.astype(np.float64)).astype(np.float32)
    rs = np.ones((128, 2, 128), np.float32)
    rs[:, 0, 0] = 0.0
    for b in range(4):
        rs[:, 1, b * 32 + 2] = 0.0
    c['c_rs'] = rs
    c['c_iota'] = s.astype(np.float32).reshape(128, 1)
    rowm = np.zeros((128, 4), np.float32)
    for b in range(4):
        rowm[b * 32:(b + 1) * 32, b] = 1.0
    c['c_rowm'] = rowm
    hm = np.zeros((128, 2), np.float32)
    hm[:64, 0] = 1.0
    hm[64:, 1] = 1.0
    c['c_hm'] = hm
    sel = np.zeros((128, 128), np.float32)
    for b in range(4):
        for j in range(2):
            sel[b * 2 + j, b * 32 + j] = 1.0
    c['c_sel'] = sel.astype(bf)
    return c


def kernel(x_prompt, x_sample, cache_ckv, cache_krope, state_gla, state_conv, page_table,
           p_prompt, p_sample, g_mix, w_in, gla_w_a2, gla_b_a, gla_g_out, mla_g_qa, mla_w_qup,
           mla_g_qn, mla_g_qr, mla_g_kva, mla_g_kr, mla_w_kvup, mla_g_kn, w_o, g_ffn,
           ffn_w_gate, ffn_w_up, ffn_conv_w, ffn_conv_b, ffn_w_down, g_ple, ple_w_gate, ple_w_proj):
    f32 = np.float32
    A_ = lambda a: np.ascontiguousarray(np.asarray(a))
    if 'nc' not in _NC_CACHE:
        _NC_CACHE['nc'] = build_program()
    nc = _NC_CACHE['nc']
    consts = _consts()
    shared = {
        'cache_cat': np.ascontiguousarray(np.concatenate(
            [A_(cache_ckv).reshape(5120 * 128, 128)[:_NPHYS * 128],
             A_(cache_krope).reshape(5120 * 128, 32)[:_NPHYS * 128]], axis=1)),
        'w_in': A_(w_in[0]), 'gla_w_a2': A_(gla_w_a2[0]), 'gla_b_a': A_(gla_b_a[0]), 'gla_g_out': A_(gla_g_out[0]),
        'mla_g_qa': A_(mla_g_qa[0]), 'mla_w_qup': A_(mla_w_qup[0]), 'mla_g_qn': A_(mla_g_qn[0]),
        'mla_g_qr': A_(mla_g_qr[0]), 'mla_g_kva': A_(mla_g_kva[0]), 'mla_g_kr': A_(mla_g_kr[0]),
        'mla_w_kvup': A_(mla_w_kvup[0]), 'mla_g_kn': A_(mla_g_kn[0]), 'w_o': A_(w_o[0]), 'g_mix': A_(g_mix[0]),
        'g_ffn': A_(g_ffn[0]), 'g_ple': A_(g_ple[0]), 'ffn_w_gate': A_(ffn_w_gate[0]), 'ffn_w_up': A_(ffn_w_up[0]),
        'ffn_conv_w': A_(ffn_conv_w[0]), 'ffn_conv_b': A_(ffn_conv_b[0]), 'ffn_w_down': A_(ffn_w_down[0]),
        'ple_w_gate': A_(ple_w_gate[0]), 'ple_w_proj': A_(ple_w_proj[0]),
    }
    def _bc(a):
        a = A_(a)
        return np.ascontiguousarray(np.broadcast_to(a[None, :], (128, a.shape[0])))
    shared['b_gqn'] = _bc(mla_g_qn[0])
    shared['b_gqr'] = _bc(mla_g_qr[0])
    shared['b_gkva'] = _bc(mla_g_kva[0])
    shared['b_gkr'] = _bc(mla_g_kr[0])
    shared['b_gkn'] = _bc(mla_g_kn[0])
    shared.update(consts)
    xp = A_(x_prompt)
    xsm = A_(x_sample)
    pp = A_(p_prompt)[0]
    psm = A_(p_sample)[0]
    in_maps = []
    rows = np.array([b * 32 + 2 + t for b in range(4) for t in range(8)])
    for c in range(8):
        x = np.zeros((NT, 128, D), f32)
        x[:16] = xp[c].reshape(16, 128, D)
        x[ST][rows] = xsm[4 * c:4 * c + 4].reshape(32, D)
        p = np.zeros((NT, 128, 256), f32)
        p[:16] = pp[c].reshape(16, 128, 256)
        p[ST][rows] = psm[4 * c:4 * c + 4].reshape(32, 256)
        m = dict(shared)
        m['x'] = x
        m['p'] = p
        m['state_gla'] = A_(state_gla[0][4 * c:4 * c + 4]).reshape(4, 2, 128, 128)
        m['state_conv'] = A_(state_conv[0][4 * c:4 * c + 4]).reshape(8, DFF)
        m['page_table'] = A_(page_table[4 * c:4 * c + 4]).reshape(512).astype(np.int32)
        if _SMALLPA:
            m['page_table'] = m['page_table'] % 8
        m['b_pt'] = _bc(m['page_table'])
        in_maps.append(m)
    res = run_bass_kernel_spmd(nc, in_maps, core_ids=list(range(8))).results

    y_p = np.zeros((8, 2048, D), f32)
    y_s = np.zeros((32, 8, D), f32)
    ckv_p = np.zeros((1, 8, 2048, 128), f32)
    kr_p = np.zeros((1, 8, 2048, 32), f32)
    gla_p = np.zeros((1, 8, 4, 64, 128), f32)
    conv_p = np.zeros((1, 8, 2, DFF), f32)
    ckv_s = np.zeros((1, 32, 8, 128), f32)
    kr_s = np.zeros((1, 32, 8, 32), f32)
    gla_s = np.zeros((1, 32, 4, 64, 128), f32)
    conv_s = np.zeros((1, 32, 2, DFF), f32)
    for c in range(8):
        r = res[c]
        y = np.asarray(r['y'])
        y_p[c] = y[:16].reshape(2048, D)
        y_s[4 * c:4 * c + 4] = y[ST][rows].reshape(4, 8, D)
        ck = np.asarray(r['ckv_o'])
        ckv_p[0, c] = ck[:16].reshape(2048, 128)
        ckv_s[0, 4 * c:4 * c + 4] = ck[ST][rows].reshape(4, 8, 128)
        kr = np.asarray(r['kr_o'])
        kr_p[0, c] = kr[:16].reshape(2048, 32)
        kr_s[0, 4 * c:4 * c + 4] = kr[ST][rows].reshape(4, 8, 32)
        gla_p[0, c] = np.asarray(r['gla_p']).reshape(4, 64, 128)
        gla_s[0, 4 * c:4 * c + 4] = np.asarray(r['gla_s']).reshape(4, 4, 64, 128)
        conv_p[0, c] = np.asarray(r['conv_p'])
        conv_s[0, 4 * c:4 * c + 4] = np.asarray(r['conv_s']).reshape(4, 2, DFF)
    return (y_p, y_s, ckv_p, kr_p, gla_p, conv_p, ckv_s, kr_s, gla_s, conv_s)
```

```python
import numpy as np
import ml_dtypes
import concourse.bass as bass
import concourse.mybir as mybir
from concourse.bass_utils import run_bass_kernel_spmd

F32 = mybir.dt.float32
BF16 = mybir.dt.bfloat16
I32 = mybir.dt.int32
AF = mybir.ActivationFunctionType
ALU = mybir.AluOpType
AX = mybir.AxisListType

D = 1024
NT = 17
ST = 16
DFF = 2816
NF = 22
INC = 1968
EPS = 1e-6
MLA_SCALE = 96.0 ** -0.5
NPG = 128
O_Q, O_K, O_V, O_G, O_A, O_MQ = 0, 256, 512, 1024, 1536, 1552


class Sched:
    def __init__(self, sems):
        self.sems = sems
        nxt = iter(range(len(sems)))
        self.q = {e: [] for e in ('pe', 'act', 'dve', 'pool', 'sp')}
        self.cnt = {e: 0 for e in ('pe', 'act', 'dve', 'pool')}
        self.csem = {e: next(nxt) for e in ('pe', 'act', 'dve', 'pool')}
        self.dsems = {'sp': [next(nxt) for _ in range(24)], 'pool': [next(nxt) for _ in range(2)],
                      'act': [next(nxt) for _ in range(8)]}
        self.dcnt = {'sp': 0, 'pool': 0, 'act': 0}
        self.lastw = {}
        self.readers = {}
        self.waited = {e: {} for e in self.q}
        self.all_tokens = {}
        self._bar = None

    def op(self, eng, fn, reads=(), writes=(), dma=False):
        deps = {}

        def add(tok):
            if tok is None:
                return
            for s, v in tok.items() if isinstance(tok, dict) else [tok]:
                if deps.get(s, 0) < v:
                    deps[s] = v
        if self._bar:
            add(self._bar)
        for k in reads:
            add(self.lastw.get(k))
        for k in writes:
            add(self.lastw.get(k))
            add(self.readers.get(k))
        if dma:
            pool = self.dsems[eng]
            i = self.dcnt[eng]
            self.dcnt[eng] += 1
            sem = pool[i % len(pool)]
            val = 16 * (i // len(pool) + 1)
            add((sem, val - 16))
            inc = 16
        else:
            self.cnt[eng] += 1
            sem = self.csem[eng]
            val = self.cnt[eng]
            inc = 1
        waits = []
        w = self.waited[eng]
        for s, v in deps.items():
            if v <= 0:
                continue
            if eng == 'pe' and s == self.csem['pe']:
                continue
            if w.get(s, 0) >= v:
                continue
            w[s] = v
            waits.append((s, v))
        self.q[eng].append((waits, fn, sem, inc))
        tok = (sem, val)
        self.all_tokens[sem] = max(self.all_tokens.get(sem, 0), val)
        for k in reads:
            r = self.readers.setdefault(k, {})
            if r.get(sem, 0) < val:
                r[sem] = val
        for k in writes:
            self.lastw[k] = tok
            self.readers[k] = {}
        return tok

    def barrier(self):
        snap = dict(self.all_tokens)
        for k in list(self.lastw.keys()):
            self.lastw[k] = snap
            self.readers[k] = {}
        self._bar = snap

    def emit(self, engname, engine):
        for waits, fn, sem, inc in self.q[engname]:
            for s, v in waits:
                engine.wait_ge(self.sems[s], v)
            ins = fn(engine)
            ins.then_inc(self.sems[sem], inc)

    def final_wait(self, engine):
        for s, v in self.all_tokens.items():
            engine.wait_ge(self.sems[s], v)


class _Stop(Exception):
    pass


import os
_STOP = float(os.environ.get("KSTOP", "0")) or None
_SKIP_PA = bool(os.environ.get("KSKIP"))
_SMALLPA = bool(os.environ.get("KSMALLPA"))
_NPHYS = 8 if (_SMALLPA or _SKIP_PA or (_STOP is not None and _STOP <= 7)) else 5120


_HOOK = [None]


def chk(stage, active=True):
    if not active:
        return
    if _STOP is not None and stage >= _STOP:
        if _HOOK[0] is not None and os.environ.get("KDBG") == "5":
            _HOOK[0]()
        raise _Stop()


def build_program():
    nc = bass.Bass("TRN2", target_bir_lowering=False)
    S = None
    try:
        S = _build_body(nc)
    except _Stop as e:
        S = e.args[0] if e.args else S
    _emit(nc, _S_HOLD[0])
    return nc


_S_HOLD = [None]


def _emit(nc, S):
    with nc.Block() as block:
        @block.sync
        def _(e):
            S.emit('sp', e)
            S.final_wait(e)

        @block.tensor
        def _(e):
            S.emit('pe', e)

        @block.scalar
        def _(e):
            S.emit('act', e)

        @block.vector
        def _(e):
            S.emit('dve', e)

        @block.gpsimd
        def _(e):
            S.emit('pool', e)


def _build_body(nc):

    def din(name, shape, dt=F32):
        return nc.dram_tensor(name, list(shape), dt, kind="ExternalInput").ap()

    def dout(name, shape, dt=F32):
        return nc.dram_tensor(name, list(shape), dt, kind="ExternalOutput").ap()

    x_d = din("x", [NT, 128, D])
    p_d = din("p", [NT, 128, 256])
    cc_d = din("cache_cat", [_NPHYS * 128, 160])
    sgla_d = din("state_gla", [4, 2, 128, 128])
    sconv_d = din("state_conv", [8, DFF])
    pt_d = din("page_table", [512], I32)
    w_in_d = din("w_in", [D, INC])
    wa2_d = din("gla_w_a2", [16, 256])
    ba_d = din("gla_b_a", [256])
    gout_d = din("gla_g_out", [128])
    gqa_d = din("mla_g_qa", [256])
    wqup_d = din("mla_w_qup", [256, 768])
    gqn_d = din("mla_g_qn", [64])
    gqr_d = din("mla_g_qr", [32])
    gkva_d = din("mla_g_kva", [128])
    gkr_d = din("mla_g_kr", [32])
    wkv_d = din("mla_w_kvup", [128, 1024])
    gkn_d = din("mla_g_kn", [64])
    wo_d = din("w_o", [D, D])
    gmix_d = din("g_mix", [D])
    gffn_d = din("g_ffn", [D])
    gple_d = din("g_ple", [D])
    wg_d = din("ffn_w_gate", [D, DFF])
    wu_d = din("ffn_w_up", [D, DFF])
    cw_d = din("ffn_conv_w", [3, DFF])
    cb_d = din("ffn_conv_b", [DFF])
    wd_d = din("ffn_w_down", [DFF, D])
    wpg_d = din("ple_w_gate", [D, D])
    wpp_d = din("ple_w_proj", [256, D])
    gqnb_d = din("b_gqn", [128, 64])
    gqrb_d = din("b_gqr", [128, 32])
    gkvab_d = din("b_gkva", [128, 128])
    gkrb_d = din("b_gkr", [128, 32])
    gknb_d = din("b_gkn", [128, 64])
    ptb_d = din("b_pt", [128, 512], I32)
    c_ident_d = din("c_ident", [128, 128], BF16)
    c_identf_d = din("c_identf", [128, 128])
    c_mask_d = din("c_mask", [128, 128], BF16)
    c_masks_d = din("c_masks", [128, 128], BF16)
    c_maska_d = din("c_maska", [128, 4, 64], BF16)
    c_cos_d = din("c_cos", [128, NT, 16])
    c_sin_d = din("c_sin", [128, NT, 16])
    c_rs_d = din("c_rs", [128, 2, 128])
    c_iota_d = din("c_iota", [128, 1])
    c_sel_d = din("c_sel", [128, 128], BF16)
    c_rowm_d = din("c_rowm", [128, 4])
    c_hm_d = din("c_hm", [128, 2])

    y_o = dout("y", [NT, 128, D])
    ckv_o = dout("ckv_o", [NT, 128, 128])
    kr_o = dout("kr_o", [NT, 128, 32])
    glap_o = dout("gla_p", [2, 128, 128])
    glas_o = dout("gla_s", [4, 2, 128, 128])
    convp_o = dout("conv_p", [2, DFF])
    convs_o = dout("conv_s", [8, DFF])

    sems = [nc.alloc_semaphore(f"s{i}") for i in range(60)]
    S = Sched(sems)
    _S_HOLD[0] = S

    class Arena:
        def __init__(self):
            self.base = 16512
            self.top = 229344
            self.off = self.base
            self.n = 0

        def alloc(self, name, shape, dt):
            sz = 1
            for s in shape[1:]:
                sz *= s
            sz *= mybir.dt.size(dt)
            sz = (sz + 31) // 32 * 32
            assert self.off + sz <= self.top, f"SBUF overflow at {name}: {self.off + sz}"
            self.n += 1
            t = nc.alloc_sbuf_tensor_at(f"{name}_{self.n}", list(shape), dt, offset=self.off)
            self.off += sz
            return t

    A = Arena()

    ptr = [nc.alloc_psum_tensor(f"ptr{i}", [128, 1024], BF16) for i in range(2)]
    pm = [nc.alloc_psum_tensor(f"pm{i}", [128, 512], F32) for i in range(2)]
    pw = [nc.alloc_psum_tensor(f"pw{i}", [128, 1024], F32) for i in range(2)]

    def dma(eng, out, in_, reads=(), writes=(), **kw):
        if out.dtype != in_.dtype:
            eng = 'pool'
        return S.op(eng, lambda e: e.dma_start(out=out, in_=in_, **kw), reads, writes, dma=True)

    def mm(out, lhsT, rhs, start, stop, reads, writes, **kw):
        return S.op('pe', lambda e: e.matmul(out, lhsT, rhs, start=start, stop=stop, **kw), reads, writes)

    def tr(out, in_, ident, reads, writes):
        return S.op('pe', lambda e: e.transpose(out, in_, ident), reads, writes)

    def act(out, in_, func, reads, writes, **kw):
        return S.op('act', lambda e: e.activation(out, in_, func, **kw), reads, writes)

    def tt(eng, out, in0, in1, op, reads, writes):
        return S.op(eng, lambda e: e.tensor_tensor(out, in0, in1, op), reads, writes)

    def ts(eng, out, in0, s1, s2, op0, op1, reads, writes):
        if s2 is None:
            return S.op(eng, lambda e: e.tensor_scalar(out, in0, s1, None, op0), reads, writes)
        return S.op(eng, lambda e: e.tensor_scalar(out, in0, s1, s2, op0, op1), reads, writes)

    def stt(out, in0, sc, in1, op0, op1, reads, writes):
        return S.op('dve', lambda e: e.scalar_tensor_tensor(out, in0, sc, in1, op0, op1), reads, writes)

    def cp(eng, out, in_, reads, writes):
        if eng == 'act':
            return S.op('act', lambda e: e.copy(out, in_), reads, writes)
        return S.op(eng, lambda e: e.tensor_copy(out, in_), reads, writes)

    def memset(eng, ap, val, writes):
        return S.op(eng, lambda e: e.memset(ap, val), (), writes)

    def bc_last(ap, n):
        sh = list(ap.shape)
        return ap.unsqueeze(len(sh)).broadcast_to(sh + [n])

    def bc_mid(ap, n):
        sh = list(ap.shape)
        return ap.unsqueeze(1).broadcast_to([sh[0], n] + sh[1:])

    ident = A.alloc("ident", [128, 128], BF16)
    identf = A.alloc("identf", [128, 128], F32)
    mask01 = A.alloc("mask01", [128, 128], BF16)
    masks = A.alloc("masks", [128, 128], BF16)
    maska = A.alloc("maska", [128, 4, 64], BF16)
    ones_bf = A.alloc("ones_bf", [128, 128], BF16)
    eps_c = A.alloc("eps_c", [128, 1], F32)
    gmixT = A.alloc("gmixT", [128, 8], F32)
    gffnT = A.alloc("gffnT", [128, 8], F32)
    gpleT = A.alloc("gpleT", [128, 8], F32)
    ogT = A.alloc("ogT", [128, 4, NT * 128], BF16)
    omT = A.alloc("omT", [128, 4, NT * 128], BF16)
    stat = A.alloc("stat", [128, 8], F32)
    xs = A.alloc("xs", [128, D], BF16)
    persist_off = A.off

    dma('sp', ident[:, :], c_ident_d, (), ['ident'])
    def _hook():
        S.barrier()
        dma('sp', y_o[14][:, 0:128], identf[:, :], ['identf'], ['y_o'])
    _HOOK[0] = _hook
    if os.environ.get("KDBG") == "5":
        dma('sp', identf[:, :], c_identf_d, (), ['identf'])
        dma('sp', y_o[15][:, 0:128], identf[:, :], ['identf'], ['y_o'])
    dma('sp', identf[:, :], c_identf_d, (), ['identf'])
    dma('sp', mask01[:, :], c_mask_d, (), ['mask01'])
    dma('sp', masks[:, :], c_masks_d, (), ['masks'])
    dma('sp', maska[:, :, :], c_maska_d, (), ['maska'])
    for gt, gd, k in ((gmixT, gmix_d, 'gmixT'), (gffnT, gffn_d, 'gffnT'), (gpleT, gple_d, 'gpleT')):
        dma('sp', gt[:, :], gd.rearrange("(c p) -> p c", p=128), (), [k], allow_slow_non_contiguous=True)
    memset('dve', ones_bf[:, :], 1.0, ['ones'])
    memset('dve', eps_c[:, :], EPS, ['eps'])

    def rstd_chain(ms_ap, out_ap, inv_n, rk, wk):
        act(out_ap, ms_ap, AF.Ln, list(rk) + ['eps'], wk, bias=eps_c[0:ms_ap.shape[0], 0:1], scale=inv_n)
        act(out_ap, out_ap, AF.Exp, wk, wk, scale=-0.5)

    def norm_tile(src, src_keys, gT, gkey, dst, dst_keys, pslot):
        act(xs[:, :], src, AF.Square, src_keys, ['xs', 'stat'], accum_out=stat[:, 0:1])
        rstd_chain(stat[:, 0:1], stat[:, 1:2], 1.0 / D, ['stat'], ['stat'])
        ts('dve', xs[:, :], src, stat[:, 1:2], None, ALU.mult, None, list(src_keys) + ['stat'], ['xs'])
        pk = f'ptr{pslot}'
        for k in range(8):
            tr(ptr[pslot][:, k * 128:(k + 1) * 128], xs[:, k * 128:(k + 1) * 128], ident[:, :],
               ['xs', 'ident'], [pk])
        tt('dve', dst, ptr[pslot][:, :].rearrange("p (c t) -> p c t", c=8), bc_last(gT[:, :], 128),
           ALU.mult, [pk, gkey], dst_keys)

    w_in_sb = A.alloc("w_in", [128, 8, INC], BF16)
    wqup_sb = A.alloc("wqup", [128, 2, 768], BF16)
    wkv_sb = A.alloc("wkv", [128, 1024], BF16)
    wkn_sb = A.alloc("wkn", [128, 512], BF16)
    wa2_sb = A.alloc("wa2", [16, 256], F32)
    nba = A.alloc("nba", [128, 2], F32)
    goutc = A.alloc("goutc", [128, 1], F32)
    gqa_c = A.alloc("gqa", [128, 2], F32)
    gknc = A.alloc("gknc", [128, 1], F32)
    Gqn = A.alloc("Gqn", [128, 64], F32)
    Gqr = A.alloc("Gqr", [128, 32], F32)
    Gkva = A.alloc("Gkva", [128, 128], F32)
    Gkr = A.alloc("Gkr", [128, 32], F32)
    Gkn = A.alloc("Gkn", [128, 64], F32)
    cos_sb = A.alloc("cos", [128, NT, 16], F32)
    sin_sb = A.alloc("sin", [128, NT, 16], F32)
    rs_sb = A.alloc("rs", [128, 2, 128], F32)
    iota_sb = A.alloc("iota", [128, 1], F32)
    x_sb = A.alloc("x_sb", [128, 2, D], F32)
    nT = A.alloc("nT", [128, 2, 8, 128], BF16)
    gaT = A.alloc("gaT", [16, 128], F32)
    sp_ = A.alloc("sp", [128, 2, 128], F32)
    bpos = A.alloc("bpos", [128, 2, 128], F32)
    etmp = A.alloc("etmp", [128, 2, 128], F32)
    nbl = A.alloc("nbl", [128, 8], F32)
    ebl = A.alloc("ebl", [128, 8], F32)
    qdT = A.alloc("qdT", [128, 4, 128], BF16)
    ke_z = A.alloc("ke_z", [128, 4, 256], BF16)
    rowm = A.alloc("rowm", [128, 4], F32)
    hm = A.alloc("hm", [128, 2], F32)
    wvp_sb = A.alloc("wvp", [128, 4, 128], BF16)
    kiT = A.alloc("kiT", [128, 2, 128], BF16)
    keT = A.alloc("keT", [128, 2, 128], BF16)
    ke_tok = A.alloc("ke_tok", [128, 256], BF16)
    v_tok = A.alloc("v_tok", [128, 512], BF16)
    sgT = A.alloc("sgT", [128, 4, 128], BF16)
    AT = A.alloc("AT", [128, 4, 128], BF16)
    S_f = A.alloc("S_f", [128, 2, 128], F32)
    S_b = A.alloc("S_b", [128, 2, 128], BF16)
    S0_f = A.alloc("S0_f", [128, 4, 2, 128], F32)
    S0_b = A.alloc("S0_b", [128, 4, 2, 128], BF16)
    osq = A.alloc("osq", [128, 512], BF16)
    orst = A.alloc("orst", [128, 512], F32)
    u_sb = A.alloc("u_sb", [128, 416], F32)
    ms = A.alloc("ms", [128, 32], F32)
    rst = A.alloc("rst", [128, 32], F32)
    cq = A.alloc("cq", [128, 256], BF16)
    cqT = A.alloc("cqT", [128, 2, 128], BF16)
    ckv_f = A.alloc("ckv_f", [128, 2, 128], F32)
    ckv_b = A.alloc("ckv_b", [128, 132], BF16)
    ckvT = A.alloc("ckvT", [128, 128], BF16)
    sqq = A.alloc("sqq", [128, 768], F32)
    sqk = A.alloc("sqk", [128, 8, 64], F32)
    Rn = A.alloc("Rn", [128, 9, 32], F32)
    Rr = A.alloc("Rr", [128, 2, 9, 32], F32)
    rt1 = A.alloc("rt1", [128, 9, 16], F32)
    rt2 = A.alloc("rt2", [128, 9, 16], F32)
    q_tok = A.alloc("q_tok", [128, 8, 64], BF16)
    qr_tok = A.alloc("qr_tok", [128, 8, 32], BF16)
    k_tok = A.alloc("k_tok", [128, 8, 64], BF16)
    kr_tok = A.alloc("kr_tok", [128, 32], BF16)
    QnT = A.alloc("QnT", [128, 8, 128], BF16)
    QrT = A.alloc("QrT", [128, 8, 128], BF16)
    KnT = A.alloc("KnT", [128, 4, NT * 128], BF16)
    KrT = A.alloc("KrT", [128, NT * 128], BF16)
    Vext = A.alloc("Vext", [128, 16, 8, 65], BF16)
    PT = A.alloc("PT", [128, 2, 4, 128], BF16)
    rl = A.alloc("rl", [128, 8], F32)
    o_tok = A.alloc("o_tok", [128, 8, 64], BF16)
    IDX = A.alloc("IDX", [128, 512], I32)
    PTi = A.alloc("PTi", [128, 512], I32)
    Cbuf = A.alloc("Cbuf", [128, 2, 4, 160], F32)
    Cb = A.alloc("Cb", [128, 2, 4, 132], BF16)
    KRb = A.alloc("KRb", [128, 2, 4, 32], BF16)
    CT = A.alloc("CT", [128, 2, 4, 128], BF16)
    KRT = A.alloc("KRT", [128, 2, 4, 128], BF16)
    sqs = A.alloc("sqs", [128, 2, 1024], BF16)
    ssum = A.alloc("ssum", [128, 32], F32)
    rstk = A.alloc("rstk", [128, 32], F32)
    rstk2 = A.alloc("rstk2", [128, 2, 32], F32)
    sc1 = A.alloc("sc1", [128, 256], F32)
    sc2 = A.alloc("sc2", [128, 256], F32)
    PTs = A.alloc("PTs", [128, 2, 4, 64], BF16)
    WkT = A.alloc("WkT", [128, 4, 128], BF16)
    qgT = A.alloc("qgT", [128, 8, 128], BF16)
    qabsT = A.alloc("qabsT", [128, 8, 128], BF16)
    acc_sb = A.alloc("acc_sb", [64, 132], F32)
    accn = A.alloc("accn", [128, 128], BF16)
    accnT = A.alloc("accnT", [128, 128], BF16)

    for k in range(8):
        dma('sp', w_in_sb[:, k, :], w_in_d[k * 128:(k + 1) * 128, :], (), ['w_in'])
    dma('sp', wqup_sb[:, :, :], wqup_d.rearrange("(c p) n -> p c n", p=128), (), ['wqup'])
    dma('sp', wkv_sb[:, :], wkv_d, (), ['wkv'])
    dma('sp', wa2_sb[:, :], wa2_d, (), ['wa2'])
    dma('sp', nba[:, :], ba_d.rearrange("(c p) -> p c", p=128), (), ['nba'], allow_slow_non_contiguous=True)
    dma('sp', goutc[:, :], gout_d.rearrange("(p o) -> p o", o=1), (), ['goutc'], allow_slow_non_contiguous=True)
    dma('sp', gqa_c[:, :], gqa_d.rearrange("(c p) -> p c", p=128), (), ['gqa'], allow_slow_non_contiguous=True)
    dma('sp', gknc[0:64, :], gkn_d.rearrange("(p o) -> p o", o=1), (), ['gknc'], allow_slow_non_contiguous=True)
    dma('sp', gknc[64:128, :], gkn_d.rearrange("(p o) -> p o", o=1), (), ['gknc'], allow_slow_non_contiguous=True)
    for gt, gd, k in ((Gqn, gqnb_d, 'Gqn'), (Gqr, gqrb_d, 'Gqr'), (Gkva, gkvab_d, 'Gkva'), (Gkr, gkrb_d, 'Gkr'),
                      (Gkn, gknb_d, 'Gkn')):
        dma('sp', gt[:, :], gd, (), [k])
    dma('sp', cos_sb[:, :, :], c_cos_d, (), ['cos'])
    dma('sp', sin_sb[:, :, :], c_sin_d, (), ['sin'])
    dma('sp', rs_sb[:, :, :], c_rs_d, (), ['rs'])
    dma('sp', iota_sb[:, :], c_iota_d, (), ['iota'])
    dma('sp', PTi[:, :], ptb_d, (), ['PTi'])
    for b in range(4):
        for pr in range(2):
            dma('sp', S0_f[:, b, pr, :], sgla_d[b, pr], (), ['S0_f'])
    ts('dve', nba[:, :], nba[:, :], -1.0, None, ALU.mult, None, ['nba'], ['nba'])
    for c in range(2):
        ts('dve', wqup_sb[:, c, :], wqup_sb[:, c, :], gqa_c[:, c:c + 1], None, ALU.mult, None,
           ['wqup', 'gqa'], ['wqup'])
    cp('dve', wkn_sb[:, :].rearrange("p (h d) -> p h d", h=8),
       wkv_sb[:, :].rearrange("p (h e) -> p h e", h=8)[:, :, 0:64], ['wkv'], ['wkn'])
    memset('dve', S_f[:, :, :], 0.0, ['S_f'])
    memset('dve', nbl[:, :], 0.0, ['nbl'])
    memset('dve', qdT[:, :, :], 0.0, ['qdT'])
    memset('dve', QnT[:, :, :], 0.0, ['QnT'])
    memset('dve', QrT[:, :, :], 0.0, ['QrT'])
    memset('dve', KrT[:, :], 0.0, ['KrT'])
    memset('dve', KRT[:, :, :, :], 0.0, ['KRT0', 'KRT1'])
    memset('dve', accn[:, :], 0.0, ['accn'])
    dma('sp', rowm[:, :], c_rowm_d, (), ['rowm'])
    dma('sp', hm[:, :], c_hm_d, (), ['hm'])
    cp('dve', wvp_sb[:, :, :].rearrange("p c (a d) -> p c a d", a=2),
       wkv_sb[:, :].rearrange("p (c a e) -> p c a e", c=4, a=2)[:, :, :, 64:128], ['wkv'], ['wvp'])
    memset('dve', S_b[:, :, :], 0.0, ['S_b'])
    cp('dve', S0_b[:, :, :, :], S0_f[:, :, :, :], ['S0_f'], ['S0_b'])
    memset('dve', ckv_b[:, 128:132], 1.0, ['ckv_b1'])
    memset('dve', Cb[:, :, :, 128:132], 1.0, ['Cb1'])
    memset('dve', Vext[:, :, :, 64:65], 1.0, ['Vext1'])
    memset('dve', omT[:, :, ST * 128:], 0.0, ['omT'])
    ts('dve', IDX[:, :], PTi[:, :], 128.0, iota_sb[:, 0:1], ALU.mult, ALU.add, ['PTi', 'iota'], ['IDX'])

    def load_x(i):
        dma('sp', x_sb[:, i % 2, :], x_d[i], (), [f'x{i % 2}'])

    if os.environ.get("KDBG") == "2":
        load_x(0)
        dma('sp', y_o[0], x_sb[:, 0, :], ['x0'], ['y_o'])
        dma('pool', y_o[3][:, 0:1024], w_in_sb[:, 0, 0:1024], ['w_in'], ['y_o'])
        dma('sp', y_o[4][:, 0:128], identf[:, :], ['identf'], ['y_o'])
    chk(1)
    load_x(0)

    pmi = [0]

    def next_pm():
        pmi[0] ^= 1
        return pmi[0]

    for i in range(NT):
        samp = (i == ST)
        sl = i % 2
        xk = f'x{sl}'
        nk = f'nT{sl}'
        nTi = nT[:, sl, :, :]
        tcol = slice(i * 128, (i + 1) * 128)
        if i + 1 < NT:
            load_x(i + 1)
        norm_tile(x_sb[:, sl, :], [xk], gmixT, 'gmixT', nTi, [nk], 0)
        if i == 0 and os.environ.get("KDBG") == "3":
            dma('pool', y_o[2][:, 0:1024], nT[:, 0, :, :].rearrange("p c t -> p (c t)"), [nk], ['y_o'])
            dma('pool', y_o[4][:, 0:1024], xs[:, :], ['xs'], ['y_o'])
            dma('sp', y_o[5][:, 0:8], stat[:, :], ['stat'], ['y_o'])
            dma('sp', y_o[0], x_sb[:, 0, :], ['x0'], ['y_o'])
        if i == 0 and os.environ.get("KDBG") == "4":
            stg = x_sb[:, 1, :]
            cp('dve', stg, xs[:, :], ['xs'], ['x1'])
            dma('sp', y_o[4], stg, ['x1'], ['y_o'])
            cp('dve', stg, nT[:, 0, :, :].rearrange("p c t -> p (c t)"), [nk], ['x1'])
            dma('sp', y_o[2], stg, ['x1'], ['y_o'])
            cp('dve', x_sb[:, 1, 0:8], stat[:, :], ['stat'], ['x1'])
            cp('dve', x_sb[:, 1, 8:16], gmixT[:, :], ['gmixT'], ['x1'])
            dma('sp', y_o[5], stg, ['x1'], ['y_o'])
        chk(2, i == 0)

        j = next_pm()
        for k in range(8):
            mm(pm[j][0:16, 0:128], w_in_sb[:, k, O_A:O_A + 16], nTi[:, k, :], k == 0, k == 7,
               ['w_in', nk], [f'pm{j}'])
        cp('act', gaT[:, :], pm[j][0:16, 0:128], [f'pm{j}'], ['gaT'])
        chk(2.1, i == 0)
        j = next_pm()
        for pr in range(2):
            mm(pm[j][:, pr * 128:(pr + 1) * 128], wa2_sb[:, pr * 128:(pr + 1) * 128], gaT[:, :], True, True,
               ['wa2', 'gaT'], [f'pm{j}'])
        for pr in range(2):
            act(sp_[:, pr, :], pm[j][:, pr * 128:(pr + 1) * 128], AF.Exp, [f'pm{j}', 'nba'], ['sp'],
                bias=nba[:, pr:pr + 1], scale=-1.0)
        act(sp_[:, :, :], sp_[:, :, :], AF.Ln, ['sp'], ['sp'], bias=1.0, scale=1.0)
        chk(2.3, i == 0)
        rsi = 1 if samp else 0
        for pr in range(2):
            S.op('dve', lambda e, pr=pr, rsi=rsi: e.tensor_tensor_scan(bpos[:, pr, :], rs_sb[:, rsi, :], sp_[:, pr, :],
                                                              0.0, ALU.mult, ALU.add),
                 ['sp', 'rs'], ['bpos'])
        segs = [(b * 32, 32, b * 32 + 9, b) for b in range(4)] if samp else [(0, 128, 127, 0)]
        chk(2.4, i == 0)
        for (c0, ncol, lc, si) in segs:
            ts('dve', nbl[:, si * 2:si * 2 + 2], bpos[:, :, lc], -1.0 / 16, None, ALU.mult, None,
               ['bpos'], ['nbl'])
        act(ebl[:, :], nbl[:, :], AF.Exp, ['nbl'], ['ebl'])
        chk(2.5, i == 0)
        j = next_pm()
        for pr in range(2):
            for k in range(8):
                mm(pm[j][:, pr * 128:(pr + 1) * 128], w_in_sb[:, k, O_Q + pr * 128:O_Q + (pr + 1) * 128],
                   nTi[:, k, :], k == 0, k == 7, ['w_in', nk], [f'pm{j}'])
        act(etmp[:, :, :], bpos[:, :, :], AF.Exp, ['bpos'], ['etmp'], scale=-1.0 / 16)
        for hh in range(2):
            r = slice(hh * 64, hh * 64 + 64)
            stt(qdT[r, hh::2, :], pm[j][r, 0:256].rearrange("p (a t) -> p a t", a=2), 0.125, etmp[r, :, :],
                ALU.mult, ALU.mult, [f'pm{j}', 'etmp'], ['qdT'])
        chk(2.6, i == 0)
        j = next_pm()
        for pr in range(2):
            for k in range(8):
                mm(pm[j][:, pr * 128:(pr + 1) * 128], w_in_sb[:, k, O_K + pr * 128:O_K + (pr + 1) * 128],
                   nTi[:, k, :], k == 0, k == 7, ['w_in', nk], [f'pm{j}'])
        act(etmp[:, :, :], bpos[:, :, :], AF.Exp, ['bpos', 'qdT'], ['etmp'], scale=1.0 / 16)
        tt('dve', kiT[:, :, :], pm[j][:, 0:256].rearrange("p (a t) -> p a t", a=2), etmp[:, :, :], ALU.mult,
           [f'pm{j}', 'etmp'], ['kiT'])
        for (c0, ncol, lc, si) in segs:
            for pr in range(2):
                act(etmp[:, pr, c0:c0 + ncol], bpos[:, pr, c0:c0 + ncol], AF.Exp, ['bpos', 'nbl', 'kiT'],
                    ['etmp'], bias=nbl[:, si * 2 + pr:si * 2 + pr + 1], scale=1.0 / 16)
        tt('dve', keT[:, :, :], pm[j][:, 0:256].rearrange("p (a t) -> p a t", a=2), etmp[:, :, :], ALU.mult,
           [f'pm{j}', 'etmp'], ['keT'])
        for pr in range(2):
            tr(ptr[1][:, pr * 128:(pr + 1) * 128], keT[:, pr, :], ident[:, :], ['keT', 'ident'], ['ptr1'])
        cp('dve', ke_tok[:, :], ptr[1][:, 0:256], ['ptr1'], ['ke_tok'])
        chk(2.7, i == 0)
        j = next_pm()
        for c in range(4):
            for k in range(8):
                mm(pm[j][:, c * 128:(c + 1) * 128], w_in_sb[:, k, O_G + c * 128:O_G + (c + 1) * 128],
                   nTi[:, k, :], k == 0, k == 7, ['w_in', nk], [f'pm{j}'])
        act(sgT[:, :, :], pm[j][:, :].rearrange("p (c t) -> p c t", c=4), AF.Silu, [f'pm{j}'], ['sgT'])
        j = next_pm()
        for k in range(8):
            mm(pm[j][:, :], nTi[:, k, :], w_in_sb[:, k, O_V:O_V + 512], k == 0, k == 7, ['w_in', nk], [f'pm{j}'])
        cp('act', v_tok[:, :], pm[j][:, :], [f'pm{j}'], ['v_tok'])
        chk(2.8, i == 0)
        j = next_pm()
        for h in range(4):
            pr = h // 2
            mm(pm[j][:, h * 128:(h + 1) * 128], kiT[:, pr, :], qdT[:, h, :], True, True,
               ['kiT', 'qdT'], [f'pm{j}'])
        mk, mkey = (masks, 'masks') if samp else (mask01, 'mask01')
        tt('dve', AT[:, :, :], pm[j][:, :].rearrange("p (h t) -> p h t", h=4), bc_mid(mk[:, :], 4), ALU.mult,
           [f'pm{j}', mkey], ['AT'])
        chk(2.9, i == 0)
        j = next_pm()
        for h in range(4):
            pr, b0 = h // 2, (h % 2) * 64
            oh = pm[j][:, h * 128:(h + 1) * 128]
            mm(oh, v_tok[:, h * 128:(h + 1) * 128], AT[:, h, :], True, False, ['v_tok', 'AT'], [f'pm{j}'])
            if not samp:
                mm(oh, S_b[:, pr, :], qdT[:, h, :], False, True, ['S_b', 'qdT'], [f'pm{j}'])
            else:
                for b in range(4):
                    mm(pm[j][:, h * 128 + b * 32:h * 128 + (b + 1) * 32], S0_b[:, b, pr, :],
                       qdT[:, h, b * 32:(b + 1) * 32], False, b == 3, ['S0_b', 'qdT'], [f'pm{j}'])
        jo = j
        chk(2.95, i == 0)
        act(osq[:, :], pm[jo][:, :], AF.Square, [f'pm{jo}'], ['osq'])
        j = next_pm()
        mm(pm[j][:, :], ones_bf[:, :], osq[:, :], True, True, ['ones', 'osq'], [f'pm{j}'])
        rstd_chain(pm[j][:, :], orst[:, :], 1.0 / 128, [f'pm{j}'], ['orst'])
        tt('dve', orst[:, :], pm[jo][:, :], orst[:, :], ALU.mult, [f'pm{jo}', 'orst'], ['orst'])
        stt(ogT[:, :, tcol], orst[:, :].rearrange("p (h t) -> p h t", h=4), goutc[:, 0:1], sgT[:, :, :],
            ALU.mult, ALU.mult, ['orst', 'goutc', 'sgT'], ['ogT'])
        chk(2.97, i == 0)
        if not samp:
            for pr in range(2):
                j = next_pm()
                mm(pm[j][:, 0:256], ke_tok[:, pr * 128:(pr + 1) * 128], v_tok[:, pr * 256:(pr + 1) * 256],
                   True, True, ['ke_tok', 'v_tok'], [f'pm{j}'])
                ts('dve', orst[:, 0:128], pm[j][:, 0:128], hm[:, 0:1], None, ALU.mult, None, [f'pm{j}', 'hm'], ['orst'])
                stt(orst[:, 0:128], pm[j][:, 128:256], hm[:, 1:2], orst[:, 0:128], ALU.mult, ALU.add,
                    [f'pm{j}', 'hm', 'orst'], ['orst'])
                stt(S_f[:, pr, :], S_f[:, pr, :], ebl[:, pr:pr + 1], orst[:, 0:128], ALU.mult, ALU.add,
                    ['S_f', 'ebl', 'orst'], ['S_f'])
                chk(2.98, i == 0 and pr == 0)
                chk(2.99, i == 0 and pr == 1)
            cp('dve', S_b[:, :, :], S_f[:, :, :], ['S_f'], ['S_b'])
            if i == ST - 1:
                for pr in range(2):
                    dma('sp', glap_o[pr], S_f[:, pr, :], ['S_f'], ['glap_o'])
        else:
            for b in range(4):
                ts('dve', ke_z[:, b, :], ke_tok[:, :], rowm[:, b:b + 1], None, ALU.mult, None,
                   ['ke_tok', 'rowm'], ['ke_z'])
            for b in range(4):
                for pr in range(2):
                    j = next_pm()
                    mm(pm[j][:, 0:256], ke_z[:, b, pr * 128:(pr + 1) * 128],
                       v_tok[:, pr * 256:(pr + 1) * 256], True, True, ['ke_z', 'v_tok'], [f'pm{j}'])
                    ts('dve', orst[:, 0:128], pm[j][:, 0:128], hm[:, 0:1], None, ALU.mult, None, [f'pm{j}', 'hm'],
                       ['orst'])
                    stt(orst[:, 0:128], pm[j][:, 128:256], hm[:, 1:2], orst[:, 0:128], ALU.mult, ALU.add,
                        [f'pm{j}', 'hm', 'orst'], ['orst'])
                    stt(S0_f[:, b, pr, :], S0_f[:, b, pr, :], ebl[:, b * 2 + pr:b * 2 + pr + 1], orst[:, 0:128],
                        ALU.mult, ALU.add, ['S0_f', 'ebl', 'orst'], ['S0_f'])
                    dma('sp', glas_o[b, pr], S0_f[:, b, pr, :], ['S0_f'], ['glas_o'])

        chk(3, i == 0)
        j = next_pm()
        for k in range(8):
            mm(pm[j][:, 0:416], nTi[:, k, :], w_in_sb[:, k, O_MQ:O_MQ + 416], k == 0, k == 7,
               ['w_in', nk], [f'pm{j}'])
        cp('dve', u_sb[:, :], pm[j][:, 0:416], [f'pm{j}'], ['u_sb'])
        chk(3.1, i == 0)
        for (a, b_, col, n) in ((0, 256, 24, 256), (256, 384, 25, 128), (384, 416, 26, 32)):
            if True:
                act(sqq[:, a:b_], u_sb[:, a:b_], AF.Square, ['u_sb'], ['sqq', 'ms'], accum_out=ms[:, col:col + 1],
                    scale=float(n) ** -0.5)
            else:
                act(sqq[:, a:b_], u_sb[:, a:b_], AF.Square, ['u_sb'], ['sqq'], scale=float(n) ** -0.5)
                S.op('dve', lambda e, a=a, b_=b_, col=col: e.tensor_reduce(ms[:, col:col + 1], sqq[:, a:b_], AX.X, ALU.add),
                     ['sqq'], ['ms'])
        rstd_chain(ms[:, 24:27], rst[:, 24:27], 1.0, ['ms'], ['rst'])
        chk(3.2, i == 0)
        ts('dve', cq[:, :], u_sb[:, 0:256], rst[:, 24:25], None, ALU.mult, None, ['u_sb', 'rst'], ['cq'])
        cs = i % 2
        stt(ckv_f[:, cs, :], u_sb[:, 256:384], rst[:, 25:26], Gkva[:, :], ALU.mult, ALU.mult,
            ['u_sb', 'rst', 'Gkva'], [f'ckv_f{cs}'])
        dma('sp', ckv_o[i], ckv_f[:, cs, :], [f'ckv_f{cs}'], ['ckv_o'])
        cp('dve', ckv_b[:, 0:128], ckv_f[:, cs, :], [f'ckv_f{cs}'], ['ckv_b'])
        stt(Rn[:, 8, :], u_sb[:, 384:416], rst[:, 26:27], Gkr[:, :], ALU.mult, ALU.mult,
            ['u_sb', 'rst', 'Gkr'], ['Rn'])
        chk(3.3, i == 0)
        for c in range(2):
            tr(ptr[1][:, c * 128:(c + 1) * 128], cq[:, c * 128:(c + 1) * 128], ident[:, :], ['cq', 'ident'], ['ptr1'])
        tr(ptr[1][:, 256:384], ckv_b[:, 0:128], ident[:, :], ['ckv_b', 'ident'], ['ptr1'])
        cp('dve', cqT[:, :, :], ptr[1][:, 0:256].rearrange("p (c t) -> p c t", c=2), ['ptr1'], ['cqT'])
        cp('dve', ckvT[:, :], ptr[1][:, 256:384], ['ptr1'], ['ckvT'])
        chk(3.4, i == 0)
        for half, (n0, n1) in enumerate(((0, 512), (512, 768))):
            for c in range(2):
                mm(pw[0][:, n0:n1], cqT[:, c, :], wqup_sb[:, c, n0:n1], c == 0, c == 1, ['cqT', 'wqup'], ['pw0'])
        for half in range(2):
            mm(pw[1][:, half * 512:(half + 1) * 512], ckvT[:, :], wkv_sb[:, half * 512:(half + 1) * 512],
               True, True, ['ckvT', 'wkv'], ['pw1'])
        qfv = pw[0][:, 0:768].rearrange("p (h e) -> p h e", h=8)
        kvv = pw[1][:, :].rearrange("p (h e) -> p h e", h=8)
        chk(3.5, i == 0)
        act(sqq[:, :], pw[0][:, 0:768], AF.Square, ['pw0'], ['sqq'])
        sqv = sqq[:, :].rearrange("p (h e) -> p h e", h=8)
        S.op('dve', lambda e: e.tensor_reduce(ms[:, 0:8], sqv[:, :, 0:64], AX.X, ALU.add), ['sqq'], ['ms'])
        S.op('dve', lambda e: e.tensor_reduce(ms[:, 16:24], sqv[:, :, 64:96], AX.X, ALU.add), ['sqq'], ['ms'])
        act(sqk[:, :, :], kvv[:, :, 0:64], AF.Square, ['pw1'], ['sqk'])
        S.op('dve', lambda e: e.tensor_reduce(ms[:, 8:16], sqk[:, :, :], AX.X, ALU.add), ['sqk'], ['ms'])
        rstd_chain(ms[:, 0:16], rst[:, 0:16], 1.0 / 64, ['ms'], ['rst'])
        rstd_chain(ms[:, 16:24], rst[:, 16:24], 1.0 / 32, ['ms'], ['rst'])
        chk(3.6, i == 0)
        tt('dve', sqk[:, :, :], qfv[:, :, 0:64], bc_last(rst[:, 0:8], 64), ALU.mult, ['pw0', 'rst'], ['sqk'])
        tt('dve', q_tok[:, :, :], sqk[:, :, :], bc_mid(Gqn[:, :], 8), ALU.mult, ['sqk', 'Gqn'], ['q_tok'])
        tt('dve', sqk[:, :, :], kvv[:, :, 0:64], bc_last(rst[:, 8:16], 64), ALU.mult, ['pw1', 'rst', 'q_tok'],
           ['sqk'])
        tt('dve', k_tok[:, :, :], sqk[:, :, :], bc_mid(Gkn[:, :], 8), ALU.mult, ['sqk', 'Gkn'], ['k_tok'])
        chk(3.7, i == 0)
        if not samp:
            cp('act', Vext[:, i, :, 0:64], kvv[:, :, 64:128], ['pw1'], ['Vext'])
        tt('dve', Rn[:, 0:8, :], qfv[:, :, 64:96], bc_last(rst[:, 16:24], 32), ALU.mult, ['pw0', 'rst'], ['Rn'])
        tt('dve', Rn[:, 0:8, :], Rn[:, 0:8, :], bc_mid(Gqr[:, :], 8), ALU.mult, ['Rn', 'Gqr'], ['Rn'])
        cosb = bc_mid(cos_sb[:, i, :], 9)
        sinb = bc_mid(sin_sb[:, i, :], 9)
        x1, x2 = Rn[:, :, 0:16], Rn[:, :, 16:32]
        rk = f'Rr{cs}'
        tt('dve', rt1[:, :, :], x1, cosb, ALU.mult, ['Rn', 'cos'], ['rt1'])
        tt('dve', rt2[:, :, :], x2, sinb, ALU.mult, ['Rn', 'sin'], ['rt2'])
        tt('dve', Rr[:, cs, :, 0:16], rt1[:, :, :], rt2[:, :, :], ALU.subtract, ['rt1', 'rt2'], [rk])
        tt('dve', rt1[:, :, :], x2, cosb, ALU.mult, ['Rn', 'cos', rk], ['rt1'])
        tt('dve', rt2[:, :, :], x1, sinb, ALU.mult, ['Rn', 'sin', rk], ['rt2'])
        tt('dve', Rr[:, cs, :, 16:32], rt1[:, :, :], rt2[:, :, :], ALU.add, ['rt1', 'rt2'], [rk])
        dma('sp', kr_o[i], Rr[:, cs, 8, :], [rk], ['kr_o'])
        cp('dve', qr_tok[:, :, :], Rr[:, cs, 0:8, :], [rk], ['qr_tok'])
        cp('dve', kr_tok[:, :], Rr[:, cs, 8, :], [rk], ['kr_tok'])
        chk(3.8, i == 0)
        for c in range(4):
            tr(ptr[0][:, c * 128:(c + 1) * 128], q_tok[:, 2 * c:2 * c + 2, :], ident[:, :], ['q_tok', 'ident'], ['ptr0'])
            tr(ptr[0][:, 512 + c * 128:512 + (c + 1) * 128], k_tok[:, 2 * c:2 * c + 2, :], ident[:, :],
               ['k_tok', 'ident'], ['ptr0'])
        for hh in range(2):
            r = slice(hh * 64, hh * 64 + 64)
            cp('dve', QnT[r, hh::2, :], ptr[0][r, 0:512].rearrange("p (c t) -> p c t", c=4), ['ptr0'], ['QnT'])
        cp('dve', KnT[:, :, tcol], ptr[0][:, 512:1024].rearrange("p (c t) -> p c t", c=4), ['ptr0'], ['KnT'])
        for h in range(8):
            tr(ptr[1][0:32, h * 128:(h + 1) * 128], qr_tok[:, h, :], ident[:, :], ['qr_tok', 'ident'], ['ptr1'])
        cp('dve', QrT[0:32, :, :], ptr[1][0:32, :].rearrange("p (h t) -> p h t", h=8), ['ptr1'], ['QrT'])
        tr(ptr[1][0:32, 0:128], kr_tok[:, :], ident[:, :], ['kr_tok', 'ident', 'QrT'], ['ptr1'])
        cp('dve', KrT[0:32, tcol], ptr[1][0:32, 0:128], ['ptr1'], ['KrT'])

        if i == 0 and os.environ.get("KDBG"):
            dma('sp', y_o[0][:, 0:416], u_sb[:, :], ['u_sb'], ['y_o'])
            dma('sp', y_o[1][:, 0:32], rst[:, :], ['rst'], ['y_o'])
            dma('pool', y_o[2][:, 0:1024], nT[:, 0, :, :].rearrange("p c t -> p (c t)"), [nk], ['y_o'])
            dma('pool', y_o[3][:, 0:1024], w_in_sb[:, 0, 0:1024], ['w_in'], ['y_o'])
            dma('pool', y_o[4][:, 0:1024], xs[:, :], ['xs'], ['y_o'])
            dma('sp', y_o[5][:, 0:8], stat[:, :], ['stat'], ['y_o'])
            dma('sp', y_o[6][:, 0:256], ckv_f[:, :, :].rearrange("p a c -> p (a c)"), ['ckv_f0'], ['y_o'])
        if i == 0 and os.environ.get("KDBG") == "6":
            stg = x_sb[:, 1, :]
            cp('dve', x_sb[:, 1, 0:416], u_sb[:, :], ['u_sb'], ['x1'])
            cp('dve', x_sb[:, 1, 416:448], rst[:, :], ['rst'], ['x1'])
            cp('dve', x_sb[:, 1, 448:480], ms[:, :], ['ms'], ['x1'])
            cp('dve', x_sb[:, 1, 512:640], ckv_f[:, 0, :], ['ckv_f0'], ['x1'])
            cp('dve', x_sb[:, 1, 640:768], Gkva[:, :], ['Gkva'], ['x1'])
            cp('dve', x_sb[:, 1, 768:800], Rr[:, 0, 8, :], ['Rr0'], ['x1'])
            dma('sp', y_o[4], stg, ['x1'], ['y_o'])
        chk(4, i == 0)
        chk(7, samp)
        if not samp:
            groups = []
            for h in range(8):
                for kt0 in range(0, i + 1, 4):
                    groups.append((h, kt0, min(4, i + 1 - kt0)))

            def emit_scores(gi):
                h, kt0, nkt = groups[gi]
                pr, b0 = h // 2, (h % 2) * 64
                s = gi % 2
                for jj in range(nkt):
                    kt = kt0 + jj
                    kc = slice(kt * 128, (kt + 1) * 128)
                    o = pm[s][:, jj * 128:(jj + 1) * 128]
                    mm(o, KnT[:, pr, kc], QnT[:, h, :], True, False, ['KnT', 'QnT'], [f'pm{s}'])
                    mm(o, KrT[:, kc], QrT[:, h, :], False, True, ['KrT', 'QrT'], [f'pm{s}'])
                act(PT[:, s, 0:nkt, :], pm[s][:, 0:nkt * 128].rearrange("p (a t) -> p a t", a=nkt), AF.Exp,
                    [f'pm{s}'], [f'PT{s}'], scale=MLA_SCALE)
                if kt0 + nkt - 1 == i:
                    tt('dve', PT[:, s, nkt - 1, :], PT[:, s, nkt - 1, :], mask01[:, :], ALU.mult,
                       [f'PT{s}', 'mask01'], [f'PT{s}'])

            def emit_pv(gi):
                h, kt0, nkt = groups[gi]
                s = gi % 2
                bank = h // 4
                o = pw[bank][:, (h % 4) * 65:(h % 4) * 65 + 65]
                for jj in range(nkt):
                    kt = kt0 + jj
                    mm(o, PT[:, s, jj, :], Vext[:, kt, h, :], kt == 0, kt == i, [f'PT{s}', 'Vext', 'Vext1'],
                       [f'pw{bank}'])

            emit_scores(0)
            for gi in range(len(groups)):
                if gi + 1 < len(groups):
                    emit_scores(gi + 1)
                emit_pv(gi)
            for bank in range(2):
                ov = pw[bank][:, 0:260].rearrange("p (h e) -> p h e", h=4)
                S.op('dve', lambda e, ov=ov, bank=bank: e.reciprocal(rl[:, bank * 4:bank * 4 + 4], ov[:, :, 64]),
                     [f'pw{bank}'], ['rl'])
                tt('dve', o_tok[:, bank * 4:bank * 4 + 4, :], ov[:, :, 0:64], bc_last(rl[:, bank * 4:bank * 4 + 4], 64),
                   ALU.mult, [f'pw{bank}', 'rl'], ['o_tok'])
            for c in range(4):
                tr(ptr[0][:, c * 128:(c + 1) * 128], o_tok[:, 2 * c:2 * c + 2, :], ident[:, :], ['o_tok', 'ident'],
                   ['ptr0'])
            cp('dve', omT[:, :, tcol], ptr[0][:, 0:512].rearrange("p (c t) -> p c t", c=4), ['ptr0'], ['omT'])
            chk(5, i == 0)
            chk(6, i == ST - 1)
        elif not _SKIP_PA:
            for pr in range(4):
                tr(ptr[0][:, pr * 128:(pr + 1) * 128], wkn_sb[:, pr * 128:(pr + 1) * 128], ident[:, :],
                   ['wkn', 'ident'], ['ptr0'])
            cp('dve', WkT[:, :, :], ptr[0][:, 0:512].rearrange("p (c t) -> p c t", c=4), ['ptr0'], ['WkT'])
            ts('dve', qgT[:, :, :], QnT[:, :, :], gknc[:, 0:1], None, ALU.mult, None, ['QnT', 'gknc'], ['qgT'])
            for h in range(8):
                pr, b0 = h // 2, (h % 2) * 64
                bank = h // 4
                mm(pw[bank][:, (h % 4) * 128:(h % 4 + 1) * 128], WkT[:, pr, :], qgT[:, h, :],
                   True, True, ['WkT', 'qgT'], [f'pw{bank}'])
            for bank in range(2):
                cp('dve', qabsT[:, bank * 4:bank * 4 + 4, :],
                   pw[bank][:, 0:512].rearrange("p (h t) -> p h t", h=4), [f'pw{bank}'], ['qabsT'])
            cp('dve', rstk[:, 0:8], rst[:, 8:16], ['rst'], ['rstk_new'])
            chk(7.1)

            def gather(b, g):
                s = g % 2
                for jj in range(4):
                    col = b * NPG + g * 4 + jj
                    S.op('pool', lambda e, s=s, jj=jj, col=col: e.indirect_dma_start(
                        out=Cbuf[:, s, jj, :], out_offset=None, in_=cc_d,
                        in_offset=bass.IndirectOffsetOnAxis(ap=IDX[:, col:col + 1], axis=0)),
                        ['IDX'], [f'Cbuf{s}'], dma=True)

            def score_pv(b, lhs_c, lhs_r, rstd_ap, nj, rhs_pv, keys_r, first, last, maskap=None):
                qa = qabsT[:, :, b * 32 + 2:b * 32 + 10]
                qr = QrT[:, :, b * 32 + 2:b * 32 + 10]
                for jj in range(nj):
                    mm(pm[0][:, jj * 64:(jj + 1) * 64], lhs_c(jj), qa, True, True, keys_r + ['qabsT'], ['pm0'])
                    mm(pm[0][:, 256 + jj * 64:256 + (jj + 1) * 64], lhs_r(jj), qr, True, True, keys_r + ['QrT'],
                       ['pm0'])
                n = nj * 64
                tt('dve', sc1[:, 0:n].rearrange("p (a t) -> p a t", t=8),
                   pm[0][:, 0:n].rearrange("p (a t) -> p a t", t=8), bc_last(rstd_ap, 8), ALU.mult,
                   ['pm0', 'rstk', 'rstk_new'] + [k for k in keys_r if k.startswith('rk')], ['sc1'])
                tt('dve', sc2[:, 0:n], sc1[:, 0:n], pm[0][:, 256:256 + n], ALU.add, ['sc1', 'pm0'], ['sc2'])
                return n

            def stageA(b, g):
                s = g % 2
                cp('act', Cb[:, s, :, 0:128], Cbuf[:, s, :, 0:128], [f'Cbuf{s}'], [f'Cb{s}'])
                cp('dve', KRb[:, s, :, :], Cbuf[:, s, :, 128:160], [f'Cbuf{s}'], [f'KRb{s}'])
                for jj in range(4):
                    tr(ptr[0][:, jj * 128:(jj + 1) * 128], Cb[:, s, jj, 0:128], ident[:, :],
                       [f'Cb{s}', 'ident'], ['ptr0'])
                    tr(ptr[0][0:32, 512 + jj * 128:512 + (jj + 1) * 128], KRb[:, s, jj, :], ident[:, :],
                       [f'KRb{s}', 'ident'], ['ptr0'])
                cp('dve', CT[:, s, :, :], ptr[0][:, 0:512].rearrange("p (a t) -> p a t", a=4), ['ptr0'],
                   [f'CT{s}'])
                cp('dve', KRT[0:32, s, :, :], ptr[0][0:32, 512:1024].rearrange("p (a t) -> p a t", a=4), ['ptr0'],
                   [f'KRT{s}'])
                for hf in range(2):
                    for q2 in range(2):
                        jj = hf * 2 + q2
                        mm(pw[hf][:, q2 * 512:(q2 + 1) * 512], CT[:, s, jj, :], wkn_sb[:, :], True, True,
                           [f'CT{s}', 'wkn'], [f'pw{hf}'])
                    act(sqs[:, hf, :], pw[hf][:, :], AF.Square, [f'pw{hf}'], [f'sqs{hf}'])
                    S.op('dve', lambda e, hf=hf: e.tensor_reduce(
                        ssum[:, hf * 16:(hf + 1) * 16], sqs[:, hf, :].rearrange("p (a d) -> p a d", d=64),
                        AX.X, ALU.add), [f'sqs{hf}'], ['ssum'])
                rstd_chain(ssum[:, 0:32], rstk2[:, s, :], 1.0 / 64, ['ssum'], [f'rk{s}'])

            def stageB(b, g):
                s = g % 2
                score_pv(b, lambda jj: CT[:, s, jj, :], lambda jj: KRT[:, s, jj, :], rstk2[:, s, :], 4,
                         None, [f'CT{s}', f'KRT{s}', f'rk{s}'], None, None)
                act(PTs[:, s, :, :], sc2[:, 0:256].rearrange("p (a t) -> p a t", a=4), AF.Exp, ['sc2'],
                    [f'PTs{s}'], scale=MLA_SCALE)
                for jj in range(4):
                    mm(pm[1][0:64, 0:129], PTs[:, s, jj, :], Cb[:, s, jj, 0:129], g == 0 and jj == 0, False,
                       [f'PTs{s}', f'Cb{s}', 'Cb1'], ['pm1'])

            NG = NPG // 4
            for b in range(4):
                gather(b, 0)
                gather(b, 1)
                stageA(b, 0)
                for g in range(NG):
                    if g + 2 < NG:
                        gather(b, g + 2)
                    if g + 1 < NG:
                        stageA(b, g + 1)
                    stageB(b, g)
                cp('dve', rstk[:, 0:8], rst[:, 8:16], ['rst'], ['rstk'])
                score_pv(b, lambda jj: ckvT[:, :], lambda jj: KrT[:, tcol], rstk[:, 0:8], 1, None,
                         ['ckvT', 'KrT'], None, None)
                act(PTs[:, 0, 0, :], sc2[:, 0:64], AF.Exp, ['sc2'], ['PTs0'], scale=MLA_SCALE)
                tt('dve', PTs[:, 0, 0, :], PTs[:, 0, 0, :], maska[:, b, :], ALU.mult, ['PTs0', 'maska'], ['PTs0'])
                mm(pm[1][0:64, 0:129], PTs[:, 0, 0, :], ckv_b[:, 0:129], False, True, ['PTs0', 'ckv_b', 'ckv_b1'],
                   ['pm1'])
                cp('dve', acc_sb[:, 0:129], pm[1][0:64, 0:129], ['pm1'], ['acc_sb'])
                S.op('dve', lambda e: e.reciprocal(acc_sb[:, 130:131], acc_sb[:, 128:129]), ['acc_sb'], ['acc_sb'])
                ts('dve', accn[0:64, :], acc_sb[:, 0:128], acc_sb[:, 130:131], None, ALU.mult, None, ['acc_sb'], ['accn'])
                tr(ptr[1][:, 0:128], accn[:, :], ident[:, :], ['accn', 'ident'], ['ptr1'])
                cp('dve', accnT[:, :], ptr[1][:, 0:128], ['ptr1'], ['accnT'])
                for c in range(4):
                    mm(pm[0][:, c * 16:(c + 1) * 16], wvp_sb[:, c, :], accnT[:, c * 16:(c + 1) * 16], True, True,
                       ['wvp', 'accnT'], ['pm0'])
                ocol = slice(ST * 128 + b * 32 + 2, ST * 128 + b * 32 + 10)
                pv4 = pm[0][:, 0:64].rearrange("p (c t) -> p c t", c=4)
                cp('dve', omT[0:64, :, ocol], pv4[0:64, :, 0:8], ['pm0'], ['omT'])
                cp('dve', omT[64:128, :, ocol], pv4[64:128, :, 8:16], ['pm0'], ['omT'])
                chk(7.6, b == 0)

    if os.environ.get("KDBG") == "7":
        for tt_ in range(2):
            cp('dve', x_sb[:, 0, 0:512].rearrange("p (c t) -> p c t", c=4), ogT[:, :, tt_ * 128:(tt_ + 1) * 128],
               ['ogT'], ['x0'])
            cp('dve', x_sb[:, 0, 512:1024].rearrange("p (c t) -> p c t", c=4), omT[:, :, tt_ * 128:(tt_ + 1) * 128],
               ['omT'], ['x0'])
            dma('sp', y_o[4 + tt_], x_sb[:, 0, :], ['x0'], ['y_o'])
    chk(8)
    S.barrier()
    A.off = persist_off
    wd_sb = A.alloc("wd", [128, NF, D], BF16)
    wo_sb = A.alloc("wo", [128, 8, D], BF16)
    wpp_sb = A.alloc("wpp", [128, 2, D], BF16)
    wst = A.alloc("wst", [128, 2, 2, 8, 512], BF16)
    h_g = A.alloc("h_g", [128, 4, D], F32)
    nT2 = A.alloc("nT2", [128, 8, 512], BF16)
    hidT = A.alloc("hidT", [128, NF, 512], BF16)
    a_sb = A.alloc("a_sb", [128, 520], F32)
    tbuf = A.alloc("tbuf", [128, 512], F32)
    big = A.alloc("big", [128, D], F32)
    p_sb = A.alloc("p_sb", [128, 256], F32)
    p_bf = A.alloc("p_bf", [128, 256], BF16)
    pT = A.alloc("pT", [128, 2, 128], BF16)
    cw_sb = A.alloc("cw", [128, 3, NF], F32)
    cb_sb = A.alloc("cb", [128, NF], F32)
    halo = A.alloc("halo", [128, NF, 2], F32)
    cst = A.alloc("cst", [128, NF, 8], F32)
    sc_bf = A.alloc("sc_bf", [128, DFF], BF16)
    sel_sb = A.alloc("sel", [128, 128], BF16)
    ctok = A.alloc("ctok", [8, 512], F32)

    dma('pool', wo_sb[:, :, :], wo_d.rearrange("(k p) n -> p k n", p=128), (), ['wo'])

    def load_wd():
        wdv = wd_d.rearrange("(f p) n -> p f n", p=128)
        for f0 in (0, 8, 16):
            f1 = min(NF, f0 + 8)
            dma('pool', wd_sb[:, f0:f1, :], wdv[:, f0:f1, :], (), ['wd'])
    dma('sp', wpp_sb[:, :, :], wpp_d.rearrange("(c p) n -> p c n", p=128), (), ['wpp'])
    dma('sp', cw_sb[:, :, :], cw_d.rearrange("j (f p) -> p j f", p=128), (), ['cw'], allow_slow_non_contiguous=True)
    dma('sp', cb_sb[:, :], cb_d.rearrange("(f p) -> p f", p=128), (), ['cb'], allow_slow_non_contiguous=True)
    memset('dve', sc_bf[:, :], 0.0, ['sc_bf'])
    dma('sp', sc_bf[0:8, :], sconv_d, ['sc_bf'], ['sc_bf'])
    dma('sp', sel_sb[:, :], c_sel_d, (), ['sel'])
    memset('dve', halo[:, :, :], 0.0, ['halo'])
    memset('dve', a_sb[:, :], 0.0, ['a_sb'])

    blocks = [(0, 512), (512, 512), (1024, 512), (1536, 512), (2048, 512), (2560, 256)]
    wst_cnt = [0]

    def load_ffn_block(bi):
        c0, w = blocks[bi]
        s = wst_cnt[0] % 2
        wst_cnt[0] += 1
        for gu, wdram in ((0, wg_d), (1, wu_d)):
            dma('pool', wst[:, s, gu, :, 0:w], wdram.rearrange("(k p) n -> p k n", p=128)[:, :, c0:c0 + w], (),
                [f'wst{s}'])
        return s

    def load_ple_gate():
        s = wst_cnt[0] % 2
        wst_cnt[0] += 1
        for half in range(2):
            dma('pool', wst[:, s, half, :, :],
                wpg_d.rearrange("(k p) n -> p k n", p=128)[:, :, half * 512:(half + 1) * 512], (), [f'wst{s}'])
        return s

    tgroups = [[0, 1, 2, 3], [4, 5, 6, 7], [8, 9, 10, 11], [12, 13, 14, 15], [ST]]
    for gi, tiles in enumerate(tgroups):
        samp = (tiles[0] == ST)
        N = 128 * len(tiles)
        slot0 = load_ffn_block(0)
        if gi == 0:
            load_wd()
        for ti, t in enumerate(tiles):
            tcol = slice(t * 128, (t + 1) * 128)
            dma('sp', big[:, :], x_d[t], (), ['big'])
            for half in range(2):
                for c in range(8):
                    src = ogT if c < 4 else omT
                    mm(pw[0][:, half * 512:(half + 1) * 512], src[:, c % 4, tcol],
                       wo_sb[:, c, half * 512:(half + 1) * 512], c == 0, c == 7, ['ogT', 'omT', 'wo'], ['pw0'])
            tt('dve', h_g[:, ti, :], big[:, :], pw[0][:, :], ALU.add, ['big', 'pw0'], [f'h{ti}'])
            norm_tile(h_g[:, ti, :], [f'h{ti}'], gffnT, 'gffnT', nT2[:, :, ti * 128:(ti + 1) * 128], ['nT2'], 0)
        chk(9, gi == 0)
        for bi, (c0, w) in enumerate(blocks):
            s = slot0 if bi == 0 else s_next
            if bi + 1 < len(blocks):
                s_next = load_ffn_block(bi + 1)
            else:
                s_pg = load_ple_gate()
            for fi in range(w // 128):
                f = c0 // 128 + fi
                fc = slice(fi * 128, (fi + 1) * 128)
                for k in range(8):
                    mm(pm[0][:, 0:N], wst[:, s, 0, k, fc], nT2[:, k, 0:N], k == 0, (k == 7 and not samp),
                       [f'wst{s}', 'nT2'], ['pm0'])
                if samp:
                    mm(pm[0][:, 0:N], sc_bf[:, f * 128:(f + 1) * 128], sel_sb[:, :], False, True,
                       ['sc_bf', 'sel'], ['pm0'])
                for k in range(8):
                    mm(pm[1][:, 0:N], wst[:, s, 1, k, fc], nT2[:, k, 0:N], k == 0, k == 7, [f'wst{s}', 'nT2'], ['pm1'])
                if not samp:
                    cp('dve', a_sb[:, 0:2], halo[:, f, :], ['halo'], ['a_sb'])
                cp('act', a_sb[:, 2:2 + N], pm[0][:, 0:N], ['pm0'], ['a_sb'])
                if not samp:
                    cp('dve', halo[:, f, :], a_sb[:, N:N + 2], ['a_sb'], ['halo'])
                else:
                    cp('dve', cst[:, f, :].rearrange("p (b j) -> p b j", b=4),
                       a_sb[:, 2:2 + 128].rearrange("p (b c) -> p b c", b=4)[:, :, 8:10], ['a_sb'], ['cst'])
                ts('dve', tbuf[:, 0:N], a_sb[:, 0:N], cw_sb[:, 0, f:f + 1], cb_sb[:, f:f + 1], ALU.mult, ALU.add,
                   ['a_sb', 'cw', 'cb'], ['tbuf'])
                stt(tbuf[:, 0:N], a_sb[:, 1:1 + N], cw_sb[:, 1, f:f + 1], tbuf[:, 0:N], ALU.mult, ALU.add,
                    ['a_sb', 'cw', 'tbuf'], ['tbuf'])
                stt(tbuf[:, 0:N], a_sb[:, 2:2 + N], cw_sb[:, 2, f:f + 1], tbuf[:, 0:N], ALU.mult, ALU.add,
                    ['a_sb', 'cw', 'tbuf'], ['tbuf'])
                act(tbuf[:, 0:N], tbuf[:, 0:N], AF.Silu, ['tbuf'], ['tbuf'])
                tt('dve', hidT[:, f, 0:N], tbuf[:, 0:N], pm[1][:, 0:N], ALU.mult, ['tbuf', 'pm1'], ['hidT'])
        if gi == 3 or samp:
            W = 8 if samp else 2
            stg = cst if samp else halo
            skey = 'cst' if samp else 'halo'
            for q6 in range(6):
                nf = min(4, NF - q6 * 4)
                for ff in range(nf):
                    f = q6 * 4 + ff
                    S.op('pe', lambda e, f=f, ff=ff, W=W, stg=stg: e.transpose(pm[0][0:W, ff * 128:(ff + 1) * 128],
                                                                 stg[:, f, 0:W], identf[:, :]),
                         [skey, 'identf'], ['pm0'])
                cp('dve', ctok[0:W, 0:nf * 128], pm[0][0:W, 0:nf * 128], ['pm0'], ['ctok'])
                dma('sp', (convs_o if samp else convp_o)[:, q6 * 512:q6 * 512 + nf * 128], ctok[0:W, 0:nf * 128],
                    ['ctok'], ['conv_o'])
        chk(10, gi == 0)
        for ti, t in enumerate(tiles):
            tc2 = slice(ti * 128, (ti + 1) * 128)
            for half in range(2):
                for f in range(NF):
                    mm(pw[0][:, half * 512:(half + 1) * 512], hidT[:, f, tc2], wd_sb[:, f, half * 512:(half + 1) * 512],
                       f == 0, f == NF - 1, ['hidT', 'wd'], ['pw0'])
            tt('dve', h_g[:, ti, :], h_g[:, ti, :], pw[0][:, :], ALU.add, [f'h{ti}', 'pw0'], [f'h{ti}'])
        for ti, t in enumerate(tiles):
            tc2 = slice(ti * 128, (ti + 1) * 128)
            norm_tile(h_g[:, ti, :], [f'h{ti}'], gpleT, 'gpleT', nT2[:, :, tc2], ['nT2'], 0)
            dma('sp', p_sb[:, :], p_d[t], (), ['p_sb'])
            cp('dve', p_bf[:, :], p_sb[:, :], ['p_sb'], ['p_bf'])
            for c in range(2):
                tr(ptr[1][:, c * 128:(c + 1) * 128], p_bf[:, c * 128:(c + 1) * 128], ident[:, :], ['p_bf', 'ident'],
                   ['ptr1'])
            cp('dve', pT[:, :, :], ptr[1][:, 0:256].rearrange("p (c t) -> p c t", c=2), ['ptr1'], ['pT'])
            for half in range(2):
                for k in range(8):
                    mm(pw[0][:, half * 512:(half + 1) * 512], nT2[:, k, tc2], wst[:, s_pg, half, k, :],
                       k == 0, k == 7, ['nT2', f'wst{s_pg}'], ['pw0'])
                for c in range(2):
                    mm(pw[1][:, half * 512:(half + 1) * 512], pT[:, c, :], wpp_sb[:, c, half * 512:(half + 1) * 512],
                       c == 0, c == 1, ['pT', 'wpp'], ['pw1'])
            act(big[:, :], pw[0][:, :], AF.Sigmoid, ['pw0'], ['big'])
            tt('dve', big[:, :], big[:, :], pw[1][:, :], ALU.mult, ['big', 'pw1'], ['big'])
            tt('dve', big[:, :], big[:, :], h_g[:, ti, :], ALU.add, ['big', f'h{ti}'], ['big'])
            dma('sp', y_o[t], big[:, :], ['big'], ['y_o'])
        chk(11, gi == 0)
        chk(12, gi == 3)

    return S


_NC_CACHE = {}


def _consts():
    bf = ml_dtypes.bfloat16
    c = {}
    c['c_ident'] = np.eye(128, dtype=np.float32).astype(bf)
    c['c_identf'] = np.eye(128, dtype=np.float32)
    s = np.arange(128)
    c['c_mask'] = (s[:, None] <= s[None, :]).astype(np.float32).astype(bf)
    grp = s // 32
    m = (grp[:, None] == grp[None, :]) & (s[:, None] <= s[None, :])
    c['c_masks'] = m.astype(np.float32).astype(bf)
    ma = np.zeros((128, 4, 8, 8), np.float32)
    for b in range(4):
        for r in range(8):
            for t in range(8):
                if r <= t:
                    ma[b * 32 + 2 + r, b, :, t] = 1.0
    c['c_maska'] = ma.reshape(128, 4, 64).astype(bf)
    inv = 10000.0 ** (-np.arange(16, dtype=np.float32) * 2.0 / 32)
    pos = np.zeros((128, NT), np.float64)
    for i in range(16):
        pos[:, i] = i * 128 + s
    for p in range(128):
        t = (p % 32) - 2
        pos[p, ST] = 16384 + (t if 0 <= t < 8 else 0)
    ang = (pos.astype(np.float32)[:, :, None] * inv[None, None, :]).astype(np.float32)
    c['c_cos'] = np.cos(ang.astype(np.float64)).astype(np.float32)
    c['c_sin'] = np.sin(ang.astype(np.float64)).astype(np.float32)
    rs = np.ones((128, 2, 128), np.float32)
    rs[:, 0, 0] = 0.0
    for b in range(4):
        rs[:, 1, b * 32 + 2] = 0.0
    c['c_rs'] = rs
    c['c_iota'] = s.astype(np.float32).reshape(128, 1)
    rowm = np.zeros((128, 4), np.float32)
    for b in range(4):
        rowm[b * 32:(b + 1) * 32, b] = 1.0
    c['c_rowm'] = rowm
    hm = np.zeros((128, 2), np.float32)
    hm[:64, 0] = 1.0
    hm[64:, 1] = 1.0
    c['c_hm'] = hm
    sel = np.zeros((128, 128), np.float32)
    for b in range(4):
        for j in range(2):
            sel[b * 2 + j, b * 32 + j] = 1.0
    c['c_sel'] = sel.astype(bf)
    return c


def kernel(x_prompt, x_sample, cache_ckv, cache_krope, state_gla, state_conv, page_table,
           p_prompt, p_sample, g_mix, w_in, gla_w_a2, gla_b_a, gla_g_out, mla_g_qa, mla_w_qup,
           mla_g_qn, mla_g_qr, mla_g_kva, mla_g_kr, mla_w_kvup, mla_g_kn, w_o, g_ffn,
           ffn_w_gate, ffn_w_up, ffn_conv_w, ffn_conv_b, ffn_w_down, g_ple, ple_w_gate, ple_w_proj):
    f32 = np.float32
    A_ = lambda a: np.ascontiguousarray(np.asarray(a))
    if 'nc' not in _NC_CACHE:
        _NC_CACHE['nc'] = build_program()
    nc = _NC_CACHE['nc']
    consts = _consts()
    shared = {
        'cache_cat': np.ascontiguousarray(np.concatenate(
            [A_(cache_ckv).reshape(5120 * 128, 128)[:_NPHYS * 128],
             A_(cache_krope).reshape(5120 * 128, 32)[:_NPHYS * 128]], axis=1)),
        'w_in': A_(w_in[0]), 'gla_w_a2': A_(gla_w_a2[0]), 'gla_b_a': A_(gla_b_a[0]), 'gla_g_out': A_(gla_g_out[0]),
        'mla_g_qa': A_(mla_g_qa[0]), 'mla_w_qup': A_(mla_w_qup[0]), 'mla_g_qn': A_(mla_g_qn[0]),
        'mla_g_qr': A_(mla_g_qr[0]), 'mla_g_kva': A_(mla_g_kva[0]), 'mla_g_kr': A_(mla_g_kr[0]),
        'mla_w_kvup': A_(mla_w_kvup[0]), 'mla_g_kn': A_(mla_g_kn[0]), 'w_o': A_(w_o[0]), 'g_mix': A_(g_mix[0]),
        'g_ffn': A_(g_ffn[0]), 'g_ple': A_(g_ple[0]), 'ffn_w_gate': A_(ffn_w_gate[0]), 'ffn_w_up': A_(ffn_w_up[0]),
        'ffn_conv_w': A_(ffn_conv_w[0]), 'ffn_conv_b': A_(ffn_conv_b[0]), 'ffn_w_down': A_(ffn_w_down[0]),
        'ple_w_gate': A_(ple_w_gate[0]), 'ple_w_proj': A_(ple_w_proj[0]),
    }
    def _bc(a):
        a = A_(a)
        return np.ascontiguousarray(np.broadcast_to(a[None, :], (128, a.shape[0])))
    shared['b_gqn'] = _bc(mla_g_qn[0])
    shared['b_gqr'] = _bc(mla_g_qr[0])
    shared['b_gkva'] = _bc(mla_g_kva[0])
    shared['b_gkr'] = _bc(mla_g_kr[0])
    shared['b_gkn'] = _bc(mla_g_kn[0])
    shared.update(consts)
    xp = A_(x_prompt)
    xsm = A_(x_sample)
    pp = A_(p_prompt)[0]
    psm = A_(p_sample)[0]
    in_maps = []
    rows = np.array([b * 32 + 2 + t for b in range(4) for t in range(8)])
    for c in range(8):
        x = np.zeros((NT, 128, D), f32)
        x[:16] = xp[c].reshape(16, 128, D)
        x[ST][rows] = xsm[4 * c:4 * c + 4].reshape(32, D)
        p = np.zeros((NT, 128, 256), f32)
        p[:16] = pp[c].reshape(16, 128, 256)
        p[ST][rows] = psm[4 * c:4 * c + 4].reshape(32, 256)
        m = dict(shared)
        m['x'] = x
        m['p'] = p
        m['state_gla'] = A_(state_gla[0][4 * c:4 * c + 4]).reshape(4, 2, 128, 128)
        m['state_conv'] = A_(state_conv[0][4 * c:4 * c + 4]).reshape(8, DFF)
        m['page_table'] = A_(page_table[4 * c:4 * c + 4]).reshape(512).astype(np.int32)
        if _SMALLPA:
            m['page_table'] = m['page_table'] % 8
        m['b_pt'] = _bc(m['page_table'])
        in_maps.append(m)
    res = run_bass_kernel_spmd(nc, in_maps, core_ids=list(range(8))).results

    y_p = np.zeros((8, 2048, D), f32)
    y_s = np.zeros((32, 8, D), f32)
    ckv_p = np.zeros((1, 8, 2048, 128), f32)
    kr_p = np.zeros((1, 8, 2048, 32), f32)
    gla_p = np.zeros((1, 8, 4, 64, 128), f32)
    conv_p = np.zeros((1, 8, 2, DFF), f32)
    ckv_s = np.zeros((1, 32, 8, 128), f32)
    kr_s = np.zeros((1, 32, 8, 32), f32)
    gla_s = np.zeros((1, 32, 4, 64, 128), f32)
    conv_s = np.zeros((1, 32, 2, DFF), f32)
    for c in range(8):
        r = res[c]
        y = np.asarray(r['y'])
        y_p[c] = y[:16].reshape(2048, D)
        y_s[4 * c:4 * c + 4] = y[ST][rows].reshape(4, 8, D)
        ck = np.asarray(r['ckv_o'])
        ckv_p[0, c] = ck[:16].reshape(2048, 128)
        ckv_s[0, 4 * c:4 * c + 4] = ck[ST][rows].reshape(4, 8, 128)
        kr = np.asarray(r['kr_o'])
        kr_p[0, c] = kr[:16].reshape(2048, 32)
        kr_s[0, 4 * c:4 * c + 4] = kr[ST][rows].reshape(4, 8, 32)
        gla_p[0, c] = np.asarray(r['gla_p']).reshape(4, 64, 128)
        gla_s[0, 4 * c:4 * c + 4] = np.asarray(r['gla_s']).reshape(4, 4, 64, 128)
        conv_p[0, c] = np.asarray(r['conv_p'])
        conv_s[0, 4 * c:4 * c + 4] = np.asarray(r['conv_s']).reshape(4, 2, DFF)
    return (y_p, y_s, ckv_p, kr_p, gla_p, conv_p, ckv_s, kr_s, gla_s, conv_s)
```

```python
import numpy as np
import ml_dtypes
import concourse.bass as bass
import concourse.mybir as mybir
from concourse.bass_utils import run_bass_kernel_spmd

F32 = mybir.dt.float32
BF16 = mybir.dt.bfloat16
I32 = mybir.dt.int32
AF = mybir.ActivationFunctionType
ALU = mybir.AluOpType
AX = mybir.AxisListType

D = 1024
NT = 17
ST = 16
DFF = 2816
NF = 22
INC = 1968
EPS = 1e-6
MLA_SCALE = 96.0 ** -0.5
NPG = 128
O_Q, O_K, O_V, O_G, O_A, O_MQ = 0, 256, 512, 1024, 1536, 1552


class Sched:
    def __init__(self, sems):
        self.sems = sems
        nxt = iter(range(len(sems)))
        self.q = {e: [] for e in ('pe', 'act', 'dve', 'pool', 'sp')}
        self.cnt = {e: 0 for e in ('pe', 'act', 'dve', 'pool')}
        self.csem = {e: next(nxt) for e in ('pe', 'act', 'dve', 'pool')}
        self.dsems = {'sp': [next(nxt) for _ in range(24)], 'pool': [next(nxt) for _ in range(2)],
                      'act': [next(nxt) for _ in range(8)]}
        self.dcnt = {'sp': 0, 'pool': 0, 'act': 0}
        self.lastw = {}
        self.readers = {}
        self.waited = {e: {} for e in self.q}
        self.all_tokens = {}
        self._bar = None

    def op(self, eng, fn, reads=(), writes=(), dma=False):
        deps = {}

        def add(tok):
            if tok is None:
                return
            for s, v in tok.items() if isinstance(tok, dict) else [tok]:
                if deps.get(s, 0) < v:
                    deps[s] = v
        if self._bar:
            add(self._bar)
        for k in reads:
            add(self.lastw.get(k))
        for k in writes:
            add(self.lastw.get(k))
            add(self.readers.get(k))
        if dma:
            pool = self.dsems[eng]
            i = self.dcnt[eng]
            self.dcnt[eng] += 1
            sem = pool[i % len(pool)]
            val = 16 * (i // len(pool) + 1)
            add((sem, val - 16))
            inc = 16
        else:
            self.cnt[eng] += 1
            sem = self.csem[eng]
            val = self.cnt[eng]
            inc = 1
        waits = []
        w = self.waited[eng]
        for s, v in deps.items():
            if v <= 0:
                continue
            if eng == 'pe' and s == self.csem['pe']:
                continue
            if w.get(s, 0) >= v:
                continue
            w[s] = v
            waits.append((s, v))
        self.q[eng].append((waits, fn, sem, inc))
        tok = (sem, val)
        self.all_tokens[sem] = max(self.all_tokens.get(sem, 0), val)
        for k in reads:
            r = self.readers.setdefault(k, {})
            if r.get(sem, 0) < val:
                r[sem] = val
        for k in writes:
            self.lastw[k] = tok
            self.readers[k] = {}
        return tok

    def barrier(self):
        snap = dict(self.all_tokens)
        for k in list(self.lastw.keys()):
            self.lastw[k] = snap
            self.readers[k] = {}
        self._bar = snap

    def emit(self, engname, engine):
        for waits, fn, sem, inc in self.q[engname]:
            for s, v in waits:
                engine.wait_ge(self.sems[s], v)
            ins = fn(engine)
            ins.then_inc(self.sems[sem], inc)

    def final_wait(self, engine):
        for s, v in self.all_tokens.items():
            engine.wait_ge(self.sems[s], v)


class _Stop(Exception):
    pass


import os
_STOP = float(os.environ.get("KSTOP", "0")) or None
_SKIP_PA = bool(os.environ.get("KSKIP"))
_SMALLPA = bool(os.environ.get("KSMALLPA"))
_NPHYS = 8 if (_SMALLPA or _SKIP_PA or (_STOP is not None and _STOP <= 7)) else 5120


_HOOK = [None]


def chk(stage, active=True):
    if not active:
        return
    if _STOP is not None and stage >= _STOP:
        if _HOOK[0] is not None and os.environ.get("KDBG") == "5":
            _HOOK[0]()
        raise _Stop()


def build_program():
    nc = bass.Bass("TRN2", target_bir_lowering=False)
    S = None
    try:
        S = _build_body(nc)
    except _Stop as e:
        S = e.args[0] if e.args else S
    _emit(nc, _S_HOLD[0])
    return nc


_S_HOLD = [None]


def _emit(nc, S):
    with nc.Block() as block:
        @block.sync
        def _(e):
            S.emit('sp', e)
            S.final_wait(e)

        @block.tensor
        def _(e):
            S.emit('pe', e)

        @block.scalar
        def _(e):
            S.emit('act', e)

        @block.vector
        def _(e):
            S.emit('dve', e)

        @block.gpsimd
        def _(e):
            S.emit('pool', e)


def _build_body(nc):

    def din(name, shape, dt=F32):
        return nc.dram_tensor(name, list(shape), dt, kind="ExternalInput").ap()

    def dout(name, shape, dt=F32):
        return nc.dram_tensor(name, list(shape), dt, kind="ExternalOutput").ap()

    x_d = din("x", [NT, 128, D])
    p_d = din("p", [NT, 128, 256])
    cc_d = din("cache_cat", [_NPHYS * 128, 160])
    sgla_d = din("state_gla", [4, 2, 128, 128])
    sconv_d = din("state_conv", [8, DFF])
    pt_d = din("page_table", [512], I32)
    w_in_d = din("w_in", [D, INC])
    wa2_d = din("gla_w_a2", [16, 256])
    ba_d = din("gla_b_a", [256])
    gout_d = din("gla_g_out", [128])
    gqa_d = din("mla_g_qa", [256])
    wqup_d = din("mla_w_qup", [256, 768])
    gqn_d = din("mla_g_qn", [64])
    gqr_d = din("mla_g_qr", [32])
    gkva_d = din("mla_g_kva", [128])
    gkr_d = din("mla_g_kr", [32])
    wkv_d = din("mla_w_kvup", [128, 1024])
    gkn_d = din("mla_g_kn", [64])
    wo_d = din("w_o", [D, D])
    gmix_d = din("g_mix", [D])
    gffn_d = din("g_ffn", [D])
    gple_d = din("g_ple", [D])
    wg_d = din("ffn_w_gate", [D, DFF])
    wu_d = din("ffn_w_up", [D, DFF])
    cw_d = din("ffn_conv_w", [3, DFF])
    cb_d = din("ffn_conv_b", [DFF])
    wd_d = din("ffn_w_down", [DFF, D])
    wpg_d = din("ple_w_gate", [D, D])
    wpp_d = din("ple_w_proj", [256, D])
    gqnb_d = din("b_gqn", [128, 64])
    gqrb_d = din("b_gqr", [128, 32])
    gkvab_d = din("b_gkva", [128, 128])
    gkrb_d = din("b_gkr", [128, 32])
    gknb_d = din("b_gkn", [128, 64])
    ptb_d = din("b_pt", [128, 512], I32)
    c_ident_d = din("c_ident", [128, 128], BF16)
    c_identf_d = din("c_identf", [128, 128])
    c_mask_d = din("c_mask", [128, 128], BF16)
    c_masks_d = din("c_masks", [128, 128], BF16)
    c_maska_d = din("c_maska", [128, 4, 64], BF16)
    c_cos_d = din("c_cos", [128, NT, 16])
    c_sin_d = din("c_sin", [128, NT, 16])
    c_rs_d = din("c_rs", [128, 2, 128])
    c_iota_d = din("c_iota", [128, 1])
    c_sel_d = din("c_sel", [128, 128], BF16)
    c_rowm_d = din("c_rowm", [128, 4])
    c_hm_d = din("c_hm", [128, 2])

    y_o = dout("y", [NT, 128, D])
    ckv_o = dout("ckv_o", [NT, 128, 128])
    kr_o = dout("kr_o", [NT, 128, 32])
    glap_o = dout("gla_p", [2, 128, 128])
    glas_o = dout("gla_s", [4, 2, 128, 128])
    convp_o = dout("conv_p", [2, DFF])
    convs_o = dout("conv_s", [8, DFF])

    sems = [nc.alloc_semaphore(f"s{i}") for i in range(60)]
    S = Sched(sems)
    _S_HOLD[0] = S

    class Arena:
        def __init__(self):
            self.base = 16512
            self.top = 229344
            self.off = self.base
            self.n = 0

        def alloc(self, name, shape, dt):
            sz = 1
            for s in shape[1:]:
                sz *= s
            sz *= mybir.dt.size(dt)
            sz = (sz + 31) // 32 * 32
            assert self.off + sz <= self.top, f"SBUF overflow at {name}: {self.off + sz}"
            self.n += 1
            t = nc.alloc_sbuf_tensor_at(f"{name}_{self.n}", list(shape), dt, offset=self.off)
            self.off += sz
            return t

    A = Arena()

    ptr = [nc.alloc_psum_tensor(f"ptr{i}", [128, 1024], BF16) for i in range(2)]
    pm = [nc.alloc_psum_tensor(f"pm{i}", [128, 512], F32) for i in range(2)]
    pw = [nc.alloc_psum_tensor(f"pw{i}", [128, 1024], F32) for i in range(2)]

    def dma(eng, out, in_, reads=(), writes=(), **kw):
        if out.dtype != in_.dtype:
            eng = 'pool'
        return S.op(eng, lambda e: e.dma_start(out=out, in_=in_, **kw), reads, writes, dma=True)

    def mm(out, lhsT, rhs, start, stop, reads, writes, **kw):
        return S.op('pe', lambda e: e.matmul(out, lhsT, rhs, start=start, stop=stop, **kw), reads, writes)

    def tr(out, in_, ident, reads, writes):
        return S.op('pe', lambda e: e.transpose(out, in_, ident), reads, writes)

    def act(out, in_, func, reads, writes, **kw):
        return S.op('act', lambda e: e.activation(out, in_, func, **kw), reads, writes)

    def tt(eng, out, in0, in1, op, reads, writes):
        return S.op(eng, lambda e: e.tensor_tensor(out, in0, in1, op), reads, writes)

    def ts(eng, out, in0, s1, s2, op0, op1, reads, writes):
        if s2 is None:
            return S.op(eng, lambda e: e.tensor_scalar(out, in0, s1, None, op0), reads, writes)
        return S.op(eng, lambda e: e.tensor_scalar(out, in0, s1, s2, op0, op1), reads, writes)

    def stt(out, in0, sc, in1, op0, op1, reads, writes):
        return S.op('dve', lambda e: e.scalar_tensor_tensor(out, in0, sc, in1, op0, op1), reads, writes)

    def cp(eng, out, in_, reads, writes):
        if eng == 'act':
            return S.op('act', lambda e: e.copy(out, in_), reads, writes)
        return S.op(eng, lambda e: e.tensor_copy(out, in_), reads, writes)

    def memset(eng, ap, val, writes):
        return S.op(eng, lambda e: e.memset(ap, val), (), writes)

    def bc_last(ap, n):
        sh = list(ap.shape)
        return ap.unsqueeze(len(sh)).broadcast_to(sh + [n])

    def bc_mid(ap, n):
        sh = list(ap.shape)
        return ap.unsqueeze(1).broadcast_to([sh[0], n] + sh[1:])

    ident = A.alloc("ident", [128, 128], BF16)
    identf = A.alloc("identf", [128, 128], F32)
    mask01 = A.alloc("mask01", [128, 128], BF16)
    masks = A.alloc("masks", [128, 128], BF16)
    maska = A.alloc("maska", [128, 4, 64], BF16)
    ones_bf = A.alloc("ones_bf", [128, 128], BF16)
    eps_c = A.alloc("eps_c", [128, 1], F32)
    gmixT = A.alloc("gmixT", [128, 8], F32)
    gffnT = A.alloc("gffnT", [128, 8], F32)
    gpleT = A.alloc("gpleT", [128, 8], F32)
    ogT = A.alloc("ogT", [128, 4, NT * 128], BF16)
    omT = A.alloc("omT", [128, 4, NT * 128], BF16)
    stat = A.alloc("stat", [128, 8], F32)
    xs = A.alloc("xs", [128, D], BF16)
    persist_off = A.off

    dma('sp', ident[:, :], c_ident_d, (), ['ident'])
    def _hook():
        S.barrier()
        dma('sp', y_o[14][:, 0:128], identf[:, :], ['identf'], ['y_o'])
    _HOOK[0] = _hook
    if os.environ.get("KDBG") == "5":
        dma('sp', identf[:, :], c_identf_d, (), ['identf'])
        dma('sp', y_o[15][:, 0:128], identf[:, :], ['identf'], ['y_o'])
    dma('sp', identf[:, :], c_identf_d, (), ['identf'])
    dma('sp', mask01[:, :], c_mask_d, (), ['mask01'])
    dma('sp', masks[:, :], c_masks_d, (), ['masks'])
    dma('sp', maska[:, :, :], c_maska_d, (), ['maska'])
    for gt, gd, k in ((gmixT, gmix_d, 'gmixT'), (gffnT, gffn_d, 'gffnT'), (gpleT, gple_d, 'gpleT')):
        dma('sp', gt[:, :], gd.rearrange("(c p) -> p c", p=128), (), [k], allow_slow_non_contiguous=True)
    memset('dve', ones_bf[:, :], 1.0, ['ones'])
    memset('dve', eps_c[:, :], EPS, ['eps'])

    def rstd_chain(ms_ap, out_ap, inv_n, rk, wk):
        act(out_ap, ms_ap, AF.Ln, list(rk) + ['eps'], wk, bias=eps_c[0:ms_ap.shape[0], 0:1], scale=inv_n)
        act(out_ap, out_ap, AF.Exp, wk, wk, scale=-0.5)

    def norm_tile(src, src_keys, gT, gkey, dst, dst_keys, pslot):
        act(xs[:, :], src, AF.Square, src_keys, ['xs', 'stat'], accum_out=stat[:, 0:1])
        rstd_chain(stat[:, 0:1], stat[:, 1:2], 1.0 / D, ['stat'], ['stat'])
        ts('dve', xs[:, :], src, stat[:, 1:2], None, ALU.mult, None, list(src_keys) + ['stat'], ['xs'])
        pk = f'ptr{pslot}'
        for k in range(8):
            tr(ptr[pslot][:, k * 128:(k + 1) * 128], xs[:, k * 128:(k + 1) * 128], ident[:, :],
               ['xs', 'ident'], [pk])
        tt('dve', dst, ptr[pslot][:, :].rearrange("p (c t) -> p c t", c=8), bc_last(gT[:, :], 128),
           ALU.mult, [pk, gkey], dst_keys)

    w_in_sb = A.alloc("w_in", [128, 8, INC], BF16)
    wqup_sb = A.alloc("wqup", [128, 2, 768], BF16)
    wkv_sb = A.alloc("wkv", [128, 1024], BF16)
    wkn_sb = A.alloc("wkn", [128, 512], BF16)
    wa2_sb = A.alloc("wa2", [16, 256], F32)
    nba = A.alloc("nba", [128, 2], F32)
    goutc = A.alloc("goutc", [128, 1], F32)
    gqa_c = A.alloc("gqa", [128, 2], F32)
    gknc = A.alloc("gknc", [128, 1], F32)
    Gqn = A.alloc("Gqn", [128, 64], F32)
    Gqr = A.alloc("Gqr", [128, 32], F32)
    Gkva = A.alloc("Gkva", [128, 128], F32)
    Gkr = A.alloc("Gkr", [128, 32], F32)
    Gkn = A.alloc("Gkn", [128, 64], F32)
    cos_sb = A.alloc("cos", [128, NT, 16], F32)
    sin_sb = A.alloc("sin", [128, NT, 16], F32)
    rs_sb = A.alloc("rs", [128, 2, 128], F32)
    iota_sb = A.alloc("iota", [128, 1], F32)
    x_sb = A.alloc("x_sb", [128, 2, D], F32)
    nT = A.alloc("nT", [128, 2, 8, 128], BF16)
    gaT = A.alloc("gaT", [16, 128], F32)
    sp_ = A.alloc("sp", [128, 2, 128], F32)
    bpos = A.alloc("bpos", [128, 2, 128], F32)
    etmp = A.alloc("etmp", [128, 2, 128], F32)
    nbl = A.alloc("nbl", [128, 8], F32)
    ebl = A.alloc("ebl", [128, 8], F32)
    qdT = A.alloc("qdT", [128, 4, 128], BF16)
    ke_z = A.alloc("ke_z", [128, 4, 256], BF16)
    rowm = A.alloc("rowm", [128, 4], F32)
    hm = A.alloc("hm", [128, 2], F32)
    wvp_sb = A.alloc("wvp", [128, 4, 128], BF16)
    kiT = A.alloc("kiT", [128, 2, 128], BF16)
    keT = A.alloc("keT", [128, 2, 128], BF16)
    ke_tok = A.alloc("ke_tok", [128, 256], BF16)
    v_tok = A.alloc("v_tok", [128, 512], BF16)
    sgT = A.alloc("sgT", [128, 4, 128], BF16)
    AT = A.alloc("AT", [128, 4, 128], BF16)
    S_f = A.alloc("S_f", [128, 2, 128], F32)
    S_b = A.alloc("S_b", [128, 2, 128], BF16)
    S0_f = A.alloc("S0_f", [128, 4, 2, 128], F32)
    S0_b = A.alloc("S0_b", [128, 4, 2, 128], BF16)
    osq = A.alloc("osq", [128, 512], BF16)
    orst = A.alloc("orst", [128, 512], F32)
    u_sb = A.alloc("u_sb", [128, 416], F32)
    ms = A.alloc("ms", [128, 32], F32)
    rst = A.alloc("rst", [128, 32], F32)
    cq = A.alloc("cq", [128, 256], BF16)
    cqT = A.alloc("cqT", [128, 2, 128], BF16)
    ckv_f = A.alloc("ckv_f", [128, 2, 128], F32)
    ckv_b = A.alloc("ckv_b", [128, 132], BF16)
    ckvT = A.alloc("ckvT", [128, 128], BF16)
    sqq = A.alloc("sqq", [128, 768], F32)
    sqk = A.alloc("sqk", [128, 8, 64], F32)
    Rn = A.alloc("Rn", [128, 9, 32], F32)
    Rr = A.alloc("Rr", [128, 2, 9, 32], F32)
    rt1 = A.alloc("rt1", [128, 9, 16], F32)
    rt2 = A.alloc("rt2", [128, 9, 16], F32)
    q_tok = A.alloc("q_tok", [128, 8, 64], BF16)
    qr_tok = A.alloc("qr_tok", [128, 8, 32], BF16)
    k_tok = A.alloc("k_tok", [128, 8, 64], BF16)
    kr_tok = A.alloc("kr_tok", [128, 32], BF16)
    QnT = A.alloc("QnT", [128, 8, 128], BF16)
    QrT = A.alloc("QrT", [128, 8, 128], BF16)
    KnT = A.alloc("KnT", [128, 4, NT * 128], BF16)
    KrT = A.alloc("KrT", [128, NT * 128], BF16)
    Vext = A.alloc("Vext", [128, 16, 8, 65], BF16)
    PT = A.alloc("PT", [128, 2, 4, 128], BF16)
    rl = A.alloc("rl", [128, 8], F32)
    o_tok = A.alloc("o_tok", [128, 8, 64], BF16)
    IDX = A.alloc("IDX", [128, 512], I32)
    PTi = A.alloc("PTi", [128, 512], I32)
    Cbuf = A.alloc("Cbuf", [128, 2, 4, 160], F32)
    Cb = A.alloc("Cb", [128, 2, 4, 132], BF16)
    KRb = A.alloc("KRb", [128, 2, 4, 32], BF16)
    CT = A.alloc("CT", [128, 2, 4, 128], BF16)
    KRT = A.alloc("KRT", [128, 2, 4, 128], BF16)
    sqs = A.alloc("sqs", [128, 2, 1024], BF16)
    ssum = A.alloc("ssum", [128, 32], F32)
    rstk = A.alloc("rstk", [128, 32], F32)
    rstk2 = A.alloc("rstk2", [128, 2, 32], F32)
    sc1 = A.alloc("sc1", [128, 256], F32)
    sc2 = A.alloc("sc2", [128, 256], F32)
    PTs = A.alloc("PTs", [128, 2, 4, 64], BF16)
    WkT = A.alloc("WkT", [128, 4, 128], BF16)
    qgT = A.alloc("qgT", [128, 8, 128], BF16)
    qabsT = A.alloc("qabsT", [128, 8, 128], BF16)
    acc_sb = A.alloc("acc_sb", [64, 132], F32)
    accn = A.alloc("accn", [128, 128], BF16)
    accnT = A.alloc("accnT", [128, 128], BF16)

    for k in range(8):
        dma('sp', w_in_sb[:, k, :], w_in_d[k * 128:(k + 1) * 128, :], (), ['w_in'])
    dma('sp', wqup_sb[:, :, :], wqup_d.rearrange("(c p) n -> p c n", p=128), (), ['wqup'])
    dma('sp', wkv_sb[:, :], wkv_d, (), ['wkv'])
    dma('sp', wa2_sb[:, :], wa2_d, (), ['wa2'])
    dma('sp', nba[:, :], ba_d.rearrange("(c p) -> p c", p=128), (), ['nba'], allow_slow_non_contiguous=True)
    dma('sp', goutc[:, :], gout_d.rearrange("(p o) -> p o", o=1), (), ['goutc'], allow_slow_non_contiguous=True)
    dma('sp', gqa_c[:, :], gqa_d.rearrange("(c p) -> p c", p=128), (), ['gqa'], allow_slow_non_contiguous=True)
    dma('sp', gknc[0:64, :], gkn_d.rearrange("(p o) -> p o", o=1), (), ['gknc'], allow_slow_non_contiguous=True)
    dma('sp', gknc[64:128, :], gkn_d.rearrange("(p o) -> p o", o=1), (), ['gknc'], allow_slow_non_contiguous=True)
    for gt, gd, k in ((Gqn, gqnb_d, 'Gqn'), (Gqr, gqrb_d, 'Gqr'), (Gkva, gkvab_d, 'Gkva'), (Gkr, gkrb_d, 'Gkr'),
                      (Gkn, gknb_d, 'Gkn')):
        dma('sp', gt[:, :], gd, (), [k])
    dma('sp', cos_sb[:, :, :], c_cos_d, (), ['cos'])
    dma('sp', sin_sb[:, :, :], c_sin_d, (), ['sin'])
    dma('sp', rs_sb[:, :, :], c_rs_d, (), ['rs'])
    dma('sp', iota_sb[:, :], c_iota_d, (), ['iota'])
    dma('sp', PTi[:, :], ptb_d, (), ['PTi'])
    for b in range(4):
        for pr in range(2):
            dma('sp', S0_f[:, b, pr, :], sgla_d[b, pr], (), ['S0_f'])
    ts('dve', nba[:, :], nba[:, :], -1.0, None, ALU.mult, None, ['nba'], ['nba'])
    for c in range(2):
        ts('dve', wqup_sb[:, c, :], wqup_sb[:, c, :], gqa_c[:, c:c + 1], None, ALU.mult, None,
           ['wqup', 'gqa'], ['wqup'])
    cp('dve', wkn_sb[:, :].rearrange("p (h d) -> p h d", h=8),
       wkv_sb[:, :].rearrange("p (h e) -> p h e", h=8)[:, :, 0:64], ['wkv'], ['wkn'])
    memset('dve', S_f[:, :, :], 0.0, ['S_f'])
    memset('dve', nbl[:, :], 0.0, ['nbl'])
    memset('dve', qdT[:, :, :], 0.0, ['qdT'])
    memset('dve', QnT[:, :, :], 0.0, ['QnT'])
    memset('dve', QrT[:, :, :], 0.0, ['QrT'])
    memset('dve', KrT[:, :], 0.0, ['KrT'])
    memset('dve', KRT[:, :, :, :], 0.0, ['KRT0', 'KRT1'])
    memset('dve', accn[:, :], 0.0, ['accn'])
    dma('sp', rowm[:, :], c_rowm_d, (), ['rowm'])
    dma('sp', hm[:, :], c_hm_d, (), ['hm'])
    cp('dve', wvp_sb[:, :, :].rearrange("p c (a d) -> p c a d", a=2),
       wkv_sb[:, :].rearrange("p (c a e) -> p c a e", c=4, a=2)[:, :, :, 64:128], ['wkv'], ['wvp'])
    memset('dve', S_b[:, :, :], 0.0, ['S_b'])
    cp('dve', S0_b[:, :, :, :], S0_f[:, :, :, :], ['S0_f'], ['S0_b'])
    memset('dve', ckv_b[:, 128:132], 1.0, ['ckv_b1'])
    memset('dve', Cb[:, :, :, 128:132], 1.0, ['Cb1'])
    memset('dve', Vext[:, :, :, 64:65], 1.0, ['Vext1'])
    memset('dve', omT[:, :, ST * 128:], 0.0, ['omT'])
    ts('dve', IDX[:, :], PTi[:, :], 128.0, iota_sb[:, 0:1], ALU.mult, ALU.add, ['PTi', 'iota'], ['IDX'])

    def load_x(i):
        dma('sp', x_sb[:, i % 2, :], x_d[i], (), [f'x{i % 2}'])

    if os.environ.get("KDBG") == "2":
        load_x(0)
        dma('sp', y_o[0], x_sb[:, 0, :], ['x0'], ['y_o'])
        dma('pool', y_o[3][:, 0:1024], w_in_sb[:, 0, 0:1024], ['w_in'], ['y_o'])
        dma('sp', y_o[4][:, 0:128], identf[:, :], ['identf'], ['y_o'])
    chk(1)
    load_x(0)

    pmi = [0]

    def next_pm():
        pmi[0] ^= 1
        return pmi[0]

    for i in range(NT):
        samp = (i == ST)
        sl = i % 2
        xk = f'x{sl}'
        nk = f'nT{sl}'
        nTi = nT[:, sl, :, :]
        tcol = slice(i * 128, (i + 1) * 128)
        if i + 1 < NT:
            load_x(i + 1)
        norm_tile(x_sb[:, sl, :], [xk], gmixT, 'gmixT', nTi, [nk], 0)
        if i == 0 and os.environ.get("KDBG") == "3":
            dma('pool', y_o[2][:, 0:1024], nT[:, 0, :, :].rearrange("p c t -> p (c t)"), [nk], ['y_o'])
            dma('pool', y_o[4][:, 0:1024], xs[:, :], ['xs'], ['y_o'])
            dma('sp', y_o[5][:, 0:8], stat[:, :], ['stat'], ['y_o'])
            dma('sp', y_o[0], x_sb[:, 0, :], ['x0'], ['y_o'])
        if i == 0 and os.environ.get("KDBG") == "4":
            stg = x_sb[:, 1, :]
            cp('dve', stg, xs[:, :], ['xs'], ['x1'])
            dma('sp', y_o[4], stg, ['x1'], ['y_o'])
            cp('dve', stg, nT[:, 0, :, :].rearrange("p c t -> p (c t)"), [nk], ['x1'])
            dma('sp', y_o[2], stg, ['x1'], ['y_o'])
            cp('dve', x_sb[:, 1, 0:8], stat[:, :], ['stat'], ['x1'])
            cp('dve', x_sb[:, 1, 8:16], gmixT[:, :], ['gmixT'], ['x1'])
            dma('sp', y_o[5], stg, ['x1'], ['y_o'])
        chk(2, i == 0)

        j = next_pm()
        for k in range(8):
            mm(pm[j][0:16, 0:128], w_in_sb[:, k, O_A:O_A + 16], nTi[:, k, :], k == 0, k == 7,
               ['w_in', nk], [f'pm{j}'])
        cp('act', gaT[:, :], pm[j][0:16, 0:128], [f'pm{j}'], ['gaT'])
        chk(2.1, i == 0)
        j = next_pm()
        for pr in range(2):
            mm(pm[j][:, pr * 128:(pr + 1) * 128], wa2_sb[:, pr * 128:(pr + 1) * 128], gaT[:, :], True, True,
               ['wa2', 'gaT'], [f'pm{j}'])
        for pr in range(2):
            act(sp_[:, pr, :], pm[j][:, pr * 128:(pr + 1) * 128], AF.Exp, [f'pm{j}', 'nba'], ['sp'],
                bias=nba[:, pr:pr + 1], scale=-1.0)
        act(sp_[:, :, :], sp_[:, :, :], AF.Ln, ['sp'], ['sp'], bias=1.0, scale=1.0)
        chk(2.3, i == 0)
        rsi = 1 if samp else 0
        for pr in range(2):
            S.op('dve', lambda e, pr=pr, rsi=rsi: e.tensor_tensor_scan(bpos[:, pr, :], rs_sb[:, rsi, :], sp_[:, pr, :],
                                                              0.0, ALU.mult, ALU.add),
                 ['sp', 'rs'], ['bpos'])
        segs = [(b * 32, 32, b * 32 + 9, b) for b in range(4)] if samp else [(0, 128, 127, 0)]
        chk(2.4, i == 0)
        for (c0, ncol, lc, si) in segs:
            ts('dve', nbl[:, si * 2:si * 2 + 2], bpos[:, :, lc], -1.0 / 16, None, ALU.mult, None,
               ['bpos'], ['nbl'])
        act(ebl[:, :], nbl[:, :], AF.Exp, ['nbl'], ['ebl'])
        chk(2.5, i == 0)
        j = next_pm()
        for pr in range(2):
            for k in range(8):
                mm(pm[j][:, pr * 128:(pr + 1) * 128], w_in_sb[:, k, O_Q + pr * 128:O_Q + (pr + 1) * 128],
                   nTi[:, k, :], k == 0, k == 7, ['w_in', nk], [f'pm{j}'])
        act(etmp[:, :, :], bpos[:, :, :], AF.Exp, ['bpos'], ['etmp'], scale=-1.0 / 16)
        for hh in range(2):
            r = slice(hh * 64, hh * 64 + 64)
            stt(qdT[r, hh::2, :], pm[j][r, 0:256].rearrange("p (a t) -> p a t", a=2), 0.125, etmp[r, :, :],
                ALU.mult, ALU.mult, [f'pm{j}', 'etmp'], ['qdT'])
        chk(2.6, i == 0)
        j = next_pm()
        for pr in range(2):
            for k in range(8):
                mm(pm[j][:, pr * 128:(pr + 1) * 128], w_in_sb[:, k, O_K + pr * 128:O_K + (pr + 1) * 128],
                   nTi[:, k, :], k == 0, k == 7, ['w_in', nk], [f'pm{j}'])
        act(etmp[:, :, :], bpos[:, :, :], AF.Exp, ['bpos', 'qdT'], ['etmp'], scale=1.0 / 16)
        tt('dve', kiT[:, :, :], pm[j][:, 0:256].rearrange("p (a t) -> p a t", a=2), etmp[:, :, :], ALU.mult,
           [f'pm{j}', 'etmp'], ['kiT'])
        for (c0, ncol, lc, si) in segs:
            for pr in range(2):
                act(etmp[:, pr, c0:c0 + ncol], bpos[:, pr, c0:c0 + ncol], AF.Exp, ['bpos', 'nbl', 'kiT'],
                    ['etmp'], bias=nbl[:, si * 2 + pr:si * 2 + pr + 1], scale=1.0 / 16)
        tt('dve', keT[:, :, :], pm[j][:, 0:256].rearrange("p (a t) -> p a t", a=2), etmp[:, :, :], ALU.mult,
           [f'pm{j}', 'etmp'], ['keT'])
        for pr in range(2):
            tr(ptr[1][:, pr * 128:(pr + 1) * 128], keT[:, pr, :], ident[:, :], ['keT', 'ident'], ['ptr1'])
        cp('dve', ke_tok[:, :], ptr[1][:, 0:256], ['ptr1'], ['ke_tok'])
        chk(2.7, i == 0)
        j = next_pm()
        for c in range(4):
            for k in range(8):
                mm(pm[j][:, c * 128:(c + 1) * 128], w_in_sb[:, k, O_G + c * 128:O_G + (c + 1) * 128],
                   nTi[:, k, :], k == 0, k == 7, ['w_in', nk], [f'pm{j}'])
        act(sgT[:, :, :], pm[j][:, :].rearrange("p (c t) -> p c t", c=4), AF.Silu, [f'pm{j}'], ['sgT'])
        j = next_pm()
        for k in range(8):
            mm(pm[j][:, :], nTi[:, k, :], w_in_sb[:, k, O_V:O_V + 512], k == 0, k == 7, ['w_in', nk], [f'pm{j}'])
        cp('act', v_tok[:, :], pm[j][:, :], [f'pm{j}'], ['v_tok'])
        chk(2.8, i == 0)
        j = next_pm()
        for h in range(4):
            pr = h // 2
            mm(pm[j][:, h * 128:(h + 1) * 128], kiT[:, pr, :], qdT[:, h, :], True, True,
               ['kiT', 'qdT'], [f'pm{j}'])
        mk, mkey = (masks, 'masks') if samp else (mask01, 'mask01')
        tt('dve', AT[:, :, :], pm[j][:, :].rearrange("p (h t) -> p h t", h=4), bc_mid(mk[:, :], 4), ALU.mult,
           [f'pm{j}', mkey], ['AT'])
        chk(2.9, i == 0)
        j = next_pm()
        for h in range(4):
            pr, b0 = h // 2, (h % 2) * 64
            oh = pm[j][:, h * 128:(h + 1) * 128]
            mm(oh, v_tok[:, h * 128:(h + 1) * 128], AT[:, h, :], True, False, ['v_tok', 'AT'], [f'pm{j}'])
            if not samp:
                mm(oh, S_b[:, pr, :], qdT[:, h, :], False, True, ['S_b', 'qdT'], [f'pm{j}'])
            else:
                for b in range(4):
                    mm(pm[j][:, h * 128 + b * 32:h * 128 + (b + 1) * 32], S0_b[:, b, pr, :],
                       qdT[:, h, b * 32:(b + 1) * 32], False, b == 3, ['S0_b', 'qdT'], [f'pm{j}'])
        jo = j
        chk(2.95, i == 0)
        act(osq[:, :], pm[jo][:, :], AF.Square, [f'pm{jo}'], ['osq'])
        j = next_pm()
        mm(pm[j][:, :], ones_bf[:, :], osq[:, :], True, True, ['ones', 'osq'], [f'pm{j}'])
        rstd_chain(pm[j][:, :], orst[:, :], 1.0 / 128, [f'pm{j}'], ['orst'])
        tt('dve', orst[:, :], pm[jo][:, :], orst[:, :], ALU.mult, [f'pm{jo}', 'orst'], ['orst'])
        stt(ogT[:, :, tcol], orst[:, :].rearrange("p (h t) -> p h t", h=4), goutc[:, 0:1], sgT[:, :, :],
            ALU.mult, ALU.mult, ['orst', 'goutc', 'sgT'], ['ogT'])
        chk(2.97, i == 0)
        if not samp:
            for pr in range(2):
                j = next_pm()
                mm(pm[j][:, 0:256], ke_tok[:, pr * 128:(pr + 1) * 128], v_tok[:, pr * 256:(pr + 1) * 256],
                   True, True, ['ke_tok', 'v_tok'], [f'pm{j}'])
                ts('dve', orst[:, 0:128], pm[j][:, 0:128], hm[:, 0:1], None, ALU.mult, None, [f'pm{j}', 'hm'], ['orst'])
                stt(orst[:, 0:128], pm[j][:, 128:256], hm[:, 1:2], orst[:, 0:128], ALU.mult, ALU.add,
                    [f'pm{j}', 'hm', 'orst'], ['orst'])
                stt(S_f[:, pr, :], S_f[:, pr, :], ebl[:, pr:pr + 1], orst[:, 0:128], ALU.mult, ALU.add,
                    ['S_f', 'ebl', 'orst'], ['S_f'])
                chk(2.98, i == 0 and pr == 0)
                chk(2.99, i == 0 and pr == 1)
            cp('dve', S_b[:, :, :], S_f[:, :, :], ['S_f'], ['S_b'])
            if i == ST - 1:
                for pr in range(2):
                    dma('sp', glap_o[pr], S_f[:, pr, :], ['S_f'], ['glap_o'])
        else:
            for b in range(4):
                ts('dve', ke_z[:, b, :], ke_tok[:, :], rowm[:, b:b + 1], None, ALU.mult, None,
                   ['ke_tok', 'rowm'], ['ke_z'])
            for b in range(4):
                for pr in range(2):
                    j = next_pm()
                    mm(pm[j][:, 0:256], ke_z[:, b, pr * 128:(pr + 1) * 128],
                       v_tok[:, pr * 256:(pr + 1) * 256], True, True, ['ke_z', 'v_tok'], [f'pm{j}'])
                    ts('dve', orst[:, 0:128], pm[j][:, 0:128], hm[:, 0:1], None, ALU.mult, None, [f'pm{j}', 'hm'],
                       ['orst'])
                    stt(orst[:, 0:128], pm[j][:, 128:256], hm[:, 1:2], orst[:, 0:128], ALU.mult, ALU.add,
                        [f'pm{j}', 'hm', 'orst'], ['orst'])
                    stt(S0_f[:, b, pr, :], S0_f[:, b, pr, :], ebl[:, b * 2 + pr:b * 2 + pr + 1], orst[:, 0:128],
                        ALU.mult, ALU.add, ['S0_f', 'ebl', 'orst'], ['S0_f'])
                    dma('sp', glas_o[b, pr], S0_f[:, b, pr, :], ['S0_f'], ['glas_o'])

        chk(3, i == 0)
        j = next_pm()
        for k in range(8):
            mm(pm[j][:, 0:416], nTi[:, k, :], w_in_sb[:, k, O_MQ:O_MQ + 416], k == 0, k == 7,
               ['w_in', nk], [f'pm{j}'])
        cp('dve', u_sb[:, :], pm[j][:, 0:416], [f'pm{j}'], ['u_sb'])
        chk(3.1, i == 0)
        for (a, b_, col, n) in ((0, 256, 24, 256), (256, 384, 25, 128), (384, 416, 26, 32)):
            if True:
                act(sqq[:, a:b_], u_sb[:, a:b_], AF.Square, ['u_sb'], ['sqq', 'ms'], accum_out=ms[:, col:col + 1],
                    scale=float(n) ** -0.5)
            else:
                act(sqq[:, a:b_], u_sb[:, a:b_], AF.Square, ['u_sb'], ['sqq'], scale=float(n) ** -0.5)
                S.op('dve', lambda e, a=a, b_=b_, col=col: e.tensor_reduce(ms[:, col:col + 1], sqq[:, a:b_], AX.X, ALU.add),
                     ['sqq'], ['ms'])
        rstd_chain(ms[:, 24:27], rst[:, 24:27], 1.0, ['ms'], ['rst'])
        chk(3.2, i == 0)
        ts('dve', cq[:, :], u_sb[:, 0:256], rst[:, 24:25], None, ALU.mult, None, ['u_sb', 'rst'], ['cq'])
        cs = i % 2
        stt(ckv_f[:, cs, :], u_sb[:, 256:384], rst[:, 25:26], Gkva[:, :], ALU.mult, ALU.mult,
            ['u_sb', 'rst', 'Gkva'], [f'ckv_f{cs}'])
        dma('sp', ckv_o[i], ckv_f[:, cs, :], [f'ckv_f{cs}'], ['ckv_o'])
        cp('dve', ckv_b[:, 0:128], ckv_f[:, cs, :], [f'ckv_f{cs}'], ['ckv_b'])
        stt(Rn[:, 8, :], u_sb[:, 384:416], rst[:, 26:27], Gkr[:, :], ALU.mult, ALU.mult,
            ['u_sb', 'rst', 'Gkr'], ['Rn'])
        chk(3.3, i == 0)
        for c in range(2):
            tr(ptr[1][:, c * 128:(c + 1) * 128], cq[:, c * 128:(c + 1) * 128], ident[:, :], ['cq', 'ident'], ['ptr1'])
        tr(ptr[1][:, 256:384], ckv_b[:, 0:128], ident[:, :], ['ckv_b', 'ident'], ['ptr1'])
        cp('dve', cqT[:, :, :], ptr[1][:, 0:256].rearrange("p (c t) -> p c t", c=2), ['ptr1'], ['cqT'])
        cp('dve', ckvT[:, :], ptr[1][:, 256:384], ['ptr1'], ['ckvT'])
        chk(3.4, i == 0)
        for half, (n0, n1) in enumerate(((0, 512), (512, 768))):
            for c in range(2):
                mm(pw[0][:, n0:n1], cqT[:, c, :], wqup_sb[:, c, n0:n1], c == 0, c == 1, ['cqT', 'wqup'], ['pw0'])
        for half in range(2):
            mm(pw[1][:, half * 512:(half + 1) * 512], ckvT[:, :], wkv_sb[:, half * 512:(half + 1) * 512],
               True, True, ['ckvT', 'wkv'], ['pw1'])
        qfv = pw[0][:, 0:768].rearrange("p (h e) -> p h e", h=8)
        kvv = pw[1][:, :].rearrange("p (h e) -> p h e", h=8)
        chk(3.5, i == 0)
        act(sqq[:, :], pw[0][:, 0:768], AF.Square, ['pw0'], ['sqq'])
        sqv = sqq[:, :].rearrange("p (h e) -> p h e", h=8)
        S.op('dve', lambda e: e.tensor_reduce(ms[:, 0:8], sqv[:, :, 0:64], AX.X, ALU.add), ['sqq'], ['ms'])
        S.op('dve', lambda e: e.tensor_reduce(ms[:, 16:24], sqv[:, :, 64:96], AX.X, ALU.add), ['sqq'], ['ms'])
        act(sqk[:, :, :], kvv[:, :, 0:64], AF.Square, ['pw1'], ['sqk'])
        S.op('dve', lambda e: e.tensor_reduce(ms[:, 8:16], sqk[:, :, :], AX.X, ALU.add), ['sqk'], ['ms'])
        rstd_chain(ms[:, 0:16], rst[:, 0:16], 1.0 / 64, ['ms'], ['rst'])
        rstd_chain(ms[:, 16:24], rst[:, 16:24], 1.0 / 32, ['ms'], ['rst'])
        chk(3.6, i == 0)
        tt('dve', sqk[:, :, :], qfv[:, :, 0:64], bc_last(rst[:, 0:8], 64), ALU.mult, ['pw0', 'rst'], ['sqk'])
        tt('dve', q_tok[:, :, :], sqk[:, :, :], bc_mid(Gqn[:, :], 8), ALU.mult, ['sqk', 'Gqn'], ['q_tok'])
        tt('dve', sqk[:, :, :], kvv[:, :, 0:64], bc_last(rst[:, 8:16], 64), ALU.mult, ['pw1', 'rst', 'q_tok'],
           ['sqk'])
        tt('dve', k_tok[:, :, :], sqk[:, :, :], bc_mid(Gkn[:, :], 8), ALU.mult, ['sqk', 'Gkn'], ['k_tok'])
        chk(3.7, i == 0)
        if not samp:
            cp('act', Vext[:, i, :, 0:64], kvv[:, :, 64:128], ['pw1'], ['Vext'])
        tt('dve', Rn[:, 0:8, :], qfv[:, :, 64:96], bc_last(rst[:, 16:24], 32), ALU.mult, ['pw0', 'rst'], ['Rn'])
        tt('dve', Rn[:, 0:8, :], Rn[:, 0:8, :], bc_mid(Gqr[:, :], 8), ALU.mult, ['Rn', 'Gqr'], ['Rn'])
        cosb = bc_mid(cos_sb[:, i, :], 9)
        sinb = bc_mid(sin_sb[:, i, :], 9)
        x1, x2 = Rn[:, :, 0:16], Rn[:, :, 16:32]
        rk = f'Rr{cs}'
        tt('dve', rt1[:, :, :], x1, cosb, ALU.mult, ['Rn', 'cos'], ['rt1'])
        tt('dve', rt2[:, :, :], x2, sinb, ALU.mult, ['Rn', 'sin'], ['rt2'])
        tt('dve', Rr[:, cs, :, 0:16], rt1[:, :, :], rt2[:, :, :], ALU.subtract, ['rt1', 'rt2'], [rk])
        tt('dve', rt1[:, :, :], x2, cosb, ALU.mult, ['Rn', 'cos', rk], ['rt1'])
        tt('dve', rt2[:, :, :], x1, sinb, ALU.mult, ['Rn', 'sin', rk], ['rt2'])
        tt('dve', Rr[:, cs, :, 16:32], rt1[:, :, :], rt2[:, :, :], ALU.add, ['rt1', 'rt2'], [rk])
        dma('sp', kr_o[i], Rr[:, cs, 8, :], [rk], ['kr_o'])
        cp('dve', qr_tok[:, :, :], Rr[:, cs, 0:8, :], [rk], ['qr_tok'])
        cp('dve', kr_tok[:, :], Rr[:, cs, 8, :], [rk], ['kr_tok'])
        chk(3.8, i == 0)
        for c in range(4):
            tr(ptr[0][:, c * 128:(c + 1) * 128], q_tok[:, 2 * c:2 * c + 2, :], ident[:, :], ['q_tok', 'ident'], ['ptr0'])
            tr(ptr[0][:, 512 + c * 128:512 + (c + 1) * 128], k_tok[:, 2 * c:2 * c + 2, :], ident[:, :],
               ['k_tok', 'ident'], ['ptr0'])
        for hh in range(2):
            r = slice(hh * 64, hh * 64 + 64)
            cp('dve', QnT[r, hh::2, :], ptr[0][r, 0:512].rearrange("p (c t) -> p c t", c=4), ['ptr0'], ['QnT'])
        cp('dve', KnT[:, :, tcol], ptr[0][:, 512:1024].rearrange("p (c t) -> p c t", c=4), ['ptr0'], ['KnT'])
        for h in range(8):
            tr(ptr[1][0:32, h * 128:(h + 1) * 128], qr_tok[:, h, :], ident[:, :], ['qr_tok', 'ident'], ['ptr1'])
        cp('dve', QrT[0:32, :, :], ptr[1][0:32, :].rearrange("p (h t) -> p h t", h=8), ['ptr1'], ['QrT'])
        tr(ptr[1][0:32, 0:128], kr_tok[:, :], ident[:, :], ['kr_tok', 'ident', 'QrT'], ['ptr1'])
        cp('dve', KrT[0:32, tcol], ptr[1][0:32, 0:128], ['ptr1'], ['KrT'])

        if i == 0 and os.environ.get("KDBG"):
            dma('sp', y_o[0][:, 0:416], u_sb[:, :], ['u_sb'], ['y_o'])
            dma('sp', y_o[1][:, 0:32], rst[:, :], ['rst'], ['y_o'])
            dma('pool', y_o[2][:, 0:1024], nT[:, 0, :, :].rearrange("p c t -> p (c t)"), [nk], ['y_o'])
            dma('pool', y_o[3][:, 0:1024], w_in_sb[:, 0, 0:1024], ['w_in'], ['y_o'])
            dma('pool', y_o[4][:, 0:1024], xs[:, :], ['xs'], ['y_o'])
            dma('sp', y_o[5][:, 0:8], stat[:, :], ['stat'], ['y_o'])
            dma('sp', y_o[6][:, 0:256], ckv_f[:, :, :].rearrange("p a c -> p (a c)"), ['ckv_f0'], ['y_o'])
        if i == 0 and os.environ.get("KDBG") == "6":
            stg = x_sb[:, 1, :]
            cp('dve', x_sb[:, 1, 0:416], u_sb[:, :], ['u_sb'], ['x1'])
            cp('dve', x_sb[:, 1, 416:448], rst[:, :], ['rst'], ['x1'])
            cp('dve', x_sb[:, 1, 448:480], ms[:, :], ['ms'], ['x1'])
            cp('dve', x_sb[:, 1, 512:640], ckv_f[:, 0, :], ['ckv_f0'], ['x1'])
            cp('dve', x_sb[:, 1, 640:768], Gkva[:, :], ['Gkva'], ['x1'])
            cp('dve', x_sb[:, 1, 768:800], Rr[:, 0, 8, :], ['Rr0'], ['x1'])
            dma('sp', y_o[4], stg, ['x1'], ['y_o'])
        chk(4, i == 0)
        chk(7, samp)
        if not samp:
            groups = []
            for h in range(8):
                for kt0 in range(0, i + 1, 4):
                    groups.append((h, kt0, min(4, i + 1 - kt0)))

            def emit_scores(gi):
                h, kt0, nkt = groups[gi]
                pr, b0 = h // 2, (h % 2) * 64
                s = gi % 2
                for jj in range(nkt):
                    kt = kt0 + jj
                    kc = slice(kt * 128, (kt + 1) * 128)
                    o = pm[s][:, jj * 128:(jj + 1) * 128]
                    mm(o, KnT[:, pr, kc], QnT[:, h, :], True, False, ['KnT', 'QnT'], [f'pm{s}'])
                    mm(o, KrT[:, kc], QrT[:, h, :], False, True, ['KrT', 'QrT'], [f'pm{s}'])
                act(PT[:, s, 0:nkt, :], pm[s][:, 0:nkt * 128].rearrange("p (a t) -> p a t", a=nkt), AF.Exp,
                    [f'pm{s}'], [f'PT{s}'], scale=MLA_SCALE)
                if kt0 + nkt - 1 == i:
                    tt('dve', PT[:, s, nkt - 1, :], PT[:, s, nkt - 1, :], mask01[:, :], ALU.mult,
                       [f'PT{s}', 'mask01'], [f'PT{s}'])

            def emit_pv(gi):
                h, kt0, nkt = groups[gi]
                s = gi % 2
                bank = h // 4
                o = pw[bank][:, (h % 4) * 65:(h % 4) * 65 + 65]
                for jj in range(nkt):
                    kt = kt0 + jj
                    mm(o, PT[:, s, jj, :], Vext[:, kt, h, :], kt == 0, kt == i, [f'PT{s}', 'Vext', 'Vext1'],
                       [f'pw{bank}'])

            emit_scores(0)
            for gi in range(len(groups)):
                if gi + 1 < len(groups):
                    emit_scores(gi + 1)
                emit_pv(gi)
            for bank in range(2):
                ov = pw[bank][:, 0:260].rearrange("p (h e) -> p h e", h=4)
                S.op('dve', lambda e, ov=ov, bank=bank: e.reciprocal(rl[:, bank * 4:bank * 4 + 4], ov[:, :, 64]),
                     [f'pw{bank}'], ['rl'])
                tt('dve', o_tok[:, bank * 4:bank * 4 + 4, :], ov[:, :, 0:64], bc_last(rl[:, bank * 4:bank * 4 + 4], 64),
                   ALU.mult, [f'pw{bank}', 'rl'], ['o_tok'])
            for c in range(4):
                tr(ptr[0][:, c * 128:(c + 1) * 128], o_tok[:, 2 * c:2 * c + 2, :], ident[:, :], ['o_tok', 'ident'],
                   ['ptr0'])
            cp('dve', omT[:, :, tcol], ptr[0][:, 0:512].rearrange("p (c t) -> p c t", c=4), ['ptr0'], ['omT'])
            chk(5, i == 0)
            chk(6, i == ST - 1)
        elif not _SKIP_PA:
            for pr in range(4):
                tr(ptr[0][:, pr * 128:(pr + 1) * 128], wkn_sb[:, pr * 128:(pr + 1) * 128], ident[:, :],
                   ['wkn', 'ident'], ['ptr0'])
            cp('dve', WkT[:, :, :], ptr[0][:, 0:512].rearrange("p (c t) -> p c t", c=4), ['ptr0'], ['WkT'])
            ts('dve', qgT[:, :, :], QnT[:, :, :], gknc[:, 0:1], None, ALU.mult, None, ['QnT', 'gknc'], ['qgT'])
            for h in range(8):
                pr, b0 = h // 2, (h % 2) * 64
                bank = h // 4
                mm(pw[bank][:, (h % 4) * 128:(h % 4 + 1) * 128], WkT[:, pr, :], qgT[:, h, :],
                   True, True, ['WkT', 'qgT'], [f'pw{bank}'])
            for bank in range(2):
                cp('dve', qabsT[:, bank * 4:bank * 4 + 4, :],
                   pw[bank][:, 0:512].rearrange("p (h t) -> p h t", h=4), [f'pw{bank}'], ['qabsT'])
            cp('dve', rstk[:, 0:8], rst[:, 8:16], ['rst'], ['rstk_new'])
            chk(7.1)

            def gather(b, g):
                s = g % 2
                for jj in range(4):
                    col = b * NPG + g * 4 + jj
                    S.op('pool', lambda e, s=s, jj=jj, col=col: e.indirect_dma_start(
                        out=Cbuf[:, s, jj, :], out_offset=None, in_=cc_d,
                        in_offset=bass.IndirectOffsetOnAxis(ap=IDX[:, col:col + 1], axis=0)),
                        ['IDX'], [f'Cbuf{s}'], dma=True)

            def score_pv(b, lhs_c, lhs_r, rstd_ap, nj, rhs_pv, keys_r, first, last, maskap=None):
                qa = qabsT[:, :, b * 32 + 2:b * 32 + 10]
                qr = QrT[:, :, b * 32 + 2:b * 32 + 10]
                for jj in range(nj):
                    mm(pm[0][:, jj * 64:(jj + 1) * 64], lhs_c(jj), qa, True, True, keys_r + ['qabsT'], ['pm0'])
                    mm(pm[0][:, 256 + jj * 64:256 + (jj + 1) * 64], lhs_r(jj), qr, True, True, keys_r + ['QrT'],
                       ['pm0'])
                n = nj * 64
                tt('dve', sc1[:, 0:n].rearrange("p (a t) -> p a t", t=8),
                   pm[0][:, 0:n].rearrange("p (a t) -> p a t", t=8), bc_last(rstd_ap, 8), ALU.mult,
                   ['pm0', 'rstk', 'rstk_new'] + [k for k in keys_r if k.startswith('rk')], ['sc1'])
                tt('dve', sc2[:, 0:n], sc1[:, 0:n], pm[0][:, 256:256 + n], ALU.add, ['sc1', 'pm0'], ['sc2'])
                return n

            def stageA(b, g):
                s = g % 2
                cp('act', Cb[:, s, :, 0:128], Cbuf[:, s, :, 0:128], [f'Cbuf{s}'], [f'Cb{s}'])
                cp('dve', KRb[:, s, :, :], Cbuf[:, s, :, 128:160], [f'Cbuf{s}'], [f'KRb{s}'])
                for jj in range(4):
                    tr(ptr[0][:, jj * 128:(jj + 1) * 128], Cb[:, s, jj, 0:128], ident[:, :],
                       [f'Cb{s}', 'ident'], ['ptr0'])
                    tr(ptr[0][0:32, 512 + jj * 128:512 + (jj + 1) * 128], KRb[:, s, jj, :], ident[:, :],
                       [f'KRb{s}', 'ident'], ['ptr0'])
                cp('dve', CT[:, s, :, :], ptr[0][:, 0:512].rearrange("p (a t) -> p a t", a=4), ['ptr0'],
                   [f'CT{s}'])
                cp('dve', KRT[0:32, s, :, :], ptr[0][0:32, 512:1024].rearrange("p (a t) -> p a t", a=4), ['ptr0'],
                   [f'KRT{s}'])
                for hf in range(2):
                    for q2 in range(2):
                        jj = hf * 2 + q2
                        mm(pw[hf][:, q2 * 512:(q2 + 1) * 512], CT[:, s, jj, :], wkn_sb[:, :], True, True,
                           [f'CT{s}', 'wkn'], [f'pw{hf}'])
                    act(sqs[:, hf, :], pw[hf][:, :], AF.Square, [f'pw{hf}'], [f'sqs{hf}'])
                    S.op('dve', lambda e, hf=hf: e.tensor_reduce(
                        ssum[:, hf * 16:(hf + 1) * 16], sqs[:, hf, :].rearrange("p (a d) -> p a d", d=64),
                        AX.X, ALU.add), [f'sqs{hf}'], ['ssum'])
                rstd_chain(ssum[:, 0:32], rstk2[:, s, :], 1.0 / 64, ['ssum'], [f'rk{s}'])

            def stageB(b, g):
                s = g % 2
                score_pv(b, lambda jj: CT[:, s, jj, :], lambda jj: KRT[:, s, jj, :], rstk2[:, s, :], 4,
                         None, [f'CT{s}', f'KRT{s}', f'rk{s}'], None, None)
                act(PTs[:, s, :, :], sc2[:, 0:256].rearrange("p (a t) -> p a t", a=4), AF.Exp, ['sc2'],
                    [f'PTs{s}'], scale=MLA_SCALE)
                for jj in range(4):
                    mm(pm[1][0:64, 0:129], PTs[:, s, jj, :], Cb[:, s, jj, 0:129], g == 0 and jj == 0, False,
                       [f'PTs{s}', f'Cb{s}', 'Cb1'], ['pm1'])

            NG = NPG // 4
            for b in range(4):
                gather(b, 0)
                gather(b, 1)
                stageA(b, 0)
                for g in range(NG):
                    if g + 2 < NG:
                        gather(b, g + 2)
                    if g + 1 < NG:
                        stageA(b, g + 1)
                    stageB(b, g)
                cp('dve', rstk[:, 0:8], rst[:, 8:16], ['rst'], ['rstk'])
                score_pv(b, lambda jj: ckvT[:, :], lambda jj: KrT[:, tcol], rstk[:, 0:8], 1, None,
                         ['ckvT', 'KrT'], None, None)
                act(PTs[:, 0, 0, :], sc2[:, 0:64], AF.Exp, ['sc2'], ['PTs0'], scale=MLA_SCALE)
                tt('dve', PTs[:, 0, 0, :], PTs[:, 0, 0, :], maska[:, b, :], ALU.mult, ['PTs0', 'maska'], ['PTs0'])
                mm(pm[1][0:64, 0:129], PTs[:, 0, 0, :], ckv_b[:, 0:129], False, True, ['PTs0', 'ckv_b', 'ckv_b1'],
                   ['pm1'])
                cp('dve', acc_sb[:, 0:129], pm[1][0:64, 0:129], ['pm1'], ['acc_sb'])
                S.op('dve', lambda e: e.reciprocal(acc_sb[:, 130:131], acc_sb[:, 128:129]), ['acc_sb'], ['acc_sb'])
                ts('dve', accn[0:64, :], acc_sb[:, 0:128], acc_sb[:, 130:131], None, ALU.mult, None, ['acc_sb'], ['accn'])
                tr(ptr[1][:, 0:128], accn[:, :], ident[:, :], ['accn', 'ident'], ['ptr1'])
                cp('dve', accnT[:, :], ptr[1][:, 0:128], ['ptr1'], ['accnT'])
                for c in range(4):
                    mm(pm[0][:, c * 16:(c + 1) * 16], wvp_sb[:, c, :], accnT[:, c * 16:(c + 1) * 16], True, True,
                       ['wvp', 'accnT'], ['pm0'])
                ocol = slice(ST * 128 + b * 32 + 2, ST * 128 + b * 32 + 10)
                pv4 = pm[0][:, 0:64].rearrange("p (c t) -> p c t", c=4)
                cp('dve', omT[0:64, :, ocol], pv4[0:64, :, 0:8], ['pm0'], ['omT'])
                cp('dve', omT[64:128, :, ocol], pv4[64:128, :, 8:16], ['pm0'], ['omT'])
                chk(7.6, b == 0)

    if os.environ.get("KDBG") == "7":
        for tt_ in range(2):
            cp('dve', x_sb[:, 0, 0:512].rearrange("p (c t) -> p c t", c=4), ogT[:, :, tt_ * 128:(tt_ + 1) * 128],
               ['ogT'], ['x0'])
            cp('dve', x_sb[:, 0, 512:1024].rearrange("p (c t) -> p c t", c=4), omT[:, :, tt_ * 128:(tt_ + 1) * 128],
               ['omT'], ['x0'])
            dma('sp', y_o[4 + tt_], x_sb[:, 0, :], ['x0'], ['y_o'])
    chk(8)
    S.barrier()
    A.off = persist_off
    wd_sb = A.alloc("wd", [128, NF, D], BF16)
    wo_sb = A.alloc("wo", [128, 8, D], BF16)
    wpp_sb = A.alloc("wpp", [128, 2, D], BF16)
    wst = A.alloc("wst", [128, 2, 2, 8, 512], BF16)
    h_g = A.alloc("h_g", [128, 4, D], F32)
    nT2 = A.alloc("nT2", [128, 8, 512], BF16)
    hidT = A.alloc("hidT", [128, NF, 512], BF16)
    a_sb2 = A.alloc("a_sb", [128, 2, 520], F32)
    tbuf2 = A.alloc("tbuf", [128, 2, 512], F32)
    big = A.alloc("big", [128, D], F32)
    p_sb = A.alloc("p_sb", [128, 256], F32)
    p_bf = A.alloc("p_bf", [128, 256], BF16)
    pT = A.alloc("pT", [128, 2, 128], BF16)
    cw_sb = A.alloc("cw", [128, 3, NF], F32)
    cb_sb = A.alloc("cb", [128, NF], F32)
    halo = A.alloc("halo", [128, NF, 2], F32)
    cst = A.alloc("cst", [128, NF, 8], F32)
    sc_bf = A.alloc("sc_bf", [128, DFF], BF16)
    sel_sb = A.alloc("sel", [128, 128], BF16)
    ctok = A.alloc("ctok", [8, 512], F32)

    dma('pool', wo_sb[:, :, :], wo_d.rearrange("(k p) n -> p k n", p=128), (), ['wo'])

    def load_wd():
        wdv = wd_d.rearrange("(f p) n -> p f n", p=128)
        for f0 in (0, 8, 16):
            f1 = min(NF, f0 + 8)
            dma('pool', wd_sb[:, f0:f1, :], wdv[:, f0:f1, :], (), ['wd'])
    dma('sp', wpp_sb[:, :, :], wpp_d.rearrange("(c p) n -> p c n", p=128), (), ['wpp'])
    dma('sp', cw_sb[:, :, :], cw_d.rearrange("j (f p) -> p j f", p=128), (), ['cw'], allow_slow_non_contiguous=True)
    dma('sp', cb_sb[:, :], cb_d.rearrange("(f p) -> p f", p=128), (), ['cb'], allow_slow_non_contiguous=True)
    memset('dve', sc_bf[:, :], 0.0, ['sc_bf'])
    dma('sp', sc_bf[0:8, :], sconv_d, ['sc_bf'], ['sc_bf'])
    dma('sp', sel_sb[:, :], c_sel_d, (), ['sel'])
    memset('dve', halo[:, :, :], 0.0, ['halo'])
    memset('dve', a_sb2[:, :, :], 0.0, ['a_sb0', 'a_sb1'])

    blocks = [(0, 512), (512, 512), (1024, 512), (1536, 512), (2048, 512), (2560, 256)]
    wst_cnt = [0]

    def load_ffn_block(bi):
        c0, w = blocks[bi]
        s = wst_cnt[0] % 2
        wst_cnt[0] += 1
        for gu, wdram in ((0, wg_d), (1, wu_d)):
            dma('pool', wst[:, s, gu, :, 0:w], wdram.rearrange("(k p) n -> p k n", p=128)[:, :, c0:c0 + w], (),
                [f'wst{s}'])
        return s

    def load_ple_gate():
        s = wst_cnt[0] % 2
        wst_cnt[0] += 1
        for half in range(2):
            dma('pool', wst[:, s, half, :, :],
                wpg_d.rearrange("(k p) n -> p k n", p=128)[:, :, half * 512:(half + 1) * 512], (), [f'wst{s}'])
        return s

    tgroups = [[0, 1, 2, 3], [4, 5, 6, 7], [8, 9, 10, 11], [12, 13, 14, 15], [ST]]
    for gi, tiles in enumerate(tgroups):
        samp = (tiles[0] == ST)
        N = 128 * len(tiles)
        slot0 = load_ffn_block(0)
        if gi == 0:
            load_wd()
        for ti, t in enumerate(tiles):
            tcol = slice(t * 128, (t + 1) * 128)
            dma('sp', big[:, :], x_d[t], (), ['big'])
            for half in range(2):
                for c in range(8):
                    src = ogT if c < 4 else omT
                    mm(pw[0][:, half * 512:(half + 1) * 512], src[:, c % 4, tcol],
                       wo_sb[:, c, half * 512:(half + 1) * 512], c == 0, c == 7, ['ogT', 'omT', 'wo'], ['pw0'])
            tt('dve', h_g[:, ti, :], big[:, :], pw[0][:, :], ALU.add, ['big', 'pw0'], [f'h{ti}'])
            norm_tile(h_g[:, ti, :], [f'h{ti}'], gffnT, 'gffnT', nT2[:, :, ti * 128:(ti + 1) * 128], ['nT2'], 0)
        chk(9, gi == 0)
        for bi, (c0, w) in enumerate(blocks):
            s = slot0 if bi == 0 else s_next
            if bi + 1 < len(blocks):
                s_next = load_ffn_block(bi + 1)
            else:
                s_pg = load_ple_gate()
            for fi in range(w // 128):
                f = c0 // 128 + fi
                fc = slice(fi * 128, (fi + 1) * 128)
                par = f % 2
                pg, pgk = (pm[0], 'pm0') if par == 0 else (pw[0], 'pw0')
                pu, puk = (pm[1], 'pm1') if par == 0 else (pw[1], 'pw1')
                a_sb = a_sb2[:, par, :]
                tbuf = tbuf2[:, par, :]
                ak, tk = f'a_sb{par}', f'tbuf{par}'
                for k in range(8):
                    mm(pg[:, 0:N], wst[:, s, 0, k, fc], nT2[:, k, 0:N], k == 0, (k == 7 and not samp),
                       [f'wst{s}', 'nT2'], [pgk])
                if samp:
                    mm(pg[:, 0:N], sc_bf[:, f * 128:(f + 1) * 128], sel_sb[:, :], False, True,
                       ['sc_bf', 'sel'], [pgk])
                for k in range(8):
                    mm(pu[:, 0:N], wst[:, s, 1, k, fc], nT2[:, k, 0:N], k == 0, k == 7, [f'wst{s}', 'nT2'], [puk])
                if not samp:
                    cp('dve', a_sb[:, 0:2], halo[:, f, :], ['halo'], [ak])
                cp('act', a_sb[:, 2:2 + N], pg[:, 0:N], [pgk], [ak])
                if not samp:
                    cp('dve', halo[:, f, :], a_sb[:, N:N + 2], [ak], ['halo'])
                else:
                    cp('dve', cst[:, f, :].rearrange("p (b j) -> p b j", b=4),
                       a_sb[:, 2:2 + 128].rearrange("p (b c) -> p b c", b=4)[:, :, 8:10], [ak], ['cst'])
                ts('dve', tbuf[:, 0:N], a_sb[:, 0:N], cw_sb[:, 0, f:f + 1], cb_sb[:, f:f + 1], ALU.mult, ALU.add,
                   [ak, 'cw', 'cb'], [tk])
                stt(tbuf[:, 0:N], a_sb[:, 1:1 + N], cw_sb[:, 1, f:f + 1], tbuf[:, 0:N], ALU.mult, ALU.add,
                    [ak, 'cw', tk], [tk])
                stt(tbuf[:, 0:N], a_sb[:, 2:2 + N], cw_sb[:, 2, f:f + 1], tbuf[:, 0:N], ALU.mult, ALU.add,
                    [ak, 'cw', tk], [tk])
                act(tbuf[:, 0:N], tbuf[:, 0:N], AF.Silu, [tk], [tk])
                tt('dve', hidT[:, f, 0:N], tbuf[:, 0:N], pu[:, 0:N], ALU.mult, [tk, puk], ['hidT'])
        if gi == 3 or samp:
            W = 8 if samp else 2
            stg = cst if samp else halo
            skey = 'cst' if samp else 'halo'
            for q6 in range(6):
                nf = min(4, NF - q6 * 4)
                for ff in range(nf):
                    f = q6 * 4 + ff
                    S.op('pe', lambda e, f=f, ff=ff, W=W, stg=stg: e.transpose(pm[0][0:W, ff * 128:(ff + 1) * 128],
                                                                 stg[:, f, 0:W], identf[:, :]),
                         [skey, 'identf'], ['pm0'])
                cp('dve', ctok[0:W, 0:nf * 128], pm[0][0:W, 0:nf * 128], ['pm0'], ['ctok'])
                dma('sp', (convs_o if samp else convp_o)[:, q6 * 512:q6 * 512 + nf * 128], ctok[0:W, 0:nf * 128],
                    ['ctok'], ['conv_o'])
        chk(10, gi == 0)
        for ti, t in enumerate(tiles):
            tc2 = slice(ti * 128, (ti + 1) * 128)
            for half in range(2):
                for f in range(NF):
                    mm(pw[0][:, half * 512:(half + 1) * 512], hidT[:, f, tc2], wd_sb[:, f, half * 512:(half + 1) * 512],
                       f == 0, f == NF - 1, ['hidT', 'wd'], ['pw0'])
            tt('dve', h_g[:, ti, :], h_g[:, ti, :], pw[0][:, :], ALU.add, [f'h{ti}', 'pw0'], [f'h{ti}'])
        for ti, t in enumerate(tiles):
            tc2 = slice(ti * 128, (ti + 1) * 128)
            norm_tile(h_g[:, ti, :], [f'h{ti}'], gpleT, 'gpleT', nT2[:, :, tc2], ['nT2'], 0)
            dma('sp', p_sb[:, :], p_d[t], (), ['p_sb'])
            cp('dve', p_bf[:, :], p_sb[:, :], ['p_sb'], ['p_bf'])
            for c in range(2):
                tr(ptr[1][:, c * 128:(c + 1) * 128], p_bf[:, c * 128:(c + 1) * 128], ident[:, :], ['p_bf', 'ident'],
                   ['ptr1'])
            cp('dve', pT[:, :, :], ptr[1][:, 0:256].rearrange("p (c t) -> p c t", c=2), ['ptr1'], ['pT'])
            for half in range(2):
                for k in range(8):
                    mm(pw[0][:, half * 512:(half + 1) * 512], nT2[:, k, tc2], wst[:, s_pg, half, k, :],
                       k == 0, k == 7, ['nT2', f'wst{s_pg}'], ['pw0'])
                for c in range(2):
                    mm(pw[1][:, half * 512:(half + 1) * 512], pT[:, c, :], wpp_sb[:, c, half * 512:(half + 1) * 512],
                       c == 0, c == 1, ['pT', 'wpp'], ['pw1'])
            act(big[:, :], pw[0][:, :], AF.Sigmoid, ['pw0'], ['big'])
            tt('dve', big[:, :], big[:, :], pw[1][:, :], ALU.mult, ['big', 'pw1'], ['big'])
            tt('dve', big[:, :], big[:, :], h_g[:, ti, :], ALU.add, ['big', f'h{ti}'], ['big'])
            dma('sp', y_o[t], big[:, :], ['big'], ['y_o'])
        chk(11, gi == 0)
        chk(12, gi == 3)

    return S


_NC_CACHE = {}


def _consts():
    bf = ml_dtypes.bfloat16
    c = {}
    c['c_ident'] = np.eye(128, dtype=np.float32).astype(bf)
    c['c_identf'] = np.eye(128, dtype=np.float32)
    s = np.arange(128)
    c['c_mask'] = (s[:, None] <= s[None, :]).astype(np.float32).astype(bf)
    grp = s // 32
    m = (grp[:, None] == grp[None, :]) & (s[:, None] <= s[None, :])
    c['c_masks'] = m.astype(np.float32).astype(bf)
    ma = np.zeros((128, 4, 8, 8), np.float32)
    for b in range(4):
        for r in range(8):
            for t in range(8):
                if r <= t:
                    ma[b * 32 + 2 + r, b, :, t] = 1.0
    c['c_maska'] = ma.reshape(128, 4, 64).astype(bf)
    inv = 10000.0 ** (-np.arange(16, dtype=np.float32) * 2.0 / 32)
    pos = np.zeros((128, NT), np.float64)
    for i in range(16):
        pos[:, i] = i * 128 + s
    for p in range(128):
        t = (p % 32) - 2
        pos[p, ST] = 16384 + (t if 0 <= t < 8 else 0)
    ang = (pos.astype(np.float32)[:, :, None] * inv[None, None, :]).astype(np.float32)
    c['c_cos'] = np.cos(ang.astype(np.float64)).astype(np.float32)
    c['c_sin'] = np.sin(ang.astype(np.float64)).astype(np.float32)
    rs = np.ones((128, 2, 128), np.float32)
    rs[:, 0, 0] = 0.0
    for b in range(4):
        rs[:, 1, b * 32 + 2] = 0.0
    c['c_rs'] = rs
    c['c_iota'] = s.astype(np.float32).reshape(128, 1)
    rowm = np.zeros((128, 4), np.float32)
    for b in range(4):
        rowm[b * 32:(b + 1) * 32, b] = 1.0
    c['c_rowm'] = rowm
    hm = np.zeros((128, 2), np.float32)
    hm[:64, 0] = 1.0
    hm[64:, 1] = 1.0
    c['c_hm'] = hm
    sel = np.zeros((128, 128), np.float32)
    for b in range(4):
        for j in range(2):
            sel[b * 2 + j, b * 32 + j] = 1.0
    c['c_sel'] = sel.astype(bf)
    return c


def kernel(x_prompt, x_sample, cache_ckv, cache_krope, state_gla, state_conv, page_table,
           p_prompt, p_sample, g_mix, w_in, gla_w_a2, gla_b_a, gla_g_out, mla_g_qa, mla_w_qup,
           mla_g_qn, mla_g_qr, mla_g_kva, mla_g_kr, mla_w_kvup, mla_g_kn, w_o, g_ffn,
           ffn_w_gate, ffn_w_up, ffn_conv_w, ffn_conv_b, ffn_w_down, g_ple, ple_w_gate, ple_w_proj):
    f32 = np.float32
    A_ = lambda a: np.ascontiguousarray(np.asarray(a))
    if 'nc' not in _NC_CACHE:
        _NC_CACHE['nc'] = build_program()
    nc = _NC_CACHE['nc']
    consts = _consts()
    shared = {
        'cache_cat': np.ascontiguousarray(np.concatenate(
            [A_(cache_ckv).reshape(5120 * 128, 128)[:_NPHYS * 128],
             A_(cache_krope).reshape(5120 * 128, 32)[:_NPHYS * 128]], axis=1)),
        'w_in': A_(w_in[0]), 'gla_w_a2': A_(gla_w_a2[0]), 'gla_b_a': A_(gla_b_a[0]), 'gla_g_out': A_(gla_g_out[0]),
        'mla_g_qa': A_(mla_g_qa[0]), 'mla_w_qup': A_(mla_w_qup[0]), 'mla_g_qn': A_(mla_g_qn[0]),
        'mla_g_qr': A_(mla_g_qr[0]), 'mla_g_kva': A_(mla_g_kva[0]), 'mla_g_kr': A_(mla_g_kr[0]),
        'mla_w_kvup': A_(mla_w_kvup[0]), 'mla_g_kn': A_(mla_g_kn[0]), 'w_o': A_(w_o[0]), 'g_mix': A_(g_mix[0]),
        'g_ffn': A_(g_ffn[0]), 'g_ple': A_(g_ple[0]), 'ffn_w_gate': A_(ffn_w_gate[0]), 'ffn_w_up': A_(ffn_w_up[0]),
        'ffn_conv_w': A_(ffn_conv_w[0]), 'ffn_conv_b': A_(ffn_conv_b[0]), 'ffn_w_down': A_(ffn_w_down[0]),
        'ple_w_gate': A_(ple_w_gate[0]), 'ple_w_proj': A_(ple_w_proj[0]),
    }
    def _bc(a):
        a = A_(a)
        return np.ascontiguousarray(np.broadcast_to(a[None, :], (128, a.shape[0])))
    shared['b_gqn'] = _bc(mla_g_qn[0])
    shared['b_gqr'] = _bc(mla_g_qr[0])
    shared['b_gkva'] = _bc(mla_g_kva[0])
    shared['b_gkr'] = _bc(mla_g_kr[0])
    shared['b_gkn'] = _bc(mla_g_kn[0])
    shared.update(consts)
    xp = A_(x_prompt)
    xsm = A_(x_sample)
    pp = A_(p_prompt)[0]
    psm = A_(p_sample)[0]
    in_maps = []
    rows = np.array([b * 32 + 2 + t for b in range(4) for t in range(8)])
    for c in range(8):
        x = np.zeros((NT, 128, D), f32)
        x[:16] = xp[c].reshape(16, 128, D)
        x[ST][rows] = xsm[4 * c:4 * c + 4].reshape(32, D)
        p = np.zeros((NT, 128, 256), f32)
        p[:16] = pp[c].reshape(16, 128, 256)
        p[ST][rows] = psm[4 * c:4 * c + 4].reshape(32, 256)
        m = dict(shared)
        m['x'] = x
        m['p'] = p
        m['state_gla'] = A_(state_gla[0][4 * c:4 * c + 4]).reshape(4, 2, 128, 128)
        m['state_conv'] = A_(state_conv[0][4 * c:4 * c + 4]).reshape(8, DFF)
        m['page_table'] = A_(page_table[4 * c:4 * c + 4]).reshape(512).astype(np.int32)
        if _SMALLPA:
            m['page_table'] = m['page_table'] % 8
        m['b_pt'] = _bc(m['page_table'])
        in_maps.append(m)
    res = run_bass_kernel_spmd(nc, in_maps, core_ids=list(range(8))).results

    y_p = np.zeros((8, 2048, D), f32)
    y_s = np.zeros((32, 8, D), f32)
    ckv_p = np.zeros((1, 8, 2048, 128), f32)
    kr_p = np.zeros((1, 8, 2048, 32), f32)
    gla_p = np.zeros((1, 8, 4, 64, 128), f32)
    conv_p = np.zeros((1, 8, 2, DFF), f32)
    ckv_s = np.zeros((1, 32, 8, 128), f32)
    kr_s = np.zeros((1, 32, 8, 32), f32)
    gla_s = np.zeros((1, 32, 4, 64, 128), f32)
    conv_s = np.zeros((1, 32, 2, DFF), f32)
    for c in range(8):
        r = res[c]
        y = np.asarray(r['y'])
        y_p[c] = y[:16].reshape(2048, D)
        y_s[4 * c:4 * c + 4] = y[ST][rows].reshape(4, 8, D)
        ck = np.asarray(r['ckv_o'])
        ckv_p[0, c] = ck[:16].reshape(2048, 128)
        ckv_s[0, 4 * c:4 * c + 4] = ck[ST][rows].reshape(4, 8, 128)
        kr = np.asarray(r['kr_o'])
        kr_p[0, c] = kr[:16].reshape(2048, 32)
        kr_s[0, 4 * c:4 * c + 4] = kr[ST][rows].reshape(4, 8, 32)
        gla_p[0, c] = np.asarray(r['gla_p']).reshape(4, 64, 128)
        gla_s[0, 4 * c:4 * c + 4] = np.asarray(r['gla_s']).reshape(4, 4, 64, 128)
        conv_p[0, c] = np.asarray(r['conv_p'])
        conv_s[0, 4 * c:4 * c + 4] = np.asarray(r['conv_s']).reshape(4, 2, DFF)
    return (y_p, y_s, ckv_p, kr_p, gla_p, conv_p, ckv_s, kr_s, gla_s, conv_s)
```
